# Optimizing a Trainium2 kernel written in Bass

```python
import math
import jax, jax.numpy as jnp
from jax import lax
import numpy as np

D_MODEL = 1024
BATCH = 8
SEQ = 4096
DEPTH = 2

N_META = 16
Q_BLOCK = 128
NORM_EPS = 1e-6
D_FF = 2816

LRU_WIDTH = 256
LRU_BLOCKS = 4
LRU_BLOCK_DIM = LRU_WIDTH // LRU_BLOCKS
CONV_WIDTH = 4
LRU_C = 8.0
LRU_A_MIN = 0.9
LRU_A_MAX = 0.999

DIFF_HEADS = 4
DIFF_QK_DIM = 64
DIFF_V_DIM = 2 * DIFF_QK_DIM

MLA_HEADS = 4
MLA_Q_LORA = 256
MLA_KV_LORA = 128
MLA_NOPE_DIM = 64
MLA_ROPE_DIM = 32
MLA_V_DIM = 64
ROPE_THETA = 10000.0

IN_WIDTHS = (LRU_WIDTH, LRU_WIDTH,
             2 * DIFF_HEADS * DIFF_QK_DIM, 2 * DIFF_HEADS * DIFF_QK_DIM, DIFF_HEADS * DIFF_V_DIM,
             MLA_Q_LORA, MLA_KV_LORA, MLA_ROPE_DIM)
D_IN = 2 * LRU_WIDTH + 4 * DIFF_HEADS * DIFF_QK_DIM + DIFF_HEADS * DIFF_V_DIM + MLA_Q_LORA + MLA_KV_LORA + MLA_ROPE_DIM
D_MIX = LRU_WIDTH + DIFF_HEADS * DIFF_V_DIM + MLA_HEADS * MLA_V_DIM

kernel_name = "hymba_style_rglru_diffattn_mla_macaron"


def rmsnorm(x, g):
    x32 = x.astype(jnp.float32)
    y = x32 * lax.rsqrt(jnp.mean(x32 * x32, axis=-1, keepdims=True) + NORM_EPS)
    return (y * g.astype(jnp.float32)).astype(x.dtype)


def swiglu_ffn(x, w_in, w_out):
    gate, up = jnp.split(x @ w_in, 2, axis=-1)
    return (jax.nn.silu(gate) * up) @ w_out


def apply_rope(x, pos):
    half = x.shape[-1] // 2
    inv_freq = ROPE_THETA ** (-jnp.arange(half, dtype=jnp.float32) / half)
    ang = pos.astype(jnp.float32)[:, None] * inv_freq[None, :]
    cos = jnp.cos(ang)[None, :, None, :]
    sin = jnp.sin(ang)[None, :, None, :]
    x32 = x.astype(jnp.float32)
    x1, x2 = x32[..., :half], x32[..., half:]
    return jnp.concatenate([x1 * cos - x2 * sin, x2 * cos + x1 * sin], axis=-1).astype(x.dtype)


def causal_attention(q, k, v, scale):
    B, T = q.shape[0], q.shape[1]
    n_blocks = T // Q_BLOCK
    q_blocks = jnp.moveaxis(q.reshape(B, n_blocks, Q_BLOCK, *q.shape[2:]), 1, 0)
    k_pos = jnp.arange(T)

    def block(args):
        q_blk, blk_idx = args
        s = jnp.einsum('bqmhd,bkmhd->bmhqk', q_blk, k, preferred_element_type=jnp.float32) * scale
        q_pos = blk_idx * Q_BLOCK + jnp.arange(Q_BLOCK)
        s = jnp.where(k_pos[None, :] <= q_pos[:, None], s, -jnp.inf)
        p = jax.nn.softmax(s, axis=-1).astype(v.dtype)
        return jnp.einsum('bmhqk,bkhd->bqmhd', p, v)

    out = lax.map(block, (q_blocks, jnp.arange(n_blocks)))
    return jnp.moveaxis(out, 0, 1).reshape(B, T, *out.shape[3:])


def causal_depthwise_conv(u, w, b):
    out = lax.conv_general_dilated(
        u, w[:, None, :].astype(u.dtype), window_strides=(1,),
        padding=[(CONV_WIDTH - 1, 0)], dimension_numbers=('NWC', 'WIO', 'NWC'),
        feature_group_count=u.shape[-1])
    return out + b.astype(u.dtype)


def rg_lru(u, wa, ba, wx, bx, lam):
    B, T, C = u.shape
    ub = u.reshape(B, T, LRU_BLOCKS, LRU_BLOCK_DIM)
    r = jax.nn.sigmoid(jnp.einsum('btnc,ncd->btnd', ub, wa) + ba).reshape(B, T, C)
    i = jax.nn.sigmoid(jnp.einsum('btnc,ncd->btnd', ub, wx) + bx).reshape(B, T, C)
    log_a = -LRU_C * r.astype(jnp.float32) * jax.nn.softplus(-lam.astype(jnp.float32))
    a = jnp.exp(log_a)
    b = jnp.sqrt(-jnp.expm1(2.0 * log_a)) * (i * u).astype(jnp.float32)

    def combine(left, right):
        a1, b1 = left
        a2, b2 = right
        return a1 * a2, a2 * b1 + b2

    _, h = lax.associative_scan(combine, (a, b), axis=1)
    return h.astype(u.dtype)


def token_mixer(h, layer_idx, pos, w_in, conv_w, conv_b, lru_wa, lru_ba, lru_wx, lru_bx,
                lru_lambda, lru_out_norm, lam_q1, lam_k1, lam_q2, lam_k2, diff_subln,
                q_norm, w_uq, kv_norm, w_ukv, mla_out_norm, w_out):
    B, T, _ = h.shape
    offsets = [int(o) for o in np.cumsum(IN_WIDTHS)[:-1]]
    g_a, u_a, q_d, k_d, v_d, c_q, c_kv, k_r = jnp.split(h @ w_in, offsets, axis=-1)

    u_a = causal_depthwise_conv(u_a, conv_w, conv_b)
    y_a = rg_lru(u_a, lru_wa, lru_ba, lru_wx, lru_bx, lru_lambda) * jax.nn.gelu(g_a)
    y_a = rmsnorm(y_a, lru_out_norm)

    q_d = q_d.reshape(B, T, DIFF_HEADS, 2, DIFF_QK_DIM).swapaxes(2, 3)
    k_d = k_d.reshape(B, T, DIFF_HEADS, 2, DIFF_QK_DIM).swapaxes(2, 3)
    v_d = v_d.reshape(B, T, DIFF_HEADS, DIFF_V_DIM)
    o_d = causal_attention(q_d, k_d, v_d, DIFF_QK_DIM ** -0.5)
    lam_init = 0.8 - 0.6 * math.exp(-0.3 * layer_idx)
    lam = (jnp.exp(jnp.sum(lam_q1.astype(jnp.float32) * lam_k1.astype(jnp.float32)))
           - jnp.exp(jnp.sum(lam_q2.astype(jnp.float32) * lam_k2.astype(jnp.float32)))
           + lam_init).astype(h.dtype)
    o_d = o_d[:, :, 0] - lam * o_d[:, :, 1]
    y_b = (rmsnorm(o_d, diff_subln) * (1.0 - lam_init)).reshape(B, T, DIFF_HEADS * DIFF_V_DIM)

    q = (rmsnorm(c_q, q_norm) @ w_uq).reshape(B, T, MLA_HEADS, MLA_NOPE_DIM + MLA_ROPE_DIM)
    q = jnp.concatenate([q[..., :MLA_NOPE_DIM], apply_rope(q[..., MLA_NOPE_DIM:], pos)], axis=-1)
    kv = (rmsnorm(c_kv, kv_norm) @ w_ukv).reshape(B, T, MLA_HEADS, MLA_NOPE_DIM + MLA_V_DIM)
    k_nope, v_c = kv[..., :MLA_NOPE_DIM], kv[..., MLA_NOPE_DIM:]
    k_rope = jnp.broadcast_to(apply_rope(k_r[:, :, None, :], pos), (B, T, MLA_HEADS, MLA_ROPE_DIM))
    k = jnp.concatenate([k_nope, k_rope], axis=-1)
    o_c = causal_attention(q[:, :, None], k[:, :, None], v_c,
                           (MLA_NOPE_DIM + MLA_ROPE_DIM) ** -0.5)[:, :, 0]
    y_c = rmsnorm(o_c.reshape(B, T, MLA_HEADS * MLA_V_DIM), mla_out_norm)

    return jnp.concatenate([y_a, y_b, y_c], axis=-1) @ w_out


def setup_inputs(seed: int = 0) -> dict:
    key = jax.random.key(seed)
    ks = iter(jax.random.split(key, 40))
    f32 = jnp.float32

    def nrm(shape, scale):
        return jax.random.normal(next(ks), shape, f32) * scale

    def gain(shape):
        return 1.0 + 0.01 * jax.random.normal(next(ks), shape, f32)

    L = DEPTH
    u = jax.random.uniform(next(ks), (L, LRU_WIDTH), f32, minval=LRU_A_MIN, maxval=LRU_A_MAX)
    s = u ** (1.0 / LRU_C)
    lru_lambda = jnp.log(s) - jnp.log1p(-s)
    return {
        "x": nrm((BATCH, SEQ, D_MODEL), 1.0),
        "meta_tokens": nrm((N_META, D_MODEL), 1.0),
        "ffn1_norm": gain((L, D_MODEL)),
        "ffn1_in": nrm((L, D_MODEL, 2 * D_FF), D_MODEL ** -0.5),
        "ffn1_out": nrm((L, D_FF, D_MODEL), D_FF ** -0.5),
        "mix_norm": gain((L, D_MODEL)),
        "w_in": nrm((L, D_MODEL, D_IN), D_MODEL ** -0.5),
        "conv_w": nrm((L, CONV_WIDTH, LRU_WIDTH), CONV_WIDTH ** -0.5),
        "conv_b": nrm((L, LRU_WIDTH), 0.01),
        "lru_wa": nrm((L, LRU_BLOCKS, LRU_BLOCK_DIM, LRU_BLOCK_DIM), LRU_BLOCK_DIM ** -0.5),
        "lru_ba": nrm((L, LRU_BLOCKS, LRU_BLOCK_DIM), 0.01),
        "lru_wx": nrm((L, LRU_BLOCKS, LRU_BLOCK_DIM, LRU_BLOCK_DIM), LRU_BLOCK_DIM ** -0.5),
        "lru_bx": nrm((L, LRU_BLOCKS, LRU_BLOCK_DIM), 0.01),
        "lru_lambda": lru_lambda,
        "lru_out_norm": gain((L, LRU_WIDTH)),
        "lam_q1": nrm((L, DIFF_QK_DIM), 0.1),
        "lam_k1": nrm((L, DIFF_QK_DIM), 0.1),
        "lam_q2": nrm((L, DIFF_QK_DIM), 0.1),
        "lam_k2": nrm((L, DIFF_QK_DIM), 0.1),
        "diff_subln": gain((L, DIFF_V_DIM)),
        "q_norm": gain((L, MLA_Q_LORA)),
        "w_uq": nrm((L, MLA_Q_LORA, MLA_HEADS * (MLA_NOPE_DIM + MLA_ROPE_DIM)), MLA_Q_LORA ** -0.5),
        "kv_norm": gain((L, MLA_KV_LORA)),
        "w_ukv": nrm((L, MLA_KV_LORA, MLA_HEADS * (MLA_NOPE_DIM + MLA_V_DIM)), MLA_KV_LORA ** -0.5),
        "mla_out_norm": gain((L, MLA_HEADS * MLA_V_DIM)),
        "w_out": nrm((L, D_MIX, D_MODEL), D_MIX ** -0.5),
        "ffn2_norm": gain((L, D_MODEL)),
        "ffn2_in": nrm((L, D_MODEL, 2 * D_FF), D_MODEL ** -0.5),
        "ffn2_out": nrm((L, D_FF, D_MODEL), D_FF ** -0.5),
        "final_norm": gain((D_MODEL,)),
    }


def reference(x, meta_tokens, ffn1_norm, ffn1_in, ffn1_out, mix_norm, w_in, conv_w, conv_b,
              lru_wa, lru_ba, lru_wx, lru_bx, lru_lambda, lru_out_norm, lam_q1, lam_k1,
              lam_q2, lam_k2, diff_subln, q_norm, w_uq, kv_norm, w_ukv, mla_out_norm, w_out,
              ffn2_norm, ffn2_in, ffn2_out, final_norm):
    B, S, D = x.shape
    T = N_META + S
    T_pad = -(-T // Q_BLOCK) * Q_BLOCK
    meta = jnp.broadcast_to(meta_tokens[None].astype(x.dtype), (B, N_META, D))
    h = jnp.concatenate([meta, x], axis=1)
    h = jnp.pad(h, ((0, 0), (0, T_pad - T), (0, 0)))
    pos = jnp.arange(T_pad, dtype=jnp.int32)
    for l in range(DEPTH):
        h = h + 0.5 * swiglu_ffn(rmsnorm(h, ffn1_norm[l]), ffn1_in[l], ffn1_out[l])
        h = h + token_mixer(rmsnorm(h, mix_norm[l]), l, pos, w_in[l], conv_w[l], conv_b[l],
                            lru_wa[l], lru_ba[l], lru_wx[l], lru_bx[l], lru_lambda[l],
                            lru_out_norm[l], lam_q1[l], lam_k1[l], lam_q2[l], lam_k2[l],
                            diff_subln[l], q_norm[l], w_uq[l], kv_norm[l], w_ukv[l],
                            mla_out_norm[l], w_out[l])
        h = h + 0.5 * swiglu_ffn(rmsnorm(h, ffn2_norm[l]), ffn2_in[l], ffn2_out[l])
    h = rmsnorm(h, final_norm)
    return h[:, N_META:N_META + S]
```

```python
import contextlib
import math
import numpy as np
import ml_dtypes
import concourse.bass as bass
import concourse.mybir as mybir
from concourse.bass_utils import run_bass_kernel_spmd

F32 = mybir.dt.float32
BF16 = mybir.dt.bfloat16
AF = mybir.ActivationFunctionType
ALU = mybir.AluOpType
AX = mybir.AxisListType

D = 1024
SEQ = 4096
NMETA = 16
T = SEQ + NMETA
DFF = 2816
NF = DFF // 128
NCH = D // 128
DEPTH = 2
EPS = 1e-6
DIN = 2464


class Res:
    __slots__ = ("name", "w", "r", "slot")

    def __init__(self, name):
        self.name = name
        self.w = None
        self.r = []
        self.slot = None


class SemSlot:
    __slots__ = ("sem", "dcount")

    def __init__(self):
        self.sem = None
        self.dcount = 0


class Op:
    __slots__ = ("eng", "fn", "deps", "dma", "sem_res", "val", "needs_inc", "idx", "gen")

    def __init__(self, eng, fn, dma=False):
        self.eng = eng
        self.fn = fn
        self.deps = []
        self.dma = dma
        self.sem_res = None
        self.val = 0
        self.needs_inc = False


ENGS = ("pe", "act", "dve", "pool", "sp")
MAX_DMA_INFLIGHT = 4


class Prog:
    def __init__(self):
        self.q = {e: [] for e in ENGS}
        self.pool = []
        self.stage_k = 0
        self.gen = 0
        self.last_dma = {}
        self.dma_fifo = []

    def barrier(self):
        deps = []
        for e in ENGS:
            for op in reversed(self.q[e]):
                if not op.dma and op.fn is not None:
                    deps.append((op, 0))
                    break
        for op in self.last_dma.values():
            deps.append((op, op.sem_res.dcount))
        for e in ENGS:
            op = Op(e, None)
            op.gen = self.gen
            op.deps = [(o, s) for (o, s) in deps if not (o.eng == e and not o.dma)]
            for o, _ in op.deps:
                o.needs_inc = True
            self.q[e].append(op)
        self.gen += 1
        self.stage_k = 0
        self.last_dma = {}

    def add(self, eng, fn, reads=(), writes=(), dma=False, sem_res=None):
        op = Op(eng, fn, dma)
        op.gen = self.gen
        deps = {}

        def dep(o):
            if o is None or o.gen < self.gen:
                return
            if (not o.dma) and o.eng == "pe" and eng == "pe" and not dma:
                return
            snap = o.sem_res.dcount if o.dma else 0
            deps[id(o)] = (o, snap)

        for r in reads:
            dep(r.w)
        for w in writes:
            dep(w.w)
            for o in w.r:
                dep(o)
        if dma:
            if len(self.dma_fifo) >= MAX_DMA_INFLIGHT:
                old = self.dma_fifo[-MAX_DMA_INFLIGHT]
                if old.gen == self.gen and id(old) not in deps:
                    deps[id(old)] = (old, 0)
            self.dma_fifo.append(op)
        op.deps = list(deps.values())
        for o, _ in op.deps:
            o.needs_inc = True
        if dma:
            assert sem_res is not None
            if sem_res.slot is None or sem_res.slot[1] != self.gen:
                if self.stage_k >= len(self.pool):
                    self.pool.append(SemSlot())
                sem_res.slot = (self.pool[self.stage_k], self.gen)
                self.stage_k += 1
            sl = sem_res.slot[0]
            sl.dcount += 16
            op.sem_res = sl
            op.val = sl.dcount
            self.last_dma[id(sl)] = op
        for r in reads:
            if not dma:
                r.r = [o for o in r.r if o.dma or o.eng != eng]
            r.r.append(op)
        for w in writes:
            w.w = op
            w.r = []
        self.q[eng].append(op)
        return op

    def emit(self, nc, es):
        esem = {e: es.enter_context(nc.semaphore("s_" + e)) for e in ENGS}
        for i, sl in enumerate(self.pool):
            sl.sem = es.enter_context(nc.semaphore("d%d" % i))
        for e in ENGS:
            cnt = 0
            for op in self.q[e]:
                if op.dma or op.fn is None:
                    continue
                if op.needs_inc:
                    cnt += 1
                    op.val = cnt
        block = es.enter_context(nc.Block())
        engobj = {"pe": "tensor", "act": "scalar", "dve": "vector", "pool": "gpsimd", "sp": "sync"}

        def body_for(e):
            ops = self.q[e]

            def body(eng):
                seen = {}
                for op in ops:
                    for d, snap in op.deps:
                        if d.dma:
                            sem, val = d.sem_res.sem, max(d.val, snap)
                        else:
                            sem, val = esem[d.eng], d.val
                        key = id(sem)
                        if seen.get(key, 0) >= val:
                            continue
                        seen[key] = val
                        eng.wait_ge(sem, val)
                    if op.fn is None:
                        continue
                    ins = op.fn(eng)
                    if op.dma:
                        ins.then_inc(op.sem_res.sem, 16)
                    elif op.needs_inc:
                        ins.then_inc(esem[e], 1)
            return body

        for e in ENGS:
            getattr(block, engobj[e])(body_for(e))


class Ring:
    def __init__(self, aps, name):
        self.aps = aps
        self.res = [Res("%s%d" % (name, i)) for i in range(len(aps))]
        self.i = 0

    def next(self):
        k = self.i % len(self.aps)
        self.i += 1
        return self.aps[k], self.res[k]


ARENA_BYTES = 79360


def dsize(dt):
    return 4 if dt == F32 else 2


class Builder:
    def __init__(self, n_layers=DEPTH, debug=None):
        self.n_layers = n_layers
        self.debug = debug or {}
        self.nc = bass.Bass("TRN2", target_bir_lowering=False)
        self.P = Prog()
        self.es = contextlib.ExitStack()
        self.inputs = {}
        self.aoff = 0

    def din(self, name, shape, dt=F32):
        ap = self.nc.dram_tensor(name, list(shape), dt, kind="ExternalInput").ap()
        self.inputs[name] = ap
        return ap

    def dout(self, name, shape, dt=F32):
        return self.nc.dram_tensor(name, list(shape), dt, kind="ExternalOutput").ap()

    def dscratch(self, name, shape, dt):
        return self.nc.dram_tensor(name, list(shape), dt, kind="Internal").ap()

    def sb(self, name, shape, dt):
        return self.es.enter_context(self.nc.sbuf_tensor(name, list(shape), dt))

    def ps(self, name, shape, dt=F32):
        return self.es.enter_context(self.nc.psum_tensor(name, list(shape), dt))

    def stage_begin(self):
        self.P.barrier()
        self.aoff = 0

    def alloc(self, shape, dt):
        n = int(np.prod(shape))
        nbytes = (n * dsize(dt) + 31) // 32 * 32
        assert self.aoff + nbytes <= ARENA_BYTES, (self.aoff, nbytes)
        ap = self.arena[:, self.aoff // 4:(self.aoff + nbytes) // 4]
        self.aoff += nbytes
        if dt != F32:
            ap = ap.bitcast(dt)
        ap = ap[:, 0:n]
        if len(shape) == 2:
            ap = ap.rearrange("p (a b) -> p a b", a=shape[0])
        elif len(shape) == 3:
            ap = ap.rearrange("p (a b c) -> p a b c", a=shape[0], b=shape[1])
        return ap


    def act(self, out, in_, func, reads, writes, **kw):
        return self.P.add("act", lambda e: e.activation(out=out, in_=in_, func=func, **kw), reads=reads, writes=writes)

    def tt(self, eng, out, in0, in1, op, reads, writes):
        return self.P.add(eng, lambda e: e.tensor_tensor(out=out, in0=in0, in1=in1, op=op), reads=reads, writes=writes)

    def ts(self, eng, out, in0, s1, s2, op0, op1, reads, writes):
        if s2 is None:
            return self.P.add(eng, lambda e: e.tensor_scalar(out=out, in0=in0, scalar1=s1, scalar2=None, op0=op0),
                              reads=reads, writes=writes)
        return self.P.add(eng, lambda e: e.tensor_scalar(out=out, in0=in0, scalar1=s1, scalar2=s2, op0=op0, op1=op1),
                          reads=reads, writes=writes)

    def stt(self, out, in0, scalar, in1, op0, op1, reads, writes):
        return self.P.add("dve", lambda e: e.scalar_tensor_tensor(out=out, in0=in0, scalar=scalar, in1=in1, op0=op0, op1=op1),
                          reads=reads, writes=writes)

    def cp(self, eng, out, in_, reads, writes):
        if eng == "act":
            return self.act(out, in_, AF.Copy, reads, writes)
        return self.P.add(eng, lambda e: e.tensor_copy(out=out, in_=in_), reads=reads, writes=writes)

    def mm(self, out, lhsT, rhs, start, stop, reads, writes, **kw):
        return self.P.add("pe", lambda e: e.matmul(out, lhsT=lhsT, rhs=rhs, start=start, stop=stop, **kw),
                          reads=reads, writes=writes)

    def load(self, out, in_, res, reads=()):
        return self.P.add("sp", lambda e: e.dma_start(out=out, in_=in_), reads=list(reads), writes=[res], dma=True, sem_res=res)

    def store(self, out, in_, res):
        return self.P.add("sp", lambda e: e.dma_start(out=out, in_=in_), reads=[res], dma=True, sem_res=res)

    def rsqrt_inplace(self, ap, res, scale, in_ap=None, in_reads=()):
        src = ap if in_ap is None else in_ap
        self.ts("dve", ap, src, scale, EPS, ALU.mult, ALU.add, [res] + list(in_reads), [res])
        self.act(ap, ap, AF.Sqrt, [res], [res])
        self.P.add("dve", lambda e: e.reciprocal(out=ap, in_=ap), reads=[res], writes=[res])

    def h_res(self, c, c0, n):
        out = []
        if c0 < NMETA:
            out.append(self.hres[c][0])
        lo = max(c0, NMETA) - NMETA
        hi = c0 + n - NMETA
        if hi > lo:
            for i in range(lo // 128, (hi - 1) // 128 + 1):
                out.append(self.hres[c][1 + i])
        return out

    def build(self):
        nc, P = self.nc, self.P
        self.x = self.din("x", [SEQ, D])
        self.meta = self.din("meta_tokens", [NMETA, D])
        self.final_norm = self.din("final_norm", [D])
        self.ident_f = self.din("ident_f", [128, 128])
        L = DEPTH
        self.w = {}
        for nm, shp in [("ffn1_norm", [L, D]), ("ffn1_in", [L, D, 2 * DFF]), ("ffn1_out", [L, DFF, D]),
                        ("ffn2_norm", [L, D]), ("ffn2_in", [L, D, 2 * DFF]), ("ffn2_out", [L, DFF, D])]:
            self.w[nm] = self.din(nm, shp)
        for nm, shp in [("mix_norm", [L, D]), ("w_in", [L, D, DIN]), ("conv_w", [L, 4, 256]), ("conv_b", [L, 256]),
                        ("lru_wa", [L, 4, 64, 64]), ("lru_ba", [L, 4, 64]), ("lru_wx", [L, 4, 64, 64]), ("lru_bx", [L, 4, 64]),
                        ("lru_lambda", [L, 256]), ("lru_out_norm", [L, 256]), ("lam_q1", [L, 64]), ("lam_k1", [L, 64]),
                        ("lam_q2", [L, 64]), ("lam_k2", [L, 64]), ("diff_subln", [L, 128]), ("q_norm", [L, 256]),
                        ("w_uq", [L, 256, 384]), ("kv_norm", [L, 128]), ("w_ukv", [L, 128, 512]),
                        ("mla_out_norm", [L, 256]), ("w_out", [L, D, D])]:
            self.w[nm] = self.din(nm, shp)
        self.rope_cs = self.din("rope_cs", [96, 2, T])
        self.tri_in = self.din("tri_in", [128, 128])
        self.out = self.dout("out", [SEQ, D])
        self.LRU_IN = self.dscratch("lru_in", [4, 128, T], F32)
        self.QKD = self.dscratch("qkd", [8, 128, T], BF16)
        self.VD = self.dscratch("vd", [T, 512], BF16)
        self.QM = self.dscratch("qm", [4, 96, T], BF16)
        self.KM = self.dscratch("km", [4, 96, T], BF16)
        self.VM = self.dscratch("vm", [T, 260], BF16)
        self.YT = self.dscratch("yt", [8, 128, T], BF16)
        self.wf = {}
        for l in range(L):
            for which in (1, 2):
                self.wf[(l, which)] = self.dscratch("wf_%d_%d" % (l, which), [NF, 128, 3072], BF16)
        self.wf_r = {k: Res("wf%d%d" % k) for k in self.wf}

        self.hT = self.sb("hT", [128, NCH, T], F32)
        self.hres = [[Res("h%d_%d" % (c, i)) for i in range(1 + SEQ // 128)] for c in range(NCH)]
        self.arena = self.sb("arena", [128, ARENA_BYTES // 4], F32)
        self.identf = self.sb("identf", [128, 128], F32)
        self.identf_r = Res("identf")
        self.ones_bf = self.sb("ones_bf", [128, 128], BF16)
        self.ones_r = Res("ones")
        self.gains = self.sb("gains", [128, 8, NCH], F32)
        self.gains_r = Res("gains")

        P.add("sp", lambda e: e.dma_start(out=self.identf[:], in_=self.ident_f[:, :]),
              writes=[self.identf_r], dma=True, sem_res=self.identf_r)
        P.add("pool", lambda e: e.memset(self.ones_bf[:], 1.0), writes=[self.ones_r])
        self.gain_idx = {}
        gi = 0
        for l in range(L):
            for nm in ("ffn1_norm", "ffn2_norm", "mix_norm"):
                src = self.w[nm][l, :].rearrange("(c p) -> p c", p=128)
                P.add("sp", lambda e, k=gi, src=src: e.dma_start(out=self.gains[:, k, :], in_=src),
                      writes=[self.gains_r], dma=True, sem_res=self.gains_r)
                self.gain_idx[(nm, l)] = gi
                gi += 1
        src = self.final_norm.rearrange("(c p) -> p c", p=128)
        P.add("sp", lambda e, k=gi, src=src: e.dma_start(out=self.gains[:, k, :], in_=src),
              writes=[self.gains_r], dma=True, sem_res=self.gains_r)
        self.gain_idx["final"] = gi

        self.psum = [self.ps("ps%d" % i, [128, 512], F32) for i in range(8)]
        self.psum_r = [Res("ps%d" % i) for i in range(8)]

        self.stage_load_x()
        ffns = [(l, w) for l in range(self.n_layers) for w in (1, 2) if self.debug.get("ffn%d" % w, True)]
        for (l, w) in ffns:
            self.stage_convert_ffn(l, w)
        for l in range(self.n_layers):
            if self.debug.get("ffn1", True):
                self.stage_ffn(l, 1)
            if self.debug.get("mixer", True):
                if self.debug.get("proj1", True):
                    self.stage_proj1(l)
                if self.debug.get("proj2", True):
                    self.stage_proj2(l)
                if self.debug.get("lru", True):
                    self.stage_lru(l)
                if self.debug.get("diff", True):
                    self.stage_diff(l)
                if self.debug.get("mla", True):
                    self.stage_mla(l)
                if self.debug.get("oproj", True):
                    self.stage_oproj(l)
            if self.debug.get("ffn2", True):
                self.stage_ffn(l, 2)
        self.stage_final()
        with nc.allow_non_contiguous_dma(reason="small strided parameter loads"):
            P.emit(nc, self.es)
        return nc

    def psring(self, idxs, name):
        r = Ring([self.psum[i] for i in idxs], name)
        r.res = [self.psum_r[i] for i in idxs]
        return r

    def stage_load_x(self):
        P = self.P
        self.stage_begin()
        ring = Ring([self.alloc([D], F32) for i in range(3)], "xin")
        psr = self.psring([0, 1, 2, 3], "x")
        n = 0
        for i in range(-1, SEQ // 128):
            buf, br = ring.next()
            if i < 0:
                rows, c0 = NMETA, 0
                src = self.meta[:, :]
            else:
                rows, c0 = 128, NMETA + 128 * i
                src = self.x[128 * i:128 * (i + 1), :]
            P.add("sp", lambda e, buf=buf, rows=rows, src=src: e.dma_start(out=buf[0:rows, :], in_=src),
                  writes=[br], dma=True, sem_res=br)
            for half in range(2):
                pt, pr = psr.next()
                for cc in range(4):
                    c = half * 4 + cc
                    P.add("pe", lambda e, pt=pt, cc=cc, buf=buf, rows=rows, c=c: e.transpose(
                        out=pt[:, cc * 128:cc * 128 + rows], in_=buf[0:rows, c * 128:(c + 1) * 128],
                        identity=self.identf[0:rows, 0:rows]),
                        reads=[br, self.identf_r], writes=[pr])
                eng = "act" if (n % 2 == 0) else "dve"
                n += 1
                hr = []
                for cc in range(4):
                    hr += self.h_res(half * 4 + cc, c0, rows)
                dst = self.hT[:, half * 4:half * 4 + 4, c0:c0 + rows]
                srcp = pt[:, :].rearrange("p (c t) -> p c t", c=4)[:, :, 0:rows]
                if eng == "act":
                    P.add("act", lambda e, dst=dst, srcp=srcp: e.activation(out=dst, in_=srcp, func=AF.Copy),
                          reads=[pr], writes=hr)
                else:
                    P.add("dve", lambda e, dst=dst, srcp=srcp: e.tensor_copy(out=dst, in_=srcp),
                          reads=[pr], writes=hr)

    def stage_convert_ffn(self, l, which):
        P = self.P
        self.stage_begin()
        w_in = self.w["ffn%d_in" % which][l]
        w_out = self.w["ffn%d_out" % which][l]
        WF = self.wf[(l, which)]
        wfr = self.wf_r[(l, which)]
        s32 = Ring([self.alloc([2, NCH, 256], F32) for _ in range(2)], "cv32")
        s32o = Ring([self.alloc([2, D], F32) for _ in range(2)], "cv32o")
        s16 = Ring([self.alloc([2, 3072], BF16) for _ in range(2)], "cv16")
        engs = ["act", "dve", "pool"]
        n = 0

        def cast(eng, out, in_):
            if eng == "act":
                return lambda e: e.activation(out=out, in_=in_, func=AF.Copy)
            return lambda e: e.tensor_copy(out=out, in_=in_)

        for fp in range(NF // 2):
            f0 = 2 * fp
            a, ar = s32.next()
            ao, aor = s32o.next()
            b, br = s16.next()
            for gu in range(2):
                col = gu * DFF + f0 * 128
                src = w_in[:, col:col + 256].rearrange("(k p) j -> p k j", p=128)
                P.add("sp", lambda e, a=a, gu=gu, src=src: e.dma_start(out=a[:, gu, :, :], in_=src),
                      writes=[ar], dma=True, sem_res=ar)
            src = w_out[f0 * 128:(f0 + 2) * 128, :].rearrange("(f p) c -> p f c", p=128)
            P.add("sp", lambda e, ao=ao, src=src: e.dma_start(out=ao[:, :, :], in_=src),
                  writes=[aor], dma=True, sem_res=aor)
            for ff in range(2):
                for gu in range(2):
                    eng = engs[n % 3]
                    n += 1
                    out = b[:, ff, 0:2048].rearrange("p (k j) -> p k j", k=NCH)[:, :, gu * 128:(gu + 1) * 128]
                    in_ = a[:, gu, :, ff * 128:(ff + 1) * 128]
                    P.add(eng, cast(eng, out, in_), reads=[ar], writes=[br])
            eng = engs[n % 3]
            n += 1
            P.add(eng, cast(eng, b[:, :, 2048:3072], ao[:, :, :]), reads=[aor], writes=[br])
            dst = WF[f0:f0 + 2, :, :].rearrange("f p c -> p f c")
            P.add("sp", lambda e, b=b, dst=dst: e.dma_start(out=dst, in_=b[:, :, :]),
                  reads=[br], writes=[wfr], dma=True, sem_res=br)

    def stage_ffn(self, l, which):
        P = self.P
        self.stage_begin()
        WF = self.wf[(l, which)]
        wfr = self.wf_r[(l, which)]
        gk = self.gain_idx[("ffn%d_norm" % which, l)]
        xn = self.alloc([NCH, 1040], BF16)
        hid = Ring([self.alloc([1040], BF16) for _ in range(4)], "hid")
        wsl = Ring([self.alloc([3072], BF16) for _ in range(6)], "wsl")
        sqr = Ring([self.alloc([512], BF16) for _ in range(2)], "sq")
        rstd = self.alloc([1040], F32)
        rstd_r = Res("rstd")
        sgr = Ring([self.alloc([512], F32) for _ in range(2)], "sg")
        ps_gu = self.psring([0, 1, 2, 3], "gu")
        ps_o = self.psring([4, 5, 6], "o")
        ps_n, ps_nr = self.psum[7], self.psum_r[7]

        sts = [[(0, 16), (16, 512), (528, 512)]]
        for s in range(1, 4):
            b0 = 1040 + 1024 * (s - 1)
            sts.append([(b0, 512), (b0 + 512, 512)])
        nsq = 0
        for subs in sts:
            base = subs[0][0]
            xn_r = [Res("xn%d" % i) for i in range(len(subs))]
            for si, (c0, n) in enumerate(subs):
                lc = c0 - base
                for c in range(NCH):
                    sq, sq_r = sqr.next()
                    src = self.hT[:, c, c0:c0 + n]
                    if nsq % 2 == 0:
                        P.add("act", lambda e, sq=sq, src=src, n=n: e.activation(out=sq[:, 0:n], in_=src, func=AF.Square),
                              reads=self.h_res(c, c0, n), writes=[sq_r])
                    else:
                        P.add("pool", lambda e, sq=sq, src=src, n=n: e.tensor_tensor(out=sq[:, 0:n], in0=src, in1=src, op=ALU.mult),
                              reads=self.h_res(c, c0, n), writes=[sq_r])
                    nsq += 1
                    P.add("pe", lambda e, sq=sq, n=n, c=c: e.matmul(ps_n[:, 0:n], lhsT=self.ones_bf[:, :], rhs=sq[:, 0:n],
                                                                     start=(c == 0), stop=(c == NCH - 1)),
                          reads=[sq_r, self.ones_r], writes=[ps_nr])
                rs = rstd[:, lc:lc + n]
                P.add("dve", lambda e, rs=rs, n=n: e.tensor_scalar(out=rs, in0=ps_n[:, 0:n], scalar1=1.0 / D, scalar2=EPS,
                                                                   op0=ALU.mult, op1=ALU.add),
                      reads=[ps_nr], writes=[rstd_r])
                P.add("act", lambda e, rs=rs: e.activation(out=rs, in_=rs, func=AF.Sqrt), reads=[rstd_r], writes=[rstd_r])
                P.add("dve", lambda e, rs=rs: e.reciprocal(out=rs, in_=rs), reads=[rstd_r], writes=[rstd_r])
                for c in range(NCH):
                    P.add("dve", lambda e, c=c, c0=c0, n=n, lc=lc, rs=rs: e.scalar_tensor_tensor(
                        out=xn[:, c, lc:lc + n], in0=self.hT[:, c, c0:c0 + n], scalar=self.gains[:, gk, c:c + 1],
                        in1=rs, op0=ALU.mult, op1=ALU.mult),
                        reads=self.h_res(c, c0, n) + [rstd_r, self.gains_r], writes=[xn_r[si]])

            def GU(f):
                slot, slot_r = wsl.next()
                P.add("sp", lambda e, slot=slot, f=f: e.dma_start(out=slot[:, :], in_=WF[f, :, :]),
                      reads=[wfr], writes=[slot_r], dma=True, sem_res=slot_r)
                hb, hb_r = hid.next()
                for si, (c0, n) in enumerate(subs):
                    lc = c0 - base
                    pg, pgr = ps_gu.next()
                    pu, pur = ps_gu.next()
                    for gu, (pp, ppr) in enumerate(((pg, pgr), (pu, pur))):
                        for k in range(NCH):
                            P.add("pe", lambda e, pp=pp, slot=slot, k=k, gu=gu, lc=lc, n=n: e.matmul(
                                pp[:, 0:n], lhsT=slot[:, k * 256 + gu * 128:k * 256 + gu * 128 + 128],
                                rhs=xn[:, k, lc:lc + n], start=(k == 0), stop=(k == NCH - 1)),
                                reads=[slot_r, xn_r[si]], writes=[ppr])
                    sg, sg_r = sgr.next()
                    P.add("act", lambda e, sg=sg, pg=pg, n=n: e.activation(out=sg[:, 0:n], in_=pg[:, 0:n], func=AF.Silu),
                          reads=[pgr], writes=[sg_r])
                    P.add("dve", lambda e, sg=sg, pu=pu, hb=hb, lc=lc, n=n: e.tensor_tensor(
                        out=hb[:, lc:lc + n], in0=sg[:, 0:n], in1=pu[:, 0:n], op=ALU.mult),
                        reads=[sg_r, pur], writes=[hb_r])
                return (slot, slot_r, hb, hb_r)

            def W2(group):
                for o in range(NCH):
                    for si, (c0, n) in enumerate(subs):
                        lc = c0 - base
                        po, por = ps_o.next()
                        for gi_, (slot, slot_r, hb, hb_r) in enumerate(group):
                            P.add("pe", lambda e, po=po, slot=slot, hb=hb, o=o, lc=lc, n=n, gi_=gi_, ng=len(group): e.matmul(
                                po[:, 0:n], lhsT=slot[:, 2048 + o * 128:2048 + (o + 1) * 128], rhs=hb[:, lc:lc + n],
                                start=(gi_ == 0), stop=(gi_ == ng - 1)),
                                reads=[slot_r, hb_r], writes=[por])
                        hr = self.h_res(o, c0, n)
                        P.add("dve", lambda e, po=po, o=o, c0=c0, n=n: e.scalar_tensor_tensor(
                            out=self.hT[:, o, c0:c0 + n], in0=po[:, 0:n], scalar=0.5, in1=self.hT[:, o, c0:c0 + n],
                            op0=ALU.mult, op1=ALU.add),
                            reads=[por] + hr, writes=hr)

            prev = None
            for g in range(NF // 2):
                grp = [GU(2 * g), GU(2 * g + 1)]
                if prev is not None:
                    W2(prev)
                prev = grp
            W2(prev)

    def tiles9(self):
        return [(0, NMETA)] + [(NMETA + 512 * j, 512) for j in range(SEQ // 512)]

    def norm_tile(self, c0, n, gk, xn, xn_r, sqr, rstd, rstd_r, ps_n, ps_nr, cnt=[0]):
        P = self.P
        for c in range(NCH):
            sq, sq_r = sqr.next()
            src = self.hT[:, c, c0:c0 + n]
            if cnt[0] % 2 == 0:
                self.act(sq[:, 0:n], src, AF.Square, self.h_res(c, c0, n), [sq_r])
            else:
                self.tt("pool", sq[:, 0:n], src, src, ALU.mult, self.h_res(c, c0, n), [sq_r])
            cnt[0] += 1
            self.mm(ps_n[:, 0:n], self.ones_bf[:, :], sq[:, 0:n], c == 0, c == NCH - 1, [sq_r, self.ones_r], [ps_nr])
        rs = rstd[:, 0:n]
        self.rsqrt_inplace(rs, rstd_r, 1.0 / D, in_ap=ps_n[:, 0:n], in_reads=[ps_nr])
        for c in range(NCH):
            self.stt(xn[:, c, 0:n], self.hT[:, c, c0:c0 + n], self.gains[:, gk, c:c + 1], rs, ALU.mult, ALU.mult,
                     self.h_res(c, c0, n) + [rstd_r, self.gains_r], [xn_r])

    def load_w_bf16(self, dst, dst_r, src, ncols, slab=160):
        st = Ring([self.alloc([src.shape[0] // 128, slab], F32) for _ in range(2)], "wslab")
        engs = ["act", "dve", "pool"]
        i = 0
        for c in range(0, ncols, slab):
            w = min(slab, ncols - c)
            a, ar = st.next()
            self.load(a[:, :, 0:w], src[:, c:c + w].rearrange("(k p) j -> p k j", p=128), ar)
            self.cp(engs[i % 3], dst[:, :, c:c + w], a[:, :, 0:w], [ar], [dst_r])
            i += 1

    def stage_proj1(self, l):
        self.stage_begin()
        gk = self.gain_idx[("mix_norm", l)]
        w_in = self.w["w_in"][l]
        Wp = self.alloc([NCH, 2048], BF16)
        Wp_r = Res("Wp")
        self.load_w_bf16(Wp, Wp_r, w_in[:, 0:2048], 2048)
        xnr = Ring([self.alloc([NCH, 512], BF16) for _ in range(2)], "xn")
        sqr = Ring([self.alloc([512], BF16) for _ in range(2)], "sq")
        rstr = Ring([self.alloc([512], F32) for _ in range(2)], "rstd")
        st32 = Ring([self.alloc([512], F32) for _ in range(3)], "st32")
        st16 = Ring([self.alloc([512], BF16) for _ in range(4)], "st16")
        psr = self.psring([0, 1, 2, 3, 4, 5], "p1")
        ps_n, ps_nr = self.psum[7], self.psum_r[7]
        ne = 0
        for (c0, n) in self.tiles9():
            xn, xn_r = xnr.next()
            rstd, rstd_r = rstr.next()
            self.norm_tile(c0, n, gk, xn, xn_r, sqr, rstd, rstd_r, ps_n, ps_nr)
            for g in range(12):
                pt, pr = psr.next()
                for k in range(NCH):
                    self.mm(pt[:, 0:n], Wp[:, k, g * 128:(g + 1) * 128], xn[:, k, 0:n], k == 0, k == NCH - 1,
                            [Wp_r, xn_r], [pr])
                eng = "act" if ne % 2 == 0 else "dve"
                ne += 1
                if g < 4:
                    sb, sr = st32.next()
                    self.cp(eng, sb[:, 0:n], pt[:, 0:n], [pr], [sr])
                    self.store(self.LRU_IN[g, :, c0:c0 + n], sb[:, 0:n], sr)
                else:
                    sb, sr = st16.next()
                    self.cp(eng, sb[:, 0:n], pt[:, 0:n], [pr], [sr])
                    self.store(self.QKD[g - 4, :, c0:c0 + n], sb[:, 0:n], sr)
            for j in range(0, n, 128):
                m = min(128, n - j)
                pt, pr = psr.next()
                for k in range(NCH):
                    self.mm(pt[0:m, 0:512], xn[:, k, j:j + m], Wp[:, k, 1536:2048], k == 0, k == NCH - 1,
                            [Wp_r, xn_r], [pr])
                sb, sr = st16.next()
                eng = "act" if ne % 2 == 0 else "dve"
                ne += 1
                self.cp(eng, sb[0:m, 0:512], pt[0:m, 0:512], [pr], [sr])
                self.store(self.VD[c0 + j:c0 + j + m, :], sb[0:m, 0:512], sr)

    def stage_proj2(self, l):
        self.stage_begin()
        gk = self.gain_idx[("mix_norm", l)]
        w_in = self.w["w_in"][l]
        Wp = self.alloc([NCH, 512], BF16)
        Wp_r = Res("Wp2")
        self.P.add("pool", lambda e: e.memset(Wp[:, :, 416:480], 0.0), writes=[Wp_r])
        self.load_w_bf16(Wp, Wp_r, w_in[:, 2048:2464], 416, slab=104)
        self.cp("dve", Wp[:, :, 480:496], Wp[:, :, 400:416], [Wp_r], [Wp_r])
        self.cp("dve", Wp[:, :, 496:512], Wp[:, :, 384:400], [Wp_r], [Wp_r])
        Wq = self.alloc([2, 768], BF16)
        Wq_r = Res("Wq")
        self.load_w_bf16(Wq, Wq_r, self.w["w_uq"][l], 384, slab=192)
        for h in range(4):
            b = 384 + h * 96
            a = h * 96
            self.cp("dve", Wq[:, :, b:b + 64], Wq[:, :, a:a + 64], [Wq_r], [Wq_r])
            self.cp("dve", Wq[:, :, b + 64:b + 80], Wq[:, :, a + 80:a + 96], [Wq_r], [Wq_r])
            self.cp("dve", Wq[:, :, b + 80:b + 96], Wq[:, :, a + 64:a + 80], [Wq_r], [Wq_r])
        Wkv = self.alloc([512], BF16)
        Wkv_r = Res("Wkv")
        kv32 = self.alloc([512], F32)
        kv32_r = Res("kv32")
        self.load(kv32[:, :], self.w["w_ukv"][l][:, :], kv32_r)
        self.cp("dve", Wkv[:, :].rearrange("p (t h d) -> p h t d", t=2, h=4),
                kv32[:, :].rearrange("p (h t d) -> p h t d", h=4, t=2), [kv32_r], [Wkv_r])
        gq = self.alloc([2], F32)
        gkv = self.alloc([1], F32)
        g_r = Res("gqkv")
        self.load(gq[:, :], self.w["q_norm"][l].rearrange("(c p) -> p c", p=128), g_r)
        self.load(gkv[:, :], self.w["kv_norm"][l].rearrange("(c p) -> p c", p=128), g_r)

        xnr = Ring([self.alloc([NCH, 512], BF16) for _ in range(1)], "xn")
        sqr = Ring([self.alloc([512], BF16) for _ in range(2)], "sq")
        rstr = Ring([self.alloc([512], F32) for _ in range(2)], "rstd")
        cq = self.alloc([2, 512], F32)
        cq_r = Res("cq")
        ckv = self.alloc([512], F32)
        ckv_r = Res("ckv")
        cqn = self.alloc([2, 512], BF16)
        cqn_r = Res("cqn")
        ckvn = self.alloc([512], BF16)
        ckvn_r = Res("ckvn")
        rsq = self.alloc([512], F32)
        rsq_r = Res("rsq")
        rskv = self.alloc([512], F32)
        rskv_r = Res("rskv")
        cs = self.alloc([2, 512], F32)
        cs_r = Res("cs")
        t1r = Ring([self.alloc([512], F32) for _ in range(2)], "t1")
        t2r = Ring([self.alloc([512], F32) for _ in range(2)], "t2")
        qst = Ring([self.alloc([4, 512], BF16) for _ in range(1)], "qst")
        kst = Ring([self.alloc([4, 512], BF16) for _ in range(1)], "kst")
        vst = Ring([self.alloc([4, 65], BF16) for _ in range(2)], "vst")
        psr = self.psring([0, 1, 2, 3, 4, 5], "p2")
        ps_n, ps_nr = self.psum[7], self.psum_r[7]
        ps_m, ps_mr = self.psum[6], self.psum_r[6]
        for (c0, n) in self.tiles9():
            xn, xn_r = xnr.next()
            rstd, rstd_r = rstr.next()
            self.norm_tile(c0, n, gk, xn, xn_r, sqr, rstd, rstd_r, ps_n, ps_nr)
            self.load(cs[0:96, :, 0:n], self.rope_cs[:, :, c0:c0 + n], cs_r)
            for j in range(3):
                pt, pr = psr.next()
                for k in range(NCH):
                    self.mm(pt[:, 0:n], Wp[:, k, j * 128:(j + 1) * 128], xn[:, k, 0:n], k == 0, k == NCH - 1,
                            [Wp_r, xn_r], [pr])
                if j < 2:
                    self.cp("act", cq[:, j, 0:n], pt[:, 0:n], [pr], [cq_r])
                else:
                    self.cp("act", ckv[:, 0:n], pt[:, 0:n], [pr], [ckv_r])
            for j in range(2):
                sq, sq_r = sqr.next()
                self.tt("pool", sq[:, 0:n], cq[:, j, 0:n], cq[:, j, 0:n], ALU.mult, [cq_r], [sq_r])
                self.mm(ps_m[:, 0:n], self.ones_bf[:, :], sq[:, 0:n], j == 0, j == 1, [sq_r, self.ones_r], [ps_mr])
            self.rsqrt_inplace(rsq[:, 0:n], rsq_r, 1.0 / 256, in_ap=ps_m[:, 0:n], in_reads=[ps_mr])
            for j in range(2):
                self.stt(cqn[:, j, 0:n], cq[:, j, 0:n], gq[:, j:j + 1], rsq[:, 0:n], ALU.mult, ALU.mult,
                         [cq_r, g_r, rsq_r], [cqn_r])
            sq, sq_r = sqr.next()
            self.tt("pool", sq[:, 0:n], ckv[:, 0:n], ckv[:, 0:n], ALU.mult, [ckv_r], [sq_r])
            self.mm(ps_m[:, 0:n], self.ones_bf[:, :], sq[:, 0:n], True, True, [sq_r, self.ones_r], [ps_mr])
            self.rsqrt_inplace(rskv[:, 0:n], rskv_r, 1.0 / 128, in_ap=ps_m[:, 0:n], in_reads=[ps_mr])
            self.stt(ckvn[:, 0:n], ckv[:, 0:n], gkv[:, 0:1], rskv[:, 0:n], ALU.mult, ALU.mult,
                     [ckv_r, g_r, rskv_r], [ckvn_r])
            qs, qs_r = qst.next()
            for h in range(4):
                pa, par = psr.next()
                pb, pbr = psr.next()
                for k in range(2):
                    self.mm(pa[0:96, 0:n], Wq[:, k, h * 96:(h + 1) * 96], cqn[:, k, 0:n], k == 0, k == 1, [Wq_r, cqn_r], [par])
                for k in range(2):
                    self.mm(pb[0:96, 0:n], Wq[:, k, 384 + h * 96:384 + (h + 1) * 96], cqn[:, k, 0:n], k == 0, k == 1,
                            [Wq_r, cqn_r], [pbr])
                t1, t1_r = t1r.next()
                t2, t2_r = t2r.next()
                self.tt("dve", t1[0:96, 0:n], pa[0:96, 0:n], cs[0:96, 0, 0:n], ALU.mult, [par, cs_r], [t1_r])
                self.tt("dve", t2[0:96, 0:n], pb[0:96, 0:n], cs[0:96, 1, 0:n], ALU.mult, [pbr, cs_r], [t2_r])
                self.tt("pool", qs[0:96, h, 0:n], t1[0:96, 0:n], t2[0:96, 0:n], ALU.add, [t1_r, t2_r], [qs_r])
            self.store(self.QM[:, :, c0:c0 + n].rearrange("h r t -> r h t"), qs[0:96, :, 0:n], qs_r)
            ks, ks_r = kst.next()
            for h in range(4):
                pt, pr = psr.next()
                self.mm(pt[0:64, 0:n], Wkv[:, h * 64:(h + 1) * 64], ckvn[:, 0:n], True, True, [Wkv_r, ckvn_r], [pr])
                self.cp("act", ks[0:64, h, 0:n], pt[0:64, 0:n], [pr], [ks_r])
            pa, par = psr.next()
            pb, pbr = psr.next()
            for k in range(NCH):
                self.mm(pa[0:96, 0:n], Wp[:, k, 320:416], xn[:, k, 0:n], k == 0, k == NCH - 1, [Wp_r, xn_r], [par])
            for k in range(NCH):
                self.mm(pb[0:96, 0:n], Wp[:, k, 416:512], xn[:, k, 0:n], k == 0, k == NCH - 1, [Wp_r, xn_r], [pbr])
            t1, t1_r = t1r.next()
            t2, t2_r = t2r.next()
            self.tt("dve", t1[64:96, 0:n], pa[64:96, 0:n], cs[64:96, 0, 0:n], ALU.mult, [par, cs_r], [t1_r])
            self.tt("dve", t2[64:96, 0:n], pb[64:96, 0:n], cs[64:96, 1, 0:n], ALU.mult, [pbr, cs_r], [t2_r])
            self.tt("pool", t1[64:96, 0:n], t1[64:96, 0:n], t2[64:96, 0:n], ALU.add, [t1_r, t2_r], [t1_r])
            for h in range(4):
                self.cp("pool" if h % 2 else "act", ks[64:96, h, 0:n], t1[64:96, 0:n], [t1_r], [ks_r])
            self.store(self.KM[:, :, c0:c0 + n].rearrange("h r t -> r h t"), ks[0:96, :, 0:n], ks_r)
            for j in range(0, n, 128):
                m = min(128, n - j)
                pt, pr = psr.next()
                self.mm(pt[0:m, 0:256], ckvn[:, j:j + m], Wkv[:, 256:512], True, True, [Wkv_r, ckvn_r], [pr])
                vs, vs_r = vst.next()
                self.P.add("pool", lambda e, vs=vs: e.memset(vs[:, :, 64:65], 1.0), writes=[vs_r])
                self.cp("act", vs[0:m, :, 0:64], pt[0:m, 0:256].rearrange("p (h d) -> p h d", h=4), [pr], [vs_r])
                self.store(self.VM[c0 + j:c0 + j + m, :], vs[0:m, :, :].rearrange("p h d -> p (h d)"), vs_r)

    def stage_lru(self, l):
        self.stage_begin()
        P = self.P
        pr_ = Res("lrup")
        convw = self.alloc([2, 4], F32)
        convb = self.alloc([2], F32)
        ba = self.alloc([2], F32)
        bx = self.alloc([2], F32)
        lam = self.alloc([2], F32)
        gl = self.alloc([2], F32)
        for c in range(2):
            for j in range(4):
                self.load(convw[:, c, j:j + 1], self.w["conv_w"][l][j, c * 128:(c + 1) * 128].rearrange("(p o) -> p o", o=1), pr_)
        self.load(convb[:, :], self.w["conv_b"][l].rearrange("(c p) -> p c", p=128), pr_)
        self.load(ba[:, :], self.w["lru_ba"][l].rearrange("n d -> (n d)").rearrange("(c p) -> p c", p=128), pr_)
        self.load(bx[:, :], self.w["lru_bx"][l].rearrange("n d -> (n d)").rearrange("(c p) -> p c", p=128), pr_)
        self.load(lam[:, :], self.w["lru_lambda"][l].rearrange("(c p) -> p c", p=128), pr_)
        self.load(gl[:, :], self.w["lru_out_norm"][l].rearrange("(c p) -> p c", p=128), pr_)
        nsp = self.alloc([2], F32)
        nsp2 = self.alloc([2], F32)
        self.act(nsp[:, :], lam[:, :], AF.Exp, [pr_], [pr_], scale=-1.0)
        self.ts("dve", nsp[:, :], nsp[:, :], 1.0, None, ALU.add, None, [pr_], [pr_])
        self.act(nsp[:, :], nsp[:, :], AF.Ln, [pr_], [pr_])
        self.ts("dve", nsp2[:, :], nsp[:, :], -16.0, None, ALU.mult, None, [pr_], [pr_])
        self.ts("dve", nsp[:, :], nsp[:, :], -8.0, None, ALU.mult, None, [pr_], [pr_])
        bd32 = self.alloc([2, 2, 128], F32)
        bd = self.alloc([2, 2, 128], BF16)
        bd_r = Res("bd")
        P.add("pool", lambda e: e.memset(bd32[:, :, :, :], 0.0), writes=[bd_r])
        for wi, nm in enumerate(("lru_wa", "lru_wx")):
            for nb in range(4):
                c, hb = nb // 2, nb % 2
                self.load(bd32[hb * 64:(hb + 1) * 64, wi, c, hb * 64:(hb + 1) * 64], self.w[nm][l][nb, :, :], bd_r)
        self.cp("dve", bd[:, :, :, :], bd32[:, :, :, :], [bd_r], [bd_r])
        carry = self.alloc([2], F32)
        carry_r = Res("carry")
        NB = 515
        uar = Ring([self.alloc([2, NB], F32) for _ in range(2)], "ua")
        gar = Ring([self.alloc([2, 512], F32) for _ in range(2)], "ga")
        fr = Ring([self.alloc([512], F32) for _ in range(10)], "lf")
        ucb_r = Ring([self.alloc([512], BF16) for _ in range(2)], "ucb")
        sqr = Ring([self.alloc([512], BF16) for _ in range(2)], "sq")
        yp = self.alloc([2, 512], F32)
        yp_r = Res("yp")
        rstd = self.alloc([512], F32)
        rstd_r = Res("rstd")
        yst = Ring([self.alloc([2, 512], BF16) for _ in range(2)], "yst")
        psr = self.psring([0, 1, 2, 3], "lru")
        ps_n, ps_nr = self.psum[7], self.psum_r[7]
        first = True
        for (c0, n) in self.tiles9():
            ua, ua_r = uar.next()
            ga, ga_r = gar.next()
            if first:
                P.add("pool", lambda e, ua=ua: e.memset(ua[:, :, 0:3], 0.0), writes=[ua_r])
                self.load(ua[:, :, 3:3 + n], self.LRU_IN[2:4, :, c0:c0 + n].rearrange("c p t -> p c t"), ua_r)
            else:
                self.load(ua[:, :, 0:3 + n], self.LRU_IN[2:4, :, c0 - 3:c0 + n].rearrange("c p t -> p c t"), ua_r)
            self.load(ga[:, :, 0:n], self.LRU_IN[0:2, :, c0:c0 + n].rearrange("c p t -> p c t"), ga_r)
            for j in range(2):
                uc, uc_r = fr.next()
                self.ts("dve", uc[:, 0:n], ua[:, j, 0:n], convw[:, j, 0:1], convb[:, j:j + 1], ALU.mult, ALU.add,
                        [ua_r, pr_], [uc_r])
                for t in range(1, 4):
                    self.stt(uc[:, 0:n], ua[:, j, t:t + n], convw[:, j, t:t + 1], uc[:, 0:n], ALU.mult, ALU.add,
                             [ua_r, pr_, uc_r], [uc_r])
                ucb, ucb_rr = ucb_r.next()
                self.cp("act", ucb[:, 0:n], uc[:, 0:n], [uc_r], [ucb_rr])
                pa, par = psr.next()
                px, pxr = psr.next()
                self.mm(pa[:, 0:n], bd[:, 0, j, :], ucb[:, 0:n], True, True, [bd_r, ucb_rr], [par])
                self.mm(px[:, 0:n], bd[:, 1, j, :], ucb[:, 0:n], True, True, [bd_r, ucb_rr], [pxr])
                r, r_r = fr.next()
                ig, ig_r = fr.next()
                self.act(r[:, 0:n], pa[:, 0:n], AF.Sigmoid, [par, pr_], [r_r], bias=ba[:, j:j + 1])
                self.act(ig[:, 0:n], px[:, 0:n], AF.Sigmoid, [pxr, pr_], [ig_r], bias=bx[:, j:j + 1])
                a, a_r = fr.next()
                a2, a2_r = fr.next()
                self.act(a[:, 0:n], r[:, 0:n], AF.Exp, [r_r, pr_], [a_r], scale=nsp[:, j:j + 1])
                self.act(a2[:, 0:n], r[:, 0:n], AF.Exp, [r_r, pr_], [a2_r], scale=nsp2[:, j:j + 1])
                self.ts("pool", a2[:, 0:n], a2[:, 0:n], -1.0, 1.0, ALU.mult, ALU.add, [a2_r], [a2_r])
                self.act(a2[:, 0:n], a2[:, 0:n], AF.Sqrt, [a2_r], [a2_r])
                self.tt("pool", ig[:, 0:n], ig[:, 0:n], uc[:, 0:n], ALU.mult, [ig_r, uc_r], [ig_r])
                self.tt("pool", ig[:, 0:n], ig[:, 0:n], a2[:, 0:n], ALU.mult, [ig_r, a2_r], [ig_r])
                hs, hs_r = fr.next()
                if first:
                    P.add("dve", lambda e, hs=hs, a=a, ig=ig, n=n: e.tensor_tensor_scan(
                        out=hs[:, 0:n], data0=a[:, 0:n], data1=ig[:, 0:n], initial=0.0, op0=ALU.mult, op1=ALU.add),
                        reads=[a_r, ig_r], writes=[hs_r])
                else:
                    P.add("dve", lambda e, hs=hs, a=a, ig=ig, n=n, j=j: e.tensor_tensor_scan(
                        out=hs[:, 0:n], data0=a[:, 0:n], data1=ig[:, 0:n], initial=carry[:, j:j + 1],
                        op0=ALU.mult, op1=ALU.add),
                        reads=[a_r, ig_r, carry_r], writes=[hs_r])
                self.cp("dve", carry[:, j:j + 1], hs[:, n - 1:n], [hs_r], [carry_r])
                g = ga[:, j, 0:n]
                t, t_r = fr.next()
                self.tt("pool", t[:, 0:n], g, g, ALU.mult, [ga_r], [t_r])
                self.ts("pool", t[:, 0:n], t[:, 0:n], 0.044715, 1.0, ALU.mult, ALU.add, [t_r], [t_r])
                self.tt("pool", t[:, 0:n], t[:, 0:n], g, ALU.mult, [t_r, ga_r], [t_r])
                self.act(t[:, 0:n], t[:, 0:n], AF.Sigmoid, [t_r], [t_r], scale=1.5957691216057308)
                self.tt("pool", t[:, 0:n], t[:, 0:n], g, ALU.mult, [t_r, ga_r], [t_r])
                self.tt("dve", yp[:, j, 0:n], hs[:, 0:n], t[:, 0:n], ALU.mult, [hs_r, t_r], [yp_r])
                sq, sq_r = sqr.next()
                self.tt("pool", sq[:, 0:n], yp[:, j, 0:n], yp[:, j, 0:n], ALU.mult, [yp_r], [sq_r])
                self.mm(ps_n[:, 0:n], self.ones_bf[:, :], sq[:, 0:n], j == 0, j == 1, [sq_r, self.ones_r], [ps_nr])
            self.rsqrt_inplace(rstd[:, 0:n], rstd_r, 1.0 / 256, in_ap=ps_n[:, 0:n], in_reads=[ps_nr])
            ys, ys_r = yst.next()
            for j in range(2):
                self.stt(ys[:, j, 0:n], yp[:, j, 0:n], gl[:, j:j + 1], rstd[:, 0:n], ALU.mult, ALU.mult,
                         [yp_r, pr_, rstd_r], [ys_r])
            self.store(self.YT[0:2, :, c0:c0 + n].rearrange("c p t -> p c t"), ys[:, :, 0:n], ys_r)
            first = False

    def attn_sweep(self, qi, Kt, Qt, r0, r1, kq_r, V, v_r, dvp, obanks, scale, s_ring, pt_ring, tri, tri_r):
        if qi < 0:
            q0, nq, nsub = 0, NMETA, 1
            chunks = [(0, 0, NMETA, 0)]
        else:
            q0, nq, nsub = NMETA + 512 * qi, 512, 4
            chunks = [(0, 0, NMETA, -1)] + [(1 + i, NMETA + 128 * i, 128, -1) for i in range(4 * qi)]
            chunks += [(1 + 4 * qi + c, NMETA + 128 * (4 * qi + c), 128, c) for c in range(4)]
        started = set()
        for (kc, kcol, nk, diag) in chunks:
            s0 = max(diag, 0)
            qa = q0 + 128 * s0 if qi >= 0 else 0
            nqa = nq - 128 * s0 if qi >= 0 else nq
            ps, ps_r = s_ring.next()
            self.mm(ps[0:nk, 0:nqa], Kt[r0:r1, kcol:kcol + nk], Qt[r0:r1, qa:qa + nqa], True, True, [kq_r], [ps_r])
            pt, pt_r = pt_ring.next()
            self.act(pt[0:nk, 0:nqa], ps[0:nk, 0:nqa], AF.Exp, [ps_r], [pt_r], scale=scale)
            if diag >= 0:
                w = min(128, nqa)
                self.tt("dve", pt[0:nk, 0:w], pt[0:nk, 0:w], tri[0:nk, 0:w], ALU.mult, [pt_r, tri_r], [pt_r])
            for s in range(s0, nsub):
                ob, ob_r, oc = obanks[s]
                ws = min(128, nq)
                lo = (s - s0) * 128
                bank_key = id(ob)
                st = bank_key not in started
                started.add(bank_key)
                self.mm(ob[0:ws, oc:oc + dvp], pt[0:nk, lo:lo + ws], V[0:nk, kc, 0:dvp], st, False,
                        [pt_r, v_r], [ob_r], skip_group_check=True)

    def load_consts_attn(self):
        tri32 = self.alloc([128], F32)
        tri = self.alloc([128], BF16)
        tri_r = Res("tri")
        self.load(tri32[:, :], self.tri_in[:, :], tri_r)
        self.cp("dve", tri[:, :], tri32[:, :], [tri_r], [tri_r])
        idb32 = self.alloc([128], F32)
        idb = self.alloc([128], BF16)
        idb_r = Res("idb")
        self.cp("dve", idb[:, :], self.identf[:, :], [self.identf_r], [idb_r])
        return tri, tri_r, idb, idb_r

    def stage_diff(self, l):
        self.stage_begin()
        P = self.P
        lam_init = 0.8 - 0.6 * math.exp(-0.3 * l)
        tri, tri_r, idb, idb_r = self.load_consts_attn()
        lv = self.alloc([4, 64], F32)
        lv_r = Res("lv")
        for i, nm in enumerate(("lam_q1", "lam_k1", "lam_q2", "lam_k2")):
            self.load(lv[:, i, :], self.w[nm][l].partition_broadcast(128), lv_r)
        lsc = self.alloc([8], F32)
        self.tt("dve", lv[:, 0, :], lv[:, 0, :], lv[:, 1, :], ALU.mult, [lv_r], [lv_r])
        self.tt("dve", lv[:, 2, :], lv[:, 2, :], lv[:, 3, :], ALU.mult, [lv_r], [lv_r])
        P.add("dve", lambda e: e.reduce_sum(out=lsc[:, 0:1], in_=lv[:, 0, :], axis=AX.X), reads=[lv_r], writes=[lv_r])
        P.add("dve", lambda e: e.reduce_sum(out=lsc[:, 1:2], in_=lv[:, 2, :], axis=AX.X), reads=[lv_r], writes=[lv_r])
        self.act(lsc[:, 0:2], lsc[:, 0:2], AF.Exp, [lv_r], [lv_r])
        self.tt("dve", lsc[:, 2:3], lsc[:, 1:2], lsc[:, 0:1], ALU.subtract, [lv_r], [lv_r])
        self.ts("dve", lsc[:, 2:3], lsc[:, 2:3], -lam_init, None, ALU.add, None, [lv_r], [lv_r])
        gsub = self.alloc([128], F32)
        self.load(gsub[:, :], self.w["diff_subln"][l].partition_broadcast(128), lv_r)
        self.ts("dve", gsub[:, :], gsub[:, :], 1.0 - lam_init, None, ALU.mult, None, [lv_r], [lv_r])

        Kt = self.alloc([T], BF16)
        Qt = self.alloc([T], BF16)
        kq_r = Res("kq")
        V = self.alloc([33, 129], BF16)
        v_r = Res("V")
        pt_ring = Ring([self.alloc([512], BF16) for _ in range(3)], "pt")
        o1r = Ring([self.alloc([128], F32) for _ in range(2)], "o1")
        o2r = Ring([self.alloc([128], F32) for _ in range(2)], "o2")
        scr = Ring([self.alloc([128], F32) for _ in range(2)], "scr")
        smr = Ring([self.alloc([8], F32) for _ in range(4)], "sm")
        ybr = Ring([self.alloc([128], BF16) for _ in range(2)], "yb")
        ytr = Ring([self.alloc([512], BF16) for _ in range(2)], "yts")
        s_ring = self.psring([0, 1, 2], "s")
        tps = self.psum[7][:, :].bitcast(BF16)
        tps_r = self.psum_r[7]
        for h in range(4):
            self.load(Qt[:, :], self.QKD[h, :, :], kq_r)
            self.load(Kt[:, :], self.QKD[4 + h, :, :], kq_r)
            P.add("pool", lambda e: e.memset(V[:, :, 128:129], 1.0), writes=[v_r])
            self.load(V[0:NMETA, 0, 0:128], self.VD[0:NMETA, h * 128:(h + 1) * 128], v_r)
            for i0 in range(0, 32, 8):
                self.load(V[:, 1 + i0:9 + i0, 0:128],
                          self.VD[NMETA + 128 * i0:NMETA + 128 * (i0 + 8), h * 128:(h + 1) * 128].rearrange("(i p) d -> p i d", p=128), v_r)
            for qi in range(-1, SEQ // 512):
                nsub = 1 if qi < 0 else 4
                ws = NMETA if qi < 0 else 128
                q0 = 0 if qi < 0 else NMETA + 512 * qi
                ob = {}
                for m in range(2):
                    banks = [(self.psum[3 + 2 * m], self.psum_r[3 + 2 * m], 0), (self.psum[3 + 2 * m], self.psum_r[3 + 2 * m], 129),
                             (self.psum[4 + 2 * m], self.psum_r[4 + 2 * m], 0), (self.psum[4 + 2 * m], self.psum_r[4 + 2 * m], 129)]
                    ob[m] = banks
                    self.attn_sweep(qi, Kt, Qt, 64 * m, 64 * m + 64, kq_r, V, v_r, 129, banks, 0.125, s_ring, pt_ring, tri, tri_r)
                yts, yts_r = ytr.next()
                for s in range(nsub):
                    sm, sm_r = smr.next()
                    o1, o1_r = o1r.next()
                    o2, o2_r = o2r.next()
                    for m, (o, o_r) in enumerate(((o1, o1_r), (o2, o2_r))):
                        b, b_r, oc = ob[m][s]
                        P.add("dve", lambda e, sm=sm, b=b, oc=oc, m=m, ws=ws: e.reciprocal(
                            out=sm[0:ws, m:m + 1], in_=b[0:ws, oc + 128:oc + 129]), reads=[b_r], writes=[sm_r])
                        self.act(o[0:ws, :], b[0:ws, oc:oc + 128], AF.Copy, [b_r, sm_r], [o_r], scale=sm[0:ws, m:m + 1])
                    self.stt(o1[0:ws, :], o2[0:ws, :], lsc[0:ws, 2:3], o1[0:ws, :], ALU.mult, ALU.add, [o1_r, o2_r, lv_r], [o1_r])
                    sc, sc_r = scr.next()
                    self.act(sc[0:ws, :], o1[0:ws, :], AF.Square, [o1_r], [sc_r, sm_r], accum_out=sm[0:ws, 2:3])
                    self.rsqrt_inplace(sm[0:ws, 2:3], sm_r, 1.0 / 128)
                    yb, yb_r = ybr.next()
                    self.stt(yb[0:ws, :], o1[0:ws, :], sm[0:ws, 2:3], gsub[0:ws, :], ALU.mult, ALU.mult, [o1_r, sm_r, lv_r], [yb_r])
                    P.add("pe", lambda e, yb=yb, s=s, ws=ws: e.transpose(out=tps[:, s * 128:s * 128 + ws], in_=yb[0:ws, :],
                                                                          identity=idb[0:ws, 0:ws]),
                          reads=[yb_r, idb_r], writes=[tps_r])
                    self.cp("pool" if False else "act", yts[:, s * 128:s * 128 + ws], tps[:, s * 128:s * 128 + ws], [tps_r], [yts_r])
                nq = NMETA if qi < 0 else 512
                self.store(self.YT[2 + h, :, q0:q0 + nq], yts[:, 0:nq], yts_r)

    def stage_mla(self, l):
        self.stage_begin()
        P = self.P
        tri, tri_r, idb, idb_r = self.load_consts_attn()
        gm = self.alloc([256], F32)
        gm_r = Res("gm")
        self.load(gm[:, :], self.w["mla_out_norm"][l].partition_broadcast(128), gm_r)
        Kt = self.alloc([4, T], BF16)
        kq_r = Res("kq")
        V = self.alloc([33, 4, 65], BF16)
        v_r = Res("V")
        self.load(Kt[0:96, :, :], self.KM[:, :, :].rearrange("h r t -> r h t"), kq_r)
        Vf = V.rearrange("p i h d -> p i (h d)")
        self.load(Vf[0:NMETA, 0, :], self.VM[0:NMETA, :], v_r)
        for i0 in range(0, 32, 8):
            self.load(Vf[:, 1 + i0:9 + i0, :],
                      self.VM[NMETA + 128 * i0:NMETA + 128 * (i0 + 8), :].rearrange("(i p) d -> p i d", p=128), v_r)
        qtr = Ring([self.alloc([4, 512], BF16) for _ in range(2)], "qt")
        pt_ring = Ring([self.alloc([512], BF16) for _ in range(3)], "pt")
        ocr = Ring([self.alloc([256], F32) for _ in range(2)], "oc")
        scr = Ring([self.alloc([256], F32) for _ in range(2)], "scr")
        smr = Ring([self.alloc([8], F32) for _ in range(4)], "sm")
        ybr = Ring([self.alloc([256], BF16) for _ in range(2)], "yb")
        ytr = Ring([self.alloc([2, 512], BF16) for _ in range(2)], "yts")
        s_ring = self.psring([0, 1, 2], "s")
        tps = self.psum[7][:, :].bitcast(BF16)
        tps_r = self.psum_r[7]
        scale = 96.0 ** -0.5
        for qi in range(-1, SEQ // 512):
            nsub = 1 if qi < 0 else 4
            ws = NMETA if qi < 0 else 128
            q0 = 0 if qi < 0 else NMETA + 512 * qi
            nq = NMETA if qi < 0 else 512
            qt, qt_r = qtr.next()
            self.load(qt[0:96, :, 0:nq], self.QM[:, :, q0:q0 + nq].rearrange("h r t -> r h t"), qt_r)
            ob = {}
            for h in range(4):
                banks = [(self.psum[3 + h], self.psum_r[3 + h], 65 * s) for s in range(4)]
                ob[h] = banks
                self.attn_sweep_local(qi, Kt[:, h, :], qt[:, h, :], kq_r, qt_r, V[:, :, h, :], v_r, 65, banks, scale,
                                      s_ring, pt_ring, tri, tri_r)
            yts, yts_r = ytr.next()
            for s in range(nsub):
                sm, sm_r = smr.next()
                oc, oc_r = ocr.next()
                for h in range(4):
                    b, b_r, c = ob[h][s]
                    P.add("dve", lambda e, sm=sm, b=b, c=c, h=h, ws=ws: e.reciprocal(
                        out=sm[0:ws, h:h + 1], in_=b[0:ws, c + 64:c + 65]), reads=[b_r], writes=[sm_r])
                    self.act(oc[0:ws, h * 64:(h + 1) * 64], b[0:ws, c:c + 64], AF.Copy, [b_r, sm_r], [oc_r],
                             scale=sm[0:ws, h:h + 1])
                sc, sc_r = scr.next()
                self.act(sc[0:ws, :], oc[0:ws, :], AF.Square, [oc_r], [sc_r, sm_r], accum_out=sm[0:ws, 4:5])
                self.rsqrt_inplace(sm[0:ws, 4:5], sm_r, 1.0 / 256)
                yb, yb_r = ybr.next()
                self.stt(yb[0:ws, :], oc[0:ws, :], sm[0:ws, 4:5], gm[0:ws, :], ALU.mult, ALU.mult, [oc_r, sm_r, gm_r], [yb_r])
                for c in range(2):
                    P.add("pe", lambda e, yb=yb, s=s, ws=ws, c=c: e.transpose(
                        out=tps[:, c * 512 + s * 128:c * 512 + s * 128 + ws], in_=yb[0:ws, c * 128:(c + 1) * 128],
                        identity=idb[0:ws, 0:ws]), reads=[yb_r, idb_r], writes=[tps_r])
                    self.cp("act", yts[:, c, s * 128:s * 128 + ws], tps[:, c * 512 + s * 128:c * 512 + s * 128 + ws],
                            [tps_r], [yts_r])
            self.store(self.YT[6:8, :, q0:q0 + nq].rearrange("c p t -> p c t"), yts[:, :, 0:nq], yts_r)

    def attn_sweep_local(self, qi, Kt, Qtile, kq_r, qt_r, V, v_r, dvp, obanks, scale, s_ring, pt_ring, tri, tri_r):
        if qi < 0:
            nq, nsub = NMETA, 1
            chunks = [(0, 0, NMETA, 0)]
        else:
            nq, nsub = 512, 4
            chunks = [(0, 0, NMETA, -1)] + [(1 + i, NMETA + 128 * i, 128, -1) for i in range(4 * qi)]
            chunks += [(1 + 4 * qi + c, NMETA + 128 * (4 * qi + c), 128, c) for c in range(4)]
        started = set()
        for (kc, kcol, nk, diag) in chunks:
            s0 = max(diag, 0)
            qa = 128 * s0 if qi >= 0 else 0
            nqa = nq - qa
            ps, ps_r = s_ring.next()
            self.mm(ps[0:nk, 0:nqa], Kt[0:96, kcol:kcol + nk], Qtile[0:96, qa:qa + nqa], True, True, [kq_r, qt_r], [ps_r])
            pt, pt_r = pt_ring.next()
            self.act(pt[0:nk, 0:nqa], ps[0:nk, 0:nqa], AF.Exp, [ps_r], [pt_r], scale=scale)
            if diag >= 0:
                w = min(128, nqa)
                self.tt("dve", pt[0:nk, 0:w], pt[0:nk, 0:w], tri[0:nk, 0:w], ALU.mult, [pt_r, tri_r], [pt_r])
            for s in range(s0, nsub):
                ob, ob_r, oc = obanks[s]
                ws = min(128, nq)
                lo = (s - s0) * 128
                st = id(ob) not in started
                started.add(id(ob))
                self.mm(ob[0:ws, oc:oc + dvp], pt[0:nk, lo:lo + ws], V[0:nk, kc, 0:dvp], st, False,
                        [pt_r, v_r], [ob_r], skip_group_check=True)

    def stage_oproj(self, l):
        self.stage_begin()
        Wo = self.alloc([NCH, D], BF16)
        Wo_r = Res("Wo")
        self.load_w_bf16(Wo, Wo_r, self.w["w_out"][l], D, slab=256)
        ytr = Ring([self.alloc([NCH, 512], BF16) for _ in range(2)], "yt")
        psr = self.psring([0, 1, 2, 3], "op")
        for (c0, n) in self.tiles9():
            yt, yt_r = ytr.next()
            self.load(yt[:, :, 0:n], self.YT[:, :, c0:c0 + n].rearrange("c p t -> p c t"), yt_r)
            for o in range(NCH):
                po, por = psr.next()
                for k in range(NCH):
                    self.mm(po[:, 0:n], Wo[:, k, o * 128:(o + 1) * 128], yt[:, k, 0:n], k == 0, k == NCH - 1, [Wo_r, yt_r], [por])
                hr = self.h_res(o, c0, n)
                self.stt(self.hT[:, o, c0:c0 + n], po[:, 0:n], 1.0, self.hT[:, o, c0:c0 + n], ALU.mult, ALU.add,
                         [por] + hr, hr)


    def stage_final(self):
        P = self.P
        self.stage_begin()
        gbc = self.alloc([D], F32)
        gbc_r = Res("gbc")
        P.add("sp", lambda e: e.dma_start(out=gbc[:, :], in_=self.final_norm.partition_broadcast(128)),
              writes=[gbc_r], dma=True, sem_res=gbc_r)
        otr = Ring([self.alloc([D], F32) for i in range(2)], "ot")
        sq = self.alloc([512], F32)
        sq_r = Res("fsq")
        ssr = Ring([self.alloc([4], F32) for i in range(2)], "fss")
        psr = self.psring([0, 1, 2, 3], "f")
        for i in range(SEQ // 128):
            c0 = NMETA + 128 * i
            o, o_r = otr.next()
            s, s_r = ssr.next()
            halves = []
            for half in range(2):
                pt, pr = psr.next()
                for cc in range(4):
                    c = half * 4 + cc
                    P.add("pe", lambda e, pt=pt, cc=cc, c=c, c0=c0: e.transpose(
                        out=pt[:, cc * 128:(cc + 1) * 128], in_=self.hT[:, c, c0:c0 + 128],
                        identity=self.identf[:, :]),
                        reads=self.h_res(c, c0, 128) + [self.identf_r], writes=[pr])
                halves.append((pt, pr))
            for half, (pt, pr) in enumerate(halves):
                P.add("act", lambda e, pt=pt, s=s, half=half: e.activation(
                    out=sq[:, 0:512], in_=pt[:, :], func=AF.Square, accum_out=s[:, half:half + 1]),
                    reads=[pr], writes=[sq_r, s_r])
            P.add("dve", lambda e, s=s: e.tensor_tensor(out=s[:, 2:3], in0=s[:, 0:1], in1=s[:, 1:2], op=ALU.add),
                  reads=[s_r], writes=[s_r])
            P.add("dve", lambda e, s=s: e.tensor_scalar(out=s[:, 2:3], in0=s[:, 2:3], scalar1=1.0 / D, scalar2=EPS,
                                                         op0=ALU.mult, op1=ALU.add),
                  reads=[s_r], writes=[s_r])
            P.add("act", lambda e, s=s: e.activation(out=s[:, 3:4], in_=s[:, 2:3], func=AF.Sqrt),
                  reads=[s_r], writes=[s_r])
            P.add("dve", lambda e, s=s: e.reciprocal(out=s[:, 3:4], in_=s[:, 3:4]),
                  reads=[s_r], writes=[s_r])
            for half, (pt, pr) in enumerate(halves):
                P.add("dve", lambda e, pt=pt, s=s, o=o, half=half: e.scalar_tensor_tensor(
                    out=o[:, half * 512:(half + 1) * 512], in0=pt[:, :], scalar=s[:, 3:4],
                    in1=gbc[:, half * 512:(half + 1) * 512], op0=ALU.mult, op1=ALU.mult),
                    reads=[pr, s_r, gbc_r], writes=[o_r])
            P.add("sp", lambda e, o=o, i=i: e.dma_start(out=self.out[128 * i:128 * (i + 1), :], in_=o[:, :]),
                  reads=[o_r], dma=True, sem_res=o_r)
        self.P.barrier()


def build_program(n_layers=DEPTH, debug=None):
    b = Builder(n_layers, debug)
    nc = b.build()
    return nc, b


_CONSTS = None


def consts():
    global _CONSTS
    if _CONSTS is None:
        half = 16
        inv = (np.float32(10000.0) ** (-(np.arange(half, dtype=np.float32)) / np.float32(half))).astype(np.float32)
        ang = (np.arange(T, dtype=np.float32)[None, :] * inv[:, None]).astype(np.float32).astype(np.float64)
        cs = np.zeros((96, 2, T), np.float32)
        cs[0:64, 0, :] = 1.0
        cs[64:80, 0, :] = np.cos(ang)
        cs[80:96, 0, :] = np.cos(ang)
        cs[64:80, 1, :] = -np.sin(ang)
        cs[80:96, 1, :] = np.sin(ang)
        tri = (np.arange(128)[:, None] <= np.arange(128)[None, :]).astype(np.float32)
        _CONSTS = {"ident_f": np.eye(128, dtype=np.float32), "rope_cs": cs, "tri_in": tri}
    return _CONSTS


def make_in_maps(inputs, b):
    c = consts()
    maps = []
    for core in range(8):
        m = {}
        for nm in b.inputs:
            if nm == "x":
                m[nm] = np.ascontiguousarray(inputs["x"][core])
            elif nm in c:
                m[nm] = c[nm]
            else:
                m[nm] = np.ascontiguousarray(inputs[nm])
        maps.append(m)
    return maps


def kernel(**inputs):
    inputs = {k: np.asarray(v) for k, v in inputs.items()}
    nc, b = build_program()
    maps = make_in_maps(inputs, b)
    res = run_bass_kernel_spmd(nc, maps, core_ids=list(range(8)))
    out = np.stack([np.asarray(res.results[i]["out"]) for i in range(8)], axis=0)
    return out.astype(np.float32)
```

```python
import contextlib
import math
import numpy as np
import ml_dtypes
import concourse.bass as bass
import concourse.mybir as mybir
from concourse.bass_utils import run_bass_kernel_spmd

F32 = mybir.dt.float32
BF16 = mybir.dt.bfloat16
AF = mybir.ActivationFunctionType
ALU = mybir.AluOpType
AX = mybir.AxisListType

D = 1024
SEQ = 4096
NMETA = 16
T = SEQ + NMETA
DFF = 2816
NF = DFF // 128
NCH = D // 128
DEPTH = 2
EPS = 1e-6
DIN = 2464


class Res:
    __slots__ = ("name", "w", "r", "slot")

    def __init__(self, name):
        self.name = name
        self.w = None
        self.r = []
        self.slot = None


class SemSlot:
    __slots__ = ("sem", "dcount")

    def __init__(self):
        self.sem = None
        self.dcount = 0


class Op:
    __slots__ = ("eng", "fn", "deps", "dma", "sem_res", "val", "needs_inc", "idx", "gen")

    def __init__(self, eng, fn, dma=False):
        self.eng = eng
        self.fn = fn
        self.deps = []
        self.dma = dma
        self.sem_res = None
        self.val = 0
        self.needs_inc = False


ENGS = ("pe", "act", "dve", "pool", "sp")
MAX_DMA_INFLIGHT = 4


class Prog:
    def __init__(self):
        self.q = {e: [] for e in ENGS}
        self.pool = []
        self.stage_k = 0
        self.gen = 0
        self.last_dma = {}
        self.dma_fifo = []

    def barrier(self):
        deps = []
        for e in ENGS:
            for op in reversed(self.q[e]):
                if not op.dma and op.fn is not None:
                    deps.append((op, 0))
                    break
        for op in self.last_dma.values():
            deps.append((op, op.sem_res.dcount))
        for e in ENGS:
            op = Op(e, None)
            op.gen = self.gen
            op.deps = [(o, s) for (o, s) in deps if not (o.eng == e and not o.dma)]
            for o, _ in op.deps:
                o.needs_inc = True
            self.q[e].append(op)
        self.gen += 1
        self.stage_k = 0
        self.last_dma = {}

    def add(self, eng, fn, reads=(), writes=(), dma=False, sem_res=None):
        op = Op(eng, fn, dma)
        op.gen = self.gen
        deps = {}

        def dep(o):
            if o is None or o.gen < self.gen:
                return
            if (not o.dma) and o.eng == "pe" and eng == "pe" and not dma:
                return
            snap = o.sem_res.dcount if o.dma else 0
            deps[id(o)] = (o, snap)

        for r in reads:
            dep(r.w)
        for w in writes:
            dep(w.w)
            for o in w.r:
                dep(o)
        if dma:
            if len(self.dma_fifo) >= MAX_DMA_INFLIGHT:
                old = self.dma_fifo[-MAX_DMA_INFLIGHT]
                if old.gen == self.gen and id(old) not in deps:
                    deps[id(old)] = (old, 0)
            self.dma_fifo.append(op)
        op.deps = list(deps.values())
        for o, _ in op.deps:
            o.needs_inc = True
        if dma:
            assert sem_res is not None
            if sem_res.slot is None or sem_res.slot[1] != self.gen:
                if self.stage_k >= len(self.pool):
                    self.pool.append(SemSlot())
                sem_res.slot = (self.pool[self.stage_k], self.gen)
                self.stage_k += 1
            sl = sem_res.slot[0]
            sl.dcount += 16
            op.sem_res = sl
            op.val = sl.dcount
            self.last_dma[id(sl)] = op
        for r in reads:
            if not dma:
                r.r = [o for o in r.r if o.dma or o.eng != eng]
            r.r.append(op)
        for w in writes:
            w.w = op
            w.r = []
        self.q[eng].append(op)
        return op

    def emit(self, nc, es):
        esem = {e: es.enter_context(nc.semaphore("s_" + e)) for e in ENGS}
        for i, sl in enumerate(self.pool):
            sl.sem = es.enter_context(nc.semaphore("d%d" % i))
        for e in ENGS:
            cnt = 0
            for op in self.q[e]:
                if op.dma or op.fn is None:
                    continue
                if op.needs_inc:
                    cnt += 1
                    op.val = cnt
        block = es.enter_context(nc.Block())
        engobj = {"pe": "tensor", "act": "scalar", "dve": "vector", "pool": "gpsimd", "sp": "sync"}

        def body_for(e):
            ops = self.q[e]

            def body(eng):
                seen = {}
                for op in ops:
                    for d, snap in op.deps:
                        if d.dma:
                            sem, val = d.sem_res.sem, max(d.val, snap)
                        else:
                            sem, val = esem[d.eng], d.val
                        key = id(sem)
                        if seen.get(key, 0) >= val:
                            continue
                        seen[key] = val
                        eng.wait_ge(sem, val)
                    if op.fn is None:
                        continue
                    ins = op.fn(eng)
                    if op.dma:
                        ins.then_inc(op.sem_res.sem, 16)
                    elif op.needs_inc:
                        ins.then_inc(esem[e], 1)
            return body

        for e in ENGS:
            getattr(block, engobj[e])(body_for(e))


class Ring:
    def __init__(self, aps, name):
        self.aps = aps
        self.res = [Res("%s%d" % (name, i)) for i in range(len(aps))]
        self.i = 0

    def next(self):
        k = self.i % len(self.aps)
        self.i += 1
        return self.aps[k], self.res[k]


ARENA_BYTES = 79360


def dsize(dt):
    return 4 if dt == F32 else 2


class Builder:
    def __init__(self, n_layers=DEPTH, debug=None):
        self.n_layers = n_layers
        self.debug = debug or {}
        self.nc = bass.Bass("TRN2", target_bir_lowering=False)
        self.P = Prog()
        self.es = contextlib.ExitStack()
        self.inputs = {}
        self.aoff = 0

    def din(self, name, shape, dt=F32):
        ap = self.nc.dram_tensor(name, list(shape), dt, kind="ExternalInput").ap()
        self.inputs[name] = ap
        return ap

    def dout(self, name, shape, dt=F32):
        return self.nc.dram_tensor(name, list(shape), dt, kind="ExternalOutput").ap()

    def dscratch(self, name, shape, dt):
        return self.nc.dram_tensor(name, list(shape), dt, kind="Internal").ap()

    def sb(self, name, shape, dt):
        return self.es.enter_context(self.nc.sbuf_tensor(name, list(shape), dt))

    def ps(self, name, shape, dt=F32):
        return self.es.enter_context(self.nc.psum_tensor(name, list(shape), dt))

    def stage_begin(self):
        self.P.barrier()
        self.aoff = 0

    def alloc(self, shape, dt):
        n = int(np.prod(shape))
        nbytes = (n * dsize(dt) + 31) // 32 * 32
        assert self.aoff + nbytes <= ARENA_BYTES, (self.aoff, nbytes)
        ap = self.arena[:, self.aoff // 4:(self.aoff + nbytes) // 4]
        self.aoff += nbytes
        if dt != F32:
            ap = ap.bitcast(dt)
        ap = ap[:, 0:n]
        if len(shape) == 2:
            ap = ap.rearrange("p (a b) -> p a b", a=shape[0])
        elif len(shape) == 3:
            ap = ap.rearrange("p (a b c) -> p a b c", a=shape[0], b=shape[1])
        return ap


    def act(self, out, in_, func, reads, writes, **kw):
        return self.P.add("act", lambda e: e.activation(out=out, in_=in_, func=func, **kw), reads=reads, writes=writes)

    def tt(self, eng, out, in0, in1, op, reads, writes):
        return self.P.add(eng, lambda e: e.tensor_tensor(out=out, in0=in0, in1=in1, op=op), reads=reads, writes=writes)

    def ts(self, eng, out, in0, s1, s2, op0, op1, reads, writes):
        if s2 is None:
            return self.P.add(eng, lambda e: e.tensor_scalar(out=out, in0=in0, scalar1=s1, scalar2=None, op0=op0),
                              reads=reads, writes=writes)
        return self.P.add(eng, lambda e: e.tensor_scalar(out=out, in0=in0, scalar1=s1, scalar2=s2, op0=op0, op1=op1),
                          reads=reads, writes=writes)

    def stt(self, out, in0, scalar, in1, op0, op1, reads, writes):
        return self.P.add("dve", lambda e: e.scalar_tensor_tensor(out=out, in0=in0, scalar=scalar, in1=in1, op0=op0, op1=op1),
                          reads=reads, writes=writes)

    def cp(self, eng, out, in_, reads, writes):
        if eng == "act":
            return self.act(out, in_, AF.Copy, reads, writes)
        return self.P.add(eng, lambda e: e.tensor_copy(out=out, in_=in_), reads=reads, writes=writes)

    def mm(self, out, lhsT, rhs, start, stop, reads, writes, **kw):
        return self.P.add("pe", lambda e: e.matmul(out, lhsT=lhsT, rhs=rhs, start=start, stop=stop, **kw),
                          reads=reads, writes=writes)

    def load(self, out, in_, res, reads=()):
        return self.P.add("sp", lambda e: e.dma_start(out=out, in_=in_), reads=list(reads), writes=[res], dma=True, sem_res=res)

    def store(self, out, in_, res):
        return self.P.add("sp", lambda e: e.dma_start(out=out, in_=in_), reads=[res], dma=True, sem_res=res)

    def rsqrt_inplace(self, ap, res, scale, in_ap=None, in_reads=()):
        src = ap if in_ap is None else in_ap
        self.ts("dve", ap, src, scale, EPS, ALU.mult, ALU.add, [res] + list(in_reads), [res])
        self.act(ap, ap, AF.Sqrt, [res], [res])
        self.P.add("dve", lambda e: e.reciprocal(out=ap, in_=ap), reads=[res], writes=[res])

    def h_res(self, c, c0, n):
        out = []
        if c0 < NMETA:
            out.append(self.hres[c][0])
        lo = max(c0, NMETA) - NMETA
        hi = c0 + n - NMETA
        if hi > lo:
            for i in range(lo // 128, (hi - 1) // 128 + 1):
                out.append(self.hres[c][1 + i])
        return out

    def build(self):
        nc, P = self.nc, self.P
        self.x = self.din("x", [SEQ, D])
        self.meta = self.din("meta_tokens", [NMETA, D])
        self.final_norm = self.din("final_norm", [D])
        self.ident_f = self.din("ident_f", [128, 128])
        L = DEPTH
        self.w = {}
        for nm, shp in [("ffn1_norm", [L, D]), ("ffn1_in", [L, D, 2 * DFF]), ("ffn1_out", [L, DFF, D]),
                        ("ffn2_norm", [L, D]), ("ffn2_in", [L, D, 2 * DFF]), ("ffn2_out", [L, DFF, D])]:
            self.w[nm] = self.din(nm, shp)
        for nm, shp in [("mix_norm", [L, D]), ("w_in", [L, D, DIN]), ("conv_w", [L, 4, 256]), ("conv_b", [L, 256]),
                        ("lru_wa", [L, 4, 64, 64]), ("lru_ba", [L, 4, 64]), ("lru_wx", [L, 4, 64, 64]), ("lru_bx", [L, 4, 64]),
                        ("lru_lambda", [L, 256]), ("lru_out_norm", [L, 256]), ("lam_q1", [L, 64]), ("lam_k1", [L, 64]),
                        ("lam_q2", [L, 64]), ("lam_k2", [L, 64]), ("diff_subln", [L, 128]), ("q_norm", [L, 256]),
                        ("w_uq", [L, 256, 384]), ("kv_norm", [L, 128]), ("w_ukv", [L, 128, 512]),
                        ("mla_out_norm", [L, 256]), ("w_out", [L, D, D])]:
            self.w[nm] = self.din(nm, shp)
        self.rope_cs = self.din("rope_cs", [96, 2, T])
        self.tri_in = self.din("tri_in", [128, 128])
        self.out = self.dout("out", [SEQ, D])
        self.LRU_IN = self.dscratch("lru_in", [4, 128, T], F32)
        self.QKD = self.dscratch("qkd", [8, 128, T], BF16)
        self.VD = self.dscratch("vd", [T, 512], BF16)
        self.QM = self.dscratch("qm", [4, 96, T], BF16)
        self.KM = self.dscratch("km", [4, 96, T], BF16)
        self.VM = self.dscratch("vm", [T, 260], BF16)
        self.YT = self.dscratch("yt", [8, 128, T], BF16)
        self.wf = {}
        for l in range(L):
            for which in (1, 2):
                self.wf[(l, which)] = self.dscratch("wf_%d_%d" % (l, which), [NF, 128, 3072], BF16)
        self.wf_r = {k: Res("wf%d%d" % k) for k in self.wf}

        self.hT = self.sb("hT", [128, NCH, T], F32)
        self.hres = [[Res("h%d_%d" % (c, i)) for i in range(1 + SEQ // 128)] for c in range(NCH)]
        self.arena = self.sb("arena", [128, ARENA_BYTES // 4], F32)
        self.identf = self.sb("identf", [128, 128], F32)
        self.identf_r = Res("identf")
        self.ones_bf = self.sb("ones_bf", [128, 128], BF16)
        self.ones_r = Res("ones")
        self.gains = self.sb("gains", [128, 8, NCH], F32)
        self.gains_r = Res("gains")

        P.add("sp", lambda e: e.dma_start(out=self.identf[:], in_=self.ident_f[:, :]),
              writes=[self.identf_r], dma=True, sem_res=self.identf_r)
        P.add("pool", lambda e: e.memset(self.ones_bf[:], 1.0), writes=[self.ones_r])
        self.gain_idx = {}
        gi = 0
        for l in range(L):
            for nm in ("ffn1_norm", "ffn2_norm", "mix_norm"):
                src = self.w[nm][l, :].rearrange("(c p) -> p c", p=128)
                P.add("sp", lambda e, k=gi, src=src: e.dma_start(out=self.gains[:, k, :], in_=src),
                      writes=[self.gains_r], dma=True, sem_res=self.gains_r)
                self.gain_idx[(nm, l)] = gi
                gi += 1
        src = self.final_norm.rearrange("(c p) -> p c", p=128)
        P.add("sp", lambda e, k=gi, src=src: e.dma_start(out=self.gains[:, k, :], in_=src),
              writes=[self.gains_r], dma=True, sem_res=self.gains_r)
        self.gain_idx["final"] = gi

        self.psum = [self.ps("ps%d" % i, [128, 512], F32) for i in range(8)]
        self.psum_r = [Res("ps%d" % i) for i in range(8)]

        self.stage_load_x()
        ffns = [(l, w) for l in range(self.n_layers) for w in (1, 2) if self.debug.get("ffn%d" % w, True)]
        for (l, w) in ffns:
            self.stage_convert_ffn(l, w)
        for l in range(self.n_layers):
            if self.debug.get("ffn1", True):
                self.stage_ffn(l, 1)
            if self.debug.get("mixer", True):
                if self.debug.get("proj1", True):
                    self.stage_proj1(l)
                if self.debug.get("proj2", True):
                    self.stage_proj2(l)
                if self.debug.get("lru", True):
                    self.stage_lru(l)
                if self.debug.get("diff", True):
                    self.stage_diff(l)
                if self.debug.get("mla", True):
                    self.stage_mla(l)
                if self.debug.get("oproj", True):
                    self.stage_oproj(l)
            if self.debug.get("ffn2", True):
                self.stage_ffn(l, 2)
        self.stage_final()
        with nc.allow_non_contiguous_dma(reason="small strided parameter loads"):
            P.emit(nc, self.es)
        return nc

    def psring(self, idxs, name):
        r = Ring([self.psum[i] for i in idxs], name)
        r.res = [self.psum_r[i] for i in idxs]
        return r

    def stage_load_x(self):
        P = self.P
        self.stage_begin()
        ring = Ring([self.alloc([D], F32) for i in range(3)], "xin")
        psr = self.psring([0, 1, 2, 3], "x")
        n = 0
        for i in range(-1, SEQ // 128):
            buf, br = ring.next()
            if i < 0:
                rows, c0 = NMETA, 0
                src = self.meta[:, :]
            else:
                rows, c0 = 128, NMETA + 128 * i
                src = self.x[128 * i:128 * (i + 1), :]
            P.add("sp", lambda e, buf=buf, rows=rows, src=src: e.dma_start(out=buf[0:rows, :], in_=src),
                  writes=[br], dma=True, sem_res=br)
            for half in range(2):
                pt, pr = psr.next()
                for cc in range(4):
                    c = half * 4 + cc
                    P.add("pe", lambda e, pt=pt, cc=cc, buf=buf, rows=rows, c=c: e.transpose(
                        out=pt[:, cc * 128:cc * 128 + rows], in_=buf[0:rows, c * 128:(c + 1) * 128],
                        identity=self.identf[0:rows, 0:rows]),
                        reads=[br, self.identf_r], writes=[pr])
                eng = "act" if (n % 2 == 0) else "dve"
                n += 1
                hr = []
                for cc in range(4):
                    hr += self.h_res(half * 4 + cc, c0, rows)
                dst = self.hT[:, half * 4:half * 4 + 4, c0:c0 + rows]
                srcp = pt[:, :].rearrange("p (c t) -> p c t", c=4)[:, :, 0:rows]
                if eng == "act":
                    P.add("act", lambda e, dst=dst, srcp=srcp: e.activation(out=dst, in_=srcp, func=AF.Copy),
                          reads=[pr], writes=hr)
                else:
                    P.add("dve", lambda e, dst=dst, srcp=srcp: e.tensor_copy(out=dst, in_=srcp),
                          reads=[pr], writes=hr)

    def stage_convert_ffn(self, l, which):
        P = self.P
        self.stage_begin()
        w_in = self.w["ffn%d_in" % which][l]
        w_out = self.w["ffn%d_out" % which][l]
        WF = self.wf[(l, which)]
        wfr = self.wf_r[(l, which)]
        s32 = Ring([self.alloc([2, NCH, 256], F32) for _ in range(2)], "cv32")
        s32o = Ring([self.alloc([2, D], F32) for _ in range(2)], "cv32o")
        s16 = Ring([self.alloc([2, 3072], BF16) for _ in range(2)], "cv16")
        engs = ["act", "dve", "pool"]
        n = 0

        def cast(eng, out, in_):
            if eng == "act":
                return lambda e: e.activation(out=out, in_=in_, func=AF.Copy)
            return lambda e: e.tensor_copy(out=out, in_=in_)

        for fp in range(NF // 2):
            f0 = 2 * fp
            a, ar = s32.next()
            ao, aor = s32o.next()
            b, br = s16.next()
            for gu in range(2):
                col = gu * DFF + f0 * 128
                src = w_in[:, col:col + 256].rearrange("(k p) j -> p k j", p=128)
                P.add("sp", lambda e, a=a, gu=gu, src=src: e.dma_start(out=a[:, gu, :, :], in_=src),
                      writes=[ar], dma=True, sem_res=ar)
            src = w_out[f0 * 128:(f0 + 2) * 128, :].rearrange("(f p) c -> p f c", p=128)
            P.add("sp", lambda e, ao=ao, src=src: e.dma_start(out=ao[:, :, :], in_=src),
                  writes=[aor], dma=True, sem_res=aor)
            for ff in range(2):
                for gu in range(2):
                    eng = engs[n % 3]
                    n += 1
                    out = b[:, ff, 0:2048].rearrange("p (k j) -> p k j", k=NCH)[:, :, gu * 128:(gu + 1) * 128]
                    in_ = a[:, gu, :, ff * 128:(ff + 1) * 128]
                    P.add(eng, cast(eng, out, in_), reads=[ar], writes=[br])
            eng = engs[n % 3]
            n += 1
            P.add(eng, cast(eng, b[:, :, 2048:3072], ao[:, :, :]), reads=[aor], writes=[br])
            dst = WF[f0:f0 + 2, :, :].rearrange("f p c -> p f c")
            P.add("sp", lambda e, b=b, dst=dst: e.dma_start(out=dst, in_=b[:, :, :]),
                  reads=[br], writes=[wfr], dma=True, sem_res=br)

    def stage_ffn(self, l, which):
        P = self.P
        self.stage_begin()
        WF = self.wf[(l, which)]
        wfr = self.wf_r[(l, which)]
        gk = self.gain_idx[("ffn%d_norm" % which, l)]
        xn = self.alloc([NCH, 1040], BF16)
        hid = Ring([self.alloc([1040], BF16) for _ in range(4)], "hid")
        wsl = Ring([self.alloc([3072], BF16) for _ in range(6)], "wsl")
        sqr = Ring([self.alloc([512], BF16) for _ in range(2)], "sq")
        rstd = self.alloc([1040], F32)
        rstd_r = Res("rstd")
        sgr = Ring([self.alloc([512], F32) for _ in range(2)], "sg")
        ps_gu = self.psring([0, 1, 2, 3], "gu")
        ps_o = self.psring([4, 5, 6], "o")
        ps_n, ps_nr = self.psum[7], self.psum_r[7]

        sts = [[(0, 16), (16, 512), (528, 512)]]
        for s in range(1, 4):
            b0 = 1040 + 1024 * (s - 1)
            sts.append([(b0, 512), (b0 + 512, 512)])
        nsq = 0
        for subs in sts:
            base = subs[0][0]
            xn_r = [Res("xn%d" % i) for i in range(len(subs))]
            for si, (c0, n) in enumerate(subs):
                lc = c0 - base
                for c in range(NCH):
                    sq, sq_r = sqr.next()
                    src = self.hT[:, c, c0:c0 + n]
                    if nsq % 2 == 0:
                        P.add("act", lambda e, sq=sq, src=src, n=n: e.activation(out=sq[:, 0:n], in_=src, func=AF.Square),
                              reads=self.h_res(c, c0, n), writes=[sq_r])
                    else:
                        P.add("pool", lambda e, sq=sq, src=src, n=n: e.tensor_tensor(out=sq[:, 0:n], in0=src, in1=src, op=ALU.mult),
                              reads=self.h_res(c, c0, n), writes=[sq_r])
                    nsq += 1
                    P.add("pe", lambda e, sq=sq, n=n, c=c: e.matmul(ps_n[:, 0:n], lhsT=self.ones_bf[:, :], rhs=sq[:, 0:n],
                                                                     start=(c == 0), stop=(c == NCH - 1)),
                          reads=[sq_r, self.ones_r], writes=[ps_nr])
                rs = rstd[:, lc:lc + n]
                P.add("dve", lambda e, rs=rs, n=n: e.tensor_scalar(out=rs, in0=ps_n[:, 0:n], scalar1=1.0 / D, scalar2=EPS,
                                                                   op0=ALU.mult, op1=ALU.add),
                      reads=[ps_nr], writes=[rstd_r])
                P.add("act", lambda e, rs=rs: e.activation(out=rs, in_=rs, func=AF.Sqrt), reads=[rstd_r], writes=[rstd_r])
                P.add("dve", lambda e, rs=rs: e.reciprocal(out=rs, in_=rs), reads=[rstd_r], writes=[rstd_r])
                for c in range(NCH):
                    P.add("dve", lambda e, c=c, c0=c0, n=n, lc=lc, rs=rs: e.scalar_tensor_tensor(
                        out=xn[:, c, lc:lc + n], in0=self.hT[:, c, c0:c0 + n], scalar=self.gains[:, gk, c:c + 1],
                        in1=rs, op0=ALU.mult, op1=ALU.mult),
                        reads=self.h_res(c, c0, n) + [rstd_r, self.gains_r], writes=[xn_r[si]])

            def GU(f):
                slot, slot_r = wsl.next()
                P.add("sp", lambda e, slot=slot, f=f: e.dma_start(out=slot[:, :], in_=WF[f, :, :]),
                      reads=[wfr], writes=[slot_r], dma=True, sem_res=slot_r)
                hb, hb_r = hid.next()
                for si, (c0, n) in enumerate(subs):
                    lc = c0 - base
                    pg, pgr = ps_gu.next()
                    pu, pur = ps_gu.next()
                    for gu, (pp, ppr) in enumerate(((pg, pgr), (pu, pur))):
                        for k in range(NCH):
                            P.add("pe", lambda e, pp=pp, slot=slot, k=k, gu=gu, lc=lc, n=n: e.matmul(
                                pp[:, 0:n], lhsT=slot[:, k * 256 + gu * 128:k * 256 + gu * 128 + 128],
                                rhs=xn[:, k, lc:lc + n], start=(k == 0), stop=(k == NCH - 1)),
                                reads=[slot_r, xn_r[si]], writes=[ppr])
                    sg, sg_r = sgr.next()
                    P.add("act", lambda e, sg=sg, pg=pg, n=n: e.activation(out=sg[:, 0:n], in_=pg[:, 0:n], func=AF.Silu),
                          reads=[pgr], writes=[sg_r])
                    P.add("dve", lambda e, sg=sg, pu=pu, hb=hb, lc=lc, n=n: e.tensor_tensor(
                        out=hb[:, lc:lc + n], in0=sg[:, 0:n], in1=pu[:, 0:n], op=ALU.mult),
                        reads=[sg_r, pur], writes=[hb_r])
                return (slot, slot_r, hb, hb_r)

            def W2(group):
                for o in range(NCH):
                    for si, (c0, n) in enumerate(subs):
                        lc = c0 - base
                        po, por = ps_o.next()
                        for gi_, (slot, slot_r, hb, hb_r) in enumerate(group):
                            P.add("pe", lambda e, po=po, slot=slot, hb=hb, o=o, lc=lc, n=n, gi_=gi_, ng=len(group): e.matmul(
                                po[:, 0:n], lhsT=slot[:, 2048 + o * 128:2048 + (o + 1) * 128], rhs=hb[:, lc:lc + n],
                                start=(gi_ == 0), stop=(gi_ == ng - 1)),
                                reads=[slot_r, hb_r], writes=[por])
                        hr = self.h_res(o, c0, n)
                        P.add("dve", lambda e, po=po, o=o, c0=c0, n=n: e.scalar_tensor_tensor(
                            out=self.hT[:, o, c0:c0 + n], in0=po[:, 0:n], scalar=0.5, in1=self.hT[:, o, c0:c0 + n],
                            op0=ALU.mult, op1=ALU.add),
                            reads=[por] + hr, writes=hr)

            prev = None
            for g in range(NF // 2):
                grp = [GU(2 * g), GU(2 * g + 1)]
                if prev is not None:
                    W2(prev)
                prev = grp
            W2(prev)

    def tiles9(self):
        return [(0, NMETA)] + [(NMETA + 512 * j, 512) for j in range(SEQ // 512)]

    def norm_tile(self, c0, n, gk, xn, xn_r, sqr, rstd, rstd_r, ps_n, ps_nr, cnt=[0]):
        P = self.P
        for c in range(NCH):
            sq, sq_r = sqr.next()
            src = self.hT[:, c, c0:c0 + n]
            if cnt[0] % 2 == 0:
                self.act(sq[:, 0:n], src, AF.Square, self.h_res(c, c0, n), [sq_r])
            else:
                self.tt("pool", sq[:, 0:n], src, src, ALU.mult, self.h_res(c, c0, n), [sq_r])
            cnt[0] += 1
            self.mm(ps_n[:, 0:n], self.ones_bf[:, :], sq[:, 0:n], c == 0, c == NCH - 1, [sq_r, self.ones_r], [ps_nr])
        rs = rstd[:, 0:n]
        self.rsqrt_inplace(rs, rstd_r, 1.0 / D, in_ap=ps_n[:, 0:n], in_reads=[ps_nr])
        for c in range(NCH):
            self.stt(xn[:, c, 0:n], self.hT[:, c, c0:c0 + n], self.gains[:, gk, c:c + 1], rs, ALU.mult, ALU.mult,
                     self.h_res(c, c0, n) + [rstd_r, self.gains_r], [xn_r])

    def load_w_bf16(self, dst, dst_r, src, ncols, slab=160):
        st = Ring([self.alloc([src.shape[0] // 128, slab], F32) for _ in range(2)], "wslab")
        engs = ["act", "dve", "pool"]
        i = 0
        for c in range(0, ncols, slab):
            w = min(slab, ncols - c)
            a, ar = st.next()
            self.load(a[:, :, 0:w], src[:, c:c + w].rearrange("(k p) j -> p k j", p=128), ar)
            self.cp(engs[i % 3], dst[:, :, c:c + w], a[:, :, 0:w], [ar], [dst_r])
            i += 1

    def stage_proj1(self, l):
        self.stage_begin()
        gk = self.gain_idx[("mix_norm", l)]
        w_in = self.w["w_in"][l]
        Wp = self.alloc([NCH, 2048], BF16)
        Wp_r = Res("Wp")
        self.load_w_bf16(Wp, Wp_r, w_in[:, 0:2048], 2048)
        xnr = Ring([self.alloc([NCH, 512], BF16) for _ in range(2)], "xn")
        sqr = Ring([self.alloc([512], BF16) for _ in range(2)], "sq")
        rstr = Ring([self.alloc([512], F32) for _ in range(2)], "rstd")
        st32 = Ring([self.alloc([512], F32) for _ in range(3)], "st32")
        st16 = Ring([self.alloc([512], BF16) for _ in range(4)], "st16")
        psr = self.psring([0, 1, 2, 3, 4, 5], "p1")
        ps_n, ps_nr = self.psum[7], self.psum_r[7]
        ne = 0
        for (c0, n) in self.tiles9():
            xn, xn_r = xnr.next()
            rstd, rstd_r = rstr.next()
            self.norm_tile(c0, n, gk, xn, xn_r, sqr, rstd, rstd_r, ps_n, ps_nr)
            for g in range(12):
                pt, pr = psr.next()
                for k in range(NCH):
                    self.mm(pt[:, 0:n], Wp[:, k, g * 128:(g + 1) * 128], xn[:, k, 0:n], k == 0, k == NCH - 1,
                            [Wp_r, xn_r], [pr])
                eng = "act" if ne % 2 == 0 else "dve"
                ne += 1
                if g < 4:
                    sb, sr = st32.next()
                    self.cp(eng, sb[:, 0:n], pt[:, 0:n], [pr], [sr])
                    self.store(self.LRU_IN[g, :, c0:c0 + n], sb[:, 0:n], sr)
                else:
                    sb, sr = st16.next()
                    self.cp(eng, sb[:, 0:n], pt[:, 0:n], [pr], [sr])
                    self.store(self.QKD[g - 4, :, c0:c0 + n], sb[:, 0:n], sr)
            for j in range(0, n, 128):
                m = min(128, n - j)
                pt, pr = psr.next()
                for k in range(NCH):
                    self.mm(pt[0:m, 0:512], xn[:, k, j:j + m], Wp[:, k, 1536:2048], k == 0, k == NCH - 1,
                            [Wp_r, xn_r], [pr])
                sb, sr = st16.next()
                eng = "act" if ne % 2 == 0 else "dve"
                ne += 1
                self.cp(eng, sb[0:m, 0:512], pt[0:m, 0:512], [pr], [sr])
                self.store(self.VD[c0 + j:c0 + j + m, :], sb[0:m, 0:512], sr)

    def stage_proj2(self, l):
        self.stage_begin()
        gk = self.gain_idx[("mix_norm", l)]
        w_in = self.w["w_in"][l]
        Wp = self.alloc([NCH, 512], BF16)
        Wp_r = Res("Wp2")
        self.P.add("pool", lambda e: e.memset(Wp[:, :, 416:480], 0.0), writes=[Wp_r])
        self.load_w_bf16(Wp, Wp_r, w_in[:, 2048:2464], 416, slab=104)
        self.cp("dve", Wp[:, :, 480:496], Wp[:, :, 400:416], [Wp_r], [Wp_r])
        self.cp("dve", Wp[:, :, 496:512], Wp[:, :, 384:400], [Wp_r], [Wp_r])
        Wq = self.alloc([2, 768], BF16)
        Wq_r = Res("Wq")
        self.load_w_bf16(Wq, Wq_r, self.w["w_uq"][l], 384, slab=192)
        for h in range(4):
            b = 384 + h * 96
            a = h * 96
            self.cp("dve", Wq[:, :, b:b + 64], Wq[:, :, a:a + 64], [Wq_r], [Wq_r])
            self.cp("dve", Wq[:, :, b + 64:b + 80], Wq[:, :, a + 80:a + 96], [Wq_r], [Wq_r])
            self.cp("dve", Wq[:, :, b + 80:b + 96], Wq[:, :, a + 64:a + 80], [Wq_r], [Wq_r])
        Wkv = self.alloc([512], BF16)
        Wkv_r = Res("Wkv")
        kv32 = self.alloc([512], F32)
        kv32_r = Res("kv32")
        self.load(kv32[:, :], self.w["w_ukv"][l][:, :], kv32_r)
        self.cp("dve", Wkv[:, :].rearrange("p (t h d) -> p h t d", t=2, h=4),
                kv32[:, :].rearrange("p (h t d) -> p h t d", h=4, t=2), [kv32_r], [Wkv_r])
        gq = self.alloc([2], F32)
        gkv = self.alloc([1], F32)
        g_r = Res("gqkv")
        self.load(gq[:, :], self.w["q_norm"][l].rearrange("(c p) -> p c", p=128), g_r)
        self.load(gkv[:, :], self.w["kv_norm"][l].rearrange("(c p) -> p c", p=128), g_r)

        xnr = Ring([self.alloc([NCH, 512], BF16) for _ in range(1)], "xn")
        sqr = Ring([self.alloc([512], BF16) for _ in range(2)], "sq")
        rstr = Ring([self.alloc([512], F32) for _ in range(2)], "rstd")
        cq = self.alloc([2, 512], F32)
        cq_r = Res("cq")
        ckv = self.alloc([512], F32)
        ckv_r = Res("ckv")
        cqn = self.alloc([2, 512], BF16)
        cqn_r = Res("cqn")
        ckvn = self.alloc([512], BF16)
        ckvn_r = Res("ckvn")
        rsq = self.alloc([512], F32)
        rsq_r = Res("rsq")
        rskv = self.alloc([512], F32)
        rskv_r = Res("rskv")
        cs = self.alloc([2, 512], F32)
        cs_r = Res("cs")
        t1r = Ring([self.alloc([512], F32) for _ in range(2)], "t1")
        t2r = Ring([self.alloc([512], F32) for _ in range(2)], "t2")
        qst = Ring([self.alloc([4, 512], BF16) for _ in range(1)], "qst")
        kst = Ring([self.alloc([4, 512], BF16) for _ in range(1)], "kst")
        vst = Ring([self.alloc([4, 65], BF16) for _ in range(2)], "vst")
        psr = self.psring([0, 1, 2, 3, 4, 5], "p2")
        ps_n, ps_nr = self.psum[7], self.psum_r[7]
        ps_m, ps_mr = self.psum[6], self.psum_r[6]
        for (c0, n) in self.tiles9():
            xn, xn_r = xnr.next()
            rstd, rstd_r = rstr.next()
            self.norm_tile(c0, n, gk, xn, xn_r, sqr, rstd, rstd_r, ps_n, ps_nr)
            self.load(cs[0:96, :, 0:n], self.rope_cs[:, :, c0:c0 + n], cs_r)
            for j in range(3):
                pt, pr = psr.next()
                for k in range(NCH):
                    self.mm(pt[:, 0:n], Wp[:, k, j * 128:(j + 1) * 128], xn[:, k, 0:n], k == 0, k == NCH - 1,
                            [Wp_r, xn_r], [pr])
                if j < 2:
                    self.cp("act", cq[:, j, 0:n], pt[:, 0:n], [pr], [cq_r])
                else:
                    self.cp("act", ckv[:, 0:n], pt[:, 0:n], [pr], [ckv_r])
            for j in range(2):
                sq, sq_r = sqr.next()
                self.tt("pool", sq[:, 0:n], cq[:, j, 0:n], cq[:, j, 0:n], ALU.mult, [cq_r], [sq_r])
                self.mm(ps_m[:, 0:n], self.ones_bf[:, :], sq[:, 0:n], j == 0, j == 1, [sq_r, self.ones_r], [ps_mr])
            self.rsqrt_inplace(rsq[:, 0:n], rsq_r, 1.0 / 256, in_ap=ps_m[:, 0:n], in_reads=[ps_mr])
            for j in range(2):
                self.stt(cqn[:, j, 0:n], cq[:, j, 0:n], gq[:, j:j + 1], rsq[:, 0:n], ALU.mult, ALU.mult,
                         [cq_r, g_r, rsq_r], [cqn_r])
            sq, sq_r = sqr.next()
            self.tt("pool", sq[:, 0:n], ckv[:, 0:n], ckv[:, 0:n], ALU.mult, [ckv_r], [sq_r])
            self.mm(ps_m[:, 0:n], self.ones_bf[:, :], sq[:, 0:n], True, True, [sq_r, self.ones_r], [ps_mr])
            self.rsqrt_inplace(rskv[:, 0:n], rskv_r, 1.0 / 128, in_ap=ps_m[:, 0:n], in_reads=[ps_mr])
            self.stt(ckvn[:, 0:n], ckv[:, 0:n], gkv[:, 0:1], rskv[:, 0:n], ALU.mult, ALU.mult,
                     [ckv_r, g_r, rskv_r], [ckvn_r])
            qs, qs_r = qst.next()
            for h in range(4):
                pa, par = psr.next()
                pb, pbr = psr.next()
                for k in range(2):
                    self.mm(pa[0:96, 0:n], Wq[:, k, h * 96:(h + 1) * 96], cqn[:, k, 0:n], k == 0, k == 1, [Wq_r, cqn_r], [par])
                for k in range(2):
                    self.mm(pb[0:96, 0:n], Wq[:, k, 384 + h * 96:384 + (h + 1) * 96], cqn[:, k, 0:n], k == 0, k == 1,
                            [Wq_r, cqn_r], [pbr])
                t1, t1_r = t1r.next()
                t2, t2_r = t2r.next()
                self.tt("dve", t1[0:96, 0:n], pa[0:96, 0:n], cs[0:96, 0, 0:n], ALU.mult, [par, cs_r], [t1_r])
                self.tt("dve", t2[0:96, 0:n], pb[0:96, 0:n], cs[0:96, 1, 0:n], ALU.mult, [pbr, cs_r], [t2_r])
                self.tt("pool", qs[0:96, h, 0:n], t1[0:96, 0:n], t2[0:96, 0:n], ALU.add, [t1_r, t2_r], [qs_r])
            self.store(self.QM[:, :, c0:c0 + n].rearrange("h r t -> r h t"), qs[0:96, :, 0:n], qs_r)
            ks, ks_r = kst.next()
            for h in range(4):
                pt, pr = psr.next()
                self.mm(pt[0:64, 0:n], Wkv[:, h * 64:(h + 1) * 64], ckvn[:, 0:n], True, True, [Wkv_r, ckvn_r], [pr])
                self.cp("act", ks[0:64, h, 0:n], pt[0:64, 0:n], [pr], [ks_r])
            pa, par = psr.next()
            pb, pbr = psr.next()
            for k in range(NCH):
                self.mm(pa[0:96, 0:n], Wp[:, k, 320:416], xn[:, k, 0:n], k == 0, k == NCH - 1, [Wp_r, xn_r], [par])
            for k in range(NCH):
                self.mm(pb[0:96, 0:n], Wp[:, k, 416:512], xn[:, k, 0:n], k == 0, k == NCH - 1, [Wp_r, xn_r], [pbr])
            t1, t1_r = t1r.next()
            t2, t2_r = t2r.next()
            self.tt("dve", t1[64:96, 0:n], pa[64:96, 0:n], cs[64:96, 0, 0:n], ALU.mult, [par, cs_r], [t1_r])
            self.tt("dve", t2[64:96, 0:n], pb[64:96, 0:n], cs[64:96, 1, 0:n], ALU.mult, [pbr, cs_r], [t2_r])
            self.tt("pool", t1[64:96, 0:n], t1[64:96, 0:n], t2[64:96, 0:n], ALU.add, [t1_r, t2_r], [t1_r])
            for h in range(4):
                self.cp("pool" if h % 2 else "act", ks[64:96, h, 0:n], t1[64:96, 0:n], [t1_r], [ks_r])
            self.store(self.KM[:, :, c0:c0 + n].rearrange("h r t -> r h t"), ks[0:96, :, 0:n], ks_r)
            for j in range(0, n, 128):
                m = min(128, n - j)
                pt, pr = psr.next()
                self.mm(pt[0:m, 0:256], ckvn[:, j:j + m], Wkv[:, 256:512], True, True, [Wkv_r, ckvn_r], [pr])
                vs, vs_r = vst.next()
                self.P.add("pool", lambda e, vs=vs: e.memset(vs[:, :, 64:65], 1.0), writes=[vs_r])
                self.cp("act", vs[0:m, :, 0:64], pt[0:m, 0:256].rearrange("p (h d) -> p h d", h=4), [pr], [vs_r])
                self.store(self.VM[c0 + j:c0 + j + m, :], vs[0:m, :, :].rearrange("p h d -> p (h d)"), vs_r)

    def stage_lru(self, l):
        self.stage_begin()
        P = self.P
        pr_ = Res("lrup")
        convw = self.alloc([2, 4], F32)
        convb = self.alloc([2], F32)
        ba = self.alloc([2], F32)
        bx = self.alloc([2], F32)
        lam = self.alloc([2], F32)
        gl = self.alloc([2], F32)
        for c in range(2):
            for j in range(4):
                self.load(convw[:, c, j:j + 1], self.w["conv_w"][l][j, c * 128:(c + 1) * 128].rearrange("(p o) -> p o", o=1), pr_)
        self.load(convb[:, :], self.w["conv_b"][l].rearrange("(c p) -> p c", p=128), pr_)
        self.load(ba[:, :], self.w["lru_ba"][l].rearrange("n d -> (n d)").rearrange("(c p) -> p c", p=128), pr_)
        self.load(bx[:, :], self.w["lru_bx"][l].rearrange("n d -> (n d)").rearrange("(c p) -> p c", p=128), pr_)
        self.load(lam[:, :], self.w["lru_lambda"][l].rearrange("(c p) -> p c", p=128), pr_)
        self.load(gl[:, :], self.w["lru_out_norm"][l].rearrange("(c p) -> p c", p=128), pr_)
        nsp = self.alloc([2], F32)
        nsp2 = self.alloc([2], F32)
        self.act(nsp[:, :], lam[:, :], AF.Exp, [pr_], [pr_], scale=-1.0)
        self.ts("dve", nsp[:, :], nsp[:, :], 1.0, None, ALU.add, None, [pr_], [pr_])
        self.act(nsp[:, :], nsp[:, :], AF.Ln, [pr_], [pr_])
        self.ts("dve", nsp2[:, :], nsp[:, :], -16.0, None, ALU.mult, None, [pr_], [pr_])
        self.ts("dve", nsp[:, :], nsp[:, :], -8.0, None, ALU.mult, None, [pr_], [pr_])
        bd32 = self.alloc([2, 2, 128], F32)
        bd = self.alloc([2, 2, 128], BF16)
        bd_r = Res("bd")
        P.add("pool", lambda e: e.memset(bd32[:, :, :, :], 0.0), writes=[bd_r])
        for wi, nm in enumerate(("lru_wa", "lru_wx")):
            for nb in range(4):
                c, hb = nb // 2, nb % 2
                self.load(bd32[hb * 64:(hb + 1) * 64, wi, c, hb * 64:(hb + 1) * 64], self.w[nm][l][nb, :, :], bd_r)
        self.cp("dve", bd[:, :, :, :], bd32[:, :, :, :], [bd_r], [bd_r])
        carry = self.alloc([2], F32)
        carry_r = Res("carry")
        NB = 515
        uar = Ring([self.alloc([2, NB], F32) for _ in range(2)], "ua")
        gar = Ring([self.alloc([2, 512], F32) for _ in range(2)], "ga")
        fr = Ring([self.alloc([512], F32) for _ in range(10)], "lf")
        ucb_r = Ring([self.alloc([512], BF16) for _ in range(2)], "ucb")
        sqr = Ring([self.alloc([512], BF16) for _ in range(2)], "sq")
        yp = self.alloc([2, 512], F32)
        yp_r = Res("yp")
        rstd = self.alloc([512], F32)
        rstd_r = Res("rstd")
        yst = Ring([self.alloc([2, 512], BF16) for _ in range(2)], "yst")
        psr = self.psring([0, 1, 2, 3], "lru")
        ps_n, ps_nr = self.psum[7], self.psum_r[7]
        first = True
        for (c0, n) in self.tiles9():
            ua, ua_r = uar.next()
            ga, ga_r = gar.next()
            if first:
                P.add("pool", lambda e, ua=ua: e.memset(ua[:, :, 0:3], 0.0), writes=[ua_r])
                self.load(ua[:, :, 3:3 + n], self.LRU_IN[2:4, :, c0:c0 + n].rearrange("c p t -> p c t"), ua_r)
            else:
                self.load(ua[:, :, 0:3 + n], self.LRU_IN[2:4, :, c0 - 3:c0 + n].rearrange("c p t -> p c t"), ua_r)
            self.load(ga[:, :, 0:n], self.LRU_IN[0:2, :, c0:c0 + n].rearrange("c p t -> p c t"), ga_r)
            for j in range(2):
                uc, uc_r = fr.next()
                self.ts("dve", uc[:, 0:n], ua[:, j, 0:n], convw[:, j, 0:1], convb[:, j:j + 1], ALU.mult, ALU.add,
                        [ua_r, pr_], [uc_r])
                for t in range(1, 4):
                    self.stt(uc[:, 0:n], ua[:, j, t:t + n], convw[:, j, t:t + 1], uc[:, 0:n], ALU.mult, ALU.add,
                             [ua_r, pr_, uc_r], [uc_r])
                ucb, ucb_rr = ucb_r.next()
                self.cp("act", ucb[:, 0:n], uc[:, 0:n], [uc_r], [ucb_rr])
                pa, par = psr.next()
                px, pxr = psr.next()
                self.mm(pa[:, 0:n], bd[:, 0, j, :], ucb[:, 0:n], True, True, [bd_r, ucb_rr], [par])
                self.mm(px[:, 0:n], bd[:, 1, j, :], ucb[:, 0:n], True, True, [bd_r, ucb_rr], [pxr])
                r, r_r = fr.next()
                ig, ig_r = fr.next()
                self.act(r[:, 0:n], pa[:, 0:n], AF.Sigmoid, [par, pr_], [r_r], bias=ba[:, j:j + 1])
                self.act(ig[:, 0:n], px[:, 0:n], AF.Sigmoid, [pxr, pr_], [ig_r], bias=bx[:, j:j + 1])
                a, a_r = fr.next()
                a2, a2_r = fr.next()
                self.act(a[:, 0:n], r[:, 0:n], AF.Exp, [r_r, pr_], [a_r], scale=nsp[:, j:j + 1])
                self.act(a2[:, 0:n], r[:, 0:n], AF.Exp, [r_r, pr_], [a2_r], scale=nsp2[:, j:j + 1])
                self.ts("pool", a2[:, 0:n], a2[:, 0:n], -1.0, 1.0, ALU.mult, ALU.add, [a2_r], [a2_r])
                self.act(a2[:, 0:n], a2[:, 0:n], AF.Sqrt, [a2_r], [a2_r])
                self.tt("pool", ig[:, 0:n], ig[:, 0:n], uc[:, 0:n], ALU.mult, [ig_r, uc_r], [ig_r])
                self.tt("pool", ig[:, 0:n], ig[:, 0:n], a2[:, 0:n], ALU.mult, [ig_r, a2_r], [ig_r])
                hs, hs_r = fr.next()
                if first:
                    P.add("dve", lambda e, hs=hs, a=a, ig=ig, n=n: e.tensor_tensor_scan(
                        out=hs[:, 0:n], data0=a[:, 0:n], data1=ig[:, 0:n], initial=0.0, op0=ALU.mult, op1=ALU.add),
                        reads=[a_r, ig_r], writes=[hs_r])
                else:
                    P.add("dve", lambda e, hs=hs, a=a, ig=ig, n=n, j=j: e.tensor_tensor_scan(
                        out=hs[:, 0:n], data0=a[:, 0:n], data1=ig[:, 0:n], initial=carry[:, j:j + 1],
                        op0=ALU.mult, op1=ALU.add),
                        reads=[a_r, ig_r, carry_r], writes=[hs_r])
                self.cp("dve", carry[:, j:j + 1], hs[:, n - 1:n], [hs_r], [carry_r])
                g = ga[:, j, 0:n]
                t, t_r = fr.next()
                self.tt("pool", t[:, 0:n], g, g, ALU.mult, [ga_r], [t_r])
                self.ts("pool", t[:, 0:n], t[:, 0:n], 0.044715, 1.0, ALU.mult, ALU.add, [t_r], [t_r])
                self.tt("pool", t[:, 0:n], t[:, 0:n], g, ALU.mult, [t_r, ga_r], [t_r])
                self.act(t[:, 0:n], t[:, 0:n], AF.Sigmoid, [t_r], [t_r], scale=1.5957691216057308)
                self.tt("pool", t[:, 0:n], t[:, 0:n], g, ALU.mult, [t_r, ga_r], [t_r])
                self.tt("dve", yp[:, j, 0:n], hs[:, 0:n], t[:, 0:n], ALU.mult, [hs_r, t_r], [yp_r])
                sq, sq_r = sqr.next()
                self.tt("pool", sq[:, 0:n], yp[:, j, 0:n], yp[:, j, 0:n], ALU.mult, [yp_r], [sq_r])
                self.mm(ps_n[:, 0:n], self.ones_bf[:, :], sq[:, 0:n], j == 0, j == 1, [sq_r, self.ones_r], [ps_nr])
            self.rsqrt_inplace(rstd[:, 0:n], rstd_r, 1.0 / 256, in_ap=ps_n[:, 0:n], in_reads=[ps_nr])
            ys, ys_r = yst.next()
            for j in range(2):
                self.stt(ys[:, j, 0:n], yp[:, j, 0:n], gl[:, j:j + 1], rstd[:, 0:n], ALU.mult, ALU.mult,
                         [yp_r, pr_, rstd_r], [ys_r])
            self.store(self.YT[0:2, :, c0:c0 + n].rearrange("c p t -> p c t"), ys[:, :, 0:n], ys_r)
            first = False

    def attn_sweep(self, qi, Kt, Qt, r0, r1, kq_r, V, v_r, dvp, obanks, scale, s_ring, pt_ring, tri, tri_r):
        if qi < 0:
            q0, nq, nsub = 0, NMETA, 1
            chunks = [(0, 0, NMETA, 0)]
        else:
            q0, nq, nsub = NMETA + 512 * qi, 512, 4
            chunks = [(0, 0, NMETA, -1)] + [(1 + i, NMETA + 128 * i, 128, -1) for i in range(4 * qi)]
            chunks += [(1 + 4 * qi + c, NMETA + 128 * (4 * qi + c), 128, c) for c in range(4)]
        started = set()
        for (kc, kcol, nk, diag) in chunks:
            s0 = max(diag, 0)
            qa = q0 + 128 * s0 if qi >= 0 else 0
            nqa = nq - 128 * s0 if qi >= 0 else nq
            ps, ps_r = s_ring.next()
            self.mm(ps[0:nk, 0:nqa], Kt[r0:r1, kcol:kcol + nk], Qt[r0:r1, qa:qa + nqa], True, True, [kq_r], [ps_r])
            pt, pt_r = pt_ring.next()
            self.act(pt[0:nk, 0:nqa], ps[0:nk, 0:nqa], AF.Exp, [ps_r], [pt_r], scale=scale)
            if diag >= 0:
                w = min(128, nqa)
                self.tt("dve", pt[0:nk, 0:w], pt[0:nk, 0:w], tri[0:nk, 0:w], ALU.mult, [pt_r, tri_r], [pt_r])
            self.flush_pv()

            def pv(s0=s0, pt=pt, pt_r=pt_r, nk=nk, kc=kc):
                for s in range(s0, nsub):
                    ob, ob_r, oc = obanks[s]
                    ws = min(128, nq)
                    lo = (s - s0) * 128
                    st = id(ob) not in started
                    started.add(id(ob))
                    self.mm(ob[0:ws, oc:oc + dvp], pt[0:nk, lo:lo + ws], V[0:nk, kc, 0:dvp], st, False,
                            [pt_r, v_r], [ob_r], skip_group_check=True)
            self._pend = pv

    _pend = None

    def flush_pv(self):
        if self._pend is not None:
            p = self._pend
            self._pend = None
            p()

    def load_consts_attn(self):
        tri32 = self.alloc([128], F32)
        tri = self.alloc([128], BF16)
        tri_r = Res("tri")
        self.load(tri32[:, :], self.tri_in[:, :], tri_r)
        self.cp("dve", tri[:, :], tri32[:, :], [tri_r], [tri_r])
        idb32 = self.alloc([128], F32)
        idb = self.alloc([128], BF16)
        idb_r = Res("idb")
        self.cp("dve", idb[:, :], self.identf[:, :], [self.identf_r], [idb_r])
        return tri, tri_r, idb, idb_r

    def stage_diff(self, l):
        self.stage_begin()
        P = self.P
        lam_init = 0.8 - 0.6 * math.exp(-0.3 * l)
        tri, tri_r, idb, idb_r = self.load_consts_attn()
        lv = self.alloc([4, 64], F32)
        lv_r = Res("lv")
        for i, nm in enumerate(("lam_q1", "lam_k1", "lam_q2", "lam_k2")):
            self.load(lv[:, i, :], self.w[nm][l].partition_broadcast(128), lv_r)
        lsc = self.alloc([8], F32)
        self.tt("dve", lv[:, 0, :], lv[:, 0, :], lv[:, 1, :], ALU.mult, [lv_r], [lv_r])
        self.tt("dve", lv[:, 2, :], lv[:, 2, :], lv[:, 3, :], ALU.mult, [lv_r], [lv_r])
        P.add("dve", lambda e: e.reduce_sum(out=lsc[:, 0:1], in_=lv[:, 0, :], axis=AX.X), reads=[lv_r], writes=[lv_r])
        P.add("dve", lambda e: e.reduce_sum(out=lsc[:, 1:2], in_=lv[:, 2, :], axis=AX.X), reads=[lv_r], writes=[lv_r])
        self.act(lsc[:, 0:2], lsc[:, 0:2], AF.Exp, [lv_r], [lv_r])
        self.tt("dve", lsc[:, 2:3], lsc[:, 1:2], lsc[:, 0:1], ALU.subtract, [lv_r], [lv_r])
        self.ts("dve", lsc[:, 2:3], lsc[:, 2:3], -lam_init, None, ALU.add, None, [lv_r], [lv_r])
        gsub = self.alloc([128], F32)
        self.load(gsub[:, :], self.w["diff_subln"][l].partition_broadcast(128), lv_r)
        self.ts("dve", gsub[:, :], gsub[:, :], 1.0 - lam_init, None, ALU.mult, None, [lv_r], [lv_r])

        Kt = self.alloc([T], BF16)
        Qt = self.alloc([T], BF16)
        kq_r = Res("kq")
        V = self.alloc([33, 129], BF16)
        v_r = Res("V")
        pt_ring = Ring([self.alloc([512], BF16) for _ in range(3)], "pt")
        o1r = Ring([self.alloc([128], F32) for _ in range(2)], "o1")
        o2r = Ring([self.alloc([128], F32) for _ in range(2)], "o2")
        scr = Ring([self.alloc([128], F32) for _ in range(2)], "scr")
        smr = Ring([self.alloc([8], F32) for _ in range(4)], "sm")
        ybr = Ring([self.alloc([128], BF16) for _ in range(2)], "yb")
        ytr = Ring([self.alloc([512], BF16) for _ in range(2)], "yts")
        s_ring = self.psring([0, 1, 2], "s")
        tps = self.psum[7][:, :].bitcast(BF16)
        tps_r = self.psum_r[7]
        for h in range(4):
            self.load(Qt[:, :], self.QKD[h, :, :], kq_r)
            self.load(Kt[:, :], self.QKD[4 + h, :, :], kq_r)
            P.add("pool", lambda e: e.memset(V[:, :, 128:129], 1.0), writes=[v_r])
            self.load(V[0:NMETA, 0, 0:128], self.VD[0:NMETA, h * 128:(h + 1) * 128], v_r)
            for i0 in range(0, 32, 8):
                self.load(V[:, 1 + i0:9 + i0, 0:128],
                          self.VD[NMETA + 128 * i0:NMETA + 128 * (i0 + 8), h * 128:(h + 1) * 128].rearrange("(i p) d -> p i d", p=128), v_r)
            for qi in range(-1, SEQ // 512):
                nsub = 1 if qi < 0 else 4
                ws = NMETA if qi < 0 else 128
                q0 = 0 if qi < 0 else NMETA + 512 * qi
                ob = {}
                for m in range(2):
                    banks = [(self.psum[3 + 2 * m], self.psum_r[3 + 2 * m], 0), (self.psum[3 + 2 * m], self.psum_r[3 + 2 * m], 129),
                             (self.psum[4 + 2 * m], self.psum_r[4 + 2 * m], 0), (self.psum[4 + 2 * m], self.psum_r[4 + 2 * m], 129)]
                    ob[m] = banks
                    self.attn_sweep(qi, Kt, Qt, 64 * m, 64 * m + 64, kq_r, V, v_r, 129, banks, 0.125, s_ring, pt_ring, tri, tri_r)
                self.flush_pv()
                yts, yts_r = ytr.next()
                for s in range(nsub):
                    sm, sm_r = smr.next()
                    o1, o1_r = o1r.next()
                    o2, o2_r = o2r.next()
                    for m, (o, o_r) in enumerate(((o1, o1_r), (o2, o2_r))):
                        b, b_r, oc = ob[m][s]
                        P.add("dve", lambda e, sm=sm, b=b, oc=oc, m=m, ws=ws: e.reciprocal(
                            out=sm[0:ws, m:m + 1], in_=b[0:ws, oc + 128:oc + 129]), reads=[b_r], writes=[sm_r])
                        self.act(o[0:ws, :], b[0:ws, oc:oc + 128], AF.Copy, [b_r, sm_r], [o_r], scale=sm[0:ws, m:m + 1])
                    self.stt(o1[0:ws, :], o2[0:ws, :], lsc[0:ws, 2:3], o1[0:ws, :], ALU.mult, ALU.add, [o1_r, o2_r, lv_r], [o1_r])
                    sc, sc_r = scr.next()
                    self.act(sc[0:ws, :], o1[0:ws, :], AF.Square, [o1_r], [sc_r, sm_r], accum_out=sm[0:ws, 2:3])
                    self.rsqrt_inplace(sm[0:ws, 2:3], sm_r, 1.0 / 128)
                    yb, yb_r = ybr.next()
                    self.stt(yb[0:ws, :], o1[0:ws, :], sm[0:ws, 2:3], gsub[0:ws, :], ALU.mult, ALU.mult, [o1_r, sm_r, lv_r], [yb_r])
                    P.add("pe", lambda e, yb=yb, s=s, ws=ws: e.transpose(out=tps[:, s * 128:s * 128 + ws], in_=yb[0:ws, :],
                                                                          identity=idb[0:ws, 0:ws]),
                          reads=[yb_r, idb_r], writes=[tps_r])
                    self.cp("pool" if False else "act", yts[:, s * 128:s * 128 + ws], tps[:, s * 128:s * 128 + ws], [tps_r], [yts_r])
                nq = NMETA if qi < 0 else 512
                self.store(self.YT[2 + h, :, q0:q0 + nq], yts[:, 0:nq], yts_r)

    def stage_mla(self, l):
        self.stage_begin()
        P = self.P
        tri, tri_r, idb, idb_r = self.load_consts_attn()
        gm = self.alloc([256], F32)
        gm_r = Res("gm")
        self.load(gm[:, :], self.w["mla_out_norm"][l].partition_broadcast(128), gm_r)
        Kt = self.alloc([4, T], BF16)
        kq_r = Res("kq")
        V = self.alloc([33, 4, 65], BF16)
        v_r = Res("V")
        self.load(Kt[0:96, :, :], self.KM[:, :, :].rearrange("h r t -> r h t"), kq_r)
        Vf = V.rearrange("p i h d -> p i (h d)")
        self.load(Vf[0:NMETA, 0, :], self.VM[0:NMETA, :], v_r)
        for i0 in range(0, 32, 8):
            self.load(Vf[:, 1 + i0:9 + i0, :],
                      self.VM[NMETA + 128 * i0:NMETA + 128 * (i0 + 8), :].rearrange("(i p) d -> p i d", p=128), v_r)
        qtr = Ring([self.alloc([4, 512], BF16) for _ in range(2)], "qt")
        pt_ring = Ring([self.alloc([512], BF16) for _ in range(3)], "pt")
        ocr = Ring([self.alloc([256], F32) for _ in range(2)], "oc")
        scr = Ring([self.alloc([256], F32) for _ in range(2)], "scr")
        smr = Ring([self.alloc([8], F32) for _ in range(4)], "sm")
        ybr = Ring([self.alloc([256], BF16) for _ in range(2)], "yb")
        ytr = Ring([self.alloc([2, 512], BF16) for _ in range(2)], "yts")
        s_ring = self.psring([0, 1, 2], "s")
        tps = self.psum[7][:, :].bitcast(BF16)
        tps_r = self.psum_r[7]
        scale = 96.0 ** -0.5
        for qi in range(-1, SEQ // 512):
            nsub = 1 if qi < 0 else 4
            ws = NMETA if qi < 0 else 128
            q0 = 0 if qi < 0 else NMETA + 512 * qi
            nq = NMETA if qi < 0 else 512
            qt, qt_r = qtr.next()
            self.load(qt[0:96, :, 0:nq], self.QM[:, :, q0:q0 + nq].rearrange("h r t -> r h t"), qt_r)
            ob = {}
            for h in range(4):
                banks = [(self.psum[3 + h], self.psum_r[3 + h], 65 * s) for s in range(4)]
                ob[h] = banks
                self.attn_sweep_local(qi, Kt[:, h, :], qt[:, h, :], kq_r, qt_r, V[:, :, h, :], v_r, 65, banks, scale,
                                      s_ring, pt_ring, tri, tri_r)
            self.flush_pv()
            yts, yts_r = ytr.next()
            for s in range(nsub):
                sm, sm_r = smr.next()
                oc, oc_r = ocr.next()
                for h in range(4):
                    b, b_r, c = ob[h][s]
                    P.add("dve", lambda e, sm=sm, b=b, c=c, h=h, ws=ws: e.reciprocal(
                        out=sm[0:ws, h:h + 1], in_=b[0:ws, c + 64:c + 65]), reads=[b_r], writes=[sm_r])
                    self.act(oc[0:ws, h * 64:(h + 1) * 64], b[0:ws, c:c + 64], AF.Copy, [b_r, sm_r], [oc_r],
                             scale=sm[0:ws, h:h + 1])
                sc, sc_r = scr.next()
                self.act(sc[0:ws, :], oc[0:ws, :], AF.Square, [oc_r], [sc_r, sm_r], accum_out=sm[0:ws, 4:5])
                self.rsqrt_inplace(sm[0:ws, 4:5], sm_r, 1.0 / 256)
                yb, yb_r = ybr.next()
                self.stt(yb[0:ws, :], oc[0:ws, :], sm[0:ws, 4:5], gm[0:ws, :], ALU.mult, ALU.mult, [oc_r, sm_r, gm_r], [yb_r])
                for c in range(2):
                    P.add("pe", lambda e, yb=yb, s=s, ws=ws, c=c: e.transpose(
                        out=tps[:, c * 512 + s * 128:c * 512 + s * 128 + ws], in_=yb[0:ws, c * 128:(c + 1) * 128],
                        identity=idb[0:ws, 0:ws]), reads=[yb_r, idb_r], writes=[tps_r])
                    self.cp("act", yts[:, c, s * 128:s * 128 + ws], tps[:, c * 512 + s * 128:c * 512 + s * 128 + ws],
                            [tps_r], [yts_r])
            self.store(self.YT[6:8, :, q0:q0 + nq].rearrange("c p t -> p c t"), yts[:, :, 0:nq], yts_r)

    def attn_sweep_local(self, qi, Kt, Qtile, kq_r, qt_r, V, v_r, dvp, obanks, scale, s_ring, pt_ring, tri, tri_r):
        if qi < 0:
            nq, nsub = NMETA, 1
            chunks = [(0, 0, NMETA, 0)]
        else:
            nq, nsub = 512, 4
            chunks = [(0, 0, NMETA, -1)] + [(1 + i, NMETA + 128 * i, 128, -1) for i in range(4 * qi)]
            chunks += [(1 + 4 * qi + c, NMETA + 128 * (4 * qi + c), 128, c) for c in range(4)]
        started = set()
        for (kc, kcol, nk, diag) in chunks:
            s0 = max(diag, 0)
            qa = 128 * s0 if qi >= 0 else 0
            nqa = nq - qa
            ps, ps_r = s_ring.next()
            self.mm(ps[0:nk, 0:nqa], Kt[0:96, kcol:kcol + nk], Qtile[0:96, qa:qa + nqa], True, True, [kq_r, qt_r], [ps_r])
            pt, pt_r = pt_ring.next()
            self.act(pt[0:nk, 0:nqa], ps[0:nk, 0:nqa], AF.Exp, [ps_r], [pt_r], scale=scale)
            if diag >= 0:
                w = min(128, nqa)
                self.tt("dve", pt[0:nk, 0:w], pt[0:nk, 0:w], tri[0:nk, 0:w], ALU.mult, [pt_r, tri_r], [pt_r])
            self.flush_pv()

            def pv(s0=s0, pt=pt, pt_r=pt_r, nk=nk, kc=kc):
                for s in range(s0, nsub):
                    ob, ob_r, oc = obanks[s]
                    ws = min(128, nq)
                    lo = (s - s0) * 128
                    st = id(ob) not in started
                    started.add(id(ob))
                    self.mm(ob[0:ws, oc:oc + dvp], pt[0:nk, lo:lo + ws], V[0:nk, kc, 0:dvp], st, False,
                            [pt_r, v_r], [ob_r], skip_group_check=True)
            self._pend = pv

    def stage_oproj(self, l):
        self.stage_begin()
        Wo = self.alloc([NCH, D], BF16)
        Wo_r = Res("Wo")
        self.load_w_bf16(Wo, Wo_r, self.w["w_out"][l], D, slab=256)
        ytr = Ring([self.alloc([NCH, 512], BF16) for _ in range(2)], "yt")
        psr = self.psring([0, 1, 2, 3], "op")
        for (c0, n) in self.tiles9():
            yt, yt_r = ytr.next()
            self.load(yt[:, :, 0:n], self.YT[:, :, c0:c0 + n].rearrange("c p t -> p c t"), yt_r)
            for o in range(NCH):
                po, por = psr.next()
                for k in range(NCH):
                    self.mm(po[:, 0:n], Wo[:, k, o * 128:(o + 1) * 128], yt[:, k, 0:n], k == 0, k == NCH - 1, [Wo_r, yt_r], [por])
                hr = self.h_res(o, c0, n)
                self.stt(self.hT[:, o, c0:c0 + n], po[:, 0:n], 1.0, self.hT[:, o, c0:c0 + n], ALU.mult, ALU.add,
                         [por] + hr, hr)


    def stage_final(self):
        P = self.P
        self.stage_begin()
        gbc = self.alloc([D], F32)
        gbc_r = Res("gbc")
        P.add("sp", lambda e: e.dma_start(out=gbc[:, :], in_=self.final_norm.partition_broadcast(128)),
              writes=[gbc_r], dma=True, sem_res=gbc_r)
        otr = Ring([self.alloc([D], F32) for i in range(2)], "ot")
        sq = self.alloc([512], F32)
        sq_r = Res("fsq")
        ssr = Ring([self.alloc([4], F32) for i in range(2)], "fss")
        psr = self.psring([0, 1, 2, 3], "f")
        for i in range(SEQ // 128):
            c0 = NMETA + 128 * i
            o, o_r = otr.next()
            s, s_r = ssr.next()
            halves = []
            for half in range(2):
                pt, pr = psr.next()
                for cc in range(4):
                    c = half * 4 + cc
                    P.add("pe", lambda e, pt=pt, cc=cc, c=c, c0=c0: e.transpose(
                        out=pt[:, cc * 128:(cc + 1) * 128], in_=self.hT[:, c, c0:c0 + 128],
                        identity=self.identf[:, :]),
                        reads=self.h_res(c, c0, 128) + [self.identf_r], writes=[pr])
                halves.append((pt, pr))
            for half, (pt, pr) in enumerate(halves):
                P.add("act", lambda e, pt=pt, s=s, half=half: e.activation(
                    out=sq[:, 0:512], in_=pt[:, :], func=AF.Square, accum_out=s[:, half:half + 1]),
                    reads=[pr], writes=[sq_r, s_r])
            P.add("dve", lambda e, s=s: e.tensor_tensor(out=s[:, 2:3], in0=s[:, 0:1], in1=s[:, 1:2], op=ALU.add),
                  reads=[s_r], writes=[s_r])
            P.add("dve", lambda e, s=s: e.tensor_scalar(out=s[:, 2:3], in0=s[:, 2:3], scalar1=1.0 / D, scalar2=EPS,
                                                         op0=ALU.mult, op1=ALU.add),
                  reads=[s_r], writes=[s_r])
            P.add("act", lambda e, s=s: e.activation(out=s[:, 3:4], in_=s[:, 2:3], func=AF.Sqrt),
                  reads=[s_r], writes=[s_r])
            P.add("dve", lambda e, s=s: e.reciprocal(out=s[:, 3:4], in_=s[:, 3:4]),
                  reads=[s_r], writes=[s_r])
            for half, (pt, pr) in enumerate(halves):
                P.add("dve", lambda e, pt=pt, s=s, o=o, half=half: e.scalar_tensor_tensor(
                    out=o[:, half * 512:(half + 1) * 512], in0=pt[:, :], scalar=s[:, 3:4],
                    in1=gbc[:, half * 512:(half + 1) * 512], op0=ALU.mult, op1=ALU.mult),
                    reads=[pr, s_r, gbc_r], writes=[o_r])
            P.add("sp", lambda e, o=o, i=i: e.dma_start(out=self.out[128 * i:128 * (i + 1), :], in_=o[:, :]),
                  reads=[o_r], dma=True, sem_res=o_r)
        self.P.barrier()


def build_program(n_layers=DEPTH, debug=None):
    b = Builder(n_layers, debug)
    nc = b.build()
    return nc, b


_CONSTS = None


def consts():
    global _CONSTS
    if _CONSTS is None:
        half = 16
        inv = (np.float32(10000.0) ** (-(np.arange(half, dtype=np.float32)) / np.float32(half))).astype(np.float32)
        ang = (np.arange(T, dtype=np.float32)[None, :] * inv[:, None]).astype(np.float32).astype(np.float64)
        cs = np.zeros((96, 2, T), np.float32)
        cs[0:64, 0, :] = 1.0
        cs[64:80, 0, :] = np.cos(ang)
        cs[80:96, 0, :] = np.cos(ang)
        cs[64:80, 1, :] = -np.sin(ang)
        cs[80:96, 1, :] = np.sin(ang)
        tri = (np.arange(128)[:, None] <= np.arange(128)[None, :]).astype(np.float32)
        _CONSTS = {"ident_f": np.eye(128, dtype=np.float32), "rope_cs": cs, "tri_in": tri}
    return _CONSTS


def make_in_maps(inputs, b):
    c = consts()
    maps = []
    for core in range(8):
        m = {}
        for nm in b.inputs:
            if nm == "x":
                m[nm] = np.ascontiguousarray(inputs["x"][core])
            elif nm in c:
                m[nm] = c[nm]
            else:
                m[nm] = np.ascontiguousarray(inputs[nm])
        maps.append(m)
    return maps


def kernel(**inputs):
    inputs = {k: np.asarray(v) for k, v in inputs.items()}
    nc, b = build_program()
    maps = make_in_maps(inputs, b)
    res = run_bass_kernel_spmd(nc, maps, core_ids=list(range(8)))
    out = np.stack([np.asarray(res.results[i]["out"]) for i in range(8)], axis=0)
    return out.astype(np.float32)
```

```python
import contextlib
import math
import numpy as np
import ml_dtypes
import concourse.bass as bass
import concourse.mybir as mybir
from concourse.bass_utils import run_bass_kernel_spmd

F32 = mybir.dt.float32
BF16 = mybir.dt.bfloat16
AF = mybir.ActivationFunctionType
ALU = mybir.AluOpType
AX = mybir.AxisListType

D = 1024
SEQ = 4096
NMETA = 16
T = SEQ + NMETA
DFF = 2816
NF = DFF // 128
NCH = D // 128
DEPTH = 2
EPS = 1e-6
DIN = 2464


class Res:
    __slots__ = ("name", "w", "r", "slot")

    def __init__(self, name):
        self.name = name
        self.w = None
        self.r = []
        self.slot = None


class SemSlot:
    __slots__ = ("sem", "dcount")

    def __init__(self):
        self.sem = None
        self.dcount = 0


class Op:
    __slots__ = ("eng", "fn", "deps", "dma", "sem_res", "val", "needs_inc", "idx", "gen")

    def __init__(self, eng, fn, dma=False):
        self.eng = eng
        self.fn = fn
        self.deps = []
        self.dma = dma
        self.sem_res = None
        self.val = 0
        self.needs_inc = False


ENGS = ("pe", "act", "dve", "pool", "sp")
MAX_DMA_INFLIGHT = 4


class Prog:
    def __init__(self):
        self.q = {e: [] for e in ENGS}
        self.pool = []
        self.stage_k = 0
        self.gen = 0
        self.last_dma = {}
        self.dma_fifo = []

    def barrier(self):
        deps = []
        for e in ENGS:
            for op in reversed(self.q[e]):
                if not op.dma and op.fn is not None:
                    deps.append((op, 0))
                    break
        for op in self.last_dma.values():
            deps.append((op, op.sem_res.dcount))
        for e in ENGS:
            op = Op(e, None)
            op.gen = self.gen
            op.deps = [(o, s) for (o, s) in deps if not (o.eng == e and not o.dma)]
            for o, _ in op.deps:
                o.needs_inc = True
            self.q[e].append(op)
        self.gen += 1
        self.stage_k = 0
        self.last_dma = {}

    def add(self, eng, fn, reads=(), writes=(), dma=False, sem_res=None):
        op = Op(eng, fn, dma)
        op.gen = self.gen
        deps = {}

        def dep(o):
            if o is None or o.gen < self.gen:
                return
            if (not o.dma) and o.eng == "pe" and eng == "pe" and not dma:
                return
            snap = o.sem_res.dcount if o.dma else 0
            deps[id(o)] = (o, snap)

        for r in reads:
            dep(r.w)
        for w in writes:
            dep(w.w)
            for o in w.r:
                dep(o)
        if dma:
            if len(self.dma_fifo) >= MAX_DMA_INFLIGHT:
                old = self.dma_fifo[-MAX_DMA_INFLIGHT]
                if old.gen == self.gen and id(old) not in deps:
                    deps[id(old)] = (old, 0)
            self.dma_fifo.append(op)
        op.deps = list(deps.values())
        for o, _ in op.deps:
            o.needs_inc = True
        if dma:
            assert sem_res is not None
            if sem_res.slot is None or sem_res.slot[1] != self.gen:
                if self.stage_k >= len(self.pool):
                    self.pool.append(SemSlot())
                sem_res.slot = (self.pool[self.stage_k], self.gen)
                self.stage_k += 1
            sl = sem_res.slot[0]
            sl.dcount += 16
            op.sem_res = sl
            op.val = sl.dcount
            self.last_dma[id(sl)] = op
        for r in reads:
            if not dma:
                r.r = [o for o in r.r if o.dma or o.eng != eng]
            r.r.append(op)
        for w in writes:
            w.w = op
            w.r = []
        self.q[eng].append(op)
        return op

    def emit(self, nc, es):
        esem = {e: es.enter_context(nc.semaphore("s_" + e)) for e in ENGS}
        for i, sl in enumerate(self.pool):
            sl.sem = es.enter_context(nc.semaphore("d%d" % i))
        for e in ENGS:
            cnt = 0
            for op in self.q[e]:
                if op.dma or op.fn is None:
                    continue
                if op.needs_inc:
                    cnt += 1
                    op.val = cnt
        block = es.enter_context(nc.Block())
        engobj = {"pe": "tensor", "act": "scalar", "dve": "vector", "pool": "gpsimd", "sp": "sync"}

        def body_for(e):
            ops = self.q[e]

            def body(eng):
                seen = {}
                for op in ops:
                    for d, snap in op.deps:
                        if d.dma:
                            sem, val = d.sem_res.sem, max(d.val, snap)
                        else:
                            sem, val = esem[d.eng], d.val
                        key = id(sem)
                        if seen.get(key, 0) >= val:
                            continue
                        seen[key] = val
                        eng.wait_ge(sem, val)
                    if op.fn is None:
                        continue
                    ins = op.fn(eng)
                    if op.dma:
                        ins.then_inc(op.sem_res.sem, 16)
                    elif op.needs_inc:
                        ins.then_inc(esem[e], 1)
            return body

        for e in ENGS:
            getattr(block, engobj[e])(body_for(e))


class Ring:
    def __init__(self, aps, name):
        self.aps = aps
        self.res = [Res("%s%d" % (name, i)) for i in range(len(aps))]
        self.i = 0

    def next(self):
        k = self.i % len(self.aps)
        self.i += 1
        return self.aps[k], self.res[k]


ARENA_BYTES = 79360


def dsize(dt):
    return 4 if dt == F32 else 2


class Builder:
    def __init__(self, n_layers=DEPTH, debug=None):
        self.n_layers = n_layers
        self.debug = debug or {}
        self.nc = bass.Bass("TRN2", target_bir_lowering=False)
        self.P = Prog()
        self.es = contextlib.ExitStack()
        self.inputs = {}
        self.aoff = 0

    def din(self, name, shape, dt=F32):
        ap = self.nc.dram_tensor(name, list(shape), dt, kind="ExternalInput").ap()
        self.inputs[name] = ap
        return ap

    def dout(self, name, shape, dt=F32):
        return self.nc.dram_tensor(name, list(shape), dt, kind="ExternalOutput").ap()

    def dscratch(self, name, shape, dt):
        return self.nc.dram_tensor(name, list(shape), dt, kind="Internal").ap()

    def sb(self, name, shape, dt):
        return self.es.enter_context(self.nc.sbuf_tensor(name, list(shape), dt))

    def ps(self, name, shape, dt=F32):
        return self.es.enter_context(self.nc.psum_tensor(name, list(shape), dt))

    def stage_begin(self):
        self.P.barrier()
        self.aoff = 0

    def alloc(self, shape, dt):
        n = int(np.prod(shape))
        nbytes = (n * dsize(dt) + 31) // 32 * 32
        assert self.aoff + nbytes <= ARENA_BYTES, (self.aoff, nbytes)
        ap = self.arena[:, self.aoff // 4:(self.aoff + nbytes) // 4]
        self.aoff += nbytes
        if dt != F32:
            ap = ap.bitcast(dt)
        ap = ap[:, 0:n]
        if len(shape) == 2:
            ap = ap.rearrange("p (a b) -> p a b", a=shape[0])
        elif len(shape) == 3:
            ap = ap.rearrange("p (a b c) -> p a b c", a=shape[0], b=shape[1])
        return ap


    def act(self, out, in_, func, reads, writes, **kw):
        return self.P.add("act", lambda e: e.activation(out=out, in_=in_, func=func, **kw), reads=reads, writes=writes)

    def tt(self, eng, out, in0, in1, op, reads, writes):
        return self.P.add(eng, lambda e: e.tensor_tensor(out=out, in0=in0, in1=in1, op=op), reads=reads, writes=writes)

    def ts(self, eng, out, in0, s1, s2, op0, op1, reads, writes):
        if s2 is None:
            return self.P.add(eng, lambda e: e.tensor_scalar(out=out, in0=in0, scalar1=s1, scalar2=None, op0=op0),
                              reads=reads, writes=writes)
        return self.P.add(eng, lambda e: e.tensor_scalar(out=out, in0=in0, scalar1=s1, scalar2=s2, op0=op0, op1=op1),
                          reads=reads, writes=writes)

    def stt(self, out, in0, scalar, in1, op0, op1, reads, writes):
        return self.P.add("dve", lambda e: e.scalar_tensor_tensor(out=out, in0=in0, scalar=scalar, in1=in1, op0=op0, op1=op1),
                          reads=reads, writes=writes)

    def cp(self, eng, out, in_, reads, writes):
        if eng == "act":
            return self.act(out, in_, AF.Copy, reads, writes)
        return self.P.add(eng, lambda e: e.tensor_copy(out=out, in_=in_), reads=reads, writes=writes)

    def mm(self, out, lhsT, rhs, start, stop, reads, writes, **kw):
        return self.P.add("pe", lambda e: e.matmul(out, lhsT=lhsT, rhs=rhs, start=start, stop=stop, **kw),
                          reads=reads, writes=writes)

    def load(self, out, in_, res, reads=()):
        return self.P.add("sp", lambda e: e.dma_start(out=out, in_=in_), reads=list(reads), writes=[res], dma=True, sem_res=res)

    def store(self, out, in_, res):
        return self.P.add("sp", lambda e: e.dma_start(out=out, in_=in_), reads=[res], dma=True, sem_res=res)

    def rsqrt_inplace(self, ap, res, scale, in_ap=None, in_reads=()):
        src = ap if in_ap is None else in_ap
        self.ts("dve", ap, src, scale, EPS, ALU.mult, ALU.add, [res] + list(in_reads), [res])
        self.act(ap, ap, AF.Sqrt, [res], [res])
        self.P.add("dve", lambda e: e.reciprocal(out=ap, in_=ap), reads=[res], writes=[res])

    def h_res(self, c, c0, n):
        out = []
        if c0 < NMETA:
            out.append(self.hres[c][0])
        lo = max(c0, NMETA) - NMETA
        hi = c0 + n - NMETA
        if hi > lo:
            for i in range(lo // 128, (hi - 1) // 128 + 1):
                out.append(self.hres[c][1 + i])
        return out

    def build(self):
        nc, P = self.nc, self.P
        self.x = self.din("x", [SEQ, D])
        self.meta = self.din("meta_tokens", [NMETA, D])
        self.final_norm = self.din("final_norm", [D])
        self.ident_f = self.din("ident_f", [128, 128])
        L = DEPTH
        self.w = {}
        for nm, shp in [("ffn1_norm", [L, D]), ("ffn1_in", [L, D, 2 * DFF]), ("ffn1_out", [L, DFF, D]),
                        ("ffn2_norm", [L, D]), ("ffn2_in", [L, D, 2 * DFF]), ("ffn2_out", [L, DFF, D])]:
            self.w[nm] = self.din(nm, shp)
        for nm, shp in [("mix_norm", [L, D]), ("w_in", [L, D, DIN]), ("conv_w", [L, 4, 256]), ("conv_b", [L, 256]),
                        ("lru_wa", [L, 4, 64, 64]), ("lru_ba", [L, 4, 64]), ("lru_wx", [L, 4, 64, 64]), ("lru_bx", [L, 4, 64]),
                        ("lru_lambda", [L, 256]), ("lru_out_norm", [L, 256]), ("lam_q1", [L, 64]), ("lam_k1", [L, 64]),
                        ("lam_q2", [L, 64]), ("lam_k2", [L, 64]), ("diff_subln", [L, 128]), ("q_norm", [L, 256]),
                        ("w_uq", [L, 256, 384]), ("kv_norm", [L, 128]), ("w_ukv", [L, 128, 512]),
                        ("mla_out_norm", [L, 256]), ("w_out", [L, D, D])]:
            self.w[nm] = self.din(nm, shp)
        self.rope_cs = self.din("rope_cs", [96, 2, T])
        self.tri_in = self.din("tri_in", [128, 128])
        self.out = self.dout("out", [SEQ, D])
        self.LRU_IN = self.dscratch("lru_in", [4, 128, T], F32)
        self.QKD = self.dscratch("qkd", [8, 128, T], BF16)
        self.VD = self.dscratch("vd", [T, 512], BF16)
        self.QM = self.dscratch("qm", [4, 96, T], BF16)
        self.KM = self.dscratch("km", [4, 96, T], BF16)
        self.VM = self.dscratch("vm", [T, 260], BF16)
        self.YT = self.dscratch("yt", [8, 128, T], BF16)
        self.wf = {}
        for l in range(L):
            for which in (1, 2):
                self.wf[(l, which)] = self.dscratch("wf_%d_%d" % (l, which), [NF, 128, 3072], BF16)
        self.wf_r = {k: Res("wf%d%d" % k) for k in self.wf}

        self.hT = self.sb("hT", [128, NCH, T], F32)
        self.hres = [[Res("h%d_%d" % (c, i)) for i in range(1 + SEQ // 128)] for c in range(NCH)]
        self.arena = self.sb("arena", [128, ARENA_BYTES // 4], F32)
        self.identf = self.sb("identf", [128, 128], F32)
        self.identf_r = Res("identf")
        self.ones_bf = self.sb("ones_bf", [128, 128], BF16)
        self.ones_r = Res("ones")
        self.gains = self.sb("gains", [128, 8, NCH], F32)
        self.gains_r = Res("gains")

        P.add("sp", lambda e: e.dma_start(out=self.identf[:], in_=self.ident_f[:, :]),
              writes=[self.identf_r], dma=True, sem_res=self.identf_r)
        P.add("pool", lambda e: e.memset(self.ones_bf[:], 1.0), writes=[self.ones_r])
        self.gain_idx = {}
        gi = 0
        for l in range(L):
            for nm in ("ffn1_norm", "ffn2_norm", "mix_norm"):
                src = self.w[nm][l, :].rearrange("(c p) -> p c", p=128)
                P.add("sp", lambda e, k=gi, src=src: e.dma_start(out=self.gains[:, k, :], in_=src),
                      writes=[self.gains_r], dma=True, sem_res=self.gains_r)
                self.gain_idx[(nm, l)] = gi
                gi += 1
        src = self.final_norm.rearrange("(c p) -> p c", p=128)
        P.add("sp", lambda e, k=gi, src=src: e.dma_start(out=self.gains[:, k, :], in_=src),
              writes=[self.gains_r], dma=True, sem_res=self.gains_r)
        self.gain_idx["final"] = gi

        self.psum = [self.ps("ps%d" % i, [128, 512], F32) for i in range(8)]
        self.psum_r = [Res("ps%d" % i) for i in range(8)]

        self.stage_load_x()
        ffns = [(l, w) for l in range(self.n_layers) for w in (1, 2) if self.debug.get("ffn%d" % w, True)]
        self.conv_queue = []
        for i, (l, w) in enumerate(ffns):
            if i == 0 or not self.debug.get("mixer", True) or not self.debug.get("diff", True):
                self.stage_convert_ffn(l, w)
            else:
                self.conv_queue += [(l, w, f) for f in range(NF)]
        for l in range(self.n_layers):
            if self.debug.get("ffn1", True):
                self.stage_ffn(l, 1)
            if self.debug.get("mixer", True):
                if self.debug.get("proj1", True):
                    self.stage_proj1(l)
                if self.debug.get("proj2", True):
                    self.stage_proj2(l)
                if self.debug.get("lru", True):
                    self.stage_lru(l)
                if self.debug.get("diff", True):
                    self.stage_diff(l)
                if self.debug.get("mla", True):
                    self.stage_mla(l)
                if self.debug.get("oproj", True):
                    self.stage_oproj(l)
            if self.debug.get("ffn2", True):
                self.stage_ffn(l, 2)
        self.stage_final()
        with nc.allow_non_contiguous_dma(reason="small strided parameter loads"):
            P.emit(nc, self.es)
        return nc

    def psring(self, idxs, name):
        r = Ring([self.psum[i] for i in idxs], name)
        r.res = [self.psum_r[i] for i in idxs]
        return r

    def stage_load_x(self):
        P = self.P
        self.stage_begin()
        ring = Ring([self.alloc([D], F32) for i in range(3)], "xin")
        psr = self.psring([0, 1, 2, 3], "x")
        n = 0
        for i in range(-1, SEQ // 128):
            buf, br = ring.next()
            if i < 0:
                rows, c0 = NMETA, 0
                src = self.meta[:, :]
            else:
                rows, c0 = 128, NMETA + 128 * i
                src = self.x[128 * i:128 * (i + 1), :]
            P.add("sp", lambda e, buf=buf, rows=rows, src=src: e.dma_start(out=buf[0:rows, :], in_=src),
                  writes=[br], dma=True, sem_res=br)
            for half in range(2):
                pt, pr = psr.next()
                for cc in range(4):
                    c = half * 4 + cc
                    P.add("pe", lambda e, pt=pt, cc=cc, buf=buf, rows=rows, c=c: e.transpose(
                        out=pt[:, cc * 128:cc * 128 + rows], in_=buf[0:rows, c * 128:(c + 1) * 128],
                        identity=self.identf[0:rows, 0:rows]),
                        reads=[br, self.identf_r], writes=[pr])
                eng = "act" if (n % 2 == 0) else "dve"
                n += 1
                hr = []
                for cc in range(4):
                    hr += self.h_res(half * 4 + cc, c0, rows)
                dst = self.hT[:, half * 4:half * 4 + 4, c0:c0 + rows]
                srcp = pt[:, :].rearrange("p (c t) -> p c t", c=4)[:, :, 0:rows]
                if eng == "act":
                    P.add("act", lambda e, dst=dst, srcp=srcp: e.activation(out=dst, in_=srcp, func=AF.Copy),
                          reads=[pr], writes=hr)
                else:
                    P.add("dve", lambda e, dst=dst, srcp=srcp: e.tensor_copy(out=dst, in_=srcp),
                          reads=[pr], writes=hr)

    def conv_bufs(self):
        return dict(a=self.alloc([2, NCH, 128], F32), ar=Res("cva"), ao=self.alloc([D], F32), aor=Res("cvo"),
                    b=self.alloc([3072], BF16), br=Res("cvb"))

    def conv_job(self, job, cb, eng="pool"):
        l, which, f = job
        w_in = self.w["ffn%d_in" % which][l]
        w_out = self.w["ffn%d_out" % which][l]
        WF = self.wf[(l, which)]
        a, ar, ao, aor, b, br = cb["a"], cb["ar"], cb["ao"], cb["aor"], cb["b"], cb["br"]
        for gu in range(2):
            col = gu * DFF + f * 128
            self.load(a[:, gu, :, :], w_in[:, col:col + 128].rearrange("(k p) j -> p k j", p=128), ar)
        self.load(ao[:, :], w_out[f * 128:(f + 1) * 128, :], aor)
        for gu in range(2):
            out = b[:, 0:2048].rearrange("p (k j) -> p k j", k=NCH)[:, :, gu * 128:(gu + 1) * 128]
            self.cp(eng, out, a[:, gu, :, :], [ar], [br])
        self.cp(eng, b[:, 2048:3072], ao[:, :], [aor], [br])
        self.store(WF[f, :, :], b[:, :], br)

    def stage_convert_ffn(self, l, which):
        P = self.P
        self.stage_begin()
        w_in = self.w["ffn%d_in" % which][l]
        w_out = self.w["ffn%d_out" % which][l]
        WF = self.wf[(l, which)]
        wfr = self.wf_r[(l, which)]
        s32 = Ring([self.alloc([2, NCH, 256], F32) for _ in range(2)], "cv32")
        s32o = Ring([self.alloc([2, D], F32) for _ in range(2)], "cv32o")
        s16 = Ring([self.alloc([2, 3072], BF16) for _ in range(2)], "cv16")
        engs = ["act", "dve", "pool"]
        n = 0

        def cast(eng, out, in_):
            if eng == "act":
                return lambda e: e.activation(out=out, in_=in_, func=AF.Copy)
            return lambda e: e.tensor_copy(out=out, in_=in_)

        for fp in range(NF // 2):
            f0 = 2 * fp
            a, ar = s32.next()
            ao, aor = s32o.next()
            b, br = s16.next()
            for gu in range(2):
                col = gu * DFF + f0 * 128
                src = w_in[:, col:col + 256].rearrange("(k p) j -> p k j", p=128)
                P.add("sp", lambda e, a=a, gu=gu, src=src: e.dma_start(out=a[:, gu, :, :], in_=src),
                      writes=[ar], dma=True, sem_res=ar)
            src = w_out[f0 * 128:(f0 + 2) * 128, :].rearrange("(f p) c -> p f c", p=128)
            P.add("sp", lambda e, ao=ao, src=src: e.dma_start(out=ao[:, :, :], in_=src),
                  writes=[aor], dma=True, sem_res=aor)
            for ff in range(2):
                for gu in range(2):
                    eng = engs[n % 3]
                    n += 1
                    out = b[:, ff, 0:2048].rearrange("p (k j) -> p k j", k=NCH)[:, :, gu * 128:(gu + 1) * 128]
                    in_ = a[:, gu, :, ff * 128:(ff + 1) * 128]
                    P.add(eng, cast(eng, out, in_), reads=[ar], writes=[br])
            eng = engs[n % 3]
            n += 1
            P.add(eng, cast(eng, b[:, :, 2048:3072], ao[:, :, :]), reads=[aor], writes=[br])
            dst = WF[f0:f0 + 2, :, :].rearrange("f p c -> p f c")
            P.add("sp", lambda e, b=b, dst=dst: e.dma_start(out=dst, in_=b[:, :, :]),
                  reads=[br], writes=[wfr], dma=True, sem_res=br)

    def stage_ffn(self, l, which):
        P = self.P
        self.stage_begin()
        WF = self.wf[(l, which)]
        wfr = self.wf_r[(l, which)]
        gk = self.gain_idx[("ffn%d_norm" % which, l)]
        xn = self.alloc([NCH, 1040], BF16)
        hid = Ring([self.alloc([1040], BF16) for _ in range(4)], "hid")
        wsl = Ring([self.alloc([3072], BF16) for _ in range(6)], "wsl")
        sqr = Ring([self.alloc([512], BF16) for _ in range(2)], "sq")
        rstd = self.alloc([1040], F32)
        rstd_r = Res("rstd")
        sgr = Ring([self.alloc([512], F32) for _ in range(2)], "sg")
        ps_gu = self.psring([0, 1, 2, 3], "gu")
        ps_o = self.psring([4, 5, 6], "o")
        ps_n, ps_nr = self.psum[7], self.psum_r[7]

        sts = [[(0, 16), (16, 512), (528, 512)]]
        for s in range(1, 4):
            b0 = 1040 + 1024 * (s - 1)
            sts.append([(b0, 512), (b0 + 512, 512)])
        nsq = 0
        for subs in sts:
            base = subs[0][0]
            xn_r = [Res("xn%d" % i) for i in range(len(subs))]
            for si, (c0, n) in enumerate(subs):
                lc = c0 - base
                for c in range(NCH):
                    sq, sq_r = sqr.next()
                    src = self.hT[:, c, c0:c0 + n]
                    if nsq % 2 == 0:
                        P.add("act", lambda e, sq=sq, src=src, n=n: e.activation(out=sq[:, 0:n], in_=src, func=AF.Square),
                              reads=self.h_res(c, c0, n), writes=[sq_r])
                    else:
                        P.add("pool", lambda e, sq=sq, src=src, n=n: e.tensor_tensor(out=sq[:, 0:n], in0=src, in1=src, op=ALU.mult),
                              reads=self.h_res(c, c0, n), writes=[sq_r])
                    nsq += 1
                    P.add("pe", lambda e, sq=sq, n=n, c=c: e.matmul(ps_n[:, 0:n], lhsT=self.ones_bf[:, :], rhs=sq[:, 0:n],
                                                                     start=(c == 0), stop=(c == NCH - 1)),
                          reads=[sq_r, self.ones_r], writes=[ps_nr])
                rs = rstd[:, lc:lc + n]
                P.add("dve", lambda e, rs=rs, n=n: e.tensor_scalar(out=rs, in0=ps_n[:, 0:n], scalar1=1.0 / D, scalar2=EPS,
                                                                   op0=ALU.mult, op1=ALU.add),
                      reads=[ps_nr], writes=[rstd_r])
                P.add("act", lambda e, rs=rs: e.activation(out=rs, in_=rs, func=AF.Sqrt), reads=[rstd_r], writes=[rstd_r])
                P.add("dve", lambda e, rs=rs: e.reciprocal(out=rs, in_=rs), reads=[rstd_r], writes=[rstd_r])
                for c in range(NCH):
                    P.add("dve", lambda e, c=c, c0=c0, n=n, lc=lc, rs=rs: e.scalar_tensor_tensor(
                        out=xn[:, c, lc:lc + n], in0=self.hT[:, c, c0:c0 + n], scalar=self.gains[:, gk, c:c + 1],
                        in1=rs, op0=ALU.mult, op1=ALU.mult),
                        reads=self.h_res(c, c0, n) + [rstd_r, self.gains_r], writes=[xn_r[si]])

            def GU(f):
                slot, slot_r = wsl.next()
                P.add("sp", lambda e, slot=slot, f=f: e.dma_start(out=slot[:, :], in_=WF[f, :, :]),
                      reads=[wfr], writes=[slot_r], dma=True, sem_res=slot_r)
                hb, hb_r = hid.next()
                for si, (c0, n) in enumerate(subs):
                    lc = c0 - base
                    pg, pgr = ps_gu.next()
                    pu, pur = ps_gu.next()
                    for gu, (pp, ppr) in enumerate(((pg, pgr), (pu, pur))):
                        for k in range(NCH):
                            P.add("pe", lambda e, pp=pp, slot=slot, k=k, gu=gu, lc=lc, n=n: e.matmul(
                                pp[:, 0:n], lhsT=slot[:, k * 256 + gu * 128:k * 256 + gu * 128 + 128],
                                rhs=xn[:, k, lc:lc + n], start=(k == 0), stop=(k == NCH - 1)),
                                reads=[slot_r, xn_r[si]], writes=[ppr])
                    sg, sg_r = sgr.next()
                    P.add("act", lambda e, sg=sg, pg=pg, n=n: e.activation(out=sg[:, 0:n], in_=pg[:, 0:n], func=AF.Silu),
                          reads=[pgr], writes=[sg_r])
                    P.add("dve", lambda e, sg=sg, pu=pu, hb=hb, lc=lc, n=n: e.tensor_tensor(
                        out=hb[:, lc:lc + n], in0=sg[:, 0:n], in1=pu[:, 0:n], op=ALU.mult),
                        reads=[sg_r, pur], writes=[hb_r])
                return (slot, slot_r, hb, hb_r)

            def W2(group):
                for o in range(NCH):
                    for si, (c0, n) in enumerate(subs):
                        lc = c0 - base
                        po, por = ps_o.next()
                        for gi_, (slot, slot_r, hb, hb_r) in enumerate(group):
                            P.add("pe", lambda e, po=po, slot=slot, hb=hb, o=o, lc=lc, n=n, gi_=gi_, ng=len(group): e.matmul(
                                po[:, 0:n], lhsT=slot[:, 2048 + o * 128:2048 + (o + 1) * 128], rhs=hb[:, lc:lc + n],
                                start=(gi_ == 0), stop=(gi_ == ng - 1)),
                                reads=[slot_r, hb_r], writes=[por])
                        hr = self.h_res(o, c0, n)
                        P.add("dve", lambda e, po=po, o=o, c0=c0, n=n: e.scalar_tensor_tensor(
                            out=self.hT[:, o, c0:c0 + n], in0=po[:, 0:n], scalar=0.5, in1=self.hT[:, o, c0:c0 + n],
                            op0=ALU.mult, op1=ALU.add),
                            reads=[por] + hr, writes=hr)

            prev = None
            for g in range(NF // 2):
                grp = [GU(2 * g), GU(2 * g + 1)]
                if prev is not None:
                    W2(prev)
                prev = grp
            W2(prev)

    def tiles9(self):
        return [(0, NMETA)] + [(NMETA + 512 * j, 512) for j in range(SEQ // 512)]

    def norm_tile(self, c0, n, gk, xn, xn_r, sqr, rstd, rstd_r, ps_n, ps_nr, cnt=[0]):
        P = self.P
        for c in range(NCH):
            sq, sq_r = sqr.next()
            src = self.hT[:, c, c0:c0 + n]
            if cnt[0] % 2 == 0:
                self.act(sq[:, 0:n], src, AF.Square, self.h_res(c, c0, n), [sq_r])
            else:
                self.tt("pool", sq[:, 0:n], src, src, ALU.mult, self.h_res(c, c0, n), [sq_r])
            cnt[0] += 1
            self.mm(ps_n[:, 0:n], self.ones_bf[:, :], sq[:, 0:n], c == 0, c == NCH - 1, [sq_r, self.ones_r], [ps_nr])
        rs = rstd[:, 0:n]
        self.rsqrt_inplace(rs, rstd_r, 1.0 / D, in_ap=ps_n[:, 0:n], in_reads=[ps_nr])
        for c in range(NCH):
            self.stt(xn[:, c, 0:n], self.hT[:, c, c0:c0 + n], self.gains[:, gk, c:c + 1], rs, ALU.mult, ALU.mult,
                     self.h_res(c, c0, n) + [rstd_r, self.gains_r], [xn_r])

    def load_w_bf16(self, dst, dst_r, src, ncols, slab=160):
        st = Ring([self.alloc([src.shape[0] // 128, slab], F32) for _ in range(2)], "wslab")
        engs = ["act", "dve", "pool"]
        i = 0
        for c in range(0, ncols, slab):
            w = min(slab, ncols - c)
            a, ar = st.next()
            self.load(a[:, :, 0:w], src[:, c:c + w].rearrange("(k p) j -> p k j", p=128), ar)
            self.cp(engs[i % 3], dst[:, :, c:c + w], a[:, :, 0:w], [ar], [dst_r])
            i += 1

    def stage_proj1(self, l):
        self.stage_begin()
        gk = self.gain_idx[("mix_norm", l)]
        w_in = self.w["w_in"][l]
        Wp = self.alloc([NCH, 2048], BF16)
        Wp_r = Res("Wp")
        self.load_w_bf16(Wp, Wp_r, w_in[:, 0:2048], 2048)
        xnr = Ring([self.alloc([NCH, 512], BF16) for _ in range(2)], "xn")
        sqr = Ring([self.alloc([512], BF16) for _ in range(2)], "sq")
        rstr = Ring([self.alloc([512], F32) for _ in range(2)], "rstd")
        st32 = Ring([self.alloc([512], F32) for _ in range(3)], "st32")
        st16 = Ring([self.alloc([512], BF16) for _ in range(4)], "st16")
        psr = self.psring([0, 1, 2, 3, 4, 5], "p1")
        ps_n, ps_nr = self.psum[7], self.psum_r[7]
        ne = 0
        for (c0, n) in self.tiles9():
            xn, xn_r = xnr.next()
            rstd, rstd_r = rstr.next()
            self.norm_tile(c0, n, gk, xn, xn_r, sqr, rstd, rstd_r, ps_n, ps_nr)
            for g in range(12):
                pt, pr = psr.next()
                for k in range(NCH):
                    self.mm(pt[:, 0:n], Wp[:, k, g * 128:(g + 1) * 128], xn[:, k, 0:n], k == 0, k == NCH - 1,
                            [Wp_r, xn_r], [pr])
                eng = "act" if ne % 2 == 0 else "dve"
                ne += 1
                if g < 4:
                    sb, sr = st32.next()
                    self.cp(eng, sb[:, 0:n], pt[:, 0:n], [pr], [sr])
                    self.store(self.LRU_IN[g, :, c0:c0 + n], sb[:, 0:n], sr)
                else:
                    sb, sr = st16.next()
                    self.cp(eng, sb[:, 0:n], pt[:, 0:n], [pr], [sr])
                    self.store(self.QKD[g - 4, :, c0:c0 + n], sb[:, 0:n], sr)
            for j in range(0, n, 128):
                m = min(128, n - j)
                pt, pr = psr.next()
                for k in range(NCH):
                    self.mm(pt[0:m, 0:512], xn[:, k, j:j + m], Wp[:, k, 1536:2048], k == 0, k == NCH - 1,
                            [Wp_r, xn_r], [pr])
                sb, sr = st16.next()
                eng = "act" if ne % 2 == 0 else "dve"
                ne += 1
                self.cp(eng, sb[0:m, 0:512], pt[0:m, 0:512], [pr], [sr])
                self.store(self.VD[c0 + j:c0 + j + m, :], sb[0:m, 0:512], sr)

    def stage_proj2(self, l):
        self.stage_begin()
        gk = self.gain_idx[("mix_norm", l)]
        w_in = self.w["w_in"][l]
        Wp = self.alloc([NCH, 512], BF16)
        Wp_r = Res("Wp2")
        self.P.add("pool", lambda e: e.memset(Wp[:, :, 416:480], 0.0), writes=[Wp_r])
        self.load_w_bf16(Wp, Wp_r, w_in[:, 2048:2464], 416, slab=104)
        self.cp("dve", Wp[:, :, 480:496], Wp[:, :, 400:416], [Wp_r], [Wp_r])
        self.cp("dve", Wp[:, :, 496:512], Wp[:, :, 384:400], [Wp_r], [Wp_r])
        Wq = self.alloc([2, 768], BF16)
        Wq_r = Res("Wq")
        self.load_w_bf16(Wq, Wq_r, self.w["w_uq"][l], 384, slab=192)
        for h in range(4):
            b = 384 + h * 96
            a = h * 96
            self.cp("dve", Wq[:, :, b:b + 64], Wq[:, :, a:a + 64], [Wq_r], [Wq_r])
            self.cp("dve", Wq[:, :, b + 64:b + 80], Wq[:, :, a + 80:a + 96], [Wq_r], [Wq_r])
            self.cp("dve", Wq[:, :, b + 80:b + 96], Wq[:, :, a + 64:a + 80], [Wq_r], [Wq_r])
        Wkv = self.alloc([512], BF16)
        Wkv_r = Res("Wkv")
        kv32 = self.alloc([512], F32)
        kv32_r = Res("kv32")
        self.load(kv32[:, :], self.w["w_ukv"][l][:, :], kv32_r)
        self.cp("dve", Wkv[:, :].rearrange("p (t h d) -> p h t d", t=2, h=4),
                kv32[:, :].rearrange("p (h t d) -> p h t d", h=4, t=2), [kv32_r], [Wkv_r])
        gq = self.alloc([2], F32)
        gkv = self.alloc([1], F32)
        g_r = Res("gqkv")
        self.load(gq[:, :], self.w["q_norm"][l].rearrange("(c p) -> p c", p=128), g_r)
        self.load(gkv[:, :], self.w["kv_norm"][l].rearrange("(c p) -> p c", p=128), g_r)

        xnr = Ring([self.alloc([NCH, 512], BF16) for _ in range(1)], "xn")
        sqr = Ring([self.alloc([512], BF16) for _ in range(2)], "sq")
        rstr = Ring([self.alloc([512], F32) for _ in range(2)], "rstd")
        cq = self.alloc([2, 512], F32)
        cq_r = Res("cq")
        ckv = self.alloc([512], F32)
        ckv_r = Res("ckv")
        cqn = self.alloc([2, 512], BF16)
        cqn_r = Res("cqn")
        ckvn = self.alloc([512], BF16)
        ckvn_r = Res("ckvn")
        rsq = self.alloc([512], F32)
        rsq_r = Res("rsq")
        rskv = self.alloc([512], F32)
        rskv_r = Res("rskv")
        cs = self.alloc([2, 512], F32)
        cs_r = Res("cs")
        t1r = Ring([self.alloc([512], F32) for _ in range(2)], "t1")
        t2r = Ring([self.alloc([512], F32) for _ in range(2)], "t2")
        qst = Ring([self.alloc([4, 512], BF16) for _ in range(1)], "qst")
        kst = Ring([self.alloc([4, 512], BF16) for _ in range(1)], "kst")
        vst = Ring([self.alloc([4, 65], BF16) for _ in range(2)], "vst")
        psr = self.psring([0, 1, 2, 3, 4, 5], "p2")
        ps_n, ps_nr = self.psum[7], self.psum_r[7]
        ps_m, ps_mr = self.psum[6], self.psum_r[6]
        for (c0, n) in self.tiles9():
            xn, xn_r = xnr.next()
            rstd, rstd_r = rstr.next()
            self.norm_tile(c0, n, gk, xn, xn_r, sqr, rstd, rstd_r, ps_n, ps_nr)
            self.load(cs[0:96, :, 0:n], self.rope_cs[:, :, c0:c0 + n], cs_r)
            for j in range(3):
                pt, pr = psr.next()
                for k in range(NCH):
                    self.mm(pt[:, 0:n], Wp[:, k, j * 128:(j + 1) * 128], xn[:, k, 0:n], k == 0, k == NCH - 1,
                            [Wp_r, xn_r], [pr])
                if j < 2:
                    self.cp("act", cq[:, j, 0:n], pt[:, 0:n], [pr], [cq_r])
                else:
                    self.cp("act", ckv[:, 0:n], pt[:, 0:n], [pr], [ckv_r])
            for j in range(2):
                sq, sq_r = sqr.next()
                self.tt("pool", sq[:, 0:n], cq[:, j, 0:n], cq[:, j, 0:n], ALU.mult, [cq_r], [sq_r])
                self.mm(ps_m[:, 0:n], self.ones_bf[:, :], sq[:, 0:n], j == 0, j == 1, [sq_r, self.ones_r], [ps_mr])
            self.rsqrt_inplace(rsq[:, 0:n], rsq_r, 1.0 / 256, in_ap=ps_m[:, 0:n], in_reads=[ps_mr])
            for j in range(2):
                self.stt(cqn[:, j, 0:n], cq[:, j, 0:n], gq[:, j:j + 1], rsq[:, 0:n], ALU.mult, ALU.mult,
                         [cq_r, g_r, rsq_r], [cqn_r])
            sq, sq_r = sqr.next()
            self.tt("pool", sq[:, 0:n], ckv[:, 0:n], ckv[:, 0:n], ALU.mult, [ckv_r], [sq_r])
            self.mm(ps_m[:, 0:n], self.ones_bf[:, :], sq[:, 0:n], True, True, [sq_r, self.ones_r], [ps_mr])
            self.rsqrt_inplace(rskv[:, 0:n], rskv_r, 1.0 / 128, in_ap=ps_m[:, 0:n], in_reads=[ps_mr])
            self.stt(ckvn[:, 0:n], ckv[:, 0:n], gkv[:, 0:1], rskv[:, 0:n], ALU.mult, ALU.mult,
                     [ckv_r, g_r, rskv_r], [ckvn_r])
            qs, qs_r = qst.next()
            for h in range(4):
                pa, par = psr.next()
                pb, pbr = psr.next()
                for k in range(2):
                    self.mm(pa[0:96, 0:n], Wq[:, k, h * 96:(h + 1) * 96], cqn[:, k, 0:n], k == 0, k == 1, [Wq_r, cqn_r], [par])
                for k in range(2):
                    self.mm(pb[0:96, 0:n], Wq[:, k, 384 + h * 96:384 + (h + 1) * 96], cqn[:, k, 0:n], k == 0, k == 1,
                            [Wq_r, cqn_r], [pbr])
                t1, t1_r = t1r.next()
                t2, t2_r = t2r.next()
                self.tt("dve", t1[0:96, 0:n], pa[0:96, 0:n], cs[0:96, 0, 0:n], ALU.mult, [par, cs_r], [t1_r])
                self.tt("dve", t2[0:96, 0:n], pb[0:96, 0:n], cs[0:96, 1, 0:n], ALU.mult, [pbr, cs_r], [t2_r])
                self.tt("pool", qs[0:96, h, 0:n], t1[0:96, 0:n], t2[0:96, 0:n], ALU.add, [t1_r, t2_r], [qs_r])
            self.store(self.QM[:, :, c0:c0 + n].rearrange("h r t -> r h t"), qs[0:96, :, 0:n], qs_r)
            ks, ks_r = kst.next()
            for h in range(4):
                pt, pr = psr.next()
                self.mm(pt[0:64, 0:n], Wkv[:, h * 64:(h + 1) * 64], ckvn[:, 0:n], True, True, [Wkv_r, ckvn_r], [pr])
                self.cp("act", ks[0:64, h, 0:n], pt[0:64, 0:n], [pr], [ks_r])
            pa, par = psr.next()
            pb, pbr = psr.next()
            for k in range(NCH):
                self.mm(pa[0:96, 0:n], Wp[:, k, 320:416], xn[:, k, 0:n], k == 0, k == NCH - 1, [Wp_r, xn_r], [par])
            for k in range(NCH):
                self.mm(pb[0:96, 0:n], Wp[:, k, 416:512], xn[:, k, 0:n], k == 0, k == NCH - 1, [Wp_r, xn_r], [pbr])
            t1, t1_r = t1r.next()
            t2, t2_r = t2r.next()
            self.tt("dve", t1[64:96, 0:n], pa[64:96, 0:n], cs[64:96, 0, 0:n], ALU.mult, [par, cs_r], [t1_r])
            self.tt("dve", t2[64:96, 0:n], pb[64:96, 0:n], cs[64:96, 1, 0:n], ALU.mult, [pbr, cs_r], [t2_r])
            self.tt("pool", t1[64:96, 0:n], t1[64:96, 0:n], t2[64:96, 0:n], ALU.add, [t1_r, t2_r], [t1_r])
            for h in range(4):
                self.cp("pool" if h % 2 else "act", ks[64:96, h, 0:n], t1[64:96, 0:n], [t1_r], [ks_r])
            self.store(self.KM[:, :, c0:c0 + n].rearrange("h r t -> r h t"), ks[0:96, :, 0:n], ks_r)
            for j in range(0, n, 128):
                m = min(128, n - j)
                pt, pr = psr.next()
                self.mm(pt[0:m, 0:256], ckvn[:, j:j + m], Wkv[:, 256:512], True, True, [Wkv_r, ckvn_r], [pr])
                vs, vs_r = vst.next()
                self.P.add("pool", lambda e, vs=vs: e.memset(vs[:, :, 64:65], 1.0), writes=[vs_r])
                self.cp("act", vs[0:m, :, 0:64], pt[0:m, 0:256].rearrange("p (h d) -> p h d", h=4), [pr], [vs_r])
                self.store(self.VM[c0 + j:c0 + j + m, :], vs[0:m, :, :].rearrange("p h d -> p (h d)"), vs_r)

    def stage_lru(self, l):
        self.stage_begin()
        P = self.P
        pr_ = Res("lrup")
        convw = self.alloc([2, 4], F32)
        convb = self.alloc([2], F32)
        ba = self.alloc([2], F32)
        bx = self.alloc([2], F32)
        lam = self.alloc([2], F32)
        gl = self.alloc([2], F32)
        for c in range(2):
            for j in range(4):
                self.load(convw[:, c, j:j + 1], self.w["conv_w"][l][j, c * 128:(c + 1) * 128].rearrange("(p o) -> p o", o=1), pr_)
        self.load(convb[:, :], self.w["conv_b"][l].rearrange("(c p) -> p c", p=128), pr_)
        self.load(ba[:, :], self.w["lru_ba"][l].rearrange("n d -> (n d)").rearrange("(c p) -> p c", p=128), pr_)
        self.load(bx[:, :], self.w["lru_bx"][l].rearrange("n d -> (n d)").rearrange("(c p) -> p c", p=128), pr_)
        self.load(lam[:, :], self.w["lru_lambda"][l].rearrange("(c p) -> p c", p=128), pr_)
        self.load(gl[:, :], self.w["lru_out_norm"][l].rearrange("(c p) -> p c", p=128), pr_)
        nsp = self.alloc([2], F32)
        nsp2 = self.alloc([2], F32)
        self.act(nsp[:, :], lam[:, :], AF.Exp, [pr_], [pr_], scale=-1.0)
        self.ts("dve", nsp[:, :], nsp[:, :], 1.0, None, ALU.add, None, [pr_], [pr_])
        self.act(nsp[:, :], nsp[:, :], AF.Ln, [pr_], [pr_])
        self.ts("dve", nsp2[:, :], nsp[:, :], -16.0, None, ALU.mult, None, [pr_], [pr_])
        self.ts("dve", nsp[:, :], nsp[:, :], -8.0, None, ALU.mult, None, [pr_], [pr_])
        bd32 = self.alloc([2, 2, 128], F32)
        bd = self.alloc([2, 2, 128], BF16)
        bd_r = Res("bd")
        P.add("pool", lambda e: e.memset(bd32[:, :, :, :], 0.0), writes=[bd_r])
        for wi, nm in enumerate(("lru_wa", "lru_wx")):
            for nb in range(4):
                c, hb = nb // 2, nb % 2
                self.load(bd32[hb * 64:(hb + 1) * 64, wi, c, hb * 64:(hb + 1) * 64], self.w[nm][l][nb, :, :], bd_r)
        self.cp("dve", bd[:, :, :, :], bd32[:, :, :, :], [bd_r], [bd_r])
        carry = self.alloc([2], F32)
        carry_r = Res("carry")
        NB = 515
        uar = Ring([self.alloc([2, NB], F32) for _ in range(2)], "ua")
        gar = Ring([self.alloc([2, 512], F32) for _ in range(2)], "ga")
        fr = Ring([self.alloc([512], F32) for _ in range(10)], "lf")
        ucb_r = Ring([self.alloc([512], BF16) for _ in range(2)], "ucb")
        sqr = Ring([self.alloc([512], BF16) for _ in range(2)], "sq")
        yp = self.alloc([2, 512], F32)
        yp_r = Res("yp")
        rstd = self.alloc([512], F32)
        rstd_r = Res("rstd")
        yst = Ring([self.alloc([2, 512], BF16) for _ in range(2)], "yst")
        psr = self.psring([0, 1, 2, 3], "lru")
        ps_n, ps_nr = self.psum[7], self.psum_r[7]
        first = True
        for (c0, n) in self.tiles9():
            ua, ua_r = uar.next()
            ga, ga_r = gar.next()
            if first:
                P.add("pool", lambda e, ua=ua: e.memset(ua[:, :, 0:3], 0.0), writes=[ua_r])
                self.load(ua[:, :, 3:3 + n], self.LRU_IN[2:4, :, c0:c0 + n].rearrange("c p t -> p c t"), ua_r)
            else:
                self.load(ua[:, :, 0:3 + n], self.LRU_IN[2:4, :, c0 - 3:c0 + n].rearrange("c p t -> p c t"), ua_r)
            self.load(ga[:, :, 0:n], self.LRU_IN[0:2, :, c0:c0 + n].rearrange("c p t -> p c t"), ga_r)
            for j in range(2):
                uc, uc_r = fr.next()
                self.ts("dve", uc[:, 0:n], ua[:, j, 0:n], convw[:, j, 0:1], convb[:, j:j + 1], ALU.mult, ALU.add,
                        [ua_r, pr_], [uc_r])
                for t in range(1, 4):
                    self.stt(uc[:, 0:n], ua[:, j, t:t + n], convw[:, j, t:t + 1], uc[:, 0:n], ALU.mult, ALU.add,
                             [ua_r, pr_, uc_r], [uc_r])
                ucb, ucb_rr = ucb_r.next()
                self.cp("act", ucb[:, 0:n], uc[:, 0:n], [uc_r], [ucb_rr])
                pa, par = psr.next()
                px, pxr = psr.next()
                self.mm(pa[:, 0:n], bd[:, 0, j, :], ucb[:, 0:n], True, True, [bd_r, ucb_rr], [par])
                self.mm(px[:, 0:n], bd[:, 1, j, :], ucb[:, 0:n], True, True, [bd_r, ucb_rr], [pxr])
                r, r_r = fr.next()
                ig, ig_r = fr.next()
                self.act(r[:, 0:n], pa[:, 0:n], AF.Sigmoid, [par, pr_], [r_r], bias=ba[:, j:j + 1])
                self.act(ig[:, 0:n], px[:, 0:n], AF.Sigmoid, [pxr, pr_], [ig_r], bias=bx[:, j:j + 1])
                a, a_r = fr.next()
                a2, a2_r = fr.next()
                self.act(a[:, 0:n], r[:, 0:n], AF.Exp, [r_r, pr_], [a_r], scale=nsp[:, j:j + 1])
                self.act(a2[:, 0:n], r[:, 0:n], AF.Exp, [r_r, pr_], [a2_r], scale=nsp2[:, j:j + 1])
                self.ts("pool", a2[:, 0:n], a2[:, 0:n], -1.0, 1.0, ALU.mult, ALU.add, [a2_r], [a2_r])
                self.act(a2[:, 0:n], a2[:, 0:n], AF.Sqrt, [a2_r], [a2_r])
                self.tt("pool", ig[:, 0:n], ig[:, 0:n], uc[:, 0:n], ALU.mult, [ig_r, uc_r], [ig_r])
                self.tt("pool", ig[:, 0:n], ig[:, 0:n], a2[:, 0:n], ALU.mult, [ig_r, a2_r], [ig_r])
                hs, hs_r = fr.next()
                if first:
                    P.add("dve", lambda e, hs=hs, a=a, ig=ig, n=n: e.tensor_tensor_scan(
                        out=hs[:, 0:n], data0=a[:, 0:n], data1=ig[:, 0:n], initial=0.0, op0=ALU.mult, op1=ALU.add),
                        reads=[a_r, ig_r], writes=[hs_r])
                else:
                    P.add("dve", lambda e, hs=hs, a=a, ig=ig, n=n, j=j: e.tensor_tensor_scan(
                        out=hs[:, 0:n], data0=a[:, 0:n], data1=ig[:, 0:n], initial=carry[:, j:j + 1],
                        op0=ALU.mult, op1=ALU.add),
                        reads=[a_r, ig_r, carry_r], writes=[hs_r])
                self.cp("dve", carry[:, j:j + 1], hs[:, n - 1:n], [hs_r], [carry_r])
                g = ga[:, j, 0:n]
                t, t_r = fr.next()
                self.tt("pool", t[:, 0:n], g, g, ALU.mult, [ga_r], [t_r])
                self.ts("pool", t[:, 0:n], t[:, 0:n], 0.044715, 1.0, ALU.mult, ALU.add, [t_r], [t_r])
                self.tt("pool", t[:, 0:n], t[:, 0:n], g, ALU.mult, [t_r, ga_r], [t_r])
                self.act(t[:, 0:n], t[:, 0:n], AF.Sigmoid, [t_r], [t_r], scale=1.5957691216057308)
                self.tt("pool", t[:, 0:n], t[:, 0:n], g, ALU.mult, [t_r, ga_r], [t_r])
                self.tt("dve", yp[:, j, 0:n], hs[:, 0:n], t[:, 0:n], ALU.mult, [hs_r, t_r], [yp_r])
                sq, sq_r = sqr.next()
                self.tt("pool", sq[:, 0:n], yp[:, j, 0:n], yp[:, j, 0:n], ALU.mult, [yp_r], [sq_r])
                self.mm(ps_n[:, 0:n], self.ones_bf[:, :], sq[:, 0:n], j == 0, j == 1, [sq_r, self.ones_r], [ps_nr])
            self.rsqrt_inplace(rstd[:, 0:n], rstd_r, 1.0 / 256, in_ap=ps_n[:, 0:n], in_reads=[ps_nr])
            ys, ys_r = yst.next()
            for j in range(2):
                self.stt(ys[:, j, 0:n], yp[:, j, 0:n], gl[:, j:j + 1], rstd[:, 0:n], ALU.mult, ALU.mult,
                         [yp_r, pr_, rstd_r], [ys_r])
            self.store(self.YT[0:2, :, c0:c0 + n].rearrange("c p t -> p c t"), ys[:, :, 0:n], ys_r)
            first = False

    def attn_sweep(self, qi, Kt, Qt, r0, r1, kq_r, V, v_r, dvp, obanks, scale, s_ring, pt_ring, tri, tri_r):
        if qi < 0:
            q0, nq, nsub = 0, NMETA, 1
            chunks = [(0, 0, NMETA, 0)]
        else:
            q0, nq, nsub = NMETA + 512 * qi, 512, 4
            chunks = [(0, 0, NMETA, -1)] + [(1 + i, NMETA + 128 * i, 128, -1) for i in range(4 * qi)]
            chunks += [(1 + 4 * qi + c, NMETA + 128 * (4 * qi + c), 128, c) for c in range(4)]
        started = set()
        for (kc, kcol, nk, diag) in chunks:
            s0 = max(diag, 0)
            qa = q0 + 128 * s0 if qi >= 0 else 0
            nqa = nq - 128 * s0 if qi >= 0 else nq
            ps, ps_r = s_ring.next()
            self.mm(ps[0:nk, 0:nqa], Kt[r0:r1, kcol:kcol + nk], Qt[r0:r1, qa:qa + nqa], True, True, [kq_r], [ps_r])
            pt, pt_r = pt_ring.next()
            self.act(pt[0:nk, 0:nqa], ps[0:nk, 0:nqa], AF.Exp, [ps_r], [pt_r], scale=scale)
            if diag >= 0:
                w = min(128, nqa)
                self.tt("dve", pt[0:nk, 0:w], pt[0:nk, 0:w], tri[0:nk, 0:w], ALU.mult, [pt_r, tri_r], [pt_r])
            self.flush_pv()

            def pv(s0=s0, pt=pt, pt_r=pt_r, nk=nk, kc=kc):
                for s in range(s0, nsub):
                    ob, ob_r, oc = obanks[s]
                    ws = min(128, nq)
                    lo = (s - s0) * 128
                    st = id(ob) not in started
                    started.add(id(ob))
                    self.mm(ob[0:ws, oc:oc + dvp], pt[0:nk, lo:lo + ws], V[0:nk, kc, 0:dvp], st, False,
                            [pt_r, v_r], [ob_r], skip_group_check=True)
            self._pend = pv

    _pend = None

    def flush_pv(self):
        if self._pend is not None:
            p = self._pend
            self._pend = None
            p()

    def load_consts_attn(self):
        tri32 = self.alloc([128], F32)
        tri = self.alloc([128], BF16)
        tri_r = Res("tri")
        self.load(tri32[:, :], self.tri_in[:, :], tri_r)
        self.cp("dve", tri[:, :], tri32[:, :], [tri_r], [tri_r])
        idb32 = self.alloc([128], F32)
        idb = self.alloc([128], BF16)
        idb_r = Res("idb")
        self.cp("dve", idb[:, :], self.identf[:, :], [self.identf_r], [idb_r])
        return tri, tri_r, idb, idb_r

    def stage_diff(self, l):
        self.stage_begin()
        P = self.P
        lam_init = 0.8 - 0.6 * math.exp(-0.3 * l)
        tri, tri_r, idb, idb_r = self.load_consts_attn()
        lv = self.alloc([4, 64], F32)
        lv_r = Res("lv")
        for i, nm in enumerate(("lam_q1", "lam_k1", "lam_q2", "lam_k2")):
            self.load(lv[:, i, :], self.w[nm][l].partition_broadcast(128), lv_r)
        lsc = self.alloc([8], F32)
        self.tt("dve", lv[:, 0, :], lv[:, 0, :], lv[:, 1, :], ALU.mult, [lv_r], [lv_r])
        self.tt("dve", lv[:, 2, :], lv[:, 2, :], lv[:, 3, :], ALU.mult, [lv_r], [lv_r])
        P.add("dve", lambda e: e.reduce_sum(out=lsc[:, 0:1], in_=lv[:, 0, :], axis=AX.X), reads=[lv_r], writes=[lv_r])
        P.add("dve", lambda e: e.reduce_sum(out=lsc[:, 1:2], in_=lv[:, 2, :], axis=AX.X), reads=[lv_r], writes=[lv_r])
        self.act(lsc[:, 0:2], lsc[:, 0:2], AF.Exp, [lv_r], [lv_r])
        self.tt("dve", lsc[:, 2:3], lsc[:, 1:2], lsc[:, 0:1], ALU.subtract, [lv_r], [lv_r])
        self.ts("dve", lsc[:, 2:3], lsc[:, 2:3], -lam_init, None, ALU.add, None, [lv_r], [lv_r])
        gsub = self.alloc([128], F32)
        self.load(gsub[:, :], self.w["diff_subln"][l].partition_broadcast(128), lv_r)
        self.ts("dve", gsub[:, :], gsub[:, :], 1.0 - lam_init, None, ALU.mult, None, [lv_r], [lv_r])

        Kt = self.alloc([T], BF16)
        Qm = [self.alloc([T], BF16) for _ in range(2)]
        kq_r = Res("kq")
        P.add("pool", lambda e: e.memset(Qm[0][64:128, :], 0.0), writes=[kq_r])
        P.add("pool", lambda e: e.memset(Qm[1][0:64, :], 0.0), writes=[kq_r])
        V = self.alloc([33, 129], BF16)
        v_r = Res("V")
        pt_ring = Ring([self.alloc([512], BF16) for _ in range(3)], "pt")
        o1r = Ring([self.alloc([128], F32) for _ in range(2)], "o1")
        o2r = Ring([self.alloc([128], F32) for _ in range(2)], "o2")
        scr = Ring([self.alloc([128], F32) for _ in range(2)], "scr")
        smr = Ring([self.alloc([8], F32) for _ in range(4)], "sm")
        ybr = Ring([self.alloc([128], BF16) for _ in range(2)], "yb")
        ytr = Ring([self.alloc([512], BF16) for _ in range(2)], "yts")
        s_ring = self.psring([0, 1, 2], "s")
        tps = self.psum[7][:, :].bitcast(BF16)
        tps_r = self.psum_r[7]
        cb = self.conv_bufs()
        n_slots = 4 * (1 + SEQ // 512)
        need_now = [j for j in self.conv_queue if j[0] == l or (j[0] == l + 1 and j[1] == 1)]
        slot_i = 0
        for h in range(4):
            self.load(Qm[0][0:64, :], self.QKD[h, 0:64, :], kq_r)
            self.load(Qm[1][64:128, :], self.QKD[h, 64:128, :], kq_r)
            self.load(Kt[:, :], self.QKD[4 + h, :, :], kq_r)
            P.add("pool", lambda e: e.memset(V[:, :, 128:129], 1.0), writes=[v_r])
            self.load(V[0:NMETA, 0, 0:128], self.VD[0:NMETA, h * 128:(h + 1) * 128], v_r)
            for i0 in range(0, 32, 8):
                self.load(V[:, 1 + i0:9 + i0, 0:128],
                          self.VD[NMETA + 128 * i0:NMETA + 128 * (i0 + 8), h * 128:(h + 1) * 128].rearrange("(i p) d -> p i d", p=128), v_r)
            for qi in range(-1, SEQ // 512):
                nsub = 1 if qi < 0 else 4
                ws = NMETA if qi < 0 else 128
                q0 = 0 if qi < 0 else NMETA + 512 * qi
                ob = {}
                for m in range(2):
                    banks = [(self.psum[3 + 2 * m], self.psum_r[3 + 2 * m], 0), (self.psum[3 + 2 * m], self.psum_r[3 + 2 * m], 129),
                             (self.psum[4 + 2 * m], self.psum_r[4 + 2 * m], 0), (self.psum[4 + 2 * m], self.psum_r[4 + 2 * m], 129)]
                    ob[m] = banks
                    self.attn_sweep(qi, Kt, Qm[m], 0, 128, kq_r, V, v_r, 129, banks, 0.125, s_ring, pt_ring, tri, tri_r)
                self.flush_pv()
                left = n_slots - slot_i
                k = -(-len(need_now) // left) if left > 0 else len(need_now)
                for _ in range(k):
                    if need_now:
                        job = need_now.pop(0)
                        self.conv_queue.remove(job)
                        self.conv_job(job, cb)
                slot_i += 1
                yts, yts_r = ytr.next()
                for s in range(nsub):
                    sm, sm_r = smr.next()
                    o1, o1_r = o1r.next()
                    o2, o2_r = o2r.next()
                    for m, (o, o_r) in enumerate(((o1, o1_r), (o2, o2_r))):
                        b, b_r, oc = ob[m][s]
                        P.add("dve", lambda e, sm=sm, b=b, oc=oc, m=m, ws=ws: e.reciprocal(
                            out=sm[0:ws, m:m + 1], in_=b[0:ws, oc + 128:oc + 129]), reads=[b_r], writes=[sm_r])
                        self.act(o[0:ws, :], b[0:ws, oc:oc + 128], AF.Copy, [b_r, sm_r], [o_r], scale=sm[0:ws, m:m + 1])
                    self.stt(o1[0:ws, :], o2[0:ws, :], lsc[0:ws, 2:3], o1[0:ws, :], ALU.mult, ALU.add, [o1_r, o2_r, lv_r], [o1_r])
                    sc, sc_r = scr.next()
                    self.act(sc[0:ws, :], o1[0:ws, :], AF.Square, [o1_r], [sc_r, sm_r], accum_out=sm[0:ws, 2:3])
                    self.rsqrt_inplace(sm[0:ws, 2:3], sm_r, 1.0 / 128)
                    yb, yb_r = ybr.next()
                    self.stt(yb[0:ws, :], o1[0:ws, :], sm[0:ws, 2:3], gsub[0:ws, :], ALU.mult, ALU.mult, [o1_r, sm_r, lv_r], [yb_r])
                    P.add("pe", lambda e, yb=yb, s=s, ws=ws: e.transpose(out=tps[:, s * 128:s * 128 + ws], in_=yb[0:ws, :],
                                                                          identity=idb[0:ws, 0:ws]),
                          reads=[yb_r, idb_r], writes=[tps_r])
                    self.cp("pool" if False else "act", yts[:, s * 128:s * 128 + ws], tps[:, s * 128:s * 128 + ws], [tps_r], [yts_r])
                nq = NMETA if qi < 0 else 512
                self.store(self.YT[2 + h, :, q0:q0 + nq], yts[:, 0:nq], yts_r)

    def stage_mla(self, l):
        self.stage_begin()
        P = self.P
        tri, tri_r, idb, idb_r = self.load_consts_attn()
        gm = self.alloc([256], F32)
        gm_r = Res("gm")
        self.load(gm[:, :], self.w["mla_out_norm"][l].partition_broadcast(128), gm_r)
        Kt = self.alloc([4, T], BF16)
        kq_r = Res("kq")
        V = self.alloc([33, 4, 65], BF16)
        v_r = Res("V")
        self.load(Kt[0:96, :, :], self.KM[:, :, :].rearrange("h r t -> r h t"), kq_r)
        Vf = V.rearrange("p i h d -> p i (h d)")
        self.load(Vf[0:NMETA, 0, :], self.VM[0:NMETA, :], v_r)
        for i0 in range(0, 32, 8):
            self.load(Vf[:, 1 + i0:9 + i0, :],
                      self.VM[NMETA + 128 * i0:NMETA + 128 * (i0 + 8), :].rearrange("(i p) d -> p i d", p=128), v_r)
        qtr = Ring([self.alloc([4, 512], BF16) for _ in range(2)], "qt")
        pt_ring = Ring([self.alloc([512], BF16) for _ in range(3)], "pt")
        ocr = Ring([self.alloc([256], F32) for _ in range(2)], "oc")
        scr = Ring([self.alloc([256], F32) for _ in range(2)], "scr")
        smr = Ring([self.alloc([8], F32) for _ in range(4)], "sm")
        ybr = Ring([self.alloc([256], BF16) for _ in range(2)], "yb")
        ytr = Ring([self.alloc([2, 512], BF16) for _ in range(2)], "yts")
        s_ring = self.psring([0, 1, 2], "s")
        tps = self.psum[7][:, :].bitcast(BF16)
        tps_r = self.psum_r[7]
        scale = 96.0 ** -0.5
        for qi in range(-1, SEQ // 512):
            nsub = 1 if qi < 0 else 4
            ws = NMETA if qi < 0 else 128
            q0 = 0 if qi < 0 else NMETA + 512 * qi
            nq = NMETA if qi < 0 else 512
            qt, qt_r = qtr.next()
            self.load(qt[0:96, :, 0:nq], self.QM[:, :, q0:q0 + nq].rearrange("h r t -> r h t"), qt_r)
            ob = {}
            for h in range(4):
                banks = [(self.psum[3 + h], self.psum_r[3 + h], 65 * s) for s in range(4)]
                ob[h] = banks
                self.attn_sweep_local(qi, Kt[:, h, :], qt[:, h, :], kq_r, qt_r, V[:, :, h, :], v_r, 65, banks, scale,
                                      s_ring, pt_ring, tri, tri_r)
            self.flush_pv()
            yts, yts_r = ytr.next()
            for s in range(nsub):
                sm, sm_r = smr.next()
                oc, oc_r = ocr.next()
                for h in range(4):
                    b, b_r, c = ob[h][s]
                    P.add("dve", lambda e, sm=sm, b=b, c=c, h=h, ws=ws: e.reciprocal(
                        out=sm[0:ws, h:h + 1], in_=b[0:ws, c + 64:c + 65]), reads=[b_r], writes=[sm_r])
                    self.act(oc[0:ws, h * 64:(h + 1) * 64], b[0:ws, c:c + 64], AF.Copy, [b_r, sm_r], [oc_r],
                             scale=sm[0:ws, h:h + 1])
                sc, sc_r = scr.next()
                self.act(sc[0:ws, :], oc[0:ws, :], AF.Square, [oc_r], [sc_r, sm_r], accum_out=sm[0:ws, 4:5])
                self.rsqrt_inplace(sm[0:ws, 4:5], sm_r, 1.0 / 256)
                yb, yb_r = ybr.next()
                self.stt(yb[0:ws, :], oc[0:ws, :], sm[0:ws, 4:5], gm[0:ws, :], ALU.mult, ALU.mult, [oc_r, sm_r, gm_r], [yb_r])
                for c in range(2):
                    P.add("pe", lambda e, yb=yb, s=s, ws=ws, c=c: e.transpose(
                        out=tps[:, c * 512 + s * 128:c * 512 + s * 128 + ws], in_=yb[0:ws, c * 128:(c + 1) * 128],
                        identity=idb[0:ws, 0:ws]), reads=[yb_r, idb_r], writes=[tps_r])
                    self.cp("act", yts[:, c, s * 128:s * 128 + ws], tps[:, c * 512 + s * 128:c * 512 + s * 128 + ws],
                            [tps_r], [yts_r])
            self.store(self.YT[6:8, :, q0:q0 + nq].rearrange("c p t -> p c t"), yts[:, :, 0:nq], yts_r)

    def attn_sweep_local(self, qi, Kt, Qtile, kq_r, qt_r, V, v_r, dvp, obanks, scale, s_ring, pt_ring, tri, tri_r):
        if qi < 0:
            nq, nsub = NMETA, 1
            chunks = [(0, 0, NMETA, 0)]
        else:
            nq, nsub = 512, 4
            chunks = [(0, 0, NMETA, -1)] + [(1 + i, NMETA + 128 * i, 128, -1) for i in range(4 * qi)]
            chunks += [(1 + 4 * qi + c, NMETA + 128 * (4 * qi + c), 128, c) for c in range(4)]
        started = set()
        for (kc, kcol, nk, diag) in chunks:
            s0 = max(diag, 0)
            qa = 128 * s0 if qi >= 0 else 0
            nqa = nq - qa
            ps, ps_r = s_ring.next()
            self.mm(ps[0:nk, 0:nqa], Kt[0:96, kcol:kcol + nk], Qtile[0:96, qa:qa + nqa], True, True, [kq_r, qt_r], [ps_r])
            pt, pt_r = pt_ring.next()
            self.act(pt[0:nk, 0:nqa], ps[0:nk, 0:nqa], AF.Exp, [ps_r], [pt_r], scale=scale)
            if diag >= 0:
                w = min(128, nqa)
                self.tt("dve", pt[0:nk, 0:w], pt[0:nk, 0:w], tri[0:nk, 0:w], ALU.mult, [pt_r, tri_r], [pt_r])
            self.flush_pv()

            def pv(s0=s0, pt=pt, pt_r=pt_r, nk=nk, kc=kc):
                for s in range(s0, nsub):
                    ob, ob_r, oc = obanks[s]
                    ws = min(128, nq)
                    lo = (s - s0) * 128
                    st = id(ob) not in started
                    started.add(id(ob))
                    self.mm(ob[0:ws, oc:oc + dvp], pt[0:nk, lo:lo + ws], V[0:nk, kc, 0:dvp], st, False,
                            [pt_r, v_r], [ob_r], skip_group_check=True)
            self._pend = pv

    def stage_oproj(self, l):
        self.stage_begin()
        Wo = self.alloc([NCH, D], BF16)
        Wo_r = Res("Wo")
        self.load_w_bf16(Wo, Wo_r, self.w["w_out"][l], D, slab=256)
        ytr = Ring([self.alloc([NCH, 512], BF16) for _ in range(2)], "yt")
        psr = self.psring([0, 1, 2, 3], "op")
        for (c0, n) in self.tiles9():
            yt, yt_r = ytr.next()
            self.load(yt[:, :, 0:n], self.YT[:, :, c0:c0 + n].rearrange("c p t -> p c t"), yt_r)
            for o in range(NCH):
                po, por = psr.next()
                for k in range(NCH):
                    self.mm(po[:, 0:n], Wo[:, k, o * 128:(o + 1) * 128], yt[:, k, 0:n], k == 0, k == NCH - 1, [Wo_r, yt_r], [por])
                hr = self.h_res(o, c0, n)
                self.stt(self.hT[:, o, c0:c0 + n], po[:, 0:n], 1.0, self.hT[:, o, c0:c0 + n], ALU.mult, ALU.add,
                         [por] + hr, hr)


    def stage_final(self):
        P = self.P
        self.stage_begin()
        gbc = self.alloc([D], F32)
        gbc_r = Res("gbc")
        P.add("sp", lambda e: e.dma_start(out=gbc[:, :], in_=self.final_norm.partition_broadcast(128)),
              writes=[gbc_r], dma=True, sem_res=gbc_r)
        otr = Ring([self.alloc([D], F32) for i in range(2)], "ot")
        sq = self.alloc([512], F32)
        sq_r = Res("fsq")
        ssr = Ring([self.alloc([4], F32) for i in range(2)], "fss")
        psr = self.psring([0, 1, 2, 3], "f")
        for i in range(SEQ // 128):
            c0 = NMETA + 128 * i
            o, o_r = otr.next()
            s, s_r = ssr.next()
            halves = []
            for half in range(2):
                pt, pr = psr.next()
                for cc in range(4):
                    c = half * 4 + cc
                    P.add("pe", lambda e, pt=pt, cc=cc, c=c, c0=c0: e.transpose(
                        out=pt[:, cc * 128:(cc + 1) * 128], in_=self.hT[:, c, c0:c0 + 128],
                        identity=self.identf[:, :]),
                        reads=self.h_res(c, c0, 128) + [self.identf_r], writes=[pr])
                halves.append((pt, pr))
            for half, (pt, pr) in enumerate(halves):
                P.add("act", lambda e, pt=pt, s=s, half=half: e.activation(
                    out=sq[:, 0:512], in_=pt[:, :], func=AF.Square, accum_out=s[:, half:half + 1]),
                    reads=[pr], writes=[sq_r, s_r])
            P.add("dve", lambda e, s=s: e.tensor_tensor(out=s[:, 2:3], in0=s[:, 0:1], in1=s[:, 1:2], op=ALU.add),
                  reads=[s_r], writes=[s_r])
            P.add("dve", lambda e, s=s: e.tensor_scalar(out=s[:, 2:3], in0=s[:, 2:3], scalar1=1.0 / D, scalar2=EPS,
                                                         op0=ALU.mult, op1=ALU.add),
                  reads=[s_r], writes=[s_r])
            P.add("act", lambda e, s=s: e.activation(out=s[:, 3:4], in_=s[:, 2:3], func=AF.Sqrt),
                  reads=[s_r], writes=[s_r])
            P.add("dve", lambda e, s=s: e.reciprocal(out=s[:, 3:4], in_=s[:, 3:4]),
                  reads=[s_r], writes=[s_r])
            for half, (pt, pr) in enumerate(halves):
                P.add("dve", lambda e, pt=pt, s=s, o=o, half=half: e.scalar_tensor_tensor(
                    out=o[:, half * 512:(half + 1) * 512], in0=pt[:, :], scalar=s[:, 3:4],
                    in1=gbc[:, half * 512:(half + 1) * 512], op0=ALU.mult, op1=ALU.mult),
                    reads=[pr, s_r, gbc_r], writes=[o_r])
            P.add("sp", lambda e, o=o, i=i: e.dma_start(out=self.out[128 * i:128 * (i + 1), :], in_=o[:, :]),
                  reads=[o_r], dma=True, sem_res=o_r)
        self.P.barrier()


def build_program(n_layers=DEPTH, debug=None):
    b = Builder(n_layers, debug)
    nc = b.build()
    return nc, b


_CONSTS = None


def consts():
    global _CONSTS
    if _CONSTS is None:
        half = 16
        inv = (np.float32(10000.0) ** (-(np.arange(half, dtype=np.float32)) / np.float32(half))).astype(np.float32)
        ang = (np.arange(T, dtype=np.float32)[None, :] * inv[:, None]).astype(np.float32).astype(np.float64)
        cs = np.zeros((96, 2, T), np.float32)
        cs[0:64, 0, :] = 1.0
        cs[64:80, 0, :] = np.cos(ang)
        cs[80:96, 0, :] = np.cos(ang)
        cs[64:80, 1, :] = -np.sin(ang)
        cs[80:96, 1, :] = np.sin(ang)
        tri = (np.arange(128)[:, None] <= np.arange(128)[None, :]).astype(np.float32)
        _CONSTS = {"ident_f": np.eye(128, dtype=np.float32), "rope_cs": cs, "tri_in": tri}
    return _CONSTS


def make_in_maps(inputs, b):
    c = consts()
    maps = []
    for core in range(8):
        m = {}
        for nm in b.inputs:
            if nm == "x":
                m[nm] = np.ascontiguousarray(inputs["x"][core])
            elif nm in c:
                m[nm] = c[nm]
            else:
                m[nm] = np.ascontiguousarray(inputs[nm])
        maps.append(m)
    return maps


def kernel(**inputs):
    inputs = {k: np.asarray(v) for k, v in inputs.items()}
    nc, b = build_program()
    maps = make_in_maps(inputs, b)
    res = run_bass_kernel_spmd(nc, maps, core_ids=list(range(8)))
    out = np.stack([np.asarray(res.results[i]["out"]) for i in range(8)], axis=0)
    return out.astype(np.float32)
```

```python
import contextlib
import math
import numpy as np
import ml_dtypes
import concourse.bass as bass
import concourse.mybir as mybir
from concourse.bass_utils import run_bass_kernel_spmd

F32 = mybir.dt.float32
BF16 = mybir.dt.bfloat16
AF = mybir.ActivationFunctionType
ALU = mybir.AluOpType
AX = mybir.AxisListType

D = 1024
SEQ = 4096
NMETA = 16
T = SEQ + NMETA
DFF = 2816
NF = DFF // 128
NCH = D // 128
DEPTH = 2
EPS = 1e-6
DIN = 2464


class Res:
    __slots__ = ("name", "w", "r", "slot")

    def __init__(self, name):
        self.name = name
        self.w = None
        self.r = []
        self.slot = None


class SemSlot:
    __slots__ = ("sem", "dcount")

    def __init__(self):
        self.sem = None
        self.dcount = 0


class Op:
    __slots__ = ("eng", "fn", "deps", "dma", "sem_res", "val", "needs_inc", "idx", "gen")

    def __init__(self, eng, fn, dma=False):
        self.eng = eng
        self.fn = fn
        self.deps = []
        self.dma = dma
        self.sem_res = None
        self.val = 0
        self.needs_inc = False


ENGS = ("pe", "act", "dve", "pool", "sp")
MAX_DMA_INFLIGHT = 4


class Prog:
    def __init__(self):
        self.q = {e: [] for e in ENGS}
        self.pool = []
        self.stage_k = 0
        self.gen = 0
        self.last_dma = {}
        self.dma_fifo = []

    def barrier(self):
        deps = []
        for e in ENGS:
            for op in reversed(self.q[e]):
                if not op.dma and op.fn is not None:
                    deps.append((op, 0))
                    break
        for op in self.last_dma.values():
            deps.append((op, op.sem_res.dcount))
        for e in ENGS:
            op = Op(e, None)
            op.gen = self.gen
            op.deps = [(o, s) for (o, s) in deps if not (o.eng == e and not o.dma)]
            for o, _ in op.deps:
                o.needs_inc = True
            self.q[e].append(op)
        self.gen += 1
        self.stage_k = 0
        self.last_dma = {}

    def add(self, eng, fn, reads=(), writes=(), dma=False, sem_res=None):
        op = Op(eng, fn, dma)
        op.gen = self.gen
        deps = {}

        def dep(o):
            if o is None or o.gen < self.gen:
                return
            if (not o.dma) and o.eng == "pe" and eng == "pe" and not dma:
                return
            snap = o.sem_res.dcount if o.dma else 0
            deps[id(o)] = (o, snap)

        for r in reads:
            dep(r.w)
        for w in writes:
            dep(w.w)
            for o in w.r:
                dep(o)
        if dma:
            if len(self.dma_fifo) >= MAX_DMA_INFLIGHT:
                old = self.dma_fifo[-MAX_DMA_INFLIGHT]
                if old.gen == self.gen and id(old) not in deps:
                    deps[id(old)] = (old, 0)
            self.dma_fifo.append(op)
        op.deps = list(deps.values())
        for o, _ in op.deps:
            o.needs_inc = True
        if dma:
            assert sem_res is not None
            if sem_res.slot is None or sem_res.slot[1] != self.gen:
                if self.stage_k >= len(self.pool):
                    self.pool.append(SemSlot())
                sem_res.slot = (self.pool[self.stage_k], self.gen)
                self.stage_k += 1
            sl = sem_res.slot[0]
            sl.dcount += 16
            op.sem_res = sl
            op.val = sl.dcount
            self.last_dma[id(sl)] = op
        for r in reads:
            if not dma:
                r.r = [o for o in r.r if o.dma or o.eng != eng]
            r.r.append(op)
        for w in writes:
            w.w = op
            w.r = []
        self.q[eng].append(op)
        return op

    def emit(self, nc, es):
        esem = {e: es.enter_context(nc.semaphore("s_" + e)) for e in ENGS}
        for i, sl in enumerate(self.pool):
            sl.sem = es.enter_context(nc.semaphore("d%d" % i))
        for e in ENGS:
            cnt = 0
            for op in self.q[e]:
                if op.dma or op.fn is None:
                    continue
                if op.needs_inc:
                    cnt += 1
                    op.val = cnt
        block = es.enter_context(nc.Block())
        engobj = {"pe": "tensor", "act": "scalar", "dve": "vector", "pool": "gpsimd", "sp": "sync"}

        def body_for(e):
            ops = self.q[e]

            def body(eng):
                seen = {}
                for op in ops:
                    for d, snap in op.deps:
                        if d.dma:
                            sem, val = d.sem_res.sem, max(d.val, snap)
                        else:
                            sem, val = esem[d.eng], d.val
                        key = id(sem)
                        if seen.get(key, 0) >= val:
                            continue
                        seen[key] = val
                        eng.wait_ge(sem, val)
                    if op.fn is None:
                        continue
                    ins = op.fn(eng)
                    if op.dma:
                        ins.then_inc(op.sem_res.sem, 16)
                    elif op.needs_inc:
                        ins.then_inc(esem[e], 1)
            return body

        for e in ENGS:
            getattr(block, engobj[e])(body_for(e))


class Ring:
    def __init__(self, aps, name):
        self.aps = aps
        self.res = [Res("%s%d" % (name, i)) for i in range(len(aps))]
        self.i = 0

    def next(self):
        k = self.i % len(self.aps)
        self.i += 1
        return self.aps[k], self.res[k]


ARENA_BYTES = 79360


def dsize(dt):
    return 4 if dt == F32 else 2


class Builder:
    def __init__(self, n_layers=DEPTH, debug=None):
        self.n_layers = n_layers
        self.debug = debug or {}
        self.nc = bass.Bass("TRN2", target_bir_lowering=False)
        self.P = Prog()
        self.es = contextlib.ExitStack()
        self.inputs = {}
        self.aoff = 0

    def din(self, name, shape, dt=F32):
        ap = self.nc.dram_tensor(name, list(shape), dt, kind="ExternalInput").ap()
        self.inputs[name] = ap
        return ap

    def dout(self, name, shape, dt=F32):
        return self.nc.dram_tensor(name, list(shape), dt, kind="ExternalOutput").ap()

    def dscratch(self, name, shape, dt):
        return self.nc.dram_tensor(name, list(shape), dt, kind="Internal").ap()

    def sb(self, name, shape, dt):
        return self.es.enter_context(self.nc.sbuf_tensor(name, list(shape), dt))

    def ps(self, name, shape, dt=F32):
        return self.es.enter_context(self.nc.psum_tensor(name, list(shape), dt))

    def stage_begin(self):
        self.P.barrier()
        self.aoff = 0

    def alloc(self, shape, dt):
        n = int(np.prod(shape))
        nbytes = (n * dsize(dt) + 31) // 32 * 32
        assert self.aoff + nbytes <= ARENA_BYTES, (self.aoff, nbytes)
        ap = self.arena[:, self.aoff // 4:(self.aoff + nbytes) // 4]
        self.aoff += nbytes
        if dt != F32:
            ap = ap.bitcast(dt)
        ap = ap[:, 0:n]
        if len(shape) == 2:
            ap = ap.rearrange("p (a b) -> p a b", a=shape[0])
        elif len(shape) == 3:
            ap = ap.rearrange("p (a b c) -> p a b c", a=shape[0], b=shape[1])
        return ap


    def act(self, out, in_, func, reads, writes, **kw):
        return self.P.add("act", lambda e: e.activation(out=out, in_=in_, func=func, **kw), reads=reads, writes=writes)

    def tt(self, eng, out, in0, in1, op, reads, writes):
        return self.P.add(eng, lambda e: e.tensor_tensor(out=out, in0=in0, in1=in1, op=op), reads=reads, writes=writes)

    def ts(self, eng, out, in0, s1, s2, op0, op1, reads, writes):
        if s2 is None:
            return self.P.add(eng, lambda e: e.tensor_scalar(out=out, in0=in0, scalar1=s1, scalar2=None, op0=op0),
                              reads=reads, writes=writes)
        return self.P.add(eng, lambda e: e.tensor_scalar(out=out, in0=in0, scalar1=s1, scalar2=s2, op0=op0, op1=op1),
                          reads=reads, writes=writes)

    def stt(self, out, in0, scalar, in1, op0, op1, reads, writes):
        return self.P.add("dve", lambda e: e.scalar_tensor_tensor(out=out, in0=in0, scalar=scalar, in1=in1, op0=op0, op1=op1),
                          reads=reads, writes=writes)

    def cp(self, eng, out, in_, reads, writes):
        if eng == "act":
            return self.act(out, in_, AF.Copy, reads, writes)
        return self.P.add(eng, lambda e: e.tensor_copy(out=out, in_=in_), reads=reads, writes=writes)

    def mm(self, out, lhsT, rhs, start, stop, reads, writes, **kw):
        return self.P.add("pe", lambda e: e.matmul(out, lhsT=lhsT, rhs=rhs, start=start, stop=stop, **kw),
                          reads=reads, writes=writes)

    def load(self, out, in_, res, reads=()):
        return self.P.add("sp", lambda e: e.dma_start(out=out, in_=in_), reads=list(reads), writes=[res], dma=True, sem_res=res)

    def store(self, out, in_, res):
        return self.P.add("sp", lambda e: e.dma_start(out=out, in_=in_), reads=[res], dma=True, sem_res=res)

    def rsqrt_inplace(self, ap, res, scale, in_ap=None, in_reads=()):
        src = ap if in_ap is None else in_ap
        self.ts("dve", ap, src, scale, EPS, ALU.mult, ALU.add, [res] + list(in_reads), [res])
        self.act(ap, ap, AF.Sqrt, [res], [res])
        self.P.add("dve", lambda e: e.reciprocal(out=ap, in_=ap), reads=[res], writes=[res])

    def h_res(self, c, c0, n):
        out = []
        if c0 < NMETA:
            out.append(self.hres[c][0])
        lo = max(c0, NMETA) - NMETA
        hi = c0 + n - NMETA
        if hi > lo:
            for i in range(lo // 128, (hi - 1) // 128 + 1):
                out.append(self.hres[c][1 + i])
        return out

    def build(self):
        nc, P = self.nc, self.P
        self.x = self.din("x", [SEQ, D])
        self.meta = self.din("meta_tokens", [NMETA, D])
        self.final_norm = self.din("final_norm", [D])
        self.ident_f = self.din("ident_f", [128, 128])
        L = DEPTH
        self.w = {}
        for nm, shp in [("ffn1_norm", [L, D]), ("ffn1_in", [L, D, 2 * DFF]), ("ffn1_out", [L, DFF, D]),
                        ("ffn2_norm", [L, D]), ("ffn2_in", [L, D, 2 * DFF]), ("ffn2_out", [L, DFF, D])]:
            self.w[nm] = self.din(nm, shp)
        for nm, shp in [("mix_norm", [L, D]), ("w_in", [L, D, DIN]), ("conv_w", [L, 4, 256]), ("conv_b", [L, 256]),
                        ("lru_wa", [L, 4, 64, 64]), ("lru_ba", [L, 4, 64]), ("lru_wx", [L, 4, 64, 64]), ("lru_bx", [L, 4, 64]),
                        ("lru_lambda", [L, 256]), ("lru_out_norm", [L, 256]), ("lam_q1", [L, 64]), ("lam_k1", [L, 64]),
                        ("lam_q2", [L, 64]), ("lam_k2", [L, 64]), ("diff_subln", [L, 128]), ("q_norm", [L, 256]),
                        ("w_uq", [L, 256, 384]), ("kv_norm", [L, 128]), ("w_ukv", [L, 128, 512]),
                        ("mla_out_norm", [L, 256]), ("w_out", [L, D, D])]:
            self.w[nm] = self.din(nm, shp)
        self.rope_cs = self.din("rope_cs", [96, 2, T])
        self.tri_in = self.din("tri_in", [128, 128])
        self.out = self.dout("out", [SEQ, D])
        self.LRU_IN = self.dscratch("lru_in", [4, 128, T], F32)
        self.QKD = self.dscratch("qkd", [8, 128, T], BF16)
        self.VD = self.dscratch("vd", [T, 512], BF16)
        self.QM = self.dscratch("qm", [4, 96, T], BF16)
        self.KM = self.dscratch("km", [4, 96, T], BF16)
        self.VM = self.dscratch("vm", [T, 260], BF16)
        self.YT = self.dscratch("yt", [8, 128, T], BF16)
        self.wf = {}
        for l in range(L):
            for which in (1, 2):
                self.wf[(l, which)] = self.dscratch("wf_%d_%d" % (l, which), [NF, 128, 3072], BF16)
        self.wf_r = {k: Res("wf%d%d" % k) for k in self.wf}

        self.hT = self.sb("hT", [128, NCH, T], F32)
        self.hres = [[Res("h%d_%d" % (c, i)) for i in range(1 + SEQ // 128)] for c in range(NCH)]
        self.arena = self.sb("arena", [128, ARENA_BYTES // 4], F32)
        self.identf = self.sb("identf", [128, 128], F32)
        self.identf_r = Res("identf")
        self.ones_bf = self.sb("ones_bf", [128, 128], BF16)
        self.ones_r = Res("ones")
        self.gains = self.sb("gains", [128, 8, NCH], F32)
        self.gains_r = Res("gains")

        P.add("sp", lambda e: e.dma_start(out=self.identf[:], in_=self.ident_f[:, :]),
              writes=[self.identf_r], dma=True, sem_res=self.identf_r)
        P.add("pool", lambda e: e.memset(self.ones_bf[:], 1.0), writes=[self.ones_r])
        self.gain_idx = {}
        gi = 0
        for l in range(L):
            for nm in ("ffn1_norm", "ffn2_norm", "mix_norm"):
                src = self.w[nm][l, :].rearrange("(c p) -> p c", p=128)
                P.add("sp", lambda e, k=gi, src=src: e.dma_start(out=self.gains[:, k, :], in_=src),
                      writes=[self.gains_r], dma=True, sem_res=self.gains_r)
                self.gain_idx[(nm, l)] = gi
                gi += 1
        src = self.final_norm.rearrange("(c p) -> p c", p=128)
        P.add("sp", lambda e, k=gi, src=src: e.dma_start(out=self.gains[:, k, :], in_=src),
              writes=[self.gains_r], dma=True, sem_res=self.gains_r)
        self.gain_idx["final"] = gi

        self.psum = [self.ps("ps%d" % i, [128, 512], F32) for i in range(8)]
        self.psum_r = [Res("ps%d" % i) for i in range(8)]

        self.stage_load_x()
        ffns = [(l, w) for l in range(self.n_layers) for w in (1, 2) if self.debug.get("ffn%d" % w, True)]
        self.conv_queue = []
        for i, (l, w) in enumerate(ffns):
            if i == 0 or not self.debug.get("mixer", True) or not self.debug.get("diff", True):
                self.stage_convert_ffn(l, w)
            else:
                self.conv_queue += [(l, w, f) for f in range(NF)]
        for l in range(self.n_layers):
            if self.debug.get("ffn1", True):
                self.stage_ffn(l, 1)
            if self.debug.get("mixer", True):
                if self.debug.get("proj1", True):
                    self.stage_proj1(l)
                if self.debug.get("proj2", True):
                    self.stage_proj2(l)
                if self.debug.get("lru", True):
                    self.stage_lru(l)
                if self.debug.get("diff", True):
                    self.stage_diff(l)
                if self.debug.get("mla", True):
                    self.stage_mla(l)
                if self.debug.get("oproj", True):
                    self.stage_oproj(l)
            if self.debug.get("ffn2", True):
                self.stage_ffn(l, 2)
        self.stage_final()
        with nc.allow_non_contiguous_dma(reason="small strided parameter loads"):
            P.emit(nc, self.es)
        return nc

    def psring(self, idxs, name):
        r = Ring([self.psum[i] for i in idxs], name)
        r.res = [self.psum_r[i] for i in idxs]
        return r

    def stage_load_x(self):
        P = self.P
        self.stage_begin()
        ring = Ring([self.alloc([D], F32) for i in range(3)], "xin")
        psr = self.psring([0, 1, 2, 3], "x")
        n = 0
        for i in range(-1, SEQ // 128):
            buf, br = ring.next()
            if i < 0:
                rows, c0 = NMETA, 0
                src = self.meta[:, :]
            else:
                rows, c0 = 128, NMETA + 128 * i
                src = self.x[128 * i:128 * (i + 1), :]
            P.add("sp", lambda e, buf=buf, rows=rows, src=src: e.dma_start(out=buf[0:rows, :], in_=src),
                  writes=[br], dma=True, sem_res=br)
            for half in range(2):
                pt, pr = psr.next()
                for cc in range(4):
                    c = half * 4 + cc
                    P.add("pe", lambda e, pt=pt, cc=cc, buf=buf, rows=rows, c=c: e.transpose(
                        out=pt[:, cc * 128:cc * 128 + rows], in_=buf[0:rows, c * 128:(c + 1) * 128],
                        identity=self.identf[0:rows, 0:rows]),
                        reads=[br, self.identf_r], writes=[pr])
                eng = "act" if (n % 2 == 0) else "dve"
                n += 1
                hr = []
                for cc in range(4):
                    hr += self.h_res(half * 4 + cc, c0, rows)
                dst = self.hT[:, half * 4:half * 4 + 4, c0:c0 + rows]
                srcp = pt[:, :].rearrange("p (c t) -> p c t", c=4)[:, :, 0:rows]
                if eng == "act":
                    P.add("act", lambda e, dst=dst, srcp=srcp: e.activation(out=dst, in_=srcp, func=AF.Copy),
                          reads=[pr], writes=hr)
                else:
                    P.add("dve", lambda e, dst=dst, srcp=srcp: e.tensor_copy(out=dst, in_=srcp),
                          reads=[pr], writes=hr)

    def conv_bufs(self):
        return dict(a=self.alloc([2, NCH, 128], F32), ar=Res("cva"), ao=self.alloc([D], F32), aor=Res("cvo"),
                    b=self.alloc([3072], BF16), br=Res("cvb"))

    def conv_job(self, job, cb, eng="pool"):
        l, which, f = job
        w_in = self.w["ffn%d_in" % which][l]
        w_out = self.w["ffn%d_out" % which][l]
        WF = self.wf[(l, which)]
        a, ar, ao, aor, b, br = cb["a"], cb["ar"], cb["ao"], cb["aor"], cb["b"], cb["br"]
        for gu in range(2):
            col = gu * DFF + f * 128
            self.load(a[:, gu, :, :], w_in[:, col:col + 128].rearrange("(k p) j -> p k j", p=128), ar)
        self.load(ao[:, :], w_out[f * 128:(f + 1) * 128, :], aor)
        for gu in range(2):
            out = b[:, 0:2048].rearrange("p (k j) -> p k j", k=NCH)[:, :, gu * 128:(gu + 1) * 128]
            self.cp(eng, out, a[:, gu, :, :], [ar], [br])
        self.cp(eng, b[:, 2048:3072], ao[:, :], [aor], [br])
        self.store(WF[f, :, :], b[:, :], br)

    def stage_convert_ffn(self, l, which):
        P = self.P
        self.stage_begin()
        w_in = self.w["ffn%d_in" % which][l]
        w_out = self.w["ffn%d_out" % which][l]
        WF = self.wf[(l, which)]
        wfr = self.wf_r[(l, which)]
        s32 = Ring([self.alloc([2, NCH, 256], F32) for _ in range(2)], "cv32")
        s32o = Ring([self.alloc([2, D], F32) for _ in range(2)], "cv32o")
        s16 = Ring([self.alloc([2, 3072], BF16) for _ in range(2)], "cv16")
        engs = ["act", "dve", "pool"]
        n = 0

        def cast(eng, out, in_):
            if eng == "act":
                return lambda e: e.activation(out=out, in_=in_, func=AF.Copy)
            return lambda e: e.tensor_copy(out=out, in_=in_)

        for fp in range(NF // 2):
            f0 = 2 * fp
            a, ar = s32.next()
            ao, aor = s32o.next()
            b, br = s16.next()
            for gu in range(2):
                col = gu * DFF + f0 * 128
                src = w_in[:, col:col + 256].rearrange("(k p) j -> p k j", p=128)
                P.add("sp", lambda e, a=a, gu=gu, src=src: e.dma_start(out=a[:, gu, :, :], in_=src),
                      writes=[ar], dma=True, sem_res=ar)
            src = w_out[f0 * 128:(f0 + 2) * 128, :].rearrange("(f p) c -> p f c", p=128)
            P.add("sp", lambda e, ao=ao, src=src: e.dma_start(out=ao[:, :, :], in_=src),
                  writes=[aor], dma=True, sem_res=aor)
            for ff in range(2):
                for gu in range(2):
                    eng = engs[n % 3]
                    n += 1
                    out = b[:, ff, 0:2048].rearrange("p (k j) -> p k j", k=NCH)[:, :, gu * 128:(gu + 1) * 128]
                    in_ = a[:, gu, :, ff * 128:(ff + 1) * 128]
                    P.add(eng, cast(eng, out, in_), reads=[ar], writes=[br])
            eng = engs[n % 3]
            n += 1
            P.add(eng, cast(eng, b[:, :, 2048:3072], ao[:, :, :]), reads=[aor], writes=[br])
            dst = WF[f0:f0 + 2, :, :].rearrange("f p c -> p f c")
            P.add("sp", lambda e, b=b, dst=dst: e.dma_start(out=dst, in_=b[:, :, :]),
                  reads=[br], writes=[wfr], dma=True, sem_res=br)

    def stage_ffn(self, l, which):
        P = self.P
        self.stage_begin()
        WF = self.wf[(l, which)]
        wfr = self.wf_r[(l, which)]
        gk = self.gain_idx[("ffn%d_norm" % which, l)]
        xn = self.alloc([NCH, 1040], BF16)
        hid = Ring([self.alloc([1040], BF16) for _ in range(4)], "hid")
        wsl = Ring([self.alloc([3072], BF16) for _ in range(6)], "wsl")
        sqr = Ring([self.alloc([512], BF16) for _ in range(2)], "sq")
        rstd = self.alloc([1040], F32)
        rstd_r = Res("rstd")
        sgr = Ring([self.alloc([512], F32) for _ in range(2)], "sg")
        ps_gu = self.psring([0, 1, 2, 3], "gu")
        ps_o = self.psring([4, 5, 6], "o")
        ps_n, ps_nr = self.psum[7], self.psum_r[7]

        sts = [[(0, 16), (16, 512), (528, 512)]]
        for s in range(1, 4):
            b0 = 1040 + 1024 * (s - 1)
            sts.append([(b0, 512), (b0 + 512, 512)])
        nsq = 0
        for subs in sts:
            base = subs[0][0]
            xn_r = [Res("xn%d" % i) for i in range(len(subs))]
            for si, (c0, n) in enumerate(subs):
                lc = c0 - base
                for c in range(NCH):
                    sq, sq_r = sqr.next()
                    src = self.hT[:, c, c0:c0 + n]
                    if nsq % 2 == 0:
                        P.add("act", lambda e, sq=sq, src=src, n=n: e.activation(out=sq[:, 0:n], in_=src, func=AF.Square),
                              reads=self.h_res(c, c0, n), writes=[sq_r])
                    else:
                        P.add("pool", lambda e, sq=sq, src=src, n=n: e.tensor_tensor(out=sq[:, 0:n], in0=src, in1=src, op=ALU.mult),
                              reads=self.h_res(c, c0, n), writes=[sq_r])
                    nsq += 1
                    P.add("pe", lambda e, sq=sq, n=n, c=c: e.matmul(ps_n[:, 0:n], lhsT=self.ones_bf[:, :], rhs=sq[:, 0:n],
                                                                     start=(c == 0), stop=(c == NCH - 1)),
                          reads=[sq_r, self.ones_r], writes=[ps_nr])
                rs = rstd[:, lc:lc + n]
                P.add("dve", lambda e, rs=rs, n=n: e.tensor_scalar(out=rs, in0=ps_n[:, 0:n], scalar1=1.0 / D, scalar2=EPS,
                                                                   op0=ALU.mult, op1=ALU.add),
                      reads=[ps_nr], writes=[rstd_r])
                P.add("act", lambda e, rs=rs: e.activation(out=rs, in_=rs, func=AF.Sqrt), reads=[rstd_r], writes=[rstd_r])
                P.add("dve", lambda e, rs=rs: e.reciprocal(out=rs, in_=rs), reads=[rstd_r], writes=[rstd_r])
                for c in range(NCH):
                    P.add("dve", lambda e, c=c, c0=c0, n=n, lc=lc, rs=rs: e.scalar_tensor_tensor(
                        out=xn[:, c, lc:lc + n], in0=self.hT[:, c, c0:c0 + n], scalar=self.gains[:, gk, c:c + 1],
                        in1=rs, op0=ALU.mult, op1=ALU.mult),
                        reads=self.h_res(c, c0, n) + [rstd_r, self.gains_r], writes=[xn_r[si]])

            def GU(f):
                slot, slot_r = wsl.next()
                P.add("sp", lambda e, slot=slot, f=f: e.dma_start(out=slot[:, :], in_=WF[f, :, :]),
                      reads=[wfr], writes=[slot_r], dma=True, sem_res=slot_r)
                hb, hb_r = hid.next()
                for si, (c0, n) in enumerate(subs):
                    lc = c0 - base
                    pg, pgr = ps_gu.next()
                    pu, pur = ps_gu.next()
                    for gu, (pp, ppr) in enumerate(((pg, pgr), (pu, pur))):
                        for k in range(NCH):
                            P.add("pe", lambda e, pp=pp, slot=slot, k=k, gu=gu, lc=lc, n=n: e.matmul(
                                pp[:, 0:n], lhsT=slot[:, k * 256 + gu * 128:k * 256 + gu * 128 + 128],
                                rhs=xn[:, k, lc:lc + n], start=(k == 0), stop=(k == NCH - 1)),
                                reads=[slot_r, xn_r[si]], writes=[ppr])
                    sg, sg_r = sgr.next()
                    P.add("act", lambda e, sg=sg, pg=pg, n=n: e.activation(out=sg[:, 0:n], in_=pg[:, 0:n], func=AF.Silu),
                          reads=[pgr], writes=[sg_r])
                    P.add("dve", lambda e, sg=sg, pu=pu, hb=hb, lc=lc, n=n: e.tensor_tensor(
                        out=hb[:, lc:lc + n], in0=sg[:, 0:n], in1=pu[:, 0:n], op=ALU.mult),
                        reads=[sg_r, pur], writes=[hb_r])
                return (slot, slot_r, hb, hb_r)

            def W2(group):
                for o in range(NCH):
                    for si, (c0, n) in enumerate(subs):
                        lc = c0 - base
                        po, por = ps_o.next()
                        for gi_, (slot, slot_r, hb, hb_r) in enumerate(group):
                            P.add("pe", lambda e, po=po, slot=slot, hb=hb, o=o, lc=lc, n=n, gi_=gi_, ng=len(group): e.matmul(
                                po[:, 0:n], lhsT=slot[:, 2048 + o * 128:2048 + (o + 1) * 128], rhs=hb[:, lc:lc + n],
                                start=(gi_ == 0), stop=(gi_ == ng - 1)),
                                reads=[slot_r, hb_r], writes=[por])
                        hr = self.h_res(o, c0, n)
                        P.add("dve", lambda e, po=po, o=o, c0=c0, n=n: e.scalar_tensor_tensor(
                            out=self.hT[:, o, c0:c0 + n], in0=po[:, 0:n], scalar=0.5, in1=self.hT[:, o, c0:c0 + n],
                            op0=ALU.mult, op1=ALU.add),
                            reads=[por] + hr, writes=hr)

            prev = None
            for g in range(NF // 2):
                grp = [GU(2 * g), GU(2 * g + 1)]
                if prev is not None:
                    W2(prev)
                prev = grp
            W2(prev)

    def tiles9(self):
        return [(0, NMETA)] + [(NMETA + 512 * j, 512) for j in range(SEQ // 512)]

    def norm_tile(self, c0, n, gk, xn, xn_r, sqr, rstd, rstd_r, ps_n, ps_nr, cnt=[0]):
        P = self.P
        for c in range(NCH):
            sq, sq_r = sqr.next()
            src = self.hT[:, c, c0:c0 + n]
            if cnt[0] % 2 == 0:
                self.act(sq[:, 0:n], src, AF.Square, self.h_res(c, c0, n), [sq_r])
            else:
                self.tt("pool", sq[:, 0:n], src, src, ALU.mult, self.h_res(c, c0, n), [sq_r])
            cnt[0] += 1
            self.mm(ps_n[:, 0:n], self.ones_bf[:, :], sq[:, 0:n], c == 0, c == NCH - 1, [sq_r, self.ones_r], [ps_nr])
        rs = rstd[:, 0:n]
        self.rsqrt_inplace(rs, rstd_r, 1.0 / D, in_ap=ps_n[:, 0:n], in_reads=[ps_nr])
        for c in range(NCH):
            self.stt(xn[:, c, 0:n], self.hT[:, c, c0:c0 + n], self.gains[:, gk, c:c + 1], rs, ALU.mult, ALU.mult,
                     self.h_res(c, c0, n) + [rstd_r, self.gains_r], [xn_r])

    def load_w_bf16(self, dst, dst_r, src, ncols, slab=160):
        st = Ring([self.alloc([src.shape[0] // 128, slab], F32) for _ in range(2)], "wslab")
        engs = ["act", "dve", "pool"]
        i = 0
        for c in range(0, ncols, slab):
            w = min(slab, ncols - c)
            a, ar = st.next()
            self.load(a[:, :, 0:w], src[:, c:c + w].rearrange("(k p) j -> p k j", p=128), ar)
            self.cp(engs[i % 3], dst[:, :, c:c + w], a[:, :, 0:w], [ar], [dst_r])
            i += 1

    def stage_proj1(self, l):
        self.stage_begin()
        gk = self.gain_idx[("mix_norm", l)]
        w_in = self.w["w_in"][l]
        Wp = self.alloc([NCH, 2048], BF16)
        Wp_r = Res("Wp")
        self.load_w_bf16(Wp, Wp_r, w_in[:, 0:2048], 2048)
        xnr = Ring([self.alloc([NCH, 512], BF16) for _ in range(2)], "xn")
        sqr = Ring([self.alloc([512], BF16) for _ in range(2)], "sq")
        rstr = Ring([self.alloc([512], F32) for _ in range(2)], "rstd")
        st32 = Ring([self.alloc([512], F32) for _ in range(3)], "st32")
        st16 = Ring([self.alloc([512], BF16) for _ in range(4)], "st16")
        psr = self.psring([0, 1, 2, 3, 4, 5], "p1")
        ps_n, ps_nr = self.psum[7], self.psum_r[7]
        ne = 0
        for (c0, n) in self.tiles9():
            xn, xn_r = xnr.next()
            rstd, rstd_r = rstr.next()
            self.norm_tile(c0, n, gk, xn, xn_r, sqr, rstd, rstd_r, ps_n, ps_nr)
            for g in range(12):
                pt, pr = psr.next()
                for k in range(NCH):
                    self.mm(pt[:, 0:n], Wp[:, k, g * 128:(g + 1) * 128], xn[:, k, 0:n], k == 0, k == NCH - 1,
                            [Wp_r, xn_r], [pr])
                eng = "act" if ne % 2 == 0 else "dve"
                ne += 1
                if g < 4:
                    sb, sr = st32.next()
                    self.cp(eng, sb[:, 0:n], pt[:, 0:n], [pr], [sr])
                    self.store(self.LRU_IN[g, :, c0:c0 + n], sb[:, 0:n], sr)
                else:
                    sb, sr = st16.next()
                    self.cp(eng, sb[:, 0:n], pt[:, 0:n], [pr], [sr])
                    self.store(self.QKD[g - 4, :, c0:c0 + n], sb[:, 0:n], sr)
            for j in range(0, n, 128):
                m = min(128, n - j)
                pt, pr = psr.next()
                for k in range(NCH):
                    self.mm(pt[0:m, 0:512], xn[:, k, j:j + m], Wp[:, k, 1536:2048], k == 0, k == NCH - 1,
                            [Wp_r, xn_r], [pr])
                sb, sr = st16.next()
                eng = "act" if ne % 2 == 0 else "dve"
                ne += 1
                self.cp(eng, sb[0:m, 0:512], pt[0:m, 0:512], [pr], [sr])
                self.store(self.VD[c0 + j:c0 + j + m, :], sb[0:m, 0:512], sr)

    def stage_proj2(self, l):
        self.stage_begin()
        gk = self.gain_idx[("mix_norm", l)]
        w_in = self.w["w_in"][l]
        Wp = self.alloc([NCH, 512], BF16)
        Wp_r = Res("Wp2")
        self.P.add("pool", lambda e: e.memset(Wp[:, :, 416:480], 0.0), writes=[Wp_r])
        self.load_w_bf16(Wp, Wp_r, w_in[:, 2048:2464], 416, slab=104)
        self.cp("dve", Wp[:, :, 480:496], Wp[:, :, 400:416], [Wp_r], [Wp_r])
        self.cp("dve", Wp[:, :, 496:512], Wp[:, :, 384:400], [Wp_r], [Wp_r])
        Wq = self.alloc([2, 768], BF16)
        Wq_r = Res("Wq")
        self.load_w_bf16(Wq, Wq_r, self.w["w_uq"][l], 384, slab=192)
        for h in range(4):
            b = 384 + h * 96
            a = h * 96
            self.cp("dve", Wq[:, :, b:b + 64], Wq[:, :, a:a + 64], [Wq_r], [Wq_r])
            self.cp("dve", Wq[:, :, b + 64:b + 80], Wq[:, :, a + 80:a + 96], [Wq_r], [Wq_r])
            self.cp("dve", Wq[:, :, b + 80:b + 96], Wq[:, :, a + 64:a + 80], [Wq_r], [Wq_r])
        Wkv = self.alloc([512], BF16)
        Wkv_r = Res("Wkv")
        kv32 = self.alloc([512], F32)
        kv32_r = Res("kv32")
        self.load(kv32[:, :], self.w["w_ukv"][l][:, :], kv32_r)
        self.cp("dve", Wkv[:, :].rearrange("p (t h d) -> p h t d", t=2, h=4),
                kv32[:, :].rearrange("p (h t d) -> p h t d", h=4, t=2), [kv32_r], [Wkv_r])
        gq = self.alloc([2], F32)
        gkv = self.alloc([1], F32)
        g_r = Res("gqkv")
        self.load(gq[:, :], self.w["q_norm"][l].rearrange("(c p) -> p c", p=128), g_r)
        self.load(gkv[:, :], self.w["kv_norm"][l].rearrange("(c p) -> p c", p=128), g_r)

        xnr = Ring([self.alloc([NCH, 512], BF16) for _ in range(1)], "xn")
        sqr = Ring([self.alloc([512], BF16) for _ in range(2)], "sq")
        rstr = Ring([self.alloc([512], F32) for _ in range(2)], "rstd")
        cq = self.alloc([2, 512], F32)
        cq_r = Res("cq")
        ckv = self.alloc([512], F32)
        ckv_r = Res("ckv")
        cqn = self.alloc([2, 512], BF16)
        cqn_r = Res("cqn")
        ckvn = self.alloc([512], BF16)
        ckvn_r = Res("ckvn")
        rsq = self.alloc([512], F32)
        rsq_r = Res("rsq")
        rskv = self.alloc([512], F32)
        rskv_r = Res("rskv")
        cs = self.alloc([2, 512], F32)
        cs_r = Res("cs")
        t1r = Ring([self.alloc([512], F32) for _ in range(2)], "t1")
        t2r = Ring([self.alloc([512], F32) for _ in range(2)], "t2")
        qst = Ring([self.alloc([4, 512], BF16) for _ in range(1)], "qst")
        kst = Ring([self.alloc([4, 512], BF16) for _ in range(1)], "kst")
        vst = Ring([self.alloc([4, 65], BF16) for _ in range(2)], "vst")
        psr = self.psring([0, 1, 2, 3, 4, 5], "p2")
        ps_n, ps_nr = self.psum[7], self.psum_r[7]
        ps_m, ps_mr = self.psum[6], self.psum_r[6]
        for (c0, n) in self.tiles9():
            xn, xn_r = xnr.next()
            rstd, rstd_r = rstr.next()
            self.norm_tile(c0, n, gk, xn, xn_r, sqr, rstd, rstd_r, ps_n, ps_nr)
            self.load(cs[0:96, :, 0:n], self.rope_cs[:, :, c0:c0 + n], cs_r)
            for j in range(3):
                pt, pr = psr.next()
                for k in range(NCH):
                    self.mm(pt[:, 0:n], Wp[:, k, j * 128:(j + 1) * 128], xn[:, k, 0:n], k == 0, k == NCH - 1,
                            [Wp_r, xn_r], [pr])
                if j < 2:
                    self.cp("act", cq[:, j, 0:n], pt[:, 0:n], [pr], [cq_r])
                else:
                    self.cp("act", ckv[:, 0:n], pt[:, 0:n], [pr], [ckv_r])
            for j in range(2):
                sq, sq_r = sqr.next()
                self.tt("pool", sq[:, 0:n], cq[:, j, 0:n], cq[:, j, 0:n], ALU.mult, [cq_r], [sq_r])
                self.mm(ps_m[:, 0:n], self.ones_bf[:, :], sq[:, 0:n], j == 0, j == 1, [sq_r, self.ones_r], [ps_mr])
            self.rsqrt_inplace(rsq[:, 0:n], rsq_r, 1.0 / 256, in_ap=ps_m[:, 0:n], in_reads=[ps_mr])
            for j in range(2):
                self.stt(cqn[:, j, 0:n], cq[:, j, 0:n], gq[:, j:j + 1], rsq[:, 0:n], ALU.mult, ALU.mult,
                         [cq_r, g_r, rsq_r], [cqn_r])
            sq, sq_r = sqr.next()
            self.tt("pool", sq[:, 0:n], ckv[:, 0:n], ckv[:, 0:n], ALU.mult, [ckv_r], [sq_r])
            self.mm(ps_m[:, 0:n], self.ones_bf[:, :], sq[:, 0:n], True, True, [sq_r, self.ones_r], [ps_mr])
            self.rsqrt_inplace(rskv[:, 0:n], rskv_r, 1.0 / 128, in_ap=ps_m[:, 0:n], in_reads=[ps_mr])
            self.stt(ckvn[:, 0:n], ckv[:, 0:n], gkv[:, 0:1], rskv[:, 0:n], ALU.mult, ALU.mult,
                     [ckv_r, g_r, rskv_r], [ckvn_r])
            qs, qs_r = qst.next()
            for h in range(4):
                pa, par = psr.next()
                pb, pbr = psr.next()
                for k in range(2):
                    self.mm(pa[0:96, 0:n], Wq[:, k, h * 96:(h + 1) * 96], cqn[:, k, 0:n], k == 0, k == 1, [Wq_r, cqn_r], [par])
                for k in range(2):
                    self.mm(pb[0:96, 0:n], Wq[:, k, 384 + h * 96:384 + (h + 1) * 96], cqn[:, k, 0:n], k == 0, k == 1,
                            [Wq_r, cqn_r], [pbr])
                t1, t1_r = t1r.next()
                t2, t2_r = t2r.next()
                self.tt("dve", t1[0:96, 0:n], pa[0:96, 0:n], cs[0:96, 0, 0:n], ALU.mult, [par, cs_r], [t1_r])
                self.tt("dve", t2[0:96, 0:n], pb[0:96, 0:n], cs[0:96, 1, 0:n], ALU.mult, [pbr, cs_r], [t2_r])
                self.tt("pool", qs[0:96, h, 0:n], t1[0:96, 0:n], t2[0:96, 0:n], ALU.add, [t1_r, t2_r], [qs_r])
            self.store(self.QM[:, :, c0:c0 + n].rearrange("h r t -> r h t"), qs[0:96, :, 0:n], qs_r)
            ks, ks_r = kst.next()
            for h in range(4):
                pt, pr = psr.next()
                self.mm(pt[0:64, 0:n], Wkv[:, h * 64:(h + 1) * 64], ckvn[:, 0:n], True, True, [Wkv_r, ckvn_r], [pr])
                self.cp("act", ks[0:64, h, 0:n], pt[0:64, 0:n], [pr], [ks_r])
            pa, par = psr.next()
            pb, pbr = psr.next()
            for k in range(NCH):
                self.mm(pa[0:96, 0:n], Wp[:, k, 320:416], xn[:, k, 0:n], k == 0, k == NCH - 1, [Wp_r, xn_r], [par])
            for k in range(NCH):
                self.mm(pb[0:96, 0:n], Wp[:, k, 416:512], xn[:, k, 0:n], k == 0, k == NCH - 1, [Wp_r, xn_r], [pbr])
            t1, t1_r = t1r.next()
            t2, t2_r = t2r.next()
            self.tt("dve", t1[64:96, 0:n], pa[64:96, 0:n], cs[64:96, 0, 0:n], ALU.mult, [par, cs_r], [t1_r])
            self.tt("dve", t2[64:96, 0:n], pb[64:96, 0:n], cs[64:96, 1, 0:n], ALU.mult, [pbr, cs_r], [t2_r])
            self.tt("pool", t1[64:96, 0:n], t1[64:96, 0:n], t2[64:96, 0:n], ALU.add, [t1_r, t2_r], [t1_r])
            for h in range(4):
                self.cp("pool" if h % 2 else "act", ks[64:96, h, 0:n], t1[64:96, 0:n], [t1_r], [ks_r])
            self.store(self.KM[:, :, c0:c0 + n].rearrange("h r t -> r h t"), ks[0:96, :, 0:n], ks_r)
            for j in range(0, n, 128):
                m = min(128, n - j)
                pt, pr = psr.next()
                self.mm(pt[0:m, 0:256], ckvn[:, j:j + m], Wkv[:, 256:512], True, True, [Wkv_r, ckvn_r], [pr])
                vs, vs_r = vst.next()
                self.P.add("pool", lambda e, vs=vs: e.memset(vs[:, :, 64:65], 1.0), writes=[vs_r])
                self.cp("act", vs[0:m, :, 0:64], pt[0:m, 0:256].rearrange("p (h d) -> p h d", h=4), [pr], [vs_r])
                self.store(self.VM[c0 + j:c0 + j + m, :], vs[0:m, :, :].rearrange("p h d -> p (h d)"), vs_r)

    def stage_lru(self, l):
        self.stage_begin()
        P = self.P
        pr_ = Res("lrup")
        convw = self.alloc([2, 4], F32)
        convb = self.alloc([2], F32)
        ba = self.alloc([2], F32)
        bx = self.alloc([2], F32)
        lam = self.alloc([2], F32)
        gl = self.alloc([2], F32)
        for c in range(2):
            for j in range(4):
                self.load(convw[:, c, j:j + 1], self.w["conv_w"][l][j, c * 128:(c + 1) * 128].rearrange("(p o) -> p o", o=1), pr_)
        self.load(convb[:, :], self.w["conv_b"][l].rearrange("(c p) -> p c", p=128), pr_)
        self.load(ba[:, :], self.w["lru_ba"][l].rearrange("n d -> (n d)").rearrange("(c p) -> p c", p=128), pr_)
        self.load(bx[:, :], self.w["lru_bx"][l].rearrange("n d -> (n d)").rearrange("(c p) -> p c", p=128), pr_)
        self.load(lam[:, :], self.w["lru_lambda"][l].rearrange("(c p) -> p c", p=128), pr_)
        self.load(gl[:, :], self.w["lru_out_norm"][l].rearrange("(c p) -> p c", p=128), pr_)
        nsp = self.alloc([2], F32)
        nsp2 = self.alloc([2], F32)
        self.act(nsp[:, :], lam[:, :], AF.Exp, [pr_], [pr_], scale=-1.0)
        self.ts("dve", nsp[:, :], nsp[:, :], 1.0, None, ALU.add, None, [pr_], [pr_])
        self.act(nsp[:, :], nsp[:, :], AF.Ln, [pr_], [pr_])
        self.ts("dve", nsp2[:, :], nsp[:, :], -16.0, None, ALU.mult, None, [pr_], [pr_])
        self.ts("dve", nsp[:, :], nsp[:, :], -8.0, None, ALU.mult, None, [pr_], [pr_])
        bd32 = self.alloc([2, 2, 128], F32)
        bd = self.alloc([2, 2, 128], BF16)
        bd_r = Res("bd")
        P.add("pool", lambda e: e.memset(bd32[:, :, :, :], 0.0), writes=[bd_r])
        for wi, nm in enumerate(("lru_wa", "lru_wx")):
            for nb in range(4):
                c, hb = nb // 2, nb % 2
                self.load(bd32[hb * 64:(hb + 1) * 64, wi, c, hb * 64:(hb + 1) * 64], self.w[nm][l][nb, :, :], bd_r)
        self.cp("dve", bd[:, :, :, :], bd32[:, :, :, :], [bd_r], [bd_r])
        carry = self.alloc([2], F32)
        carry_r = Res("carry")
        NB = 515
        uar = Ring([self.alloc([2, NB], F32) for _ in range(2)], "ua")
        gar = Ring([self.alloc([2, 512], F32) for _ in range(2)], "ga")
        fr = Ring([self.alloc([512], F32) for _ in range(10)], "lf")
        ucb_r = Ring([self.alloc([512], BF16) for _ in range(2)], "ucb")
        sqr = Ring([self.alloc([512], BF16) for _ in range(2)], "sq")
        yp = self.alloc([2, 512], F32)
        yp_r = Res("yp")
        rstd = self.alloc([512], F32)
        rstd_r = Res("rstd")
        yst = Ring([self.alloc([2, 512], BF16) for _ in range(2)], "yst")
        psr = self.psring([0, 1, 2, 3], "lru")
        ps_n, ps_nr = self.psum[7], self.psum_r[7]
        first = True
        for (c0, n) in self.tiles9():
            ua, ua_r = uar.next()
            ga, ga_r = gar.next()
            if first:
                P.add("pool", lambda e, ua=ua: e.memset(ua[:, :, 0:3], 0.0), writes=[ua_r])
                self.load(ua[:, :, 3:3 + n], self.LRU_IN[2:4, :, c0:c0 + n].rearrange("c p t -> p c t"), ua_r)
            else:
                self.load(ua[:, :, 0:3 + n], self.LRU_IN[2:4, :, c0 - 3:c0 + n].rearrange("c p t -> p c t"), ua_r)
            self.load(ga[:, :, 0:n], self.LRU_IN[0:2, :, c0:c0 + n].rearrange("c p t -> p c t"), ga_r)
            for j in range(2):
                uc, uc_r = fr.next()
                self.ts("dve", uc[:, 0:n], ua[:, j, 0:n], convw[:, j, 0:1], convb[:, j:j + 1], ALU.mult, ALU.add,
                        [ua_r, pr_], [uc_r])
                for t in range(1, 4):
                    self.stt(uc[:, 0:n], ua[:, j, t:t + n], convw[:, j, t:t + 1], uc[:, 0:n], ALU.mult, ALU.add,
                             [ua_r, pr_, uc_r], [uc_r])
                ucb, ucb_rr = ucb_r.next()
                self.cp("act", ucb[:, 0:n], uc[:, 0:n], [uc_r], [ucb_rr])
                pa, par = psr.next()
                px, pxr = psr.next()
                self.mm(pa[:, 0:n], bd[:, 0, j, :], ucb[:, 0:n], True, True, [bd_r, ucb_rr], [par])
                self.mm(px[:, 0:n], bd[:, 1, j, :], ucb[:, 0:n], True, True, [bd_r, ucb_rr], [pxr])
                r, r_r = fr.next()
                ig, ig_r = fr.next()
                self.act(r[:, 0:n], pa[:, 0:n], AF.Sigmoid, [par, pr_], [r_r], bias=ba[:, j:j + 1])
                self.act(ig[:, 0:n], px[:, 0:n], AF.Sigmoid, [pxr, pr_], [ig_r], bias=bx[:, j:j + 1])
                a, a_r = fr.next()
                a2, a2_r = fr.next()
                self.act(a[:, 0:n], r[:, 0:n], AF.Exp, [r_r, pr_], [a_r], scale=nsp[:, j:j + 1])
                self.act(a2[:, 0:n], r[:, 0:n], AF.Exp, [r_r, pr_], [a2_r], scale=nsp2[:, j:j + 1])
                self.ts("pool", a2[:, 0:n], a2[:, 0:n], -1.0, 1.0, ALU.mult, ALU.add, [a2_r], [a2_r])
                self.act(a2[:, 0:n], a2[:, 0:n], AF.Sqrt, [a2_r], [a2_r])
                self.tt("pool", ig[:, 0:n], ig[:, 0:n], uc[:, 0:n], ALU.mult, [ig_r, uc_r], [ig_r])
                self.tt("pool", ig[:, 0:n], ig[:, 0:n], a2[:, 0:n], ALU.mult, [ig_r, a2_r], [ig_r])
                hs, hs_r = fr.next()
                if first:
                    P.add("dve", lambda e, hs=hs, a=a, ig=ig, n=n: e.tensor_tensor_scan(
                        out=hs[:, 0:n], data0=a[:, 0:n], data1=ig[:, 0:n], initial=0.0, op0=ALU.mult, op1=ALU.add),
                        reads=[a_r, ig_r], writes=[hs_r])
                else:
                    P.add("dve", lambda e, hs=hs, a=a, ig=ig, n=n, j=j: e.tensor_tensor_scan(
                        out=hs[:, 0:n], data0=a[:, 0:n], data1=ig[:, 0:n], initial=carry[:, j:j + 1],
                        op0=ALU.mult, op1=ALU.add),
                        reads=[a_r, ig_r, carry_r], writes=[hs_r])
                self.cp("dve", carry[:, j:j + 1], hs[:, n - 1:n], [hs_r], [carry_r])
                g = ga[:, j, 0:n]
                t, t_r = fr.next()
                self.tt("pool", t[:, 0:n], g, g, ALU.mult, [ga_r], [t_r])
                self.ts("pool", t[:, 0:n], t[:, 0:n], 0.044715, 1.0, ALU.mult, ALU.add, [t_r], [t_r])
                self.tt("pool", t[:, 0:n], t[:, 0:n], g, ALU.mult, [t_r, ga_r], [t_r])
                self.act(t[:, 0:n], t[:, 0:n], AF.Sigmoid, [t_r], [t_r], scale=1.5957691216057308)
                self.tt("pool", t[:, 0:n], t[:, 0:n], g, ALU.mult, [t_r, ga_r], [t_r])
                self.tt("dve", yp[:, j, 0:n], hs[:, 0:n], t[:, 0:n], ALU.mult, [hs_r, t_r], [yp_r])
                sq, sq_r = sqr.next()
                self.tt("pool", sq[:, 0:n], yp[:, j, 0:n], yp[:, j, 0:n], ALU.mult, [yp_r], [sq_r])
                self.mm(ps_n[:, 0:n], self.ones_bf[:, :], sq[:, 0:n], j == 0, j == 1, [sq_r, self.ones_r], [ps_nr])
            self.rsqrt_inplace(rstd[:, 0:n], rstd_r, 1.0 / 256, in_ap=ps_n[:, 0:n], in_reads=[ps_nr])
            ys, ys_r = yst.next()
            for j in range(2):
                self.stt(ys[:, j, 0:n], yp[:, j, 0:n], gl[:, j:j + 1], rstd[:, 0:n], ALU.mult, ALU.mult,
                         [yp_r, pr_, rstd_r], [ys_r])
            self.store(self.YT[0:2, :, c0:c0 + n].rearrange("c p t -> p c t"), ys[:, :, 0:n], ys_r)
            first = False

    def attn_sweep(self, qi, Kt, Qt, r0, r1, kq_r, V, v_r, dvp, obanks, scale, s_ring, pt_ring, tri, tri_r):
        if qi < 0:
            q0, nq, nsub = 0, NMETA, 1
            chunks = [(0, 0, NMETA, 0)]
        else:
            q0, nq, nsub = NMETA + 512 * qi, 512, 4
            chunks = [(0, 0, NMETA, -1)] + [(1 + i, NMETA + 128 * i, 128, -1) for i in range(4 * qi)]
            chunks += [(1 + 4 * qi + c, NMETA + 128 * (4 * qi + c), 128, c) for c in range(4)]
        started = set()
        for (kc, kcol, nk, diag) in chunks:
            s0 = max(diag, 0)
            qa = q0 + 128 * s0 if qi >= 0 else 0
            nqa = nq - 128 * s0 if qi >= 0 else nq
            ps, ps_r = s_ring.next()
            self.mm(ps[0:nk, 0:nqa], Kt[r0:r1, kcol:kcol + nk], Qt[r0:r1, qa:qa + nqa], True, True, [kq_r], [ps_r])
            pt, pt_r = pt_ring.next()
            self.act(pt[0:nk, 0:nqa], ps[0:nk, 0:nqa], AF.Exp, [ps_r], [pt_r], scale=scale)
            if diag >= 0:
                w = min(128, nqa)
                self.tt("dve", pt[0:nk, 0:w], pt[0:nk, 0:w], tri[0:nk, 0:w], ALU.mult, [pt_r, tri_r], [pt_r])
            self.flush_pv()

            def pv(s0=s0, pt=pt, pt_r=pt_r, nk=nk, kc=kc):
                for s in range(s0, nsub):
                    ob, ob_r, oc = obanks[s]
                    ws = min(128, nq)
                    lo = (s - s0) * 128
                    st = id(ob) not in started
                    started.add(id(ob))
                    self.mm(ob[0:ws, oc:oc + dvp], pt[0:nk, lo:lo + ws], V[0:nk, kc, 0:dvp], st, False,
                            [pt_r, v_r], [ob_r], skip_group_check=True)
            self._pend = pv
            self.run_deferred(4)

    _pend = None
    _deferred = None

    def defer(self, fn):
        if self._deferred is None:
            self._deferred = []
        self._deferred.append(fn)

    def run_deferred(self, k=None):
        d = self._deferred or []
        n = len(d) if k is None else min(k, len(d))
        for _ in range(n):
            d.pop(0)()

    def flush_pv(self):
        if self._pend is not None:
            p = self._pend
            self._pend = None
            p()

    def load_consts_attn(self):
        tri32 = self.alloc([128], F32)
        tri = self.alloc([128], BF16)
        tri_r = Res("tri")
        self.load(tri32[:, :], self.tri_in[:, :], tri_r)
        self.cp("dve", tri[:, :], tri32[:, :], [tri_r], [tri_r])
        idb32 = self.alloc([128], F32)
        idb = self.alloc([128], BF16)
        idb_r = Res("idb")
        self.cp("dve", idb[:, :], self.identf[:, :], [self.identf_r], [idb_r])
        return tri, tri_r, idb, idb_r

    def stage_diff(self, l):
        self.stage_begin()
        P = self.P
        lam_init = 0.8 - 0.6 * math.exp(-0.3 * l)
        tri, tri_r, idb, idb_r = self.load_consts_attn()
        lv = self.alloc([4, 64], F32)
        lv_r = Res("lv")
        for i, nm in enumerate(("lam_q1", "lam_k1", "lam_q2", "lam_k2")):
            self.load(lv[:, i, :], self.w[nm][l].partition_broadcast(128), lv_r)
        lsc = self.alloc([8], F32)
        self.tt("dve", lv[:, 0, :], lv[:, 0, :], lv[:, 1, :], ALU.mult, [lv_r], [lv_r])
        self.tt("dve", lv[:, 2, :], lv[:, 2, :], lv[:, 3, :], ALU.mult, [lv_r], [lv_r])
        P.add("dve", lambda e: e.reduce_sum(out=lsc[:, 0:1], in_=lv[:, 0, :], axis=AX.X), reads=[lv_r], writes=[lv_r])
        P.add("dve", lambda e: e.reduce_sum(out=lsc[:, 1:2], in_=lv[:, 2, :], axis=AX.X), reads=[lv_r], writes=[lv_r])
        self.act(lsc[:, 0:2], lsc[:, 0:2], AF.Exp, [lv_r], [lv_r])
        self.tt("dve", lsc[:, 2:3], lsc[:, 1:2], lsc[:, 0:1], ALU.subtract, [lv_r], [lv_r])
        self.ts("dve", lsc[:, 2:3], lsc[:, 2:3], -lam_init, None, ALU.add, None, [lv_r], [lv_r])
        gsub = self.alloc([128], F32)
        self.load(gsub[:, :], self.w["diff_subln"][l].partition_broadcast(128), lv_r)
        self.ts("dve", gsub[:, :], gsub[:, :], 1.0 - lam_init, None, ALU.mult, None, [lv_r], [lv_r])

        Kt = self.alloc([T], BF16)
        Qm = [self.alloc([T], BF16) for _ in range(2)]
        kq_r = Res("kq")
        P.add("pool", lambda e: e.memset(Qm[0][64:128, :], 0.0), writes=[kq_r])
        P.add("pool", lambda e: e.memset(Qm[1][0:64, :], 0.0), writes=[kq_r])
        V = self.alloc([33, 129], BF16)
        v_r = Res("V")
        pt_ring = Ring([self.alloc([512], BF16) for _ in range(3)], "pt")
        o1r = Ring([self.alloc([128], F32) for _ in range(8)], "o1")
        o2r = Ring([self.alloc([128], F32) for _ in range(8)], "o2")
        scr = Ring([self.alloc([128], F32) for _ in range(2)], "scr")
        smr = Ring([self.alloc([8], F32) for _ in range(8)], "sm")
        ybr = Ring([self.alloc([128], BF16) for _ in range(4)], "yb")
        ytr = Ring([self.alloc([512], BF16) for _ in range(3)], "yts")
        s_ring = self.psring([0, 1, 2], "s")
        tps = self.psum[7][:, :].bitcast(BF16)
        tps_r = self.psum_r[7]
        cb = self.conv_bufs()
        n_slots = 4 * (1 + SEQ // 512)
        need_now = [j for j in self.conv_queue if j[0] == l or (j[0] == l + 1 and j[1] == 1)]
        slot_i = 0
        for h in range(4):
            self.load(Qm[0][0:64, :], self.QKD[h, 0:64, :], kq_r)
            self.load(Qm[1][64:128, :], self.QKD[h, 64:128, :], kq_r)
            self.load(Kt[:, :], self.QKD[4 + h, :, :], kq_r)
            P.add("pool", lambda e: e.memset(V[:, :, 128:129], 1.0), writes=[v_r])
            self.load(V[0:NMETA, 0, 0:128], self.VD[0:NMETA, h * 128:(h + 1) * 128], v_r)
            for i0 in range(0, 32, 8):
                self.load(V[:, 1 + i0:9 + i0, 0:128],
                          self.VD[NMETA + 128 * i0:NMETA + 128 * (i0 + 8), h * 128:(h + 1) * 128].rearrange("(i p) d -> p i d", p=128), v_r)
            for qi in range(-1, SEQ // 512):
                nsub = 1 if qi < 0 else 4
                ws = NMETA if qi < 0 else 128
                q0 = 0 if qi < 0 else NMETA + 512 * qi
                ob = {}
                for m in range(2):
                    banks = [(self.psum[3 + 2 * m], self.psum_r[3 + 2 * m], 0), (self.psum[3 + 2 * m], self.psum_r[3 + 2 * m], 129),
                             (self.psum[4 + 2 * m], self.psum_r[4 + 2 * m], 0), (self.psum[4 + 2 * m], self.psum_r[4 + 2 * m], 129)]
                    ob[m] = banks
                    self.attn_sweep(qi, Kt, Qm[m], 0, 128, kq_r, V, v_r, 129, banks, 0.125, s_ring, pt_ring, tri, tri_r)
                self.flush_pv()
                left = n_slots - slot_i
                k = -(-len(need_now) // left) if left > 0 else len(need_now)
                for _ in range(k):
                    if need_now:
                        job = need_now.pop(0)
                        self.conv_queue.remove(job)
                        self.conv_job(job, cb)
                slot_i += 1
                yts, yts_r = ytr.next()
                for s in range(nsub):
                    sm, sm_r = smr.next()
                    o1, o1_r = o1r.next()
                    o2, o2_r = o2r.next()
                    for m, (o, o_r) in enumerate(((o1, o1_r), (o2, o2_r))):
                        b, b_r, oc = ob[m][s]
                        P.add("dve", lambda e, sm=sm, b=b, oc=oc, m=m, ws=ws: e.reciprocal(
                            out=sm[0:ws, m:m + 1], in_=b[0:ws, oc + 128:oc + 129]), reads=[b_r], writes=[sm_r])
                        self.act(o[0:ws, :], b[0:ws, oc:oc + 128], AF.Copy, [b_r, sm_r], [o_r], scale=sm[0:ws, m:m + 1])

                    def tail(s=s, sm=sm, sm_r=sm_r, o1=o1, o1_r=o1_r, o2=o2, o2_r=o2_r, ws=ws, yts=yts, yts_r=yts_r):
                        sc, sc_r = scr.next()
                        yb, yb_r = ybr.next()
                        return [
                            lambda: self.stt(o1[0:ws, :], o2[0:ws, :], lsc[0:ws, 2:3], o1[0:ws, :], ALU.mult, ALU.add,
                                             [o1_r, o2_r, lv_r], [o1_r]),
                            lambda: self.act(sc[0:ws, :], o1[0:ws, :], AF.Square, [o1_r], [sc_r, sm_r], accum_out=sm[0:ws, 2:3]),
                            lambda: self.ts("dve", sm[0:ws, 2:3], sm[0:ws, 2:3], 1.0 / 128, EPS, ALU.mult, ALU.add, [sm_r], [sm_r]),
                            lambda: self.act(sm[0:ws, 2:3], sm[0:ws, 2:3], AF.Sqrt, [sm_r], [sm_r]),
                            lambda: P.add("dve", lambda e: e.reciprocal(out=sm[0:ws, 2:3], in_=sm[0:ws, 2:3]), reads=[sm_r], writes=[sm_r]),
                            lambda: self.stt(yb[0:ws, :], o1[0:ws, :], sm[0:ws, 2:3], gsub[0:ws, :], ALU.mult, ALU.mult,
                                             [o1_r, sm_r, lv_r], [yb_r]),
                            lambda: P.add("pe", lambda e: e.transpose(out=tps[:, s * 128:s * 128 + ws], in_=yb[0:ws, :],
                                                                       identity=idb[0:ws, 0:ws]),
                                          reads=[yb_r, idb_r], writes=[tps_r]),
                            lambda: self.cp("act", yts[:, s * 128:s * 128 + ws], tps[:, s * 128:s * 128 + ws], [tps_r], [yts_r]),
                        ]
                    for fn in tail():
                        self.defer(fn)
                nq = NMETA if qi < 0 else 512
                self.defer(lambda h=h, q0=q0, nq=nq, yts=yts, yts_r=yts_r: self.store(self.YT[2 + h, :, q0:q0 + nq], yts[:, 0:nq], yts_r))
        self.run_deferred()

    def stage_mla(self, l):
        self.stage_begin()
        P = self.P
        tri, tri_r, idb, idb_r = self.load_consts_attn()
        gm = self.alloc([256], F32)
        gm_r = Res("gm")
        self.load(gm[:, :], self.w["mla_out_norm"][l].partition_broadcast(128), gm_r)
        Kt = self.alloc([4, T], BF16)
        kq_r = Res("kq")
        V = self.alloc([33, 4, 65], BF16)
        v_r = Res("V")
        self.load(Kt[0:96, :, :], self.KM[:, :, :].rearrange("h r t -> r h t"), kq_r)
        Vf = V.rearrange("p i h d -> p i (h d)")
        self.load(Vf[0:NMETA, 0, :], self.VM[0:NMETA, :], v_r)
        for i0 in range(0, 32, 8):
            self.load(Vf[:, 1 + i0:9 + i0, :],
                      self.VM[NMETA + 128 * i0:NMETA + 128 * (i0 + 8), :].rearrange("(i p) d -> p i d", p=128), v_r)
        qtr = Ring([self.alloc([4, 512], BF16) for _ in range(2)], "qt")
        pt_ring = Ring([self.alloc([512], BF16) for _ in range(3)], "pt")
        ocr = Ring([self.alloc([256], F32) for _ in range(5)], "oc")
        scr = Ring([self.alloc([256], F32) for _ in range(1)], "scr")
        smr = Ring([self.alloc([8], F32) for _ in range(8)], "sm")
        ybr = Ring([self.alloc([256], BF16) for _ in range(3)], "yb")
        ytr = Ring([self.alloc([2, 512], BF16) for _ in range(2)], "yts")
        s_ring = self.psring([0, 1, 2], "s")
        tps = self.psum[7][:, :].bitcast(BF16)
        tps_r = self.psum_r[7]
        scale = 96.0 ** -0.5
        for qi in range(-1, SEQ // 512):
            nsub = 1 if qi < 0 else 4
            ws = NMETA if qi < 0 else 128
            q0 = 0 if qi < 0 else NMETA + 512 * qi
            nq = NMETA if qi < 0 else 512
            qt, qt_r = qtr.next()
            self.load(qt[0:96, :, 0:nq], self.QM[:, :, q0:q0 + nq].rearrange("h r t -> r h t"), qt_r)
            ob = {}
            for h in range(4):
                banks = [(self.psum[3 + h], self.psum_r[3 + h], 65 * s) for s in range(4)]
                ob[h] = banks
                self.attn_sweep_local(qi, Kt[:, h, :], qt[:, h, :], kq_r, qt_r, V[:, :, h, :], v_r, 65, banks, scale,
                                      s_ring, pt_ring, tri, tri_r)
            self.flush_pv()
            yts, yts_r = ytr.next()
            for s in range(nsub):
                sm, sm_r = smr.next()
                oc, oc_r = ocr.next()
                for h in range(4):
                    b, b_r, c = ob[h][s]
                    P.add("dve", lambda e, sm=sm, b=b, c=c, h=h, ws=ws: e.reciprocal(
                        out=sm[0:ws, h:h + 1], in_=b[0:ws, c + 64:c + 65]), reads=[b_r], writes=[sm_r])
                    self.act(oc[0:ws, h * 64:(h + 1) * 64], b[0:ws, c:c + 64], AF.Copy, [b_r, sm_r], [oc_r],
                             scale=sm[0:ws, h:h + 1])

                def tail(s=s, sm=sm, sm_r=sm_r, oc=oc, oc_r=oc_r, ws=ws, yts=yts, yts_r=yts_r):
                    sc, sc_r = scr.next()
                    yb, yb_r = ybr.next()
                    fns = [
                        lambda: self.act(sc[0:ws, :], oc[0:ws, :], AF.Square, [oc_r], [sc_r, sm_r], accum_out=sm[0:ws, 4:5]),
                        lambda: self.ts("dve", sm[0:ws, 4:5], sm[0:ws, 4:5], 1.0 / 256, EPS, ALU.mult, ALU.add, [sm_r], [sm_r]),
                        lambda: self.act(sm[0:ws, 4:5], sm[0:ws, 4:5], AF.Sqrt, [sm_r], [sm_r]),
                        lambda: P.add("dve", lambda e: e.reciprocal(out=sm[0:ws, 4:5], in_=sm[0:ws, 4:5]), reads=[sm_r], writes=[sm_r]),
                        lambda: self.stt(yb[0:ws, :], oc[0:ws, :], sm[0:ws, 4:5], gm[0:ws, :], ALU.mult, ALU.mult,
                                         [oc_r, sm_r, gm_r], [yb_r]),
                    ]
                    for c in range(2):
                        fns.append(lambda c=c: P.add("pe", lambda e: e.transpose(
                            out=tps[:, c * 512 + s * 128:c * 512 + s * 128 + ws], in_=yb[0:ws, c * 128:(c + 1) * 128],
                            identity=idb[0:ws, 0:ws]), reads=[yb_r, idb_r], writes=[tps_r]))
                        fns.append(lambda c=c: self.cp("act", yts[:, c, s * 128:s * 128 + ws],
                                                       tps[:, c * 512 + s * 128:c * 512 + s * 128 + ws], [tps_r], [yts_r]))
                    return fns
                for fn in tail():
                    self.defer(fn)
            self.defer(lambda q0=q0, nq=nq, yts=yts, yts_r=yts_r: self.store(
                self.YT[6:8, :, q0:q0 + nq].rearrange("c p t -> p c t"), yts[:, :, 0:nq], yts_r))
        self.run_deferred()

    def attn_sweep_local(self, qi, Kt, Qtile, kq_r, qt_r, V, v_r, dvp, obanks, scale, s_ring, pt_ring, tri, tri_r):
        if qi < 0:
            nq, nsub = NMETA, 1
            chunks = [(0, 0, NMETA, 0)]
        else:
            nq, nsub = 512, 4
            chunks = [(0, 0, NMETA, -1)] + [(1 + i, NMETA + 128 * i, 128, -1) for i in range(4 * qi)]
            chunks += [(1 + 4 * qi + c, NMETA + 128 * (4 * qi + c), 128, c) for c in range(4)]
        started = set()
        for (kc, kcol, nk, diag) in chunks:
            s0 = max(diag, 0)
            qa = 128 * s0 if qi >= 0 else 0
            nqa = nq - qa
            ps, ps_r = s_ring.next()
            self.mm(ps[0:nk, 0:nqa], Kt[0:96, kcol:kcol + nk], Qtile[0:96, qa:qa + nqa], True, True, [kq_r, qt_r], [ps_r])
            pt, pt_r = pt_ring.next()
            self.act(pt[0:nk, 0:nqa], ps[0:nk, 0:nqa], AF.Exp, [ps_r], [pt_r], scale=scale)
            if diag >= 0:
                w = min(128, nqa)
                self.tt("dve", pt[0:nk, 0:w], pt[0:nk, 0:w], tri[0:nk, 0:w], ALU.mult, [pt_r, tri_r], [pt_r])
            self.flush_pv()

            def pv(s0=s0, pt=pt, pt_r=pt_r, nk=nk, kc=kc):
                for s in range(s0, nsub):
                    ob, ob_r, oc = obanks[s]
                    ws = min(128, nq)
                    lo = (s - s0) * 128
                    st = id(ob) not in started
                    started.add(id(ob))
                    self.mm(ob[0:ws, oc:oc + dvp], pt[0:nk, lo:lo + ws], V[0:nk, kc, 0:dvp], st, False,
                            [pt_r, v_r], [ob_r], skip_group_check=True)
            self._pend = pv
            self.run_deferred(4)

    def stage_oproj(self, l):
        self.stage_begin()
        Wo = self.alloc([NCH, D], BF16)
        Wo_r = Res("Wo")
        self.load_w_bf16(Wo, Wo_r, self.w["w_out"][l], D, slab=256)
        ytr = Ring([self.alloc([NCH, 512], BF16) for _ in range(2)], "yt")
        psr = self.psring([0, 1, 2, 3], "op")
        for (c0, n) in self.tiles9():
            yt, yt_r = ytr.next()
            self.load(yt[:, :, 0:n], self.YT[:, :, c0:c0 + n].rearrange("c p t -> p c t"), yt_r)
            for o in range(NCH):
                po, por = psr.next()
                for k in range(NCH):
                    self.mm(po[:, 0:n], Wo[:, k, o * 128:(o + 1) * 128], yt[:, k, 0:n], k == 0, k == NCH - 1, [Wo_r, yt_r], [por])
                hr = self.h_res(o, c0, n)
                self.stt(self.hT[:, o, c0:c0 + n], po[:, 0:n], 1.0, self.hT[:, o, c0:c0 + n], ALU.mult, ALU.add,
                         [por] + hr, hr)


    def stage_final(self):
        P = self.P
        self.stage_begin()
        gbc = self.alloc([D], F32)
        gbc_r = Res("gbc")
        P.add("sp", lambda e: e.dma_start(out=gbc[:, :], in_=self.final_norm.partition_broadcast(128)),
              writes=[gbc_r], dma=True, sem_res=gbc_r)
        otr = Ring([self.alloc([D], F32) for i in range(2)], "ot")
        sq = self.alloc([512], F32)
        sq_r = Res("fsq")
        ssr = Ring([self.alloc([4], F32) for i in range(2)], "fss")
        psr = self.psring([0, 1, 2, 3], "f")
        for i in range(SEQ // 128):
            c0 = NMETA + 128 * i
            o, o_r = otr.next()
            s, s_r = ssr.next()
            halves = []
            for half in range(2):
                pt, pr = psr.next()
                for cc in range(4):
                    c = half * 4 + cc
                    P.add("pe", lambda e, pt=pt, cc=cc, c=c, c0=c0: e.transpose(
                        out=pt[:, cc * 128:(cc + 1) * 128], in_=self.hT[:, c, c0:c0 + 128],
                        identity=self.identf[:, :]),
                        reads=self.h_res(c, c0, 128) + [self.identf_r], writes=[pr])
                halves.append((pt, pr))
            for half, (pt, pr) in enumerate(halves):
                P.add("act", lambda e, pt=pt, s=s, half=half: e.activation(
                    out=sq[:, 0:512], in_=pt[:, :], func=AF.Square, accum_out=s[:, half:half + 1]),
                    reads=[pr], writes=[sq_r, s_r])
            P.add("dve", lambda e, s=s: e.tensor_tensor(out=s[:, 2:3], in0=s[:, 0:1], in1=s[:, 1:2], op=ALU.add),
                  reads=[s_r], writes=[s_r])
            P.add("dve", lambda e, s=s: e.tensor_scalar(out=s[:, 2:3], in0=s[:, 2:3], scalar1=1.0 / D, scalar2=EPS,
                                                         op0=ALU.mult, op1=ALU.add),
                  reads=[s_r], writes=[s_r])
            P.add("act", lambda e, s=s: e.activation(out=s[:, 3:4], in_=s[:, 2:3], func=AF.Sqrt),
                  reads=[s_r], writes=[s_r])
            P.add("dve", lambda e, s=s: e.reciprocal(out=s[:, 3:4], in_=s[:, 3:4]),
                  reads=[s_r], writes=[s_r])
            for half, (pt, pr) in enumerate(halves):
                P.add("dve", lambda e, pt=pt, s=s, o=o, half=half: e.scalar_tensor_tensor(
                    out=o[:, half * 512:(half + 1) * 512], in0=pt[:, :], scalar=s[:, 3:4],
                    in1=gbc[:, half * 512:(half + 1) * 512], op0=ALU.mult, op1=ALU.mult),
                    reads=[pr, s_r, gbc_r], writes=[o_r])
            P.add("sp", lambda e, o=o, i=i: e.dma_start(out=self.out[128 * i:128 * (i + 1), :], in_=o[:, :]),
                  reads=[o_r], dma=True, sem_res=o_r)
        self.P.barrier()


def build_program(n_layers=DEPTH, debug=None):
    b = Builder(n_layers, debug)
    nc = b.build()
    return nc, b


_CONSTS = None


def consts():
    global _CONSTS
    if _CONSTS is None:
        half = 16
        inv = (np.float32(10000.0) ** (-(np.arange(half, dtype=np.float32)) / np.float32(half))).astype(np.float32)
        ang = (np.arange(T, dtype=np.float32)[None, :] * inv[:, None]).astype(np.float32).astype(np.float64)
        cs = np.zeros((96, 2, T), np.float32)
        cs[0:64, 0, :] = 1.0
        cs[64:80, 0, :] = np.cos(ang)
        cs[80:96, 0, :] = np.cos(ang)
        cs[64:80, 1, :] = -np.sin(ang)
        cs[80:96, 1, :] = np.sin(ang)
        tri = (np.arange(128)[:, None] <= np.arange(128)[None, :]).astype(np.float32)
        _CONSTS = {"ident_f": np.eye(128, dtype=np.float32), "rope_cs": cs, "tri_in": tri}
    return _CONSTS


def make_in_maps(inputs, b):
    c = consts()
    maps = []
    for core in range(8):
        m = {}
        for nm in b.inputs:
            if nm == "x":
                m[nm] = np.ascontiguousarray(inputs["x"][core])
            elif nm in c:
                m[nm] = c[nm]
            else:
                m[nm] = np.ascontiguousarray(inputs[nm])
        maps.append(m)
    return maps


def kernel(**inputs):
    inputs = {k: np.asarray(v) for k, v in inputs.items()}
    nc, b = build_program()
    maps = make_in_maps(inputs, b)
    res = run_bass_kernel_spmd(nc, maps, core_ids=list(range(8)))
    out = np.stack([np.asarray(res.results[i]["out"]) for i in range(8)], axis=0)
    return out.astype(np.float32)
```

```python
import contextlib
import math
import numpy as np
import ml_dtypes
import concourse.bass as bass
import concourse.mybir as mybir
from concourse.bass_utils import run_bass_kernel_spmd

F32 = mybir.dt.float32
BF16 = mybir.dt.bfloat16
AF = mybir.ActivationFunctionType
ALU = mybir.AluOpType
AX = mybir.AxisListType

D = 1024
SEQ = 4096
NMETA = 16
T = SEQ + NMETA
DFF = 2816
NF = DFF // 128
NCH = D // 128
DEPTH = 2
EPS = 1e-6
DIN = 2464


class Res:
    __slots__ = ("name", "w", "r", "slot")

    def __init__(self, name):
        self.name = name
        self.w = None
        self.r = []
        self.slot = None


class SemSlot:
    __slots__ = ("sem", "dcount")

    def __init__(self):
        self.sem = None
        self.dcount = 0


class Op:
    __slots__ = ("eng", "fn", "deps", "dma", "sem_res", "val", "needs_inc", "idx", "gen")

    def __init__(self, eng, fn, dma=False):
        self.eng = eng
        self.fn = fn
        self.deps = []
        self.dma = dma
        self.sem_res = None
        self.val = 0
        self.needs_inc = False


ENGS = ("pe", "act", "dve", "pool", "sp")
MAX_DMA_INFLIGHT = 4


class Prog:
    def __init__(self):
        self.q = {e: [] for e in ENGS}
        self.pool = []
        self.stage_k = 0
        self.gen = 0
        self.last_dma = {}
        self.dma_fifo = []

    def barrier(self):
        deps = []
        for e in ENGS:
            for op in reversed(self.q[e]):
                if not op.dma and op.fn is not None:
                    deps.append((op, 0))
                    break
        for op in self.last_dma.values():
            deps.append((op, op.sem_res.dcount))
        for e in ENGS:
            op = Op(e, None)
            op.gen = self.gen
            op.deps = [(o, s) for (o, s) in deps if not (o.eng == e and not o.dma)]
            for o, _ in op.deps:
                o.needs_inc = True
            self.q[e].append(op)
        self.gen += 1
        self.stage_k = 0
        self.last_dma = {}

    def add(self, eng, fn, reads=(), writes=(), dma=False, sem_res=None):
        op = Op(eng, fn, dma)
        op.gen = self.gen
        deps = {}

        def dep(o):
            if o is None or o.gen < self.gen:
                return
            if (not o.dma) and o.eng == "pe" and eng == "pe" and not dma:
                return
            snap = o.sem_res.dcount if o.dma else 0
            deps[id(o)] = (o, snap)

        for r in reads:
            dep(r.w)
        for w in writes:
            dep(w.w)
            for o in w.r:
                dep(o)
        if dma:
            if len(self.dma_fifo) >= MAX_DMA_INFLIGHT:
                old = self.dma_fifo[-MAX_DMA_INFLIGHT]
                if old.gen == self.gen and id(old) not in deps:
                    deps[id(old)] = (old, 0)
            self.dma_fifo.append(op)
        op.deps = list(deps.values())
        for o, _ in op.deps:
            o.needs_inc = True
        if dma:
            assert sem_res is not None
            if sem_res.slot is None or sem_res.slot[1] != self.gen:
                if self.stage_k >= len(self.pool):
                    self.pool.append(SemSlot())
                sem_res.slot = (self.pool[self.stage_k], self.gen)
                self.stage_k += 1
            sl = sem_res.slot[0]
            sl.dcount += 16
            op.sem_res = sl
            op.val = sl.dcount
            self.last_dma[id(sl)] = op
        for r in reads:
            if not dma:
                r.r = [o for o in r.r if o.dma or o.eng != eng]
            r.r.append(op)
        for w in writes:
            w.w = op
            w.r = []
        self.q[eng].append(op)
        return op

    def emit(self, nc, es):
        esem = {e: es.enter_context(nc.semaphore("s_" + e)) for e in ENGS}
        for i, sl in enumerate(self.pool):
            sl.sem = es.enter_context(nc.semaphore("d%d" % i))
        for e in ENGS:
            cnt = 0
            for op in self.q[e]:
                if op.dma or op.fn is None:
                    continue
                if op.needs_inc:
                    cnt += 1
                    op.val = cnt
        block = es.enter_context(nc.Block())
        engobj = {"pe": "tensor", "act": "scalar", "dve": "vector", "pool": "gpsimd", "sp": "sync"}

        def body_for(e):
            ops = self.q[e]

            def body(eng):
                seen = {}
                for op in ops:
                    for d, snap in op.deps:
                        if d.dma:
                            sem, val = d.sem_res.sem, max(d.val, snap)
                        else:
                            sem, val = esem[d.eng], d.val
                        key = id(sem)
                        if seen.get(key, 0) >= val:
                            continue
                        seen[key] = val
                        eng.wait_ge(sem, val)
                    if op.fn is None:
                        continue
                    ins = op.fn(eng)
                    if op.dma:
                        ins.then_inc(op.sem_res.sem, 16)
                    elif op.needs_inc:
                        ins.then_inc(esem[e], 1)
            return body

        for e in ENGS:
            getattr(block, engobj[e])(body_for(e))


class Ring:
    def __init__(self, aps, name):
        self.aps = aps
        self.res = [Res("%s%d" % (name, i)) for i in range(len(aps))]
        self.i = 0

    def next(self):
        k = self.i % len(self.aps)
        self.i += 1
        return self.aps[k], self.res[k]


ARENA_BYTES = 79360


def dsize(dt):
    return 4 if dt == F32 else 2


class Builder:
    def __init__(self, n_layers=DEPTH, debug=None):
        self.n_layers = n_layers
        self.debug = debug or {}
        self.nc = bass.Bass("TRN2", target_bir_lowering=False)
        self.P = Prog()
        self.es = contextlib.ExitStack()
        self.inputs = {}
        self.aoff = 0

    def din(self, name, shape, dt=F32):
        ap = self.nc.dram_tensor(name, list(shape), dt, kind="ExternalInput").ap()
        self.inputs[name] = ap
        return ap

    def dout(self, name, shape, dt=F32):
        return self.nc.dram_tensor(name, list(shape), dt, kind="ExternalOutput").ap()

    def dscratch(self, name, shape, dt):
        return self.nc.dram_tensor(name, list(shape), dt, kind="Internal").ap()

    def sb(self, name, shape, dt):
        return self.es.enter_context(self.nc.sbuf_tensor(name, list(shape), dt))

    def ps(self, name, shape, dt=F32):
        return self.es.enter_context(self.nc.psum_tensor(name, list(shape), dt))

    def stage_begin(self):
        self.P.barrier()
        self.aoff = 0

    def alloc(self, shape, dt):
        n = int(np.prod(shape))
        nbytes = (n * dsize(dt) + 31) // 32 * 32
        assert self.aoff + nbytes <= ARENA_BYTES, (self.aoff, nbytes)
        ap = self.arena[:, self.aoff // 4:(self.aoff + nbytes) // 4]
        self.aoff += nbytes
        if dt != F32:
            ap = ap.bitcast(dt)
        ap = ap[:, 0:n]
        if len(shape) == 2:
            ap = ap.rearrange("p (a b) -> p a b", a=shape[0])
        elif len(shape) == 3:
            ap = ap.rearrange("p (a b c) -> p a b c", a=shape[0], b=shape[1])
        return ap


    def act(self, out, in_, func, reads, writes, **kw):
        return self.P.add("act", lambda e: e.activation(out=out, in_=in_, func=func, **kw), reads=reads, writes=writes)

    def tt(self, eng, out, in0, in1, op, reads, writes):
        return self.P.add(eng, lambda e: e.tensor_tensor(out=out, in0=in0, in1=in1, op=op), reads=reads, writes=writes)

    def ts(self, eng, out, in0, s1, s2, op0, op1, reads, writes):
        if s2 is None:
            return self.P.add(eng, lambda e: e.tensor_scalar(out=out, in0=in0, scalar1=s1, scalar2=None, op0=op0),
                              reads=reads, writes=writes)
        return self.P.add(eng, lambda e: e.tensor_scalar(out=out, in0=in0, scalar1=s1, scalar2=s2, op0=op0, op1=op1),
                          reads=reads, writes=writes)

    def stt(self, out, in0, scalar, in1, op0, op1, reads, writes):
        return self.P.add("dve", lambda e: e.scalar_tensor_tensor(out=out, in0=in0, scalar=scalar, in1=in1, op0=op0, op1=op1),
                          reads=reads, writes=writes)

    def cp(self, eng, out, in_, reads, writes):
        if eng == "act":
            return self.act(out, in_, AF.Copy, reads, writes)
        return self.P.add(eng, lambda e: e.tensor_copy(out=out, in_=in_), reads=reads, writes=writes)

    def mm(self, out, lhsT, rhs, start, stop, reads, writes, **kw):
        return self.P.add("pe", lambda e: e.matmul(out, lhsT=lhsT, rhs=rhs, start=start, stop=stop, **kw),
                          reads=reads, writes=writes)

    def load(self, out, in_, res, reads=()):
        return self.P.add("sp", lambda e: e.dma_start(out=out, in_=in_), reads=list(reads), writes=[res], dma=True, sem_res=res)

    def store(self, out, in_, res):
        return self.P.add("sp", lambda e: e.dma_start(out=out, in_=in_), reads=[res], dma=True, sem_res=res)

    def rsqrt_inplace(self, ap, res, scale, in_ap=None, in_reads=()):
        src = ap if in_ap is None else in_ap
        self.ts("dve", ap, src, scale, EPS, ALU.mult, ALU.add, [res] + list(in_reads), [res])
        self.act(ap, ap, AF.Sqrt, [res], [res])
        self.P.add("dve", lambda e: e.reciprocal(out=ap, in_=ap), reads=[res], writes=[res])

    def h_res(self, c, c0, n):
        out = []
        if c0 < NMETA:
            out.append(self.hres[c][0])
        lo = max(c0, NMETA) - NMETA
        hi = c0 + n - NMETA
        if hi > lo:
            for i in range(lo // 128, (hi - 1) // 128 + 1):
                out.append(self.hres[c][1 + i])
        return out

    def build(self):
        nc, P = self.nc, self.P
        self.x = self.din("x", [SEQ, D])
        self.meta = self.din("meta_tokens", [NMETA, D])
        self.final_norm = self.din("final_norm", [D])
        self.ident_f = self.din("ident_f", [128, 128])
        L = DEPTH
        self.w = {}
        for nm, shp in [("ffn1_norm", [L, D]), ("ffn1_in", [L, D, 2 * DFF]), ("ffn1_out", [L, DFF, D]),
                        ("ffn2_norm", [L, D]), ("ffn2_in", [L, D, 2 * DFF]), ("ffn2_out", [L, DFF, D])]:
            self.w[nm] = self.din(nm, shp)
        for nm, shp in [("mix_norm", [L, D]), ("w_in", [L, D, DIN]), ("conv_w", [L, 4, 256]), ("conv_b", [L, 256]),
                        ("lru_wa", [L, 4, 64, 64]), ("lru_ba", [L, 4, 64]), ("lru_wx", [L, 4, 64, 64]), ("lru_bx", [L, 4, 64]),
                        ("lru_lambda", [L, 256]), ("lru_out_norm", [L, 256]), ("lam_q1", [L, 64]), ("lam_k1", [L, 64]),
                        ("lam_q2", [L, 64]), ("lam_k2", [L, 64]), ("diff_subln", [L, 128]), ("q_norm", [L, 256]),
                        ("w_uq", [L, 256, 384]), ("kv_norm", [L, 128]), ("w_ukv", [L, 128, 512]),
                        ("mla_out_norm", [L, 256]), ("w_out", [L, D, D])]:
            self.w[nm] = self.din(nm, shp)
        self.rope_cs = self.din("rope_cs", [96, 2, T])
        self.tri_in = self.din("tri_in", [128, 128])
        self.out = self.dout("out", [SEQ, D])
        self.LRU_IN = self.dscratch("lru_in", [4, 128, T], F32)
        self.QKD = self.dscratch("qkd", [8, 128, T], BF16)
        self.VD = self.dscratch("vd", [T, 512], BF16)
        self.QM = self.dscratch("qm", [4, 96, T], BF16)
        self.KM = self.dscratch("km", [4, 96, T], BF16)
        self.VM = self.dscratch("vm", [T, 260], BF16)
        self.YT = self.dscratch("yt", [8, 128, T], BF16)
        self.wf = {}
        for l in range(L):
            for which in (1, 2):
                self.wf[(l, which)] = self.dscratch("wf_%d_%d" % (l, which), [NF, 128, 3072], BF16)
        self.wf_r = {k: Res("wf%d%d" % k) for k in self.wf}

        self.hT = self.sb("hT", [128, NCH, T], F32)
        self.hres = [[Res("h%d_%d" % (c, i)) for i in range(1 + SEQ // 128)] for c in range(NCH)]
        self.arena = self.sb("arena", [128, ARENA_BYTES // 4], F32)
        self.identf = self.sb("identf", [128, 128], F32)
        self.identf_r = Res("identf")
        self.ones_bf = self.sb("ones_bf", [128, 128], BF16)
        self.ones_r = Res("ones")
        self.gains = self.sb("gains", [128, 8, NCH], F32)
        self.gains_r = Res("gains")

        P.add("sp", lambda e: e.dma_start(out=self.identf[:], in_=self.ident_f[:, :]),
              writes=[self.identf_r], dma=True, sem_res=self.identf_r)
        P.add("pool", lambda e: e.memset(self.ones_bf[:], 1.0), writes=[self.ones_r])
        self.gain_idx = {}
        gi = 0
        for l in range(L):
            for nm in ("ffn1_norm", "ffn2_norm", "mix_norm"):
                src = self.w[nm][l, :].rearrange("(c p) -> p c", p=128)
                P.add("sp", lambda e, k=gi, src=src: e.dma_start(out=self.gains[:, k, :], in_=src),
                      writes=[self.gains_r], dma=True, sem_res=self.gains_r)
                self.gain_idx[(nm, l)] = gi
                gi += 1
        src = self.final_norm.rearrange("(c p) -> p c", p=128)
        P.add("sp", lambda e, k=gi, src=src: e.dma_start(out=self.gains[:, k, :], in_=src),
              writes=[self.gains_r], dma=True, sem_res=self.gains_r)
        self.gain_idx["final"] = gi

        self.psum = [self.ps("ps%d" % i, [128, 512], F32) for i in range(8)]
        self.psum_r = [Res("ps%d" % i) for i in range(8)]

        self.stage_load_x()
        ffns = [(l, w) for l in range(self.n_layers) for w in (1, 2) if self.debug.get("ffn%d" % w, True)]
        self.conv_queue = []
        for i, (l, w) in enumerate(ffns):
            if i == 0 or not self.debug.get("mixer", True) or not self.debug.get("diff", True):
                self.stage_convert_ffn(l, w)
            else:
                self.conv_queue += [(l, w, f) for f in range(NF)]
        for l in range(self.n_layers):
            if self.debug.get("ffn1", True):
                self.stage_ffn(l, 1)
            if self.debug.get("mixer", True):
                if self.debug.get("proj1", True):
                    self.stage_proj1(l)
                if self.debug.get("proj2", True):
                    self.stage_proj2(l)
                if self.debug.get("lru", True):
                    self.stage_lru(l)
                if self.debug.get("diff", True):
                    self.stage_diff(l)
                if self.debug.get("mla", True):
                    self.stage_mla(l)
                if self.debug.get("oproj", True):
                    self.stage_oproj(l)
            if self.debug.get("ffn2", True):
                self.stage_ffn(l, 2)
        self.stage_final()
        with nc.allow_non_contiguous_dma(reason="small strided parameter loads"):
            P.emit(nc, self.es)
        return nc

    def psring(self, idxs, name):
        r = Ring([self.psum[i] for i in idxs], name)
        r.res = [self.psum_r[i] for i in idxs]
        return r

    def stage_load_x(self):
        P = self.P
        self.stage_begin()
        ring = Ring([self.alloc([D], F32) for i in range(3)], "xin")
        psr = self.psring([0, 1, 2, 3], "x")
        n = 0
        for i in range(-1, SEQ // 128):
            buf, br = ring.next()
            if i < 0:
                rows, c0 = NMETA, 0
                src = self.meta[:, :]
            else:
                rows, c0 = 128, NMETA + 128 * i
                src = self.x[128 * i:128 * (i + 1), :]
            P.add("sp", lambda e, buf=buf, rows=rows, src=src: e.dma_start(out=buf[0:rows, :], in_=src),
                  writes=[br], dma=True, sem_res=br)
            for half in range(2):
                pt, pr = psr.next()
                for cc in range(4):
                    c = half * 4 + cc
                    P.add("pe", lambda e, pt=pt, cc=cc, buf=buf, rows=rows, c=c: e.transpose(
                        out=pt[:, cc * 128:cc * 128 + rows], in_=buf[0:rows, c * 128:(c + 1) * 128],
                        identity=self.identf[0:rows, 0:rows]),
                        reads=[br, self.identf_r], writes=[pr])
                eng = "act" if (n % 2 == 0) else "dve"
                n += 1
                hr = []
                for cc in range(4):
                    hr += self.h_res(half * 4 + cc, c0, rows)
                dst = self.hT[:, half * 4:half * 4 + 4, c0:c0 + rows]
                srcp = pt[:, :].rearrange("p (c t) -> p c t", c=4)[:, :, 0:rows]
                if eng == "act":
                    P.add("act", lambda e, dst=dst, srcp=srcp: e.activation(out=dst, in_=srcp, func=AF.Copy),
                          reads=[pr], writes=hr)
                else:
                    P.add("dve", lambda e, dst=dst, srcp=srcp: e.tensor_copy(out=dst, in_=srcp),
                          reads=[pr], writes=hr)

    def conv_bufs(self):
        return dict(a=self.alloc([2, NCH, 128], F32), ar=Res("cva"), ao=self.alloc([D], F32), aor=Res("cvo"),
                    b=self.alloc([3072], BF16), br=Res("cvb"))

    def conv_job(self, job, cb, eng="pool"):
        l, which, f = job
        w_in = self.w["ffn%d_in" % which][l]
        w_out = self.w["ffn%d_out" % which][l]
        WF = self.wf[(l, which)]
        a, ar, ao, aor, b, br = cb["a"], cb["ar"], cb["ao"], cb["aor"], cb["b"], cb["br"]
        for gu in range(2):
            col = gu * DFF + f * 128
            self.load(a[:, gu, :, :], w_in[:, col:col + 128].rearrange("(k p) j -> p k j", p=128), ar)
        self.load(ao[:, :], w_out[f * 128:(f + 1) * 128, :], aor)
        for gu in range(2):
            out = b[:, 0:2048].rearrange("p (k j) -> p k j", k=NCH)[:, :, gu * 128:(gu + 1) * 128]
            self.cp(eng, out, a[:, gu, :, :], [ar], [br])
        self.cp(eng, b[:, 2048:3072], ao[:, :], [aor], [br])
        self.store(WF[f, :, :], b[:, :], br)

    def stage_convert_ffn(self, l, which):
        P = self.P
        self.stage_begin()
        w_in = self.w["ffn%d_in" % which][l]
        w_out = self.w["ffn%d_out" % which][l]
        WF = self.wf[(l, which)]
        wfr = self.wf_r[(l, which)]
        s32 = Ring([self.alloc([2, NCH, 256], F32) for _ in range(2)], "cv32")
        s32o = Ring([self.alloc([2, D], F32) for _ in range(2)], "cv32o")
        s16 = Ring([self.alloc([2, 3072], BF16) for _ in range(2)], "cv16")
        engs = ["act", "dve", "pool"]
        n = 0

        def cast(eng, out, in_):
            if eng == "act":
                return lambda e: e.activation(out=out, in_=in_, func=AF.Copy)
            return lambda e: e.tensor_copy(out=out, in_=in_)

        for fp in range(NF // 2):
            f0 = 2 * fp
            a, ar = s32.next()
            ao, aor = s32o.next()
            b, br = s16.next()
            for gu in range(2):
                col = gu * DFF + f0 * 128
                src = w_in[:, col:col + 256].rearrange("(k p) j -> p k j", p=128)
                P.add("sp", lambda e, a=a, gu=gu, src=src: e.dma_start(out=a[:, gu, :, :], in_=src),
                      writes=[ar], dma=True, sem_res=ar)
            src = w_out[f0 * 128:(f0 + 2) * 128, :].rearrange("(f p) c -> p f c", p=128)
            P.add("sp", lambda e, ao=ao, src=src: e.dma_start(out=ao[:, :, :], in_=src),
                  writes=[aor], dma=True, sem_res=aor)
            for ff in range(2):
                for gu in range(2):
                    eng = engs[n % 3]
                    n += 1
                    out = b[:, ff, 0:2048].rearrange("p (k j) -> p k j", k=NCH)[:, :, gu * 128:(gu + 1) * 128]
                    in_ = a[:, gu, :, ff * 128:(ff + 1) * 128]
                    P.add(eng, cast(eng, out, in_), reads=[ar], writes=[br])
            eng = engs[n % 3]
            n += 1
            P.add(eng, cast(eng, b[:, :, 2048:3072], ao[:, :, :]), reads=[aor], writes=[br])
            dst = WF[f0:f0 + 2, :, :].rearrange("f p c -> p f c")
            P.add("sp", lambda e, b=b, dst=dst: e.dma_start(out=dst, in_=b[:, :, :]),
                  reads=[br], writes=[wfr], dma=True, sem_res=br)

    def stage_ffn(self, l, which):
        P = self.P
        self.stage_begin()
        WF = self.wf[(l, which)]
        wfr = self.wf_r[(l, which)]
        gk = self.gain_idx[("ffn%d_norm" % which, l)]
        xn = self.alloc([NCH, 1040], BF16)
        hid = Ring([self.alloc([1040], BF16) for _ in range(4)], "hid")
        wsl = Ring([self.alloc([3072], BF16) for _ in range(6)], "wsl")
        sqr = Ring([self.alloc([512], BF16) for _ in range(2)], "sq")
        rstd = self.alloc([1040], F32)
        rstd_r = Res("rstd")
        sgr = Ring([self.alloc([512], F32) for _ in range(2)], "sg")
        ps_gu = self.psring([0, 1, 2, 3], "gu")
        ps_o = self.psring([4, 5, 6], "o")
        ps_n, ps_nr = self.psum[7], self.psum_r[7]

        sts = [[(0, 16), (16, 512), (528, 512)]]
        for s in range(1, 4):
            b0 = 1040 + 1024 * (s - 1)
            sts.append([(b0, 512), (b0 + 512, 512)])
        nsq = 0
        for subs in sts:
            base = subs[0][0]
            xn_r = [Res("xn%d" % i) for i in range(len(subs))]
            for si, (c0, n) in enumerate(subs):
                lc = c0 - base
                for c in range(NCH):
                    sq, sq_r = sqr.next()
                    src = self.hT[:, c, c0:c0 + n]
                    if nsq % 2 == 0:
                        P.add("act", lambda e, sq=sq, src=src, n=n: e.activation(out=sq[:, 0:n], in_=src, func=AF.Square),
                              reads=self.h_res(c, c0, n), writes=[sq_r])
                    else:
                        P.add("pool", lambda e, sq=sq, src=src, n=n: e.tensor_tensor(out=sq[:, 0:n], in0=src, in1=src, op=ALU.mult),
                              reads=self.h_res(c, c0, n), writes=[sq_r])
                    nsq += 1
                    P.add("pe", lambda e, sq=sq, n=n, c=c: e.matmul(ps_n[:, 0:n], lhsT=self.ones_bf[:, :], rhs=sq[:, 0:n],
                                                                     start=(c == 0), stop=(c == NCH - 1)),
                          reads=[sq_r, self.ones_r], writes=[ps_nr])
                rs = rstd[:, lc:lc + n]
                P.add("dve", lambda e, rs=rs, n=n: e.tensor_scalar(out=rs, in0=ps_n[:, 0:n], scalar1=1.0 / D, scalar2=EPS,
                                                                   op0=ALU.mult, op1=ALU.add),
                      reads=[ps_nr], writes=[rstd_r])
                P.add("act", lambda e, rs=rs: e.activation(out=rs, in_=rs, func=AF.Sqrt), reads=[rstd_r], writes=[rstd_r])
                P.add("dve", lambda e, rs=rs: e.reciprocal(out=rs, in_=rs), reads=[rstd_r], writes=[rstd_r])
                for c in range(NCH):
                    P.add("dve", lambda e, c=c, c0=c0, n=n, lc=lc, rs=rs: e.scalar_tensor_tensor(
                        out=xn[:, c, lc:lc + n], in0=self.hT[:, c, c0:c0 + n], scalar=self.gains[:, gk, c:c + 1],
                        in1=rs, op0=ALU.mult, op1=ALU.mult),
                        reads=self.h_res(c, c0, n) + [rstd_r, self.gains_r], writes=[xn_r[si]])

            def GU(f):
                slot, slot_r = wsl.next()
                P.add("sp", lambda e, slot=slot, f=f: e.dma_start(out=slot[:, :], in_=WF[f, :, :]),
                      reads=[wfr], writes=[slot_r], dma=True, sem_res=slot_r)
                hb, hb_r = hid.next()
                for si, (c0, n) in enumerate(subs):
                    lc = c0 - base
                    pg, pgr = ps_gu.next()
                    pu, pur = ps_gu.next()
                    for gu, (pp, ppr) in enumerate(((pg, pgr), (pu, pur))):
                        for k in range(NCH):
                            P.add("pe", lambda e, pp=pp, slot=slot, k=k, gu=gu, lc=lc, n=n: e.matmul(
                                pp[:, 0:n], lhsT=slot[:, k * 256 + gu * 128:k * 256 + gu * 128 + 128],
                                rhs=xn[:, k, lc:lc + n], start=(k == 0), stop=(k == NCH - 1)),
                                reads=[slot_r, xn_r[si]], writes=[ppr])
                    sg, sg_r = sgr.next()
                    P.add("act", lambda e, sg=sg, pg=pg, n=n: e.activation(out=sg[:, 0:n], in_=pg[:, 0:n], func=AF.Silu),
                          reads=[pgr], writes=[sg_r])
                    P.add("dve", lambda e, sg=sg, pu=pu, hb=hb, lc=lc, n=n: e.tensor_tensor(
                        out=hb[:, lc:lc + n], in0=sg[:, 0:n], in1=pu[:, 0:n], op=ALU.mult),
                        reads=[sg_r, pur], writes=[hb_r])
                return (slot, slot_r, hb, hb_r)

            def W2(group):
                for o in range(NCH):
                    for si, (c0, n) in enumerate(subs):
                        lc = c0 - base
                        po, por = ps_o.next()
                        for gi_, (slot, slot_r, hb, hb_r) in enumerate(group):
                            P.add("pe", lambda e, po=po, slot=slot, hb=hb, o=o, lc=lc, n=n, gi_=gi_, ng=len(group): e.matmul(
                                po[:, 0:n], lhsT=slot[:, 2048 + o * 128:2048 + (o + 1) * 128], rhs=hb[:, lc:lc + n],
                                start=(gi_ == 0), stop=(gi_ == ng - 1)),
                                reads=[slot_r, hb_r], writes=[por])
                        hr = self.h_res(o, c0, n)
                        P.add("dve", lambda e, po=po, o=o, c0=c0, n=n: e.scalar_tensor_tensor(
                            out=self.hT[:, o, c0:c0 + n], in0=po[:, 0:n], scalar=0.5, in1=self.hT[:, o, c0:c0 + n],
                            op0=ALU.mult, op1=ALU.add),
                            reads=[por] + hr, writes=hr)

            prev = None
            for g in range(NF // 2):
                grp = [GU(2 * g), GU(2 * g + 1)]
                if prev is not None:
                    W2(prev)
                prev = grp
            W2(prev)

    def tiles9(self):
        return [(0, NMETA)] + [(NMETA + 512 * j, 512) for j in range(SEQ // 512)]

    def norm_tile(self, c0, n, gk, xn, xn_r, sqr, rstd, rstd_r, ps_n, ps_nr, cnt=[0]):
        P = self.P
        for c in range(NCH):
            sq, sq_r = sqr.next()
            src = self.hT[:, c, c0:c0 + n]
            if cnt[0] % 2 == 0:
                self.act(sq[:, 0:n], src, AF.Square, self.h_res(c, c0, n), [sq_r])
            else:
                self.tt("pool", sq[:, 0:n], src, src, ALU.mult, self.h_res(c, c0, n), [sq_r])
            cnt[0] += 1
            self.mm(ps_n[:, 0:n], self.ones_bf[:, :], sq[:, 0:n], c == 0, c == NCH - 1, [sq_r, self.ones_r], [ps_nr])
        rs = rstd[:, 0:n]
        self.rsqrt_inplace(rs, rstd_r, 1.0 / D, in_ap=ps_n[:, 0:n], in_reads=[ps_nr])
        for c in range(NCH):
            self.stt(xn[:, c, 0:n], self.hT[:, c, c0:c0 + n], self.gains[:, gk, c:c + 1], rs, ALU.mult, ALU.mult,
                     self.h_res(c, c0, n) + [rstd_r, self.gains_r], [xn_r])

    def load_w_bf16(self, dst, dst_r, src, ncols, slab=160):
        st = Ring([self.alloc([src.shape[0] // 128, slab], F32) for _ in range(2)], "wslab")
        engs = ["act", "dve", "pool"]
        i = 0
        for c in range(0, ncols, slab):
            w = min(slab, ncols - c)
            a, ar = st.next()
            self.load(a[:, :, 0:w], src[:, c:c + w].rearrange("(k p) j -> p k j", p=128), ar)
            self.cp(engs[i % 3], dst[:, :, c:c + w], a[:, :, 0:w], [ar], [dst_r])
            i += 1

    def stage_proj1(self, l):
        self.stage_begin()
        gk = self.gain_idx[("mix_norm", l)]
        w_in = self.w["w_in"][l]
        Wp = self.alloc([NCH, 2048], BF16)
        Wp_r = Res("Wp")
        self.load_w_bf16(Wp, Wp_r, w_in[:, 0:2048], 2048)
        xnr = Ring([self.alloc([NCH, 512], BF16) for _ in range(2)], "xn")
        sqr = Ring([self.alloc([512], BF16) for _ in range(2)], "sq")
        rstr = Ring([self.alloc([512], F32) for _ in range(2)], "rstd")
        st32 = Ring([self.alloc([512], F32) for _ in range(3)], "st32")
        st16 = Ring([self.alloc([512], BF16) for _ in range(4)], "st16")
        psr = self.psring([0, 1, 2, 3, 4, 5], "p1")
        ps_n, ps_nr = self.psum[7], self.psum_r[7]
        ne = 0
        for (c0, n) in self.tiles9():
            xn, xn_r = xnr.next()
            rstd, rstd_r = rstr.next()
            self.norm_tile(c0, n, gk, xn, xn_r, sqr, rstd, rstd_r, ps_n, ps_nr)
            for g in range(12):
                pt, pr = psr.next()
                for k in range(NCH):
                    self.mm(pt[:, 0:n], Wp[:, k, g * 128:(g + 1) * 128], xn[:, k, 0:n], k == 0, k == NCH - 1,
                            [Wp_r, xn_r], [pr])
                eng = "act" if ne % 2 == 0 else "dve"
                ne += 1
                if g < 4:
                    sb, sr = st32.next()
                    self.cp(eng, sb[:, 0:n], pt[:, 0:n], [pr], [sr])
                    self.store(self.LRU_IN[g, :, c0:c0 + n], sb[:, 0:n], sr)
                else:
                    sb, sr = st16.next()
                    self.cp(eng, sb[:, 0:n], pt[:, 0:n], [pr], [sr])
                    self.store(self.QKD[g - 4, :, c0:c0 + n], sb[:, 0:n], sr)
            for j in range(0, n, 128):
                m = min(128, n - j)
                pt, pr = psr.next()
                for k in range(NCH):
                    self.mm(pt[0:m, 0:512], xn[:, k, j:j + m], Wp[:, k, 1536:2048], k == 0, k == NCH - 1,
                            [Wp_r, xn_r], [pr])
                sb, sr = st16.next()
                eng = "act" if ne % 2 == 0 else "dve"
                ne += 1
                self.cp(eng, sb[0:m, 0:512], pt[0:m, 0:512], [pr], [sr])
                self.store(self.VD[c0 + j:c0 + j + m, :], sb[0:m, 0:512], sr)

    def stage_proj2(self, l):
        self.stage_begin()
        gk = self.gain_idx[("mix_norm", l)]
        w_in = self.w["w_in"][l]
        Wp = self.alloc([NCH, 512], BF16)
        Wp_r = Res("Wp2")
        self.P.add("pool", lambda e: e.memset(Wp[:, :, 416:480], 0.0), writes=[Wp_r])
        self.load_w_bf16(Wp, Wp_r, w_in[:, 2048:2464], 416, slab=104)
        self.cp("dve", Wp[:, :, 480:496], Wp[:, :, 400:416], [Wp_r], [Wp_r])
        self.cp("dve", Wp[:, :, 496:512], Wp[:, :, 384:400], [Wp_r], [Wp_r])
        Wq = self.alloc([2, 768], BF16)
        Wq_r = Res("Wq")
        self.load_w_bf16(Wq, Wq_r, self.w["w_uq"][l], 384, slab=192)
        for h in range(4):
            b = 384 + h * 96
            a = h * 96
            self.cp("dve", Wq[:, :, b:b + 64], Wq[:, :, a:a + 64], [Wq_r], [Wq_r])
            self.cp("dve", Wq[:, :, b + 64:b + 80], Wq[:, :, a + 80:a + 96], [Wq_r], [Wq_r])
            self.cp("dve", Wq[:, :, b + 80:b + 96], Wq[:, :, a + 64:a + 80], [Wq_r], [Wq_r])
        Wkv = self.alloc([512], BF16)
        Wkv_r = Res("Wkv")
        kv32 = self.alloc([512], F32)
        kv32_r = Res("kv32")
        self.load(kv32[:, :], self.w["w_ukv"][l][:, :], kv32_r)
        self.cp("dve", Wkv[:, :].rearrange("p (t h d) -> p h t d", t=2, h=4),
                kv32[:, :].rearrange("p (h t d) -> p h t d", h=4, t=2), [kv32_r], [Wkv_r])
        gq = self.alloc([2], F32)
        gkv = self.alloc([1], F32)
        g_r = Res("gqkv")
        self.load(gq[:, :], self.w["q_norm"][l].rearrange("(c p) -> p c", p=128), g_r)
        self.load(gkv[:, :], self.w["kv_norm"][l].rearrange("(c p) -> p c", p=128), g_r)

        xnr = Ring([self.alloc([NCH, 512], BF16) for _ in range(1)], "xn")
        sqr = Ring([self.alloc([512], BF16) for _ in range(2)], "sq")
        rstr = Ring([self.alloc([512], F32) for _ in range(2)], "rstd")
        cq = self.alloc([2, 512], F32)
        cq_r = Res("cq")
        ckv = self.alloc([512], F32)
        ckv_r = Res("ckv")
        cqn = self.alloc([2, 512], BF16)
        cqn_r = Res("cqn")
        ckvn = self.alloc([512], BF16)
        ckvn_r = Res("ckvn")
        rsq = self.alloc([512], F32)
        rsq_r = Res("rsq")
        rskv = self.alloc([512], F32)
        rskv_r = Res("rskv")
        cs = self.alloc([2, 512], F32)
        cs_r = Res("cs")
        t1r = Ring([self.alloc([512], F32) for _ in range(2)], "t1")
        t2r = Ring([self.alloc([512], F32) for _ in range(2)], "t2")
        qst = Ring([self.alloc([4, 512], BF16) for _ in range(1)], "qst")
        kst = Ring([self.alloc([4, 512], BF16) for _ in range(1)], "kst")
        vst = Ring([self.alloc([4, 65], BF16) for _ in range(2)], "vst")
        psr = self.psring([0, 1, 2, 3, 4, 5], "p2")
        ps_n, ps_nr = self.psum[7], self.psum_r[7]
        ps_m, ps_mr = self.psum[6], self.psum_r[6]
        for (c0, n) in self.tiles9():
            xn, xn_r = xnr.next()
            rstd, rstd_r = rstr.next()
            self.norm_tile(c0, n, gk, xn, xn_r, sqr, rstd, rstd_r, ps_n, ps_nr)
            self.load(cs[0:96, :, 0:n], self.rope_cs[:, :, c0:c0 + n], cs_r)
            for j in range(3):
                pt, pr = psr.next()
                for k in range(NCH):
                    self.mm(pt[:, 0:n], Wp[:, k, j * 128:(j + 1) * 128], xn[:, k, 0:n], k == 0, k == NCH - 1,
                            [Wp_r, xn_r], [pr])
                if j < 2:
                    self.cp("act", cq[:, j, 0:n], pt[:, 0:n], [pr], [cq_r])
                else:
                    self.cp("act", ckv[:, 0:n], pt[:, 0:n], [pr], [ckv_r])
            for j in range(2):
                sq, sq_r = sqr.next()
                self.tt("pool", sq[:, 0:n], cq[:, j, 0:n], cq[:, j, 0:n], ALU.mult, [cq_r], [sq_r])
                self.mm(ps_m[:, 0:n], self.ones_bf[:, :], sq[:, 0:n], j == 0, j == 1, [sq_r, self.ones_r], [ps_mr])
            self.rsqrt_inplace(rsq[:, 0:n], rsq_r, 1.0 / 256, in_ap=ps_m[:, 0:n], in_reads=[ps_mr])
            for j in range(2):
                self.stt(cqn[:, j, 0:n], cq[:, j, 0:n], gq[:, j:j + 1], rsq[:, 0:n], ALU.mult, ALU.mult,
                         [cq_r, g_r, rsq_r], [cqn_r])
            sq, sq_r = sqr.next()
            self.tt("pool", sq[:, 0:n], ckv[:, 0:n], ckv[:, 0:n], ALU.mult, [ckv_r], [sq_r])
            self.mm(ps_m[:, 0:n], self.ones_bf[:, :], sq[:, 0:n], True, True, [sq_r, self.ones_r], [ps_mr])
            self.rsqrt_inplace(rskv[:, 0:n], rskv_r, 1.0 / 128, in_ap=ps_m[:, 0:n], in_reads=[ps_mr])
            self.stt(ckvn[:, 0:n], ckv[:, 0:n], gkv[:, 0:1], rskv[:, 0:n], ALU.mult, ALU.mult,
                     [ckv_r, g_r, rskv_r], [ckvn_r])
            qs, qs_r = qst.next()
            for h in range(4):
                pa, par = psr.next()
                pb, pbr = psr.next()
                for k in range(2):
                    self.mm(pa[0:96, 0:n], Wq[:, k, h * 96:(h + 1) * 96], cqn[:, k, 0:n], k == 0, k == 1, [Wq_r, cqn_r], [par])
                for k in range(2):
                    self.mm(pb[0:96, 0:n], Wq[:, k, 384 + h * 96:384 + (h + 1) * 96], cqn[:, k, 0:n], k == 0, k == 1,
                            [Wq_r, cqn_r], [pbr])
                t1, t1_r = t1r.next()
                t2, t2_r = t2r.next()
                self.tt("dve", t1[0:96, 0:n], pa[0:96, 0:n], cs[0:96, 0, 0:n], ALU.mult, [par, cs_r], [t1_r])
                self.tt("dve", t2[0:96, 0:n], pb[0:96, 0:n], cs[0:96, 1, 0:n], ALU.mult, [pbr, cs_r], [t2_r])
                self.tt("pool", qs[0:96, h, 0:n], t1[0:96, 0:n], t2[0:96, 0:n], ALU.add, [t1_r, t2_r], [qs_r])
            self.store(self.QM[:, :, c0:c0 + n].rearrange("h r t -> r h t"), qs[0:96, :, 0:n], qs_r)
            ks, ks_r = kst.next()
            for h in range(4):
                pt, pr = psr.next()
                self.mm(pt[0:64, 0:n], Wkv[:, h * 64:(h + 1) * 64], ckvn[:, 0:n], True, True, [Wkv_r, ckvn_r], [pr])
                self.cp("act", ks[0:64, h, 0:n], pt[0:64, 0:n], [pr], [ks_r])
            pa, par = psr.next()
            pb, pbr = psr.next()
            for k in range(NCH):
                self.mm(pa[0:96, 0:n], Wp[:, k, 320:416], xn[:, k, 0:n], k == 0, k == NCH - 1, [Wp_r, xn_r], [par])
            for k in range(NCH):
                self.mm(pb[0:96, 0:n], Wp[:, k, 416:512], xn[:, k, 0:n], k == 0, k == NCH - 1, [Wp_r, xn_r], [pbr])
            t1, t1_r = t1r.next()
            t2, t2_r = t2r.next()
            self.tt("dve", t1[64:96, 0:n], pa[64:96, 0:n], cs[64:96, 0, 0:n], ALU.mult, [par, cs_r], [t1_r])
            self.tt("dve", t2[64:96, 0:n], pb[64:96, 0:n], cs[64:96, 1, 0:n], ALU.mult, [pbr, cs_r], [t2_r])
            self.tt("pool", t1[64:96, 0:n], t1[64:96, 0:n], t2[64:96, 0:n], ALU.add, [t1_r, t2_r], [t1_r])
            for h in range(4):
                self.cp("pool" if h % 2 else "act", ks[64:96, h, 0:n], t1[64:96, 0:n], [t1_r], [ks_r])
            self.store(self.KM[:, :, c0:c0 + n].rearrange("h r t -> r h t"), ks[0:96, :, 0:n], ks_r)
            for j in range(0, n, 128):
                m = min(128, n - j)
                pt, pr = psr.next()
                self.mm(pt[0:m, 0:256], ckvn[:, j:j + m], Wkv[:, 256:512], True, True, [Wkv_r, ckvn_r], [pr])
                vs, vs_r = vst.next()
                self.P.add("pool", lambda e, vs=vs: e.memset(vs[:, :, 64:65], 1.0), writes=[vs_r])
                self.cp("act", vs[0:m, :, 0:64], pt[0:m, 0:256].rearrange("p (h d) -> p h d", h=4), [pr], [vs_r])
                self.store(self.VM[c0 + j:c0 + j + m, :], vs[0:m, :, :].rearrange("p h d -> p (h d)"), vs_r)

    def stage_lru(self, l):
        self.stage_begin()
        P = self.P
        pr_ = Res("lrup")
        convw = self.alloc([2, 4], F32)
        convb = self.alloc([2], F32)
        ba = self.alloc([2], F32)
        bx = self.alloc([2], F32)
        lam = self.alloc([2], F32)
        gl = self.alloc([2], F32)
        for c in range(2):
            for j in range(4):
                self.load(convw[:, c, j:j + 1], self.w["conv_w"][l][j, c * 128:(c + 1) * 128].rearrange("(p o) -> p o", o=1), pr_)
        self.load(convb[:, :], self.w["conv_b"][l].rearrange("(c p) -> p c", p=128), pr_)
        self.load(ba[:, :], self.w["lru_ba"][l].rearrange("n d -> (n d)").rearrange("(c p) -> p c", p=128), pr_)
        self.load(bx[:, :], self.w["lru_bx"][l].rearrange("n d -> (n d)").rearrange("(c p) -> p c", p=128), pr_)
        self.load(lam[:, :], self.w["lru_lambda"][l].rearrange("(c p) -> p c", p=128), pr_)
        self.load(gl[:, :], self.w["lru_out_norm"][l].rearrange("(c p) -> p c", p=128), pr_)
        nsp = self.alloc([2], F32)
        nsp2 = self.alloc([2], F32)
        self.act(nsp[:, :], lam[:, :], AF.Exp, [pr_], [pr_], scale=-1.0)
        self.ts("dve", nsp[:, :], nsp[:, :], 1.0, None, ALU.add, None, [pr_], [pr_])
        self.act(nsp[:, :], nsp[:, :], AF.Ln, [pr_], [pr_])
        self.ts("dve", nsp2[:, :], nsp[:, :], -16.0, None, ALU.mult, None, [pr_], [pr_])
        self.ts("dve", nsp[:, :], nsp[:, :], -8.0, None, ALU.mult, None, [pr_], [pr_])
        bd32 = self.alloc([2, 2, 128], F32)
        bd = self.alloc([2, 2, 128], BF16)
        bd_r = Res("bd")
        P.add("pool", lambda e: e.memset(bd32[:, :, :, :], 0.0), writes=[bd_r])
        for wi, nm in enumerate(("lru_wa", "lru_wx")):
            for nb in range(4):
                c, hb = nb // 2, nb % 2
                self.load(bd32[hb * 64:(hb + 1) * 64, wi, c, hb * 64:(hb + 1) * 64], self.w[nm][l][nb, :, :], bd_r)
        self.cp("dve", bd[:, :, :, :], bd32[:, :, :, :], [bd_r], [bd_r])
        carry = self.alloc([2], F32)
        carry_r = Res("carry")
        NB = 515
        uar = Ring([self.alloc([2, NB], F32) for _ in range(2)], "ua")
        gar = Ring([self.alloc([2, 512], F32) for _ in range(2)], "ga")
        fr = Ring([self.alloc([512], F32) for _ in range(10)], "lf")
        ucb_r = Ring([self.alloc([512], BF16) for _ in range(2)], "ucb")
        sqr = Ring([self.alloc([512], BF16) for _ in range(2)], "sq")
        yp = self.alloc([2, 512], F32)
        yp_r = Res("yp")
        rstd = self.alloc([512], F32)
        rstd_r = Res("rstd")
        yst = Ring([self.alloc([2, 512], BF16) for _ in range(2)], "yst")
        psr = self.psring([0, 1, 2, 3], "lru")
        ps_n, ps_nr = self.psum[7], self.psum_r[7]
        first = True
        for (c0, n) in self.tiles9():
            ua, ua_r = uar.next()
            ga, ga_r = gar.next()
            if first:
                P.add("pool", lambda e, ua=ua: e.memset(ua[:, :, 0:3], 0.0), writes=[ua_r])
                self.load(ua[:, :, 3:3 + n], self.LRU_IN[2:4, :, c0:c0 + n].rearrange("c p t -> p c t"), ua_r)
            else:
                self.load(ua[:, :, 0:3 + n], self.LRU_IN[2:4, :, c0 - 3:c0 + n].rearrange("c p t -> p c t"), ua_r)
            self.load(ga[:, :, 0:n], self.LRU_IN[0:2, :, c0:c0 + n].rearrange("c p t -> p c t"), ga_r)
            for j in range(2):
                uc, uc_r = fr.next()
                self.ts("dve", uc[:, 0:n], ua[:, j, 0:n], convw[:, j, 0:1], convb[:, j:j + 1], ALU.mult, ALU.add,
                        [ua_r, pr_], [uc_r])
                for t in range(1, 4):
                    self.stt(uc[:, 0:n], ua[:, j, t:t + n], convw[:, j, t:t + 1], uc[:, 0:n], ALU.mult, ALU.add,
                             [ua_r, pr_, uc_r], [uc_r])
                ucb, ucb_rr = ucb_r.next()
                self.cp("act", ucb[:, 0:n], uc[:, 0:n], [uc_r], [ucb_rr])
                pa, par = psr.next()
                px, pxr = psr.next()
                self.mm(pa[:, 0:n], bd[:, 0, j, :], ucb[:, 0:n], True, True, [bd_r, ucb_rr], [par])
                self.mm(px[:, 0:n], bd[:, 1, j, :], ucb[:, 0:n], True, True, [bd_r, ucb_rr], [pxr])
                r, r_r = fr.next()
                ig, ig_r = fr.next()
                self.act(r[:, 0:n], pa[:, 0:n], AF.Sigmoid, [par, pr_], [r_r], bias=ba[:, j:j + 1])
                self.act(ig[:, 0:n], px[:, 0:n], AF.Sigmoid, [pxr, pr_], [ig_r], bias=bx[:, j:j + 1])
                a, a_r = fr.next()
                a2, a2_r = fr.next()
                self.act(a[:, 0:n], r[:, 0:n], AF.Exp, [r_r, pr_], [a_r], scale=nsp[:, j:j + 1])
                self.act(a2[:, 0:n], r[:, 0:n], AF.Exp, [r_r, pr_], [a2_r], scale=nsp2[:, j:j + 1])
                self.ts("pool", a2[:, 0:n], a2[:, 0:n], -1.0, 1.0, ALU.mult, ALU.add, [a2_r], [a2_r])
                self.act(a2[:, 0:n], a2[:, 0:n], AF.Sqrt, [a2_r], [a2_r])
                self.tt("pool", ig[:, 0:n], ig[:, 0:n], uc[:, 0:n], ALU.mult, [ig_r, uc_r], [ig_r])
                self.tt("pool", ig[:, 0:n], ig[:, 0:n], a2[:, 0:n], ALU.mult, [ig_r, a2_r], [ig_r])
                hs, hs_r = fr.next()
                if first:
                    P.add("dve", lambda e, hs=hs, a=a, ig=ig, n=n: e.tensor_tensor_scan(
                        out=hs[:, 0:n], data0=a[:, 0:n], data1=ig[:, 0:n], initial=0.0, op0=ALU.mult, op1=ALU.add),
                        reads=[a_r, ig_r], writes=[hs_r])
                else:
                    P.add("dve", lambda e, hs=hs, a=a, ig=ig, n=n, j=j: e.tensor_tensor_scan(
                        out=hs[:, 0:n], data0=a[:, 0:n], data1=ig[:, 0:n], initial=carry[:, j:j + 1],
                        op0=ALU.mult, op1=ALU.add),
                        reads=[a_r, ig_r, carry_r], writes=[hs_r])
                self.cp("dve", carry[:, j:j + 1], hs[:, n - 1:n], [hs_r], [carry_r])
                g = ga[:, j, 0:n]
                t, t_r = fr.next()
                self.tt("pool", t[:, 0:n], g, g, ALU.mult, [ga_r], [t_r])
                self.ts("pool", t[:, 0:n], t[:, 0:n], 0.044715, 1.0, ALU.mult, ALU.add, [t_r], [t_r])
                self.tt("pool", t[:, 0:n], t[:, 0:n], g, ALU.mult, [t_r, ga_r], [t_r])
                self.act(t[:, 0:n], t[:, 0:n], AF.Sigmoid, [t_r], [t_r], scale=1.5957691216057308)
                self.tt("pool", t[:, 0:n], t[:, 0:n], g, ALU.mult, [t_r, ga_r], [t_r])
                self.tt("dve", yp[:, j, 0:n], hs[:, 0:n], t[:, 0:n], ALU.mult, [hs_r, t_r], [yp_r])
                sq, sq_r = sqr.next()
                self.tt("pool", sq[:, 0:n], yp[:, j, 0:n], yp[:, j, 0:n], ALU.mult, [yp_r], [sq_r])
                self.mm(ps_n[:, 0:n], self.ones_bf[:, :], sq[:, 0:n], j == 0, j == 1, [sq_r, self.ones_r], [ps_nr])
            self.rsqrt_inplace(rstd[:, 0:n], rstd_r, 1.0 / 256, in_ap=ps_n[:, 0:n], in_reads=[ps_nr])
            ys, ys_r = yst.next()
            for j in range(2):
                self.stt(ys[:, j, 0:n], yp[:, j, 0:n], gl[:, j:j + 1], rstd[:, 0:n], ALU.mult, ALU.mult,
                         [yp_r, pr_, rstd_r], [ys_r])
            self.store(self.YT[0:2, :, c0:c0 + n].rearrange("c p t -> p c t"), ys[:, :, 0:n], ys_r)
            first = False

    def attn_sweep(self, qi, Kt, Qt, r0, r1, kq_r, V, v_r, dvp, obanks, scale, s_ring, pt_ring, tri, tri_r):
        if qi < 0:
            q0, nq, nsub = 0, NMETA, 1
            chunks = [(0, 0, NMETA, 0)]
        else:
            q0, nq, nsub = NMETA + 512 * qi, 512, 4
            chunks = [(0, 0, NMETA, -1)] + [(1 + i, NMETA + 128 * i, 128, -1) for i in range(4 * qi)]
            chunks += [(1 + 4 * qi + c, NMETA + 128 * (4 * qi + c), 128, c) for c in range(4)]
        started = set()
        for (kc, kcol, nk, diag) in chunks:
            s0 = max(diag, 0)
            qa = q0 + 128 * s0 if qi >= 0 else 0
            nqa = nq - 128 * s0 if qi >= 0 else nq
            ps, ps_r = s_ring.next()
            self.mm(ps[0:nk, 0:nqa], Kt[r0:r1, kcol:kcol + nk], Qt[r0:r1, qa:qa + nqa], True, True, [kq_r], [ps_r])
            pt, pt_r = pt_ring.next()
            self.act(pt[0:nk, 0:nqa], ps[0:nk, 0:nqa], AF.Exp, [ps_r], [pt_r], scale=scale)
            if diag >= 0:
                w = min(128, nqa)
                self.tt("dve", pt[0:nk, 0:w], pt[0:nk, 0:w], tri[0:nk, 0:w], ALU.mult, [pt_r, tri_r], [pt_r])
            self.flush_pv()

            def pv(s0=s0, pt=pt, pt_r=pt_r, nk=nk, kc=kc):
                for s in range(s0, nsub):
                    ob, ob_r, oc = obanks[s]
                    ws = min(128, nq)
                    lo = (s - s0) * 128
                    st = id(ob) not in started
                    started.add(id(ob))
                    self.mm(ob[0:ws, oc:oc + dvp], pt[0:nk, lo:lo + ws], V[0:nk, kc, 0:dvp], st, False,
                            [pt_r, v_r], [ob_r], skip_group_check=True)
            self._pend = pv
            self.run_deferred(4)

    _pend = None
    _deferred = None

    def defer(self, fn):
        if self._deferred is None:
            self._deferred = []
        self._deferred.append(fn)

    def run_deferred(self, k=None):
        d = self._deferred or []
        n = len(d) if k is None else min(k, len(d))
        for _ in range(n):
            d.pop(0)()

    def flush_pv(self):
        if self._pend is not None:
            p = self._pend
            self._pend = None
            p()

    def load_consts_attn(self):
        tri32 = self.alloc([128], F32)
        tri = self.alloc([128], BF16)
        tri_r = Res("tri")
        self.load(tri32[:, :], self.tri_in[:, :], tri_r)
        self.cp("dve", tri[:, :], tri32[:, :], [tri_r], [tri_r])
        idb32 = self.alloc([128], F32)
        idb = self.alloc([128], BF16)
        idb_r = Res("idb")
        self.cp("dve", idb[:, :], self.identf[:, :], [self.identf_r], [idb_r])
        return tri, tri_r, idb, idb_r

    def stage_diff(self, l):
        self.stage_begin()
        P = self.P
        lam_init = 0.8 - 0.6 * math.exp(-0.3 * l)
        tri, tri_r, idb, idb_r = self.load_consts_attn()
        lv = self.alloc([4, 64], F32)
        lv_r = Res("lv")
        for i, nm in enumerate(("lam_q1", "lam_k1", "lam_q2", "lam_k2")):
            self.load(lv[:, i, :], self.w[nm][l].partition_broadcast(128), lv_r)
        lsc = self.alloc([8], F32)
        self.tt("dve", lv[:, 0, :], lv[:, 0, :], lv[:, 1, :], ALU.mult, [lv_r], [lv_r])
        self.tt("dve", lv[:, 2, :], lv[:, 2, :], lv[:, 3, :], ALU.mult, [lv_r], [lv_r])
        P.add("dve", lambda e: e.reduce_sum(out=lsc[:, 0:1], in_=lv[:, 0, :], axis=AX.X), reads=[lv_r], writes=[lv_r])
        P.add("dve", lambda e: e.reduce_sum(out=lsc[:, 1:2], in_=lv[:, 2, :], axis=AX.X), reads=[lv_r], writes=[lv_r])
        self.act(lsc[:, 0:2], lsc[:, 0:2], AF.Exp, [lv_r], [lv_r])
        self.tt("dve", lsc[:, 2:3], lsc[:, 1:2], lsc[:, 0:1], ALU.subtract, [lv_r], [lv_r])
        self.ts("dve", lsc[:, 2:3], lsc[:, 2:3], -lam_init, None, ALU.add, None, [lv_r], [lv_r])
        gsub = self.alloc([128], F32)
        self.load(gsub[:, :], self.w["diff_subln"][l].partition_broadcast(128), lv_r)
        self.ts("dve", gsub[:, :], gsub[:, :], 1.0 - lam_init, None, ALU.mult, None, [lv_r], [lv_r])

        Kt = self.alloc([T], BF16)
        Qm = [self.alloc([T], BF16) for _ in range(2)]
        kq_r = Res("kq")
        P.add("pool", lambda e: e.memset(Qm[0][64:128, :], 0.0), writes=[kq_r])
        P.add("pool", lambda e: e.memset(Qm[1][0:64, :], 0.0), writes=[kq_r])
        V = self.alloc([33, 129], BF16)
        v_r = Res("V")
        pt_ring = Ring([self.alloc([512], BF16) for _ in range(3)], "pt")
        o1r = Ring([self.alloc([128], F32) for _ in range(8)], "o1")
        o2r = Ring([self.alloc([128], F32) for _ in range(8)], "o2")
        scr = Ring([self.alloc([128], F32) for _ in range(2)], "scr")
        smr = Ring([self.alloc([8], F32) for _ in range(8)], "sm")
        ybr = Ring([self.alloc([128], BF16) for _ in range(4)], "yb")
        ytr = Ring([self.alloc([512], BF16) for _ in range(3)], "yts")
        s_ring = self.psring([0, 1, 2], "s")
        tps = self.psum[7][:, :].bitcast(BF16)
        tps_r = self.psum_r[7]
        cb = self.conv_bufs()
        n_slots = 4 * (1 + SEQ // 512)
        need_now = [j for j in self.conv_queue if j[0] == l or (j[0] == l + 1 and j[1] == 1)]
        slot_i = 0
        for h in range(4):
            self.load(Qm[0][0:64, :], self.QKD[h, 0:64, :], kq_r)
            self.load(Qm[1][64:128, :], self.QKD[h, 64:128, :], kq_r)
            self.load(Kt[:, :], self.QKD[4 + h, :, :], kq_r)
            P.add("pool", lambda e: e.memset(V[:, :, 128:129], 1.0), writes=[v_r])
            self.load(V[0:NMETA, 0, 0:128], self.VD[0:NMETA, h * 128:(h + 1) * 128], v_r)
            for i0 in range(0, 32, 8):
                self.load(V[:, 1 + i0:9 + i0, 0:128],
                          self.VD[NMETA + 128 * i0:NMETA + 128 * (i0 + 8), h * 128:(h + 1) * 128].rearrange("(i p) d -> p i d", p=128), v_r)
            for qi in range(-1, SEQ // 512):
                nsub = 1 if qi < 0 else 4
                ws = NMETA if qi < 0 else 128
                q0 = 0 if qi < 0 else NMETA + 512 * qi
                ob = {}
                for m in range(2):
                    banks = [(self.psum[3 + 2 * m], self.psum_r[3 + 2 * m], 0), (self.psum[3 + 2 * m], self.psum_r[3 + 2 * m], 129),
                             (self.psum[4 + 2 * m], self.psum_r[4 + 2 * m], 0), (self.psum[4 + 2 * m], self.psum_r[4 + 2 * m], 129)]
                    ob[m] = banks
                    self.attn_sweep(qi, Kt, Qm[m], 0, 128, kq_r, V, v_r, 129, banks, 0.125, s_ring, pt_ring, tri, tri_r)
                self.flush_pv()
                left = n_slots - slot_i
                k = -(-len(need_now) // left) if left > 0 else len(need_now)
                for _ in range(k):
                    if need_now:
                        job = need_now.pop(0)
                        self.conv_queue.remove(job)
                        self.conv_job(job, cb)
                slot_i += 1
                yts, yts_r = ytr.next()
                for s in range(nsub):
                    sm, sm_r = smr.next()
                    o1, o1_r = o1r.next()
                    o2, o2_r = o2r.next()
                    for m, (o, o_r) in enumerate(((o1, o1_r), (o2, o2_r))):
                        b, b_r, oc = ob[m][s]
                        P.add("dve", lambda e, sm=sm, b=b, oc=oc, m=m, ws=ws: e.reciprocal(
                            out=sm[0:ws, m:m + 1], in_=b[0:ws, oc + 128:oc + 129]), reads=[b_r], writes=[sm_r])
                        self.ts("dve", o[0:ws, :], b[0:ws, oc:oc + 128], sm[0:ws, m:m + 1], None, ALU.mult, None, [b_r, sm_r], [o_r])

                    def tail(s=s, sm=sm, sm_r=sm_r, o1=o1, o1_r=o1_r, o2=o2, o2_r=o2_r, ws=ws, yts=yts, yts_r=yts_r):
                        sc, sc_r = scr.next()
                        yb, yb_r = ybr.next()
                        return [
                            lambda: self.stt(o1[0:ws, :], o2[0:ws, :], lsc[0:ws, 2:3], o1[0:ws, :], ALU.mult, ALU.add,
                                             [o1_r, o2_r, lv_r], [o1_r]),
                            lambda: self.tt("dve", sc[0:ws, :], o1[0:ws, :], o1[0:ws, :], ALU.mult, [o1_r], [sc_r]),
                            lambda: P.add("dve", lambda e: e.reduce_sum(out=sm[0:ws, 2:3], in_=sc[0:ws, :], axis=AX.X),
                                          reads=[sc_r], writes=[sm_r]),
                            lambda: self.ts("dve", sm[0:ws, 2:3], sm[0:ws, 2:3], 1.0 / 128, EPS, ALU.mult, ALU.add, [sm_r], [sm_r]),
                            lambda: self.act(sm[0:ws, 2:3], sm[0:ws, 2:3], AF.Ln, [sm_r], [sm_r]),
                            lambda: self.act(sm[0:ws, 2:3], sm[0:ws, 2:3], AF.Exp, [sm_r], [sm_r], scale=-0.5),
                            lambda: self.stt(yb[0:ws, :], o1[0:ws, :], sm[0:ws, 2:3], gsub[0:ws, :], ALU.mult, ALU.mult,
                                             [o1_r, sm_r, lv_r], [yb_r]),
                            lambda: P.add("pe", lambda e: e.transpose(out=tps[:, s * 128:s * 128 + ws], in_=yb[0:ws, :],
                                                                       identity=idb[0:ws, 0:ws]),
                                          reads=[yb_r, idb_r], writes=[tps_r]),
                            lambda: self.cp("dve", yts[:, s * 128:s * 128 + ws], tps[:, s * 128:s * 128 + ws], [tps_r], [yts_r]),
                        ]
                    for fn in tail():
                        self.defer(fn)
                nq = NMETA if qi < 0 else 512
                self.defer(lambda h=h, q0=q0, nq=nq, yts=yts, yts_r=yts_r: self.store(self.YT[2 + h, :, q0:q0 + nq], yts[:, 0:nq], yts_r))
        self.run_deferred()

    def stage_mla(self, l):
        self.stage_begin()
        P = self.P
        tri, tri_r, idb, idb_r = self.load_consts_attn()
        gm = self.alloc([256], F32)
        gm_r = Res("gm")
        self.load(gm[:, :], self.w["mla_out_norm"][l].partition_broadcast(128), gm_r)
        Kt = self.alloc([4, T], BF16)
        kq_r = Res("kq")
        V = self.alloc([33, 4, 65], BF16)
        v_r = Res("V")
        self.load(Kt[0:96, :, :], self.KM[:, :, :].rearrange("h r t -> r h t"), kq_r)
        Vf = V.rearrange("p i h d -> p i (h d)")
        self.load(Vf[0:NMETA, 0, :], self.VM[0:NMETA, :], v_r)
        for i0 in range(0, 32, 8):
            self.load(Vf[:, 1 + i0:9 + i0, :],
                      self.VM[NMETA + 128 * i0:NMETA + 128 * (i0 + 8), :].rearrange("(i p) d -> p i d", p=128), v_r)
        qtr = Ring([self.alloc([4, 512], BF16) for _ in range(2)], "qt")
        pt_ring = Ring([self.alloc([512], BF16) for _ in range(3)], "pt")
        ocr = Ring([self.alloc([256], F32) for _ in range(5)], "oc")
        scr = Ring([self.alloc([256], F32) for _ in range(1)], "scr")
        smr = Ring([self.alloc([8], F32) for _ in range(8)], "sm")
        ybr = Ring([self.alloc([256], BF16) for _ in range(3)], "yb")
        ytr = Ring([self.alloc([2, 512], BF16) for _ in range(2)], "yts")
        s_ring = self.psring([0, 1, 2], "s")
        tps = self.psum[7][:, :].bitcast(BF16)
        tps_r = self.psum_r[7]
        scale = 96.0 ** -0.5
        for qi in range(-1, SEQ // 512):
            nsub = 1 if qi < 0 else 4
            ws = NMETA if qi < 0 else 128
            q0 = 0 if qi < 0 else NMETA + 512 * qi
            nq = NMETA if qi < 0 else 512
            qt, qt_r = qtr.next()
            self.load(qt[0:96, :, 0:nq], self.QM[:, :, q0:q0 + nq].rearrange("h r t -> r h t"), qt_r)
            ob = {}
            for h in range(4):
                banks = [(self.psum[3 + h], self.psum_r[3 + h], 65 * s) for s in range(4)]
                ob[h] = banks
                self.attn_sweep_local(qi, Kt[:, h, :], qt[:, h, :], kq_r, qt_r, V[:, :, h, :], v_r, 65, banks, scale,
                                      s_ring, pt_ring, tri, tri_r)
            self.flush_pv()
            yts, yts_r = ytr.next()
            for s in range(nsub):
                sm, sm_r = smr.next()
                oc, oc_r = ocr.next()
                for h in range(4):
                    b, b_r, c = ob[h][s]
                    P.add("dve", lambda e, sm=sm, b=b, c=c, h=h, ws=ws: e.reciprocal(
                        out=sm[0:ws, h:h + 1], in_=b[0:ws, c + 64:c + 65]), reads=[b_r], writes=[sm_r])
                    self.ts("dve", oc[0:ws, h * 64:(h + 1) * 64], b[0:ws, c:c + 64], sm[0:ws, h:h + 1], None, ALU.mult, None,
                            [b_r, sm_r], [oc_r])

                def tail(s=s, sm=sm, sm_r=sm_r, oc=oc, oc_r=oc_r, ws=ws, yts=yts, yts_r=yts_r):
                    sc, sc_r = scr.next()
                    yb, yb_r = ybr.next()
                    fns = [
                        lambda: self.tt("dve", sc[0:ws, :], oc[0:ws, :], oc[0:ws, :], ALU.mult, [oc_r], [sc_r]),
                        lambda: P.add("dve", lambda e: e.reduce_sum(out=sm[0:ws, 4:5], in_=sc[0:ws, :], axis=AX.X),
                                      reads=[sc_r], writes=[sm_r]),
                        lambda: self.ts("dve", sm[0:ws, 4:5], sm[0:ws, 4:5], 1.0 / 256, EPS, ALU.mult, ALU.add, [sm_r], [sm_r]),
                        lambda: self.act(sm[0:ws, 4:5], sm[0:ws, 4:5], AF.Ln, [sm_r], [sm_r]),
                        lambda: self.act(sm[0:ws, 4:5], sm[0:ws, 4:5], AF.Exp, [sm_r], [sm_r], scale=-0.5),
                        lambda: self.stt(yb[0:ws, :], oc[0:ws, :], sm[0:ws, 4:5], gm[0:ws, :], ALU.mult, ALU.mult,
                                         [oc_r, sm_r, gm_r], [yb_r]),
                    ]
                    for c in range(2):
                        fns.append(lambda c=c: P.add("pe", lambda e: e.transpose(
                            out=tps[:, c * 512 + s * 128:c * 512 + s * 128 + ws], in_=yb[0:ws, c * 128:(c + 1) * 128],
                            identity=idb[0:ws, 0:ws]), reads=[yb_r, idb_r], writes=[tps_r]))
                        fns.append(lambda c=c: self.cp("dve", yts[:, c, s * 128:s * 128 + ws],
                                                       tps[:, c * 512 + s * 128:c * 512 + s * 128 + ws], [tps_r], [yts_r]))
                    return fns
                for fn in tail():
                    self.defer(fn)
            self.defer(lambda q0=q0, nq=nq, yts=yts, yts_r=yts_r: self.store(
                self.YT[6:8, :, q0:q0 + nq].rearrange("c p t -> p c t"), yts[:, :, 0:nq], yts_r))
        self.run_deferred()

    def attn_sweep_local(self, qi, Kt, Qtile, kq_r, qt_r, V, v_r, dvp, obanks, scale, s_ring, pt_ring, tri, tri_r):
        if qi < 0:
            nq, nsub = NMETA, 1
            chunks = [(0, 0, NMETA, 0)]
        else:
            nq, nsub = 512, 4
            chunks = [(0, 0, NMETA, -1)] + [(1 + i, NMETA + 128 * i, 128, -1) for i in range(4 * qi)]
            chunks += [(1 + 4 * qi + c, NMETA + 128 * (4 * qi + c), 128, c) for c in range(4)]
        started = set()
        for (kc, kcol, nk, diag) in chunks:
            s0 = max(diag, 0)
            qa = 128 * s0 if qi >= 0 else 0
            nqa = nq - qa
            ps, ps_r = s_ring.next()
            self.mm(ps[0:nk, 0:nqa], Kt[0:96, kcol:kcol + nk], Qtile[0:96, qa:qa + nqa], True, True, [kq_r, qt_r], [ps_r])
            pt, pt_r = pt_ring.next()
            self.act(pt[0:nk, 0:nqa], ps[0:nk, 0:nqa], AF.Exp, [ps_r], [pt_r], scale=scale)
            if diag >= 0:
                w = min(128, nqa)
                self.tt("dve", pt[0:nk, 0:w], pt[0:nk, 0:w], tri[0:nk, 0:w], ALU.mult, [pt_r, tri_r], [pt_r])
            self.flush_pv()

            def pv(s0=s0, pt=pt, pt_r=pt_r, nk=nk, kc=kc):
                for s in range(s0, nsub):
                    ob, ob_r, oc = obanks[s]
                    ws = min(128, nq)
                    lo = (s - s0) * 128
                    st = id(ob) not in started
                    started.add(id(ob))
                    self.mm(ob[0:ws, oc:oc + dvp], pt[0:nk, lo:lo + ws], V[0:nk, kc, 0:dvp], st, False,
                            [pt_r, v_r], [ob_r], skip_group_check=True)
            self._pend = pv
            self.run_deferred(4)

    def stage_oproj(self, l):
        self.stage_begin()
        Wo = self.alloc([NCH, D], BF16)
        Wo_r = Res("Wo")
        self.load_w_bf16(Wo, Wo_r, self.w["w_out"][l], D, slab=256)
        ytr = Ring([self.alloc([NCH, 512], BF16) for _ in range(2)], "yt")
        psr = self.psring([0, 1, 2, 3], "op")
        for (c0, n) in self.tiles9():
            yt, yt_r = ytr.next()
            self.load(yt[:, :, 0:n], self.YT[:, :, c0:c0 + n].rearrange("c p t -> p c t"), yt_r)
            for o in range(NCH):
                po, por = psr.next()
                for k in range(NCH):
                    self.mm(po[:, 0:n], Wo[:, k, o * 128:(o + 1) * 128], yt[:, k, 0:n], k == 0, k == NCH - 1, [Wo_r, yt_r], [por])
                hr = self.h_res(o, c0, n)
                self.stt(self.hT[:, o, c0:c0 + n], po[:, 0:n], 1.0, self.hT[:, o, c0:c0 + n], ALU.mult, ALU.add,
                         [por] + hr, hr)


    def stage_final(self):
        P = self.P
        self.stage_begin()
        gbc = self.alloc([D], F32)
        gbc_r = Res("gbc")
        P.add("sp", lambda e: e.dma_start(out=gbc[:, :], in_=self.final_norm.partition_broadcast(128)),
              writes=[gbc_r], dma=True, sem_res=gbc_r)
        otr = Ring([self.alloc([D], F32) for i in range(2)], "ot")
        sq = self.alloc([512], F32)
        sq_r = Res("fsq")
        ssr = Ring([self.alloc([4], F32) for i in range(2)], "fss")
        psr = self.psring([0, 1, 2, 3], "f")
        for i in range(SEQ // 128):
            c0 = NMETA + 128 * i
            o, o_r = otr.next()
            s, s_r = ssr.next()
            halves = []
            for half in range(2):
                pt, pr = psr.next()
                for cc in range(4):
                    c = half * 4 + cc
                    P.add("pe", lambda e, pt=pt, cc=cc, c=c, c0=c0: e.transpose(
                        out=pt[:, cc * 128:(cc + 1) * 128], in_=self.hT[:, c, c0:c0 + 128],
                        identity=self.identf[:, :]),
                        reads=self.h_res(c, c0, 128) + [self.identf_r], writes=[pr])
                halves.append((pt, pr))
            for half, (pt, pr) in enumerate(halves):
                P.add("act", lambda e, pt=pt, s=s, half=half: e.activation(
                    out=sq[:, 0:512], in_=pt[:, :], func=AF.Square, accum_out=s[:, half:half + 1]),
                    reads=[pr], writes=[sq_r, s_r])
            P.add("dve", lambda e, s=s: e.tensor_tensor(out=s[:, 2:3], in0=s[:, 0:1], in1=s[:, 1:2], op=ALU.add),
                  reads=[s_r], writes=[s_r])
            P.add("dve", lambda e, s=s: e.tensor_scalar(out=s[:, 2:3], in0=s[:, 2:3], scalar1=1.0 / D, scalar2=EPS,
                                                         op0=ALU.mult, op1=ALU.add),
                  reads=[s_r], writes=[s_r])
            P.add("act", lambda e, s=s: e.activation(out=s[:, 3:4], in_=s[:, 2:3], func=AF.Sqrt),
                  reads=[s_r], writes=[s_r])
            P.add("dve", lambda e, s=s: e.reciprocal(out=s[:, 3:4], in_=s[:, 3:4]),
                  reads=[s_r], writes=[s_r])
            for half, (pt, pr) in enumerate(halves):
                P.add("dve", lambda e, pt=pt, s=s, o=o, half=half: e.scalar_tensor_tensor(
                    out=o[:, half * 512:(half + 1) * 512], in0=pt[:, :], scalar=s[:, 3:4],
                    in1=gbc[:, half * 512:(half + 1) * 512], op0=ALU.mult, op1=ALU.mult),
                    reads=[pr, s_r, gbc_r], writes=[o_r])
            P.add("sp", lambda e, o=o, i=i: e.dma_start(out=self.out[128 * i:128 * (i + 1), :], in_=o[:, :]),
                  reads=[o_r], dma=True, sem_res=o_r)
        self.P.barrier()


def build_program(n_layers=DEPTH, debug=None):
    b = Builder(n_layers, debug)
    nc = b.build()
    return nc, b


_CONSTS = None


def consts():
    global _CONSTS
    if _CONSTS is None:
        half = 16
        inv = (np.float32(10000.0) ** (-(np.arange(half, dtype=np.float32)) / np.float32(half))).astype(np.float32)
        ang = (np.arange(T, dtype=np.float32)[None, :] * inv[:, None]).astype(np.float32).astype(np.float64)
        cs = np.zeros((96, 2, T), np.float32)
        cs[0:64, 0, :] = 1.0
        cs[64:80, 0, :] = np.cos(ang)
        cs[80:96, 0, :] = np.cos(ang)
        cs[64:80, 1, :] = -np.sin(ang)
        cs[80:96, 1, :] = np.sin(ang)
        tri = (np.arange(128)[:, None] <= np.arange(128)[None, :]).astype(np.float32)
        _CONSTS = {"ident_f": np.eye(128, dtype=np.float32), "rope_cs": cs, "tri_in": tri}
    return _CONSTS


def make_in_maps(inputs, b):
    c = consts()
    maps = []
    for core in range(8):
        m = {}
        for nm in b.inputs:
            if nm == "x":
                m[nm] = np.ascontiguousarray(inputs["x"][core])
            elif nm in c:
                m[nm] = c[nm]
            else:
                m[nm] = np.ascontiguousarray(inputs[nm])
        maps.append(m)
    return maps


def kernel(**inputs):
    inputs = {k: np.asarray(v) for k, v in inputs.items()}
    nc, b = build_program()
    maps = make_in_maps(inputs, b)
    res = run_bass_kernel_spmd(nc, maps, core_ids=list(range(8)))
    out = np.stack([np.asarray(res.results[i]["out"]) for i in range(8)], axis=0)
    return out.astype(np.float32)
```

```python
import contextlib
import math
import numpy as np
import ml_dtypes
import concourse.bass as bass
import concourse.mybir as mybir
from concourse.bass_utils import run_bass_kernel_spmd

F32 = mybir.dt.float32
BF16 = mybir.dt.bfloat16
AF = mybir.ActivationFunctionType
ALU = mybir.AluOpType
AX = mybir.AxisListType

D = 1024
SEQ = 4096
NMETA = 16
T = SEQ + NMETA
DFF = 2816
NF = DFF // 128
NCH = D // 128
DEPTH = 2
EPS = 1e-6
DIN = 2464


class Res:
    __slots__ = ("name", "w", "r", "slot")

    def __init__(self, name):
        self.name = name
        self.w = None
        self.r = []
        self.slot = None


class SemSlot:
    __slots__ = ("sem", "dcount")

    def __init__(self):
        self.sem = None
        self.dcount = 0


class Op:
    __slots__ = ("eng", "fn", "deps", "dma", "sem_res", "val", "needs_inc", "idx", "gen")

    def __init__(self, eng, fn, dma=False):
        self.eng = eng
        self.fn = fn
        self.deps = []
        self.dma = dma
        self.sem_res = None
        self.val = 0
        self.needs_inc = False


ENGS = ("pe", "act", "dve", "pool", "sp")
MAX_DMA_INFLIGHT = 4


class Prog:
    def __init__(self):
        self.q = {e: [] for e in ENGS}
        self.pool = []
        self.stage_k = 0
        self.gen = 0
        self.last_dma = {}
        self.dma_fifo = []

    def barrier(self):
        deps = []
        for e in ENGS:
            for op in reversed(self.q[e]):
                if not op.dma and op.fn is not None:
                    deps.append((op, 0))
                    break
        for op in self.last_dma.values():
            deps.append((op, op.sem_res.dcount))
        for e in ENGS:
            op = Op(e, None)
            op.gen = self.gen
            op.deps = [(o, s) for (o, s) in deps if not (o.eng == e and not o.dma)]
            for o, _ in op.deps:
                o.needs_inc = True
            self.q[e].append(op)
        self.gen += 1
        self.stage_k = 0
        self.last_dma = {}

    def add(self, eng, fn, reads=(), writes=(), dma=False, sem_res=None):
        op = Op(eng, fn, dma)
        op.gen = self.gen
        deps = {}

        def dep(o):
            if o is None or o.gen < self.gen:
                return
            if (not o.dma) and o.eng == "pe" and eng == "pe" and not dma:
                return
            snap = o.sem_res.dcount if o.dma else 0
            deps[id(o)] = (o, snap)

        for r in reads:
            dep(r.w)
        for w in writes:
            dep(w.w)
            for o in w.r:
                dep(o)
        if dma:
            if len(self.dma_fifo) >= MAX_DMA_INFLIGHT:
                old = self.dma_fifo[-MAX_DMA_INFLIGHT]
                if old.gen == self.gen and id(old) not in deps:
                    deps[id(old)] = (old, 0)
            self.dma_fifo.append(op)
        op.deps = list(deps.values())
        for o, _ in op.deps:
            o.needs_inc = True
        if dma:
            assert sem_res is not None
            if sem_res.slot is None or sem_res.slot[1] != self.gen:
                if self.stage_k >= len(self.pool):
                    self.pool.append(SemSlot())
                sem_res.slot = (self.pool[self.stage_k], self.gen)
                self.stage_k += 1
            sl = sem_res.slot[0]
            sl.dcount += 16
            op.sem_res = sl
            op.val = sl.dcount
            self.last_dma[id(sl)] = op
        for r in reads:
            if not dma:
                r.r = [o for o in r.r if o.dma or o.eng != eng]
            r.r.append(op)
        for w in writes:
            w.w = op
            w.r = []
        self.q[eng].append(op)
        return op

    def emit(self, nc, es):
        esem = {e: es.enter_context(nc.semaphore("s_" + e)) for e in ENGS}
        for i, sl in enumerate(self.pool):
            sl.sem = es.enter_context(nc.semaphore("d%d" % i))
        for e in ENGS:
            cnt = 0
            for op in self.q[e]:
                if op.dma or op.fn is None:
                    continue
                if op.needs_inc:
                    cnt += 1
                    op.val = cnt
        block = es.enter_context(nc.Block())
        engobj = {"pe": "tensor", "act": "scalar", "dve": "vector", "pool": "gpsimd", "sp": "sync"}

        def body_for(e):
            ops = self.q[e]

            def body(eng):
                seen = {}
                for op in ops:
                    for d, snap in op.deps:
                        if d.dma:
                            sem, val = d.sem_res.sem, max(d.val, snap)
                        else:
                            sem, val = esem[d.eng], d.val
                        key = id(sem)
                        if seen.get(key, 0) >= val:
                            continue
                        seen[key] = val
                        eng.wait_ge(sem, val)
                    if op.fn is None:
                        continue
                    ins = op.fn(eng)
                    if op.dma:
                        ins.then_inc(op.sem_res.sem, 16)
                    elif op.needs_inc:
                        ins.then_inc(esem[e], 1)
            return body

        for e in ENGS:
            getattr(block, engobj[e])(body_for(e))


class Ring:
    def __init__(self, aps, name):
        self.aps = aps
        self.res = [Res("%s%d" % (name, i)) for i in range(len(aps))]
        self.i = 0

    def next(self):
        k = self.i % len(self.aps)
        self.i += 1
        return self.aps[k], self.res[k]


ARENA_BYTES = 79360


def dsize(dt):
    return 4 if dt == F32 else 2


class Builder:
    def __init__(self, n_layers=DEPTH, debug=None):
        self.n_layers = n_layers
        self.debug = debug or {}
        self.nc = bass.Bass("TRN2", target_bir_lowering=False)
        self.P = Prog()
        self.es = contextlib.ExitStack()
        self.inputs = {}
        self.aoff = 0

    def din(self, name, shape, dt=F32):
        ap = self.nc.dram_tensor(name, list(shape), dt, kind="ExternalInput").ap()
        self.inputs[name] = ap
        return ap

    def dout(self, name, shape, dt=F32):
        return self.nc.dram_tensor(name, list(shape), dt, kind="ExternalOutput").ap()

    def dscratch(self, name, shape, dt):
        return self.nc.dram_tensor(name, list(shape), dt, kind="Internal").ap()

    def sb(self, name, shape, dt):
        return self.es.enter_context(self.nc.sbuf_tensor(name, list(shape), dt))

    def ps(self, name, shape, dt=F32):
        return self.es.enter_context(self.nc.psum_tensor(name, list(shape), dt))

    def stage_begin(self):
        self.P.barrier()
        self.aoff = 0

    def alloc(self, shape, dt):
        n = int(np.prod(shape))
        nbytes = (n * dsize(dt) + 31) // 32 * 32
        assert self.aoff + nbytes <= ARENA_BYTES, (self.aoff, nbytes)
        ap = self.arena[:, self.aoff // 4:(self.aoff + nbytes) // 4]
        self.aoff += nbytes
        if dt != F32:
            ap = ap.bitcast(dt)
        ap = ap[:, 0:n]
        if len(shape) == 2:
            ap = ap.rearrange("p (a b) -> p a b", a=shape[0])
        elif len(shape) == 3:
            ap = ap.rearrange("p (a b c) -> p a b c", a=shape[0], b=shape[1])
        return ap


    def act(self, out, in_, func, reads, writes, **kw):
        return self.P.add("act", lambda e: e.activation(out=out, in_=in_, func=func, **kw), reads=reads, writes=writes)

    def tt(self, eng, out, in0, in1, op, reads, writes):
        return self.P.add(eng, lambda e: e.tensor_tensor(out=out, in0=in0, in1=in1, op=op), reads=reads, writes=writes)

    def ts(self, eng, out, in0, s1, s2, op0, op1, reads, writes):
        if s2 is None:
            return self.P.add(eng, lambda e: e.tensor_scalar(out=out, in0=in0, scalar1=s1, scalar2=None, op0=op0),
                              reads=reads, writes=writes)
        return self.P.add(eng, lambda e: e.tensor_scalar(out=out, in0=in0, scalar1=s1, scalar2=s2, op0=op0, op1=op1),
                          reads=reads, writes=writes)

    def stt(self, out, in0, scalar, in1, op0, op1, reads, writes):
        return self.P.add("dve", lambda e: e.scalar_tensor_tensor(out=out, in0=in0, scalar=scalar, in1=in1, op0=op0, op1=op1),
                          reads=reads, writes=writes)

    def cp(self, eng, out, in_, reads, writes):
        if eng == "act":
            return self.act(out, in_, AF.Copy, reads, writes)
        return self.P.add(eng, lambda e: e.tensor_copy(out=out, in_=in_), reads=reads, writes=writes)

    def mm(self, out, lhsT, rhs, start, stop, reads, writes, **kw):
        return self.P.add("pe", lambda e: e.matmul(out, lhsT=lhsT, rhs=rhs, start=start, stop=stop, **kw),
                          reads=reads, writes=writes)

    def load(self, out, in_, res, reads=()):
        return self.P.add("sp", lambda e: e.dma_start(out=out, in_=in_), reads=list(reads), writes=[res], dma=True, sem_res=res)

    def store(self, out, in_, res):
        return self.P.add("sp", lambda e: e.dma_start(out=out, in_=in_), reads=[res], dma=True, sem_res=res)

    def rsqrt_inplace(self, ap, res, scale, in_ap=None, in_reads=()):
        src = ap if in_ap is None else in_ap
        self.ts("dve", ap, src, scale, EPS, ALU.mult, ALU.add, [res] + list(in_reads), [res])
        self.act(ap, ap, AF.Sqrt, [res], [res])
        self.P.add("dve", lambda e: e.reciprocal(out=ap, in_=ap), reads=[res], writes=[res])

    def h_res(self, c, c0, n):
        out = []
        if c0 < NMETA:
            out.append(self.hres[c][0])
        lo = max(c0, NMETA) - NMETA
        hi = c0 + n - NMETA
        if hi > lo:
            for i in range(lo // 128, (hi - 1) // 128 + 1):
                out.append(self.hres[c][1 + i])
        return out

    def build(self):
        nc, P = self.nc, self.P
        self.x = self.din("x", [SEQ, D])
        self.meta = self.din("meta_tokens", [NMETA, D])
        self.final_norm = self.din("final_norm", [D])
        self.ident_f = self.din("ident_f", [128, 128])
        L = DEPTH
        self.w = {}
        for nm, shp in [("ffn1_norm", [L, D]), ("ffn1_in", [L, D, 2 * DFF]), ("ffn1_out", [L, DFF, D]),
                        ("ffn2_norm", [L, D]), ("ffn2_in", [L, D, 2 * DFF]), ("ffn2_out", [L, DFF, D])]:
            self.w[nm] = self.din(nm, shp)
        for nm, shp in [("mix_norm", [L, D]), ("w_in", [L, D, DIN]), ("conv_w", [L, 4, 256]), ("conv_b", [L, 256]),
                        ("lru_wa", [L, 4, 64, 64]), ("lru_ba", [L, 4, 64]), ("lru_wx", [L, 4, 64, 64]), ("lru_bx", [L, 4, 64]),
                        ("lru_lambda", [L, 256]), ("lru_out_norm", [L, 256]), ("lam_q1", [L, 64]), ("lam_k1", [L, 64]),
                        ("lam_q2", [L, 64]), ("lam_k2", [L, 64]), ("diff_subln", [L, 128]), ("q_norm", [L, 256]),
                        ("w_uq", [L, 256, 384]), ("kv_norm", [L, 128]), ("w_ukv", [L, 128, 512]),
                        ("mla_out_norm", [L, 256]), ("w_out", [L, D, D])]:
            self.w[nm] = self.din(nm, shp)
        self.rope_cs = self.din("rope_cs", [96, 2, T])
        self.tri_in = self.din("tri_in", [128, 128])
        self.out = self.dout("out", [SEQ, D])
        self.LRU_IN = self.dscratch("lru_in", [4, 128, T], F32)
        self.QKD = self.dscratch("qkd", [8, 128, T], BF16)
        self.VD = self.dscratch("vd", [T, 512], BF16)
        self.QM = self.dscratch("qm", [4, 96, T], BF16)
        self.KM = self.dscratch("km", [4, 96, T], BF16)
        self.VM = self.dscratch("vm", [T, 260], BF16)
        self.YT = self.dscratch("yt", [8, 128, T], BF16)
        self.wf = {}
        for l in range(L):
            for which in (1, 2):
                self.wf[(l, which)] = self.dscratch("wf_%d_%d" % (l, which), [NF, 128, 3072], BF16)
        self.wf_r = {k: Res("wf%d%d" % k) for k in self.wf}

        self.hT = self.sb("hT", [128, NCH, T], F32)
        self.hres = [[Res("h%d_%d" % (c, i)) for i in range(1 + SEQ // 128)] for c in range(NCH)]
        self.arena = self.sb("arena", [128, ARENA_BYTES // 4], F32)
        self.identf = self.sb("identf", [128, 128], F32)
        self.identf_r = Res("identf")
        self.ones_bf = self.sb("ones_bf", [128, 128], BF16)
        self.ones_r = Res("ones")
        self.gains = self.sb("gains", [128, 8, NCH], F32)
        self.gains_r = Res("gains")

        P.add("sp", lambda e: e.dma_start(out=self.identf[:], in_=self.ident_f[:, :]),
              writes=[self.identf_r], dma=True, sem_res=self.identf_r)
        P.add("pool", lambda e: e.memset(self.ones_bf[:], 1.0), writes=[self.ones_r])
        self.gain_idx = {}
        gi = 0
        for l in range(L):
            for nm in ("ffn1_norm", "ffn2_norm", "mix_norm"):
                src = self.w[nm][l, :].rearrange("(c p) -> p c", p=128)
                P.add("sp", lambda e, k=gi, src=src: e.dma_start(out=self.gains[:, k, :], in_=src),
                      writes=[self.gains_r], dma=True, sem_res=self.gains_r)
                self.gain_idx[(nm, l)] = gi
                gi += 1
        src = self.final_norm.rearrange("(c p) -> p c", p=128)
        P.add("sp", lambda e, k=gi, src=src: e.dma_start(out=self.gains[:, k, :], in_=src),
              writes=[self.gains_r], dma=True, sem_res=self.gains_r)
        self.gain_idx["final"] = gi

        self.psum = [self.ps("ps%d" % i, [128, 512], F32) for i in range(8)]
        self.psum_r = [Res("ps%d" % i) for i in range(8)]

        self.stage_load_x()
        ffns = [(l, w) for l in range(self.n_layers) for w in (1, 2) if self.debug.get("ffn%d" % w, True)]
        self.conv_queue = []
        for i, (l, w) in enumerate(ffns):
            if i == 0 or not self.debug.get("mixer", True) or not self.debug.get("diff", True):
                self.stage_convert_ffn(l, w)
            else:
                self.conv_queue += [(l, w, f) for f in range(NF)]
        for l in range(self.n_layers):
            if self.debug.get("ffn1", True):
                self.stage_ffn(l, 1)
            if self.debug.get("mixer", True):
                if self.debug.get("proj1", True):
                    self.stage_proj1(l)
                if self.debug.get("proj2", True):
                    self.stage_proj2(l)
                if self.debug.get("lru", True):
                    self.stage_lru(l)
                if self.debug.get("diff", True):
                    self.stage_diff(l)
                if self.debug.get("mla", True):
                    self.stage_mla(l)
                if self.debug.get("oproj", True):
                    self.stage_oproj(l)
            if self.debug.get("ffn2", True):
                self.stage_ffn(l, 2)
        self.stage_final()
        with nc.allow_non_contiguous_dma(reason="small strided parameter loads"):
            P.emit(nc, self.es)
        return nc

    def psring(self, idxs, name):
        r = Ring([self.psum[i] for i in idxs], name)
        r.res = [self.psum_r[i] for i in idxs]
        return r

    def stage_load_x(self):
        P = self.P
        self.stage_begin()
        ring = Ring([self.alloc([D], F32) for i in range(3)], "xin")
        psr = self.psring([0, 1, 2, 3], "x")
        n = 0
        for i in range(-1, SEQ // 128):
            buf, br = ring.next()
            if i < 0:
                rows, c0 = NMETA, 0
                src = self.meta[:, :]
            else:
                rows, c0 = 128, NMETA + 128 * i
                src = self.x[128 * i:128 * (i + 1), :]
            P.add("sp", lambda e, buf=buf, rows=rows, src=src: e.dma_start(out=buf[0:rows, :], in_=src),
                  writes=[br], dma=True, sem_res=br)
            for half in range(2):
                pt, pr = psr.next()
                for cc in range(4):
                    c = half * 4 + cc
                    P.add("pe", lambda e, pt=pt, cc=cc, buf=buf, rows=rows, c=c: e.transpose(
                        out=pt[:, cc * 128:cc * 128 + rows], in_=buf[0:rows, c * 128:(c + 1) * 128],
                        identity=self.identf[0:rows, 0:rows]),
                        reads=[br, self.identf_r], writes=[pr])
                eng = "act" if (n % 2 == 0) else "dve"
                n += 1
                hr = []
                for cc in range(4):
                    hr += self.h_res(half * 4 + cc, c0, rows)
                dst = self.hT[:, half * 4:half * 4 + 4, c0:c0 + rows]
                srcp = pt[:, :].rearrange("p (c t) -> p c t", c=4)[:, :, 0:rows]
                if eng == "act":
                    P.add("act", lambda e, dst=dst, srcp=srcp: e.activation(out=dst, in_=srcp, func=AF.Copy),
                          reads=[pr], writes=hr)
                else:
                    P.add("dve", lambda e, dst=dst, srcp=srcp: e.tensor_copy(out=dst, in_=srcp),
                          reads=[pr], writes=hr)

    def conv_bufs(self):
        return dict(a=self.alloc([2, NCH, 128], F32), ar=Res("cva"), ao=self.alloc([D], F32), aor=Res("cvo"),
                    b=self.alloc([3072], BF16), br=Res("cvb"))

    def conv_job(self, job, cb, eng="pool"):
        l, which, f = job
        w_in = self.w["ffn%d_in" % which][l]
        w_out = self.w["ffn%d_out" % which][l]
        WF = self.wf[(l, which)]
        a, ar, ao, aor, b, br = cb["a"], cb["ar"], cb["ao"], cb["aor"], cb["b"], cb["br"]
        for gu in range(2):
            col = gu * DFF + f * 128
            self.load(a[:, gu, :, :], w_in[:, col:col + 128].rearrange("(k p) j -> p k j", p=128), ar)
        self.load(ao[:, :], w_out[f * 128:(f + 1) * 128, :], aor)
        for gu in range(2):
            out = b[:, 0:2048].rearrange("p (k j) -> p k j", k=NCH)[:, :, gu * 128:(gu + 1) * 128]
            self.cp(eng, out, a[:, gu, :, :], [ar], [br])
        self.cp(eng, b[:, 2048:3072], ao[:, :], [aor], [br])
        self.store(WF[f, :, :], b[:, :], br)

    def stage_convert_ffn(self, l, which):
        P = self.P
        self.stage_begin()
        w_in = self.w["ffn%d_in" % which][l]
        w_out = self.w["ffn%d_out" % which][l]
        WF = self.wf[(l, which)]
        wfr = self.wf_r[(l, which)]
        s32 = Ring([self.alloc([2, NCH, 256], F32) for _ in range(2)], "cv32")
        s32o = Ring([self.alloc([2, D], F32) for _ in range(2)], "cv32o")
        s16 = Ring([self.alloc([2, 3072], BF16) for _ in range(2)], "cv16")
        engs = ["act", "dve", "pool"]
        n = 0

        def cast(eng, out, in_):
            if eng == "act":
                return lambda e: e.activation(out=out, in_=in_, func=AF.Copy)
            return lambda e: e.tensor_copy(out=out, in_=in_)

        for fp in range(NF // 2):
            f0 = 2 * fp
            a, ar = s32.next()
            ao, aor = s32o.next()
            b, br = s16.next()
            for gu in range(2):
                col = gu * DFF + f0 * 128
                src = w_in[:, col:col + 256].rearrange("(k p) j -> p k j", p=128)
                P.add("sp", lambda e, a=a, gu=gu, src=src: e.dma_start(out=a[:, gu, :, :], in_=src),
                      writes=[ar], dma=True, sem_res=ar)
            src = w_out[f0 * 128:(f0 + 2) * 128, :].rearrange("(f p) c -> p f c", p=128)
            P.add("sp", lambda e, ao=ao, src=src: e.dma_start(out=ao[:, :, :], in_=src),
                  writes=[aor], dma=True, sem_res=aor)
            for ff in range(2):
                for gu in range(2):
                    eng = engs[n % 3]
                    n += 1
                    out = b[:, ff, 0:2048].rearrange("p (k j) -> p k j", k=NCH)[:, :, gu * 128:(gu + 1) * 128]
                    in_ = a[:, gu, :, ff * 128:(ff + 1) * 128]
                    P.add(eng, cast(eng, out, in_), reads=[ar], writes=[br])
            eng = engs[n % 3]
            n += 1
            P.add(eng, cast(eng, b[:, :, 2048:3072], ao[:, :, :]), reads=[aor], writes=[br])
            dst = WF[f0:f0 + 2, :, :].rearrange("f p c -> p f c")
            P.add("sp", lambda e, b=b, dst=dst: e.dma_start(out=dst, in_=b[:, :, :]),
                  reads=[br], writes=[wfr], dma=True, sem_res=br)

    def stage_ffn(self, l, which):
        P = self.P
        self.stage_begin()
        WF = self.wf[(l, which)]
        wfr = self.wf_r[(l, which)]
        gk = self.gain_idx[("ffn%d_norm" % which, l)]
        xn = self.alloc([NCH, 1040], BF16)
        hid = Ring([self.alloc([1040], BF16) for _ in range(4)], "hid")
        wsl = Ring([self.alloc([3072], BF16) for _ in range(6)], "wsl")
        sqr = Ring([self.alloc([512], BF16) for _ in range(2)], "sq")
        rstd = self.alloc([1040], F32)
        rstd_r = Res("rstd")
        sgr = Ring([self.alloc([512], F32) for _ in range(2)], "sg")
        ps_gu = self.psring([0, 1, 2, 3], "gu")
        ps_o = self.psring([4, 5, 6], "o")
        ps_n, ps_nr = self.psum[7], self.psum_r[7]

        sts = [[(0, 16), (16, 512), (528, 512)]]
        for s in range(1, 4):
            b0 = 1040 + 1024 * (s - 1)
            sts.append([(b0, 512), (b0 + 512, 512)])
        nsq = 0
        for subs in sts:
            base = subs[0][0]
            xn_r = [Res("xn%d" % i) for i in range(len(subs))]
            for si, (c0, n) in enumerate(subs):
                lc = c0 - base
                for c in range(NCH):
                    sq, sq_r = sqr.next()
                    src = self.hT[:, c, c0:c0 + n]
                    if nsq % 2 == 0:
                        P.add("act", lambda e, sq=sq, src=src, n=n: e.activation(out=sq[:, 0:n], in_=src, func=AF.Square),
                              reads=self.h_res(c, c0, n), writes=[sq_r])
                    else:
                        P.add("pool", lambda e, sq=sq, src=src, n=n: e.tensor_tensor(out=sq[:, 0:n], in0=src, in1=src, op=ALU.mult),
                              reads=self.h_res(c, c0, n), writes=[sq_r])
                    nsq += 1
                    P.add("pe", lambda e, sq=sq, n=n, c=c: e.matmul(ps_n[:, 0:n], lhsT=self.ones_bf[:, :], rhs=sq[:, 0:n],
                                                                     start=(c == 0), stop=(c == NCH - 1)),
                          reads=[sq_r, self.ones_r], writes=[ps_nr])
                rs = rstd[:, lc:lc + n]
                P.add("dve", lambda e, rs=rs, n=n: e.tensor_scalar(out=rs, in0=ps_n[:, 0:n], scalar1=1.0 / D, scalar2=EPS,
                                                                   op0=ALU.mult, op1=ALU.add),
                      reads=[ps_nr], writes=[rstd_r])
                P.add("act", lambda e, rs=rs: e.activation(out=rs, in_=rs, func=AF.Sqrt), reads=[rstd_r], writes=[rstd_r])
                P.add("dve", lambda e, rs=rs: e.reciprocal(out=rs, in_=rs), reads=[rstd_r], writes=[rstd_r])
                for c in range(NCH):
                    P.add("dve", lambda e, c=c, c0=c0, n=n, lc=lc, rs=rs: e.scalar_tensor_tensor(
                        out=xn[:, c, lc:lc + n], in0=self.hT[:, c, c0:c0 + n], scalar=self.gains[:, gk, c:c + 1],
                        in1=rs, op0=ALU.mult, op1=ALU.mult),
                        reads=self.h_res(c, c0, n) + [rstd_r, self.gains_r], writes=[xn_r[si]])

            def GU(f):
                slot, slot_r = wsl.next()
                P.add("sp", lambda e, slot=slot, f=f: e.dma_start(out=slot[:, :], in_=WF[f, :, :]),
                      reads=[wfr], writes=[slot_r], dma=True, sem_res=slot_r)
                hb, hb_r = hid.next()
                blocks = []
                for si, (c0, n) in enumerate(subs):
                    def block(si=si, c0=c0, n=n):
                        lc = c0 - base
                        pg, pgr = ps_gu.next()
                        pu, pur = ps_gu.next()
                        for gu, (pp, ppr) in enumerate(((pg, pgr), (pu, pur))):
                            for k in range(NCH):
                                self.mm(pp[:, 0:n], slot[:, k * 256 + gu * 128:k * 256 + gu * 128 + 128],
                                        xn[:, k, lc:lc + n], k == 0, k == NCH - 1, [slot_r, xn_r[si]], [ppr])
                        sg, sg_r = sgr.next()
                        self.act(sg[:, 0:n], pg[:, 0:n], AF.Silu, [pgr], [sg_r])
                        self.tt("dve", hb[:, lc:lc + n], sg[:, 0:n], pu[:, 0:n], ALU.mult, [sg_r, pur], [hb_r])
                    blocks.append(block)
                return (slot, slot_r, hb, hb_r), blocks

            def W2(group):
                units = []
                for o in range(NCH):
                    for si, (c0, n) in enumerate(subs):
                        def unit(o=o, c0=c0, n=n):
                            lc = c0 - base
                            po, por = ps_o.next()
                            for gi_, (slot, slot_r, hb, hb_r) in enumerate(group):
                                self.mm(po[:, 0:n], slot[:, 2048 + o * 128:2048 + (o + 1) * 128], hb[:, lc:lc + n],
                                        gi_ == 0, gi_ == len(group) - 1, [slot_r, hb_r], [por])
                            hr = self.h_res(o, c0, n)
                            self.stt(self.hT[:, o, c0:c0 + n], po[:, 0:n], 0.5, self.hT[:, o, c0:c0 + n], ALU.mult, ALU.add,
                                     [por] + hr, hr)
                        units.append(unit)
                return units

            pend = []
            for g in range(NF // 2):
                grp, blocks = [], []
                for f in (2 * g, 2 * g + 1):
                    info, bl = GU(f)
                    grp.append(info)
                    blocks += bl
                per = -(-len(pend) // len(blocks))
                for bl in blocks:
                    bl()
                    for _ in range(per):
                        if pend:
                            pend.pop(0)()
                while pend:
                    pend.pop(0)()
                pend = W2(grp)
            while pend:
                pend.pop(0)()

    def tiles9(self):
        return [(0, NMETA)] + [(NMETA + 512 * j, 512) for j in range(SEQ // 512)]

    def norm_tile(self, c0, n, gk, xn, xn_r, sqr, rstd, rstd_r, ps_n, ps_nr, cnt=[0]):
        P = self.P
        for c in range(NCH):
            sq, sq_r = sqr.next()
            src = self.hT[:, c, c0:c0 + n]
            if cnt[0] % 2 == 0:
                self.act(sq[:, 0:n], src, AF.Square, self.h_res(c, c0, n), [sq_r])
            else:
                self.tt("pool", sq[:, 0:n], src, src, ALU.mult, self.h_res(c, c0, n), [sq_r])
            cnt[0] += 1
            self.mm(ps_n[:, 0:n], self.ones_bf[:, :], sq[:, 0:n], c == 0, c == NCH - 1, [sq_r, self.ones_r], [ps_nr])
        rs = rstd[:, 0:n]
        self.rsqrt_inplace(rs, rstd_r, 1.0 / D, in_ap=ps_n[:, 0:n], in_reads=[ps_nr])
        for c in range(NCH):
            self.stt(xn[:, c, 0:n], self.hT[:, c, c0:c0 + n], self.gains[:, gk, c:c + 1], rs, ALU.mult, ALU.mult,
                     self.h_res(c, c0, n) + [rstd_r, self.gains_r], [xn_r])

    def load_w_bf16(self, dst, dst_r, src, ncols, slab=160):
        st = Ring([self.alloc([src.shape[0] // 128, slab], F32) for _ in range(2)], "wslab")
        engs = ["act", "dve", "pool"]
        i = 0
        for c in range(0, ncols, slab):
            w = min(slab, ncols - c)
            a, ar = st.next()
            self.load(a[:, :, 0:w], src[:, c:c + w].rearrange("(k p) j -> p k j", p=128), ar)
            self.cp(engs[i % 3], dst[:, :, c:c + w], a[:, :, 0:w], [ar], [dst_r])
            i += 1

    def stage_proj1(self, l):
        self.stage_begin()
        gk = self.gain_idx[("mix_norm", l)]
        w_in = self.w["w_in"][l]
        Wp = self.alloc([NCH, 2048], BF16)
        Wp_r = Res("Wp")
        self.load_w_bf16(Wp, Wp_r, w_in[:, 0:2048], 2048)
        xnr = Ring([self.alloc([NCH, 512], BF16) for _ in range(2)], "xn")
        sqr = Ring([self.alloc([512], BF16) for _ in range(2)], "sq")
        rstr = Ring([self.alloc([512], F32) for _ in range(2)], "rstd")
        st32 = Ring([self.alloc([512], F32) for _ in range(3)], "st32")
        st16 = Ring([self.alloc([512], BF16) for _ in range(4)], "st16")
        psr = self.psring([0, 1, 2, 3, 4, 5], "p1")
        ps_n, ps_nr = self.psum[7], self.psum_r[7]
        ne = 0
        for (c0, n) in self.tiles9():
            xn, xn_r = xnr.next()
            rstd, rstd_r = rstr.next()
            self.norm_tile(c0, n, gk, xn, xn_r, sqr, rstd, rstd_r, ps_n, ps_nr)
            for g in range(12):
                pt, pr = psr.next()
                for k in range(NCH):
                    self.mm(pt[:, 0:n], Wp[:, k, g * 128:(g + 1) * 128], xn[:, k, 0:n], k == 0, k == NCH - 1,
                            [Wp_r, xn_r], [pr])
                eng = "act" if ne % 2 == 0 else "dve"
                ne += 1
                if g < 4:
                    sb, sr = st32.next()
                    self.cp(eng, sb[:, 0:n], pt[:, 0:n], [pr], [sr])
                    self.store(self.LRU_IN[g, :, c0:c0 + n], sb[:, 0:n], sr)
                else:
                    sb, sr = st16.next()
                    self.cp(eng, sb[:, 0:n], pt[:, 0:n], [pr], [sr])
                    self.store(self.QKD[g - 4, :, c0:c0 + n], sb[:, 0:n], sr)
            for j in range(0, n, 128):
                m = min(128, n - j)
                pt, pr = psr.next()
                for k in range(NCH):
                    self.mm(pt[0:m, 0:512], xn[:, k, j:j + m], Wp[:, k, 1536:2048], k == 0, k == NCH - 1,
                            [Wp_r, xn_r], [pr])
                sb, sr = st16.next()
                eng = "act" if ne % 2 == 0 else "dve"
                ne += 1
                self.cp(eng, sb[0:m, 0:512], pt[0:m, 0:512], [pr], [sr])
                self.store(self.VD[c0 + j:c0 + j + m, :], sb[0:m, 0:512], sr)

    def stage_proj2(self, l):
        self.stage_begin()
        gk = self.gain_idx[("mix_norm", l)]
        w_in = self.w["w_in"][l]
        Wp = self.alloc([NCH, 512], BF16)
        Wp_r = Res("Wp2")
        self.P.add("pool", lambda e: e.memset(Wp[:, :, 416:480], 0.0), writes=[Wp_r])
        self.load_w_bf16(Wp, Wp_r, w_in[:, 2048:2464], 416, slab=104)
        self.cp("dve", Wp[:, :, 480:496], Wp[:, :, 400:416], [Wp_r], [Wp_r])
        self.cp("dve", Wp[:, :, 496:512], Wp[:, :, 384:400], [Wp_r], [Wp_r])
        Wq = self.alloc([2, 768], BF16)
        Wq_r = Res("Wq")
        self.load_w_bf16(Wq, Wq_r, self.w["w_uq"][l], 384, slab=192)
        for h in range(4):
            b = 384 + h * 96
            a = h * 96
            self.cp("dve", Wq[:, :, b:b + 64], Wq[:, :, a:a + 64], [Wq_r], [Wq_r])
            self.cp("dve", Wq[:, :, b + 64:b + 80], Wq[:, :, a + 80:a + 96], [Wq_r], [Wq_r])
            self.cp("dve", Wq[:, :, b + 80:b + 96], Wq[:, :, a + 64:a + 80], [Wq_r], [Wq_r])
        Wkv = self.alloc([512], BF16)
        Wkv_r = Res("Wkv")
        kv32 = self.alloc([512], F32)
        kv32_r = Res("kv32")
        self.load(kv32[:, :], self.w["w_ukv"][l][:, :], kv32_r)
        self.cp("dve", Wkv[:, :].rearrange("p (t h d) -> p h t d", t=2, h=4),
                kv32[:, :].rearrange("p (h t d) -> p h t d", h=4, t=2), [kv32_r], [Wkv_r])
        gq = self.alloc([2], F32)
        gkv = self.alloc([1], F32)
        g_r = Res("gqkv")
        self.load(gq[:, :], self.w["q_norm"][l].rearrange("(c p) -> p c", p=128), g_r)
        self.load(gkv[:, :], self.w["kv_norm"][l].rearrange("(c p) -> p c", p=128), g_r)

        xnr = Ring([self.alloc([NCH, 512], BF16) for _ in range(1)], "xn")
        sqr = Ring([self.alloc([512], BF16) for _ in range(2)], "sq")
        rstr = Ring([self.alloc([512], F32) for _ in range(2)], "rstd")
        cq = self.alloc([2, 512], F32)
        cq_r = Res("cq")
        ckv = self.alloc([512], F32)
        ckv_r = Res("ckv")
        cqn = self.alloc([2, 512], BF16)
        cqn_r = Res("cqn")
        ckvn = self.alloc([512], BF16)
        ckvn_r = Res("ckvn")
        rsq = self.alloc([512], F32)
        rsq_r = Res("rsq")
        rskv = self.alloc([512], F32)
        rskv_r = Res("rskv")
        cs = self.alloc([2, 512], F32)
        cs_r = Res("cs")
        t1r = Ring([self.alloc([512], F32) for _ in range(2)], "t1")
        t2r = Ring([self.alloc([512], F32) for _ in range(2)], "t2")
        qst = Ring([self.alloc([4, 512], BF16) for _ in range(1)], "qst")
        kst = Ring([self.alloc([4, 512], BF16) for _ in range(1)], "kst")
        vst = Ring([self.alloc([4, 65], BF16) for _ in range(2)], "vst")
        psr = self.psring([0, 1, 2, 3, 4, 5], "p2")
        ps_n, ps_nr = self.psum[7], self.psum_r[7]
        ps_m, ps_mr = self.psum[6], self.psum_r[6]
        for (c0, n) in self.tiles9():
            xn, xn_r = xnr.next()
            rstd, rstd_r = rstr.next()
            self.norm_tile(c0, n, gk, xn, xn_r, sqr, rstd, rstd_r, ps_n, ps_nr)
            self.load(cs[0:96, :, 0:n], self.rope_cs[:, :, c0:c0 + n], cs_r)
            for j in range(3):
                pt, pr = psr.next()
                for k in range(NCH):
                    self.mm(pt[:, 0:n], Wp[:, k, j * 128:(j + 1) * 128], xn[:, k, 0:n], k == 0, k == NCH - 1,
                            [Wp_r, xn_r], [pr])
                if j < 2:
                    self.cp("act", cq[:, j, 0:n], pt[:, 0:n], [pr], [cq_r])
                else:
                    self.cp("act", ckv[:, 0:n], pt[:, 0:n], [pr], [ckv_r])
            for j in range(2):
                sq, sq_r = sqr.next()
                self.tt("pool", sq[:, 0:n], cq[:, j, 0:n], cq[:, j, 0:n], ALU.mult, [cq_r], [sq_r])
                self.mm(ps_m[:, 0:n], self.ones_bf[:, :], sq[:, 0:n], j == 0, j == 1, [sq_r, self.ones_r], [ps_mr])
            self.rsqrt_inplace(rsq[:, 0:n], rsq_r, 1.0 / 256, in_ap=ps_m[:, 0:n], in_reads=[ps_mr])
            for j in range(2):
                self.stt(cqn[:, j, 0:n], cq[:, j, 0:n], gq[:, j:j + 1], rsq[:, 0:n], ALU.mult, ALU.mult,
                         [cq_r, g_r, rsq_r], [cqn_r])
            sq, sq_r = sqr.next()
            self.tt("pool", sq[:, 0:n], ckv[:, 0:n], ckv[:, 0:n], ALU.mult, [ckv_r], [sq_r])
            self.mm(ps_m[:, 0:n], self.ones_bf[:, :], sq[:, 0:n], True, True, [sq_r, self.ones_r], [ps_mr])
            self.rsqrt_inplace(rskv[:, 0:n], rskv_r, 1.0 / 128, in_ap=ps_m[:, 0:n], in_reads=[ps_mr])
            self.stt(ckvn[:, 0:n], ckv[:, 0:n], gkv[:, 0:1], rskv[:, 0:n], ALU.mult, ALU.mult,
                     [ckv_r, g_r, rskv_r], [ckvn_r])
            qs, qs_r = qst.next()
            for h in range(4):
                pa, par = psr.next()
                pb, pbr = psr.next()
                for k in range(2):
                    self.mm(pa[0:96, 0:n], Wq[:, k, h * 96:(h + 1) * 96], cqn[:, k, 0:n], k == 0, k == 1, [Wq_r, cqn_r], [par])
                for k in range(2):
                    self.mm(pb[0:96, 0:n], Wq[:, k, 384 + h * 96:384 + (h + 1) * 96], cqn[:, k, 0:n], k == 0, k == 1,
                            [Wq_r, cqn_r], [pbr])
                t1, t1_r = t1r.next()
                t2, t2_r = t2r.next()
                self.tt("dve", t1[0:96, 0:n], pa[0:96, 0:n], cs[0:96, 0, 0:n], ALU.mult, [par, cs_r], [t1_r])
                self.tt("dve", t2[0:96, 0:n], pb[0:96, 0:n], cs[0:96, 1, 0:n], ALU.mult, [pbr, cs_r], [t2_r])
                self.tt("pool", qs[0:96, h, 0:n], t1[0:96, 0:n], t2[0:96, 0:n], ALU.add, [t1_r, t2_r], [qs_r])
            self.store(self.QM[:, :, c0:c0 + n].rearrange("h r t -> r h t"), qs[0:96, :, 0:n], qs_r)
            ks, ks_r = kst.next()
            for h in range(4):
                pt, pr = psr.next()
                self.mm(pt[0:64, 0:n], Wkv[:, h * 64:(h + 1) * 64], ckvn[:, 0:n], True, True, [Wkv_r, ckvn_r], [pr])
                self.cp("act", ks[0:64, h, 0:n], pt[0:64, 0:n], [pr], [ks_r])
            pa, par = psr.next()
            pb, pbr = psr.next()
            for k in range(NCH):
                self.mm(pa[0:96, 0:n], Wp[:, k, 320:416], xn[:, k, 0:n], k == 0, k == NCH - 1, [Wp_r, xn_r], [par])
            for k in range(NCH):
                self.mm(pb[0:96, 0:n], Wp[:, k, 416:512], xn[:, k, 0:n], k == 0, k == NCH - 1, [Wp_r, xn_r], [pbr])
            t1, t1_r = t1r.next()
            t2, t2_r = t2r.next()
            self.tt("dve", t1[64:96, 0:n], pa[64:96, 0:n], cs[64:96, 0, 0:n], ALU.mult, [par, cs_r], [t1_r])
            self.tt("dve", t2[64:96, 0:n], pb[64:96, 0:n], cs[64:96, 1, 0:n], ALU.mult, [pbr, cs_r], [t2_r])
            self.tt("pool", t1[64:96, 0:n], t1[64:96, 0:n], t2[64:96, 0:n], ALU.add, [t1_r, t2_r], [t1_r])
            for h in range(4):
                self.cp("pool" if h % 2 else "act", ks[64:96, h, 0:n], t1[64:96, 0:n], [t1_r], [ks_r])
            self.store(self.KM[:, :, c0:c0 + n].rearrange("h r t -> r h t"), ks[0:96, :, 0:n], ks_r)
            for j in range(0, n, 128):
                m = min(128, n - j)
                pt, pr = psr.next()
                self.mm(pt[0:m, 0:256], ckvn[:, j:j + m], Wkv[:, 256:512], True, True, [Wkv_r, ckvn_r], [pr])
                vs, vs_r = vst.next()
                self.P.add("pool", lambda e, vs=vs: e.memset(vs[:, :, 64:65], 1.0), writes=[vs_r])
                self.cp("act", vs[0:m, :, 0:64], pt[0:m, 0:256].rearrange("p (h d) -> p h d", h=4), [pr], [vs_r])
                self.store(self.VM[c0 + j:c0 + j + m, :], vs[0:m, :, :].rearrange("p h d -> p (h d)"), vs_r)

    def stage_lru(self, l):
        self.stage_begin()
        P = self.P
        pr_ = Res("lrup")
        convw = self.alloc([2, 4], F32)
        convb = self.alloc([2], F32)
        ba = self.alloc([2], F32)
        bx = self.alloc([2], F32)
        lam = self.alloc([2], F32)
        gl = self.alloc([2], F32)
        for c in range(2):
            for j in range(4):
                self.load(convw[:, c, j:j + 1], self.w["conv_w"][l][j, c * 128:(c + 1) * 128].rearrange("(p o) -> p o", o=1), pr_)
        self.load(convb[:, :], self.w["conv_b"][l].rearrange("(c p) -> p c", p=128), pr_)
        self.load(ba[:, :], self.w["lru_ba"][l].rearrange("n d -> (n d)").rearrange("(c p) -> p c", p=128), pr_)
        self.load(bx[:, :], self.w["lru_bx"][l].rearrange("n d -> (n d)").rearrange("(c p) -> p c", p=128), pr_)
        self.load(lam[:, :], self.w["lru_lambda"][l].rearrange("(c p) -> p c", p=128), pr_)
        self.load(gl[:, :], self.w["lru_out_norm"][l].rearrange("(c p) -> p c", p=128), pr_)
        nsp = self.alloc([2], F32)
        nsp2 = self.alloc([2], F32)
        self.act(nsp[:, :], lam[:, :], AF.Exp, [pr_], [pr_], scale=-1.0)
        self.ts("dve", nsp[:, :], nsp[:, :], 1.0, None, ALU.add, None, [pr_], [pr_])
        self.act(nsp[:, :], nsp[:, :], AF.Ln, [pr_], [pr_])
        self.ts("dve", nsp2[:, :], nsp[:, :], -16.0, None, ALU.mult, None, [pr_], [pr_])
        self.ts("dve", nsp[:, :], nsp[:, :], -8.0, None, ALU.mult, None, [pr_], [pr_])
        bd32 = self.alloc([2, 2, 128], F32)
        bd = self.alloc([2, 2, 128], BF16)
        bd_r = Res("bd")
        P.add("pool", lambda e: e.memset(bd32[:, :, :, :], 0.0), writes=[bd_r])
        for wi, nm in enumerate(("lru_wa", "lru_wx")):
            for nb in range(4):
                c, hb = nb // 2, nb % 2
                self.load(bd32[hb * 64:(hb + 1) * 64, wi, c, hb * 64:(hb + 1) * 64], self.w[nm][l][nb, :, :], bd_r)
        self.cp("dve", bd[:, :, :, :], bd32[:, :, :, :], [bd_r], [bd_r])
        carry = self.alloc([2], F32)
        carry_r = Res("carry")
        NB = 515
        uar = Ring([self.alloc([2, NB], F32) for _ in range(2)], "ua")
        gar = Ring([self.alloc([2, 512], F32) for _ in range(2)], "ga")
        fr = Ring([self.alloc([512], F32) for _ in range(14)], "lf")
        ucb_r = Ring([self.alloc([512], BF16) for _ in range(2)], "ucb")
        sqr = Ring([self.alloc([512], BF16) for _ in range(2)], "sq")
        yp = self.alloc([2, 512], F32)
        yp_r = Res("yp")
        rstd = self.alloc([512], F32)
        rstd_r = Res("rstd")
        yst = Ring([self.alloc([2, 512], BF16) for _ in range(2)], "yst")
        psr = self.psring([0, 1, 2, 3], "lru")
        ps_n, ps_nr = self.psum[7], self.psum_r[7]
        first = True
        for (c0, n) in self.tiles9():
            ua, ua_r = uar.next()
            ga, ga_r = gar.next()
            if first:
                P.add("pool", lambda e, ua=ua: e.memset(ua[:, :, 0:3], 0.0), writes=[ua_r])
                self.load(ua[:, :, 3:3 + n], self.LRU_IN[2:4, :, c0:c0 + n].rearrange("c p t -> p c t"), ua_r)
            else:
                self.load(ua[:, :, 0:3 + n], self.LRU_IN[2:4, :, c0 - 3:c0 + n].rearrange("c p t -> p c t"), ua_r)
            self.load(ga[:, :, 0:n], self.LRU_IN[0:2, :, c0:c0 + n].rearrange("c p t -> p c t"), ga_r)
            def chunk(j, first=first, c0=c0, n=n, ua=ua, ua_r=ua_r, ga=ga, ga_r=ga_r):
                uc, uc_r = fr.next()
                yield
                self.ts("dve", uc[:, 0:n], ua[:, j, 0:n], convw[:, j, 0:1], convb[:, j:j + 1], ALU.mult, ALU.add,
                        [ua_r, pr_], [uc_r])
                for t in range(1, 4):
                    self.stt(uc[:, 0:n], ua[:, j, t:t + n], convw[:, j, t:t + 1], uc[:, 0:n], ALU.mult, ALU.add,
                             [ua_r, pr_, uc_r], [uc_r])
                ucb, ucb_rr = ucb_r.next()
                yield
                self.cp("act", ucb[:, 0:n], uc[:, 0:n], [uc_r], [ucb_rr])
                yield
                pa, par = psr.next()
                yield
                px, pxr = psr.next()
                yield
                self.mm(pa[:, 0:n], bd[:, 0, j, :], ucb[:, 0:n], True, True, [bd_r, ucb_rr], [par])
                yield
                self.mm(px[:, 0:n], bd[:, 1, j, :], ucb[:, 0:n], True, True, [bd_r, ucb_rr], [pxr])
                yield
                r, r_r = fr.next()
                yield
                ig, ig_r = fr.next()
                yield
                self.act(r[:, 0:n], pa[:, 0:n], AF.Sigmoid, [par, pr_], [r_r], bias=ba[:, j:j + 1])
                yield
                self.act(ig[:, 0:n], px[:, 0:n], AF.Sigmoid, [pxr, pr_], [ig_r], bias=bx[:, j:j + 1])
                yield
                a, a_r = fr.next()
                yield
                a2, a2_r = fr.next()
                yield
                self.act(a[:, 0:n], r[:, 0:n], AF.Exp, [r_r, pr_], [a_r], scale=nsp[:, j:j + 1])
                yield
                self.act(a2[:, 0:n], r[:, 0:n], AF.Exp, [r_r, pr_], [a2_r], scale=nsp2[:, j:j + 1])
                yield
                self.ts("pool", a2[:, 0:n], a2[:, 0:n], -1.0, 1.0, ALU.mult, ALU.add, [a2_r], [a2_r])
                yield
                self.act(a2[:, 0:n], a2[:, 0:n], AF.Sqrt, [a2_r], [a2_r])
                yield
                self.tt("pool", ig[:, 0:n], ig[:, 0:n], uc[:, 0:n], ALU.mult, [ig_r, uc_r], [ig_r])
                yield
                self.tt("pool", ig[:, 0:n], ig[:, 0:n], a2[:, 0:n], ALU.mult, [ig_r, a2_r], [ig_r])
                yield
                hs, hs_r = fr.next()
                yield
                if first:
                    P.add("dve", lambda e, hs=hs, a=a, ig=ig, n=n: e.tensor_tensor_scan(
                        out=hs[:, 0:n], data0=a[:, 0:n], data1=ig[:, 0:n], initial=0.0, op0=ALU.mult, op1=ALU.add),
                        reads=[a_r, ig_r], writes=[hs_r])
                else:
                    P.add("dve", lambda e, hs=hs, a=a, ig=ig, n=n, j=j: e.tensor_tensor_scan(
                        out=hs[:, 0:n], data0=a[:, 0:n], data1=ig[:, 0:n], initial=carry[:, j:j + 1],
                        op0=ALU.mult, op1=ALU.add),
                        reads=[a_r, ig_r, carry_r], writes=[hs_r])
                self.cp("dve", carry[:, j:j + 1], hs[:, n - 1:n], [hs_r], [carry_r])
                yield
                yield
                g = ga[:, j, 0:n]
                t, t_r = fr.next()
                yield
                self.tt("pool", t[:, 0:n], g, g, ALU.mult, [ga_r], [t_r])
                yield
                self.ts("pool", t[:, 0:n], t[:, 0:n], 0.044715, 1.0, ALU.mult, ALU.add, [t_r], [t_r])
                yield
                self.tt("pool", t[:, 0:n], t[:, 0:n], g, ALU.mult, [t_r, ga_r], [t_r])
                yield
                self.act(t[:, 0:n], t[:, 0:n], AF.Sigmoid, [t_r], [t_r], scale=1.5957691216057308)
                yield
                self.tt("pool", t[:, 0:n], t[:, 0:n], g, ALU.mult, [t_r, ga_r], [t_r])
                yield
                self.tt("dve", yp[:, j, 0:n], hs[:, 0:n], t[:, 0:n], ALU.mult, [hs_r, t_r], [yp_r])
                yield
                sq, sq_r = sqr.next()
                yield
                self.tt("pool", sq[:, 0:n], yp[:, j, 0:n], yp[:, j, 0:n], ALU.mult, [yp_r], [sq_r])
                yield
                self.mm(ps_n[:, 0:n], self.ones_bf[:, :], sq[:, 0:n], j == 0, j == 1, [sq_r, self.ones_r], [ps_nr])
                yield

            import itertools
            for _ in itertools.zip_longest(chunk(0), chunk(1)):
                pass
            self.rsqrt_inplace(rstd[:, 0:n], rstd_r, 1.0 / 256, in_ap=ps_n[:, 0:n], in_reads=[ps_nr])
            ys, ys_r = yst.next()
            for j in range(2):
                self.stt(ys[:, j, 0:n], yp[:, j, 0:n], gl[:, j:j + 1], rstd[:, 0:n], ALU.mult, ALU.mult,
                         [yp_r, pr_, rstd_r], [ys_r])
            self.store(self.YT[0:2, :, c0:c0 + n].rearrange("c p t -> p c t"), ys[:, :, 0:n], ys_r)
            first = False

    def attn_sweep(self, qi, Kt, Qt, r0, r1, kq_r, V, v_r, dvp, obanks, scale, s_ring, pt_ring, tri, tri_r):
        if qi < 0:
            q0, nq, nsub = 0, NMETA, 1
            chunks = [(0, 0, NMETA, 0)]
        else:
            q0, nq, nsub = NMETA + 512 * qi, 512, 4
            chunks = [(0, 0, NMETA, -1)] + [(1 + i, NMETA + 128 * i, 128, -1) for i in range(4 * qi)]
            chunks += [(1 + 4 * qi + c, NMETA + 128 * (4 * qi + c), 128, c) for c in range(4)]
        started = set()
        for (kc, kcol, nk, diag) in chunks:
            s0 = max(diag, 0)
            qa = q0 + 128 * s0 if qi >= 0 else 0
            nqa = nq - 128 * s0 if qi >= 0 else nq
            ps, ps_r = s_ring.next()
            self.mm(ps[0:nk, 0:nqa], Kt[r0:r1, kcol:kcol + nk], Qt[r0:r1, qa:qa + nqa], True, True, [kq_r], [ps_r])
            pt, pt_r = pt_ring.next()
            self.act(pt[0:nk, 0:nqa], ps[0:nk, 0:nqa], AF.Exp, [ps_r], [pt_r], scale=scale)
            if diag >= 0:
                w = min(128, nqa)
                self.tt("dve", pt[0:nk, 0:w], pt[0:nk, 0:w], tri[0:nk, 0:w], ALU.mult, [pt_r, tri_r], [pt_r])
            self.flush_pv()

            def pv(s0=s0, pt=pt, pt_r=pt_r, nk=nk, kc=kc):
                for s in range(s0, nsub):
                    ob, ob_r, oc = obanks[s]
                    ws = min(128, nq)
                    lo = (s - s0) * 128
                    st = id(ob) not in started
                    started.add(id(ob))
                    self.mm(ob[0:ws, oc:oc + dvp], pt[0:nk, lo:lo + ws], V[0:nk, kc, 0:dvp], st, False,
                            [pt_r, v_r], [ob_r], skip_group_check=True)
            self._pend = pv
            self.run_deferred(4)

    _pend = None
    _deferred = None

    def defer(self, fn):
        if self._deferred is None:
            self._deferred = []
        self._deferred.append(fn)

    def run_deferred(self, k=None):
        d = self._deferred or []
        n = len(d) if k is None else min(k, len(d))
        for _ in range(n):
            d.pop(0)()

    def flush_pv(self):
        if self._pend is not None:
            p = self._pend
            self._pend = None
            p()

    def load_consts_attn(self):
        tri32 = self.alloc([128], F32)
        tri = self.alloc([128], BF16)
        tri_r = Res("tri")
        self.load(tri32[:, :], self.tri_in[:, :], tri_r)
        self.cp("dve", tri[:, :], tri32[:, :], [tri_r], [tri_r])
        idb32 = self.alloc([128], F32)
        idb = self.alloc([128], BF16)
        idb_r = Res("idb")
        self.cp("dve", idb[:, :], self.identf[:, :], [self.identf_r], [idb_r])
        return tri, tri_r, idb, idb_r

    def stage_diff(self, l):
        self.stage_begin()
        P = self.P
        lam_init = 0.8 - 0.6 * math.exp(-0.3 * l)
        tri, tri_r, idb, idb_r = self.load_consts_attn()
        lv = self.alloc([4, 64], F32)
        lv_r = Res("lv")
        for i, nm in enumerate(("lam_q1", "lam_k1", "lam_q2", "lam_k2")):
            self.load(lv[:, i, :], self.w[nm][l].partition_broadcast(128), lv_r)
        lsc = self.alloc([8], F32)
        self.tt("dve", lv[:, 0, :], lv[:, 0, :], lv[:, 1, :], ALU.mult, [lv_r], [lv_r])
        self.tt("dve", lv[:, 2, :], lv[:, 2, :], lv[:, 3, :], ALU.mult, [lv_r], [lv_r])
        P.add("dve", lambda e: e.reduce_sum(out=lsc[:, 0:1], in_=lv[:, 0, :], axis=AX.X), reads=[lv_r], writes=[lv_r])
        P.add("dve", lambda e: e.reduce_sum(out=lsc[:, 1:2], in_=lv[:, 2, :], axis=AX.X), reads=[lv_r], writes=[lv_r])
        self.act(lsc[:, 0:2], lsc[:, 0:2], AF.Exp, [lv_r], [lv_r])
        self.tt("dve", lsc[:, 2:3], lsc[:, 1:2], lsc[:, 0:1], ALU.subtract, [lv_r], [lv_r])
        self.ts("dve", lsc[:, 2:3], lsc[:, 2:3], -lam_init, None, ALU.add, None, [lv_r], [lv_r])
        gsub = self.alloc([128], F32)
        self.load(gsub[:, :], self.w["diff_subln"][l].partition_broadcast(128), lv_r)
        self.ts("dve", gsub[:, :], gsub[:, :], 1.0 - lam_init, None, ALU.mult, None, [lv_r], [lv_r])

        Kt = self.alloc([T], BF16)
        Qm = [self.alloc([T], BF16) for _ in range(2)]
        kq_r = Res("kq")
        P.add("pool", lambda e: e.memset(Qm[0][64:128, :], 0.0), writes=[kq_r])
        P.add("pool", lambda e: e.memset(Qm[1][0:64, :], 0.0), writes=[kq_r])
        V = self.alloc([33, 129], BF16)
        v_r = Res("V")
        pt_ring = Ring([self.alloc([512], BF16) for _ in range(3)], "pt")
        o1r = Ring([self.alloc([128], F32) for _ in range(8)], "o1")
        o2r = Ring([self.alloc([128], F32) for _ in range(8)], "o2")
        scr = Ring([self.alloc([128], F32) for _ in range(2)], "scr")
        smr = Ring([self.alloc([8], F32) for _ in range(8)], "sm")
        ybr = Ring([self.alloc([128], BF16) for _ in range(4)], "yb")
        ytr = Ring([self.alloc([512], BF16) for _ in range(3)], "yts")
        s_ring = self.psring([0, 1, 2], "s")
        tps = self.psum[7][:, :].bitcast(BF16)
        tps_r = self.psum_r[7]
        cb = self.conv_bufs()
        n_slots = 4 * (1 + SEQ // 512)
        need_now = [j for j in self.conv_queue if j[0] == l or (j[0] == l + 1 and j[1] == 1)]
        slot_i = 0
        for h in range(4):
            self.load(Qm[0][0:64, :], self.QKD[h, 0:64, :], kq_r)
            self.load(Qm[1][64:128, :], self.QKD[h, 64:128, :], kq_r)
            self.load(Kt[:, :], self.QKD[4 + h, :, :], kq_r)
            P.add("pool", lambda e: e.memset(V[:, :, 128:129], 1.0), writes=[v_r])
            self.load(V[0:NMETA, 0, 0:128], self.VD[0:NMETA, h * 128:(h + 1) * 128], v_r)
            for i0 in range(0, 32, 8):
                self.load(V[:, 1 + i0:9 + i0, 0:128],
                          self.VD[NMETA + 128 * i0:NMETA + 128 * (i0 + 8), h * 128:(h + 1) * 128].rearrange("(i p) d -> p i d", p=128), v_r)
            for qi in range(-1, SEQ // 512):
                nsub = 1 if qi < 0 else 4
                ws = NMETA if qi < 0 else 128
                q0 = 0 if qi < 0 else NMETA + 512 * qi
                ob = {}
                for m in range(2):
                    banks = [(self.psum[3 + 2 * m], self.psum_r[3 + 2 * m], 0), (self.psum[3 + 2 * m], self.psum_r[3 + 2 * m], 129),
                             (self.psum[4 + 2 * m], self.psum_r[4 + 2 * m], 0), (self.psum[4 + 2 * m], self.psum_r[4 + 2 * m], 129)]
                    ob[m] = banks
                    self.attn_sweep(qi, Kt, Qm[m], 0, 128, kq_r, V, v_r, 129, banks, 0.125, s_ring, pt_ring, tri, tri_r)
                self.flush_pv()
                left = n_slots - slot_i
                k = -(-len(need_now) // left) if left > 0 else len(need_now)
                for _ in range(k):
                    if need_now:
                        job = need_now.pop(0)
                        self.conv_queue.remove(job)
                        self.conv_job(job, cb)
                slot_i += 1
                yts, yts_r = ytr.next()
                for s in range(nsub):
                    sm, sm_r = smr.next()
                    o1, o1_r = o1r.next()
                    o2, o2_r = o2r.next()
                    for m, (o, o_r) in enumerate(((o1, o1_r), (o2, o2_r))):
                        b, b_r, oc = ob[m][s]
                        P.add("dve", lambda e, sm=sm, b=b, oc=oc, m=m, ws=ws: e.reciprocal(
                            out=sm[0:ws, m:m + 1], in_=b[0:ws, oc + 128:oc + 129]), reads=[b_r], writes=[sm_r])
                        self.ts("dve", o[0:ws, :], b[0:ws, oc:oc + 128], sm[0:ws, m:m + 1], None, ALU.mult, None, [b_r, sm_r], [o_r])

                    def tail(s=s, sm=sm, sm_r=sm_r, o1=o1, o1_r=o1_r, o2=o2, o2_r=o2_r, ws=ws, yts=yts, yts_r=yts_r):
                        sc, sc_r = scr.next()
                        yb, yb_r = ybr.next()
                        return [
                            lambda: self.stt(o1[0:ws, :], o2[0:ws, :], lsc[0:ws, 2:3], o1[0:ws, :], ALU.mult, ALU.add,
                                             [o1_r, o2_r, lv_r], [o1_r]),
                            lambda: self.tt("dve", sc[0:ws, :], o1[0:ws, :], o1[0:ws, :], ALU.mult, [o1_r], [sc_r]),
                            lambda: P.add("dve", lambda e: e.reduce_sum(out=sm[0:ws, 2:3], in_=sc[0:ws, :], axis=AX.X),
                                          reads=[sc_r], writes=[sm_r]),
                            lambda: self.ts("dve", sm[0:ws, 2:3], sm[0:ws, 2:3], 1.0 / 128, EPS, ALU.mult, ALU.add, [sm_r], [sm_r]),
                            lambda: self.act(sm[0:ws, 2:3], sm[0:ws, 2:3], AF.Ln, [sm_r], [sm_r]),
                            lambda: self.act(sm[0:ws, 2:3], sm[0:ws, 2:3], AF.Exp, [sm_r], [sm_r], scale=-0.5),
                            lambda: self.stt(yb[0:ws, :], o1[0:ws, :], sm[0:ws, 2:3], gsub[0:ws, :], ALU.mult, ALU.mult,
                                             [o1_r, sm_r, lv_r], [yb_r]),
                            lambda: P.add("pe", lambda e: e.transpose(out=tps[:, s * 128:s * 128 + ws], in_=yb[0:ws, :],
                                                                       identity=idb[0:ws, 0:ws]),
                                          reads=[yb_r, idb_r], writes=[tps_r]),
                            lambda: self.cp("dve", yts[:, s * 128:s * 128 + ws], tps[:, s * 128:s * 128 + ws], [tps_r], [yts_r]),
                        ]
                    for fn in tail():
                        self.defer(fn)
                nq = NMETA if qi < 0 else 512
                self.defer(lambda h=h, q0=q0, nq=nq, yts=yts, yts_r=yts_r: self.store(self.YT[2 + h, :, q0:q0 + nq], yts[:, 0:nq], yts_r))
        self.run_deferred()

    def stage_mla(self, l):
        self.stage_begin()
        P = self.P
        tri, tri_r, idb, idb_r = self.load_consts_attn()
        gm = self.alloc([256], F32)
        gm_r = Res("gm")
        self.load(gm[:, :], self.w["mla_out_norm"][l].partition_broadcast(128), gm_r)
        Kt = self.alloc([4, T], BF16)
        kq_r = Res("kq")
        V = self.alloc([33, 4, 65], BF16)
        v_r = Res("V")
        self.load(Kt[0:96, :, :], self.KM[:, :, :].rearrange("h r t -> r h t"), kq_r)
        Vf = V.rearrange("p i h d -> p i (h d)")
        self.load(Vf[0:NMETA, 0, :], self.VM[0:NMETA, :], v_r)
        for i0 in range(0, 32, 8):
            self.load(Vf[:, 1 + i0:9 + i0, :],
                      self.VM[NMETA + 128 * i0:NMETA + 128 * (i0 + 8), :].rearrange("(i p) d -> p i d", p=128), v_r)
        qtr = Ring([self.alloc([4, 512], BF16) for _ in range(2)], "qt")
        pt_ring = Ring([self.alloc([512], BF16) for _ in range(3)], "pt")
        ocr = Ring([self.alloc([256], F32) for _ in range(5)], "oc")
        scr = Ring([self.alloc([256], F32) for _ in range(1)], "scr")
        smr = Ring([self.alloc([8], F32) for _ in range(8)], "sm")
        ybr = Ring([self.alloc([256], BF16) for _ in range(3)], "yb")
        ytr = Ring([self.alloc([2, 512], BF16) for _ in range(2)], "yts")
        s_ring = self.psring([0, 1, 2], "s")
        tps = self.psum[7][:, :].bitcast(BF16)
        tps_r = self.psum_r[7]
        scale = 96.0 ** -0.5
        for qi in range(-1, SEQ // 512):
            nsub = 1 if qi < 0 else 4
            ws = NMETA if qi < 0 else 128
            q0 = 0 if qi < 0 else NMETA + 512 * qi
            nq = NMETA if qi < 0 else 512
            qt, qt_r = qtr.next()
            self.load(qt[0:96, :, 0:nq], self.QM[:, :, q0:q0 + nq].rearrange("h r t -> r h t"), qt_r)
            ob = {}
            for h in range(4):
                banks = [(self.psum[3 + h], self.psum_r[3 + h], 65 * s) for s in range(4)]
                ob[h] = banks
                self.attn_sweep_local(qi, Kt[:, h, :], qt[:, h, :], kq_r, qt_r, V[:, :, h, :], v_r, 65, banks, scale,
                                      s_ring, pt_ring, tri, tri_r)
            self.flush_pv()
            yts, yts_r = ytr.next()
            for s in range(nsub):
                sm, sm_r = smr.next()
                oc, oc_r = ocr.next()
                for h in range(4):
                    b, b_r, c = ob[h][s]
                    P.add("dve", lambda e, sm=sm, b=b, c=c, h=h, ws=ws: e.reciprocal(
                        out=sm[0:ws, h:h + 1], in_=b[0:ws, c + 64:c + 65]), reads=[b_r], writes=[sm_r])
                    self.ts("dve", oc[0:ws, h * 64:(h + 1) * 64], b[0:ws, c:c + 64], sm[0:ws, h:h + 1], None, ALU.mult, None,
                            [b_r, sm_r], [oc_r])

                def tail(s=s, sm=sm, sm_r=sm_r, oc=oc, oc_r=oc_r, ws=ws, yts=yts, yts_r=yts_r):
                    sc, sc_r = scr.next()
                    yb, yb_r = ybr.next()
                    fns = [
                        lambda: self.tt("dve", sc[0:ws, :], oc[0:ws, :], oc[0:ws, :], ALU.mult, [oc_r], [sc_r]),
                        lambda: P.add("dve", lambda e: e.reduce_sum(out=sm[0:ws, 4:5], in_=sc[0:ws, :], axis=AX.X),
                                      reads=[sc_r], writes=[sm_r]),
                        lambda: self.ts("dve", sm[0:ws, 4:5], sm[0:ws, 4:5], 1.0 / 256, EPS, ALU.mult, ALU.add, [sm_r], [sm_r]),
                        lambda: self.act(sm[0:ws, 4:5], sm[0:ws, 4:5], AF.Ln, [sm_r], [sm_r]),
                        lambda: self.act(sm[0:ws, 4:5], sm[0:ws, 4:5], AF.Exp, [sm_r], [sm_r], scale=-0.5),
                        lambda: self.stt(yb[0:ws, :], oc[0:ws, :], sm[0:ws, 4:5], gm[0:ws, :], ALU.mult, ALU.mult,
                                         [oc_r, sm_r, gm_r], [yb_r]),
                    ]
                    for c in range(2):
                        fns.append(lambda c=c: P.add("pe", lambda e: e.transpose(
                            out=tps[:, c * 512 + s * 128:c * 512 + s * 128 + ws], in_=yb[0:ws, c * 128:(c + 1) * 128],
                            identity=idb[0:ws, 0:ws]), reads=[yb_r, idb_r], writes=[tps_r]))
                        fns.append(lambda c=c: self.cp("dve", yts[:, c, s * 128:s * 128 + ws],
                                                       tps[:, c * 512 + s * 128:c * 512 + s * 128 + ws], [tps_r], [yts_r]))
                    return fns
                for fn in tail():
                    self.defer(fn)
            self.defer(lambda q0=q0, nq=nq, yts=yts, yts_r=yts_r: self.store(
                self.YT[6:8, :, q0:q0 + nq].rearrange("c p t -> p c t"), yts[:, :, 0:nq], yts_r))
        self.run_deferred()

    def attn_sweep_local(self, qi, Kt, Qtile, kq_r, qt_r, V, v_r, dvp, obanks, scale, s_ring, pt_ring, tri, tri_r):
        if qi < 0:
            nq, nsub = NMETA, 1
            chunks = [(0, 0, NMETA, 0)]
        else:
            nq, nsub = 512, 4
            chunks = [(0, 0, NMETA, -1)] + [(1 + i, NMETA + 128 * i, 128, -1) for i in range(4 * qi)]
            chunks += [(1 + 4 * qi + c, NMETA + 128 * (4 * qi + c), 128, c) for c in range(4)]
        started = set()
        for (kc, kcol, nk, diag) in chunks:
            s0 = max(diag, 0)
            qa = 128 * s0 if qi >= 0 else 0
            nqa = nq - qa
            ps, ps_r = s_ring.next()
            self.mm(ps[0:nk, 0:nqa], Kt[0:96, kcol:kcol + nk], Qtile[0:96, qa:qa + nqa], True, True, [kq_r, qt_r], [ps_r])
            pt, pt_r = pt_ring.next()
            self.act(pt[0:nk, 0:nqa], ps[0:nk, 0:nqa], AF.Exp, [ps_r], [pt_r], scale=scale)
            if diag >= 0:
                w = min(128, nqa)
                self.tt("dve", pt[0:nk, 0:w], pt[0:nk, 0:w], tri[0:nk, 0:w], ALU.mult, [pt_r, tri_r], [pt_r])
            self.flush_pv()

            def pv(s0=s0, pt=pt, pt_r=pt_r, nk=nk, kc=kc):
                for s in range(s0, nsub):
                    ob, ob_r, oc = obanks[s]
                    ws = min(128, nq)
                    lo = (s - s0) * 128
                    st = id(ob) not in started
                    started.add(id(ob))
                    self.mm(ob[0:ws, oc:oc + dvp], pt[0:nk, lo:lo + ws], V[0:nk, kc, 0:dvp], st, False,
                            [pt_r, v_r], [ob_r], skip_group_check=True)
            self._pend = pv
            self.run_deferred(4)

    def stage_oproj(self, l):
        self.stage_begin()
        Wo = self.alloc([NCH, D], BF16)
        Wo_r = Res("Wo")
        self.load_w_bf16(Wo, Wo_r, self.w["w_out"][l], D, slab=256)
        ytr = Ring([self.alloc([NCH, 512], BF16) for _ in range(2)], "yt")
        psr = self.psring([0, 1, 2, 3], "op")
        for (c0, n) in self.tiles9():
            yt, yt_r = ytr.next()
            self.load(yt[:, :, 0:n], self.YT[:, :, c0:c0 + n].rearrange("c p t -> p c t"), yt_r)
            for o in range(NCH):
                po, por = psr.next()
                for k in range(NCH):
                    self.mm(po[:, 0:n], Wo[:, k, o * 128:(o + 1) * 128], yt[:, k, 0:n], k == 0, k == NCH - 1, [Wo_r, yt_r], [por])
                hr = self.h_res(o, c0, n)
                self.stt(self.hT[:, o, c0:c0 + n], po[:, 0:n], 1.0, self.hT[:, o, c0:c0 + n], ALU.mult, ALU.add,
                         [por] + hr, hr)


    def stage_final(self):
        P = self.P
        self.stage_begin()
        gbc = self.alloc([D], F32)
        gbc_r = Res("gbc")
        P.add("sp", lambda e: e.dma_start(out=gbc[:, :], in_=self.final_norm.partition_broadcast(128)),
              writes=[gbc_r], dma=True, sem_res=gbc_r)
        otr = Ring([self.alloc([D], F32) for i in range(2)], "ot")
        sq = self.alloc([512], F32)
        sq_r = Res("fsq")
        ssr = Ring([self.alloc([4], F32) for i in range(2)], "fss")
        psr = self.psring([0, 1, 2, 3], "f")
        for i in range(SEQ // 128):
            c0 = NMETA + 128 * i
            o, o_r = otr.next()
            s, s_r = ssr.next()
            halves = []
            for half in range(2):
                pt, pr = psr.next()
                for cc in range(4):
                    c = half * 4 + cc
                    P.add("pe", lambda e, pt=pt, cc=cc, c=c, c0=c0: e.transpose(
                        out=pt[:, cc * 128:(cc + 1) * 128], in_=self.hT[:, c, c0:c0 + 128],
                        identity=self.identf[:, :]),
                        reads=self.h_res(c, c0, 128) + [self.identf_r], writes=[pr])
                halves.append((pt, pr))
            for half, (pt, pr) in enumerate(halves):
                P.add("act", lambda e, pt=pt, s=s, half=half: e.activation(
                    out=sq[:, 0:512], in_=pt[:, :], func=AF.Square, accum_out=s[:, half:half + 1]),
                    reads=[pr], writes=[sq_r, s_r])
            P.add("dve", lambda e, s=s: e.tensor_tensor(out=s[:, 2:3], in0=s[:, 0:1], in1=s[:, 1:2], op=ALU.add),
                  reads=[s_r], writes=[s_r])
            P.add("dve", lambda e, s=s: e.tensor_scalar(out=s[:, 2:3], in0=s[:, 2:3], scalar1=1.0 / D, scalar2=EPS,
                                                         op0=ALU.mult, op1=ALU.add),
                  reads=[s_r], writes=[s_r])
            P.add("act", lambda e, s=s: e.activation(out=s[:, 3:4], in_=s[:, 2:3], func=AF.Sqrt),
                  reads=[s_r], writes=[s_r])
            P.add("dve", lambda e, s=s: e.reciprocal(out=s[:, 3:4], in_=s[:, 3:4]),
                  reads=[s_r], writes=[s_r])
            for half, (pt, pr) in enumerate(halves):
                P.add("dve", lambda e, pt=pt, s=s, o=o, half=half: e.scalar_tensor_tensor(
                    out=o[:, half * 512:(half + 1) * 512], in0=pt[:, :], scalar=s[:, 3:4],
                    in1=gbc[:, half * 512:(half + 1) * 512], op0=ALU.mult, op1=ALU.mult),
                    reads=[pr, s_r, gbc_r], writes=[o_r])
            P.add("sp", lambda e, o=o, i=i: e.dma_start(out=self.out[128 * i:128 * (i + 1), :], in_=o[:, :]),
                  reads=[o_r], dma=True, sem_res=o_r)
        self.P.barrier()


def build_program(n_layers=DEPTH, debug=None):
    b = Builder(n_layers, debug)
    nc = b.build()
    return nc, b


_CONSTS = None


def consts():
    global _CONSTS
    if _CONSTS is None:
        half = 16
        inv = (np.float32(10000.0) ** (-(np.arange(half, dtype=np.float32)) / np.float32(half))).astype(np.float32)
        ang = (np.arange(T, dtype=np.float32)[None, :] * inv[:, None]).astype(np.float32).astype(np.float64)
        cs = np.zeros((96, 2, T), np.float32)
        cs[0:64, 0, :] = 1.0
        cs[64:80, 0, :] = np.cos(ang)
        cs[80:96, 0, :] = np.cos(ang)
        cs[64:80, 1, :] = -np.sin(ang)
        cs[80:96, 1, :] = np.sin(ang)
        tri = (np.arange(128)[:, None] <= np.arange(128)[None, :]).astype(np.float32)
        _CONSTS = {"ident_f": np.eye(128, dtype=np.float32), "rope_cs": cs, "tri_in": tri}
    return _CONSTS


def make_in_maps(inputs, b):
    c = consts()
    maps = []
    for core in range(8):
        m = {}
        for nm in b.inputs:
            if nm == "x":
                m[nm] = np.ascontiguousarray(inputs["x"][core])
            elif nm in c:
                m[nm] = c[nm]
            else:
                m[nm] = np.ascontiguousarray(inputs[nm])
        maps.append(m)
    return maps


def kernel(**inputs):
    inputs = {k: np.asarray(v) for k, v in inputs.items()}
    nc, b = build_program()
    maps = make_in_maps(inputs, b)
    res = run_bass_kernel_spmd(nc, maps, core_ids=list(range(8)))
    out = np.stack([np.asarray(res.results[i]["out"]) for i in range(8)], axis=0)
    return out.astype(np.float32)
```

```python
import contextlib
import math
import numpy as np
import ml_dtypes
import concourse.bass as bass
import concourse.mybir as mybir
from concourse.bass_utils import run_bass_kernel_spmd

F32 = mybir.dt.float32
BF16 = mybir.dt.bfloat16
AF = mybir.ActivationFunctionType
ALU = mybir.AluOpType
AX = mybir.AxisListType

D = 1024
SEQ = 4096
NMETA = 16
T = SEQ + NMETA
DFF = 2816
NF = DFF // 128
NCH = D // 128
DEPTH = 2
EPS = 1e-6
DIN = 2464


class Res:
    __slots__ = ("name", "w", "r", "slot")

    def __init__(self, name):
        self.name = name
        self.w = None
        self.r = []
        self.slot = None


class SemSlot:
    __slots__ = ("sem", "dcount")

    def __init__(self):
        self.sem = None
        self.dcount = 0


class Op:
    __slots__ = ("eng", "fn", "deps", "dma", "sem_res", "val", "needs_inc", "idx", "gen")

    def __init__(self, eng, fn, dma=False):
        self.eng = eng
        self.fn = fn
        self.deps = []
        self.dma = dma
        self.sem_res = None
        self.val = 0
        self.needs_inc = False


ENGS = ("pe", "act", "dve", "pool", "sp")
MAX_DMA_INFLIGHT = 4


class Prog:
    def __init__(self):
        self.q = {e: [] for e in ENGS}
        self.pool = []
        self.stage_k = 0
        self.gen = 0
        self.last_dma = {}
        self.dma_fifo = []

    def barrier(self):
        deps = []
        for e in ENGS:
            for op in reversed(self.q[e]):
                if not op.dma and op.fn is not None:
                    deps.append((op, 0))
                    break
        for op in self.last_dma.values():
            deps.append((op, op.sem_res.dcount))
        for e in ENGS:
            op = Op(e, None)
            op.gen = self.gen
            op.deps = [(o, s) for (o, s) in deps if not (o.eng == e and not o.dma)]
            for o, _ in op.deps:
                o.needs_inc = True
            self.q[e].append(op)
        self.gen += 1
        self.stage_k = 0
        self.last_dma = {}

    def add(self, eng, fn, reads=(), writes=(), dma=False, sem_res=None):
        op = Op(eng, fn, dma)
        op.gen = self.gen
        deps = {}

        def dep(o):
            if o is None or o.gen < self.gen:
                return
            if (not o.dma) and o.eng == "pe" and eng == "pe" and not dma:
                return
            snap = o.sem_res.dcount if o.dma else 0
            deps[id(o)] = (o, snap)

        for r in reads:
            dep(r.w)
        for w in writes:
            dep(w.w)
            for o in w.r:
                dep(o)
        if dma:
            if len(self.dma_fifo) >= MAX_DMA_INFLIGHT:
                old = self.dma_fifo[-MAX_DMA_INFLIGHT]
                if old.gen == self.gen and id(old) not in deps:
                    deps[id(old)] = (old, 0)
            self.dma_fifo.append(op)
        op.deps = list(deps.values())
        for o, _ in op.deps:
            o.needs_inc = True
        if dma:
            assert sem_res is not None
            if sem_res.slot is None or sem_res.slot[1] != self.gen:
                if self.stage_k >= len(self.pool):
                    self.pool.append(SemSlot())
                sem_res.slot = (self.pool[self.stage_k], self.gen)
                self.stage_k += 1
            sl = sem_res.slot[0]
            sl.dcount += 16
            op.sem_res = sl
            op.val = sl.dcount
            self.last_dma[id(sl)] = op
        for r in reads:
            if not dma:
                r.r = [o for o in r.r if o.dma or o.eng != eng]
            r.r.append(op)
        for w in writes:
            w.w = op
            w.r = []
        self.q[eng].append(op)
        return op

    def emit(self, nc, es):
        esem = {e: es.enter_context(nc.semaphore("s_" + e)) for e in ENGS}
        for i, sl in enumerate(self.pool):
            sl.sem = es.enter_context(nc.semaphore("d%d" % i))
        for e in ENGS:
            cnt = 0
            for op in self.q[e]:
                if op.dma or op.fn is None:
                    continue
                if op.needs_inc:
                    cnt += 1
                    op.val = cnt
        block = es.enter_context(nc.Block())
        engobj = {"pe": "tensor", "act": "scalar", "dve": "vector", "pool": "gpsimd", "sp": "sync"}

        def body_for(e):
            ops = self.q[e]

            def body(eng):
                seen = {}
                for op in ops:
                    for d, snap in op.deps:
                        if d.dma:
                            sem, val = d.sem_res.sem, max(d.val, snap)
                        else:
                            sem, val = esem[d.eng], d.val
                        key = id(sem)
                        if seen.get(key, 0) >= val:
                            continue
                        seen[key] = val
                        eng.wait_ge(sem, val)
                    if op.fn is None:
                        continue
                    ins = op.fn(eng)
                    if op.dma:
                        ins.then_inc(op.sem_res.sem, 16)
                    elif op.needs_inc:
                        ins.then_inc(esem[e], 1)
            return body

        for e in ENGS:
            getattr(block, engobj[e])(body_for(e))


class Ring:
    def __init__(self, aps, name):
        self.aps = aps
        self.res = [Res("%s%d" % (name, i)) for i in range(len(aps))]
        self.i = 0

    def next(self):
        k = self.i % len(self.aps)
        self.i += 1
        return self.aps[k], self.res[k]


ARENA_BYTES = 79360


def dsize(dt):
    return 4 if dt == F32 else 2


class Builder:
    def __init__(self, n_layers=DEPTH, debug=None):
        self.n_layers = n_layers
        self.debug = debug or {}
        self.nc = bass.Bass("TRN2", target_bir_lowering=False)
        self.P = Prog()
        self.es = contextlib.ExitStack()
        self.inputs = {}
        self.aoff = 0

    def din(self, name, shape, dt=F32):
        ap = self.nc.dram_tensor(name, list(shape), dt, kind="ExternalInput").ap()
        self.inputs[name] = ap
        return ap

    def dout(self, name, shape, dt=F32):
        return self.nc.dram_tensor(name, list(shape), dt, kind="ExternalOutput").ap()

    def dscratch(self, name, shape, dt):
        return self.nc.dram_tensor(name, list(shape), dt, kind="Internal").ap()

    def sb(self, name, shape, dt):
        return self.es.enter_context(self.nc.sbuf_tensor(name, list(shape), dt))

    def ps(self, name, shape, dt=F32):
        return self.es.enter_context(self.nc.psum_tensor(name, list(shape), dt))

    def stage_begin(self):
        self.P.barrier()
        self.aoff = 0

    def alloc(self, shape, dt):
        n = int(np.prod(shape))
        nbytes = (n * dsize(dt) + 31) // 32 * 32
        assert self.aoff + nbytes <= ARENA_BYTES, (self.aoff, nbytes)
        ap = self.arena[:, self.aoff // 4:(self.aoff + nbytes) // 4]
        self.aoff += nbytes
        if dt != F32:
            ap = ap.bitcast(dt)
        ap = ap[:, 0:n]
        if len(shape) == 2:
            ap = ap.rearrange("p (a b) -> p a b", a=shape[0])
        elif len(shape) == 3:
            ap = ap.rearrange("p (a b c) -> p a b c", a=shape[0], b=shape[1])
        return ap


    def act(self, out, in_, func, reads, writes, **kw):
        return self.P.add("act", lambda e: e.activation(out=out, in_=in_, func=func, **kw), reads=reads, writes=writes)

    def tt(self, eng, out, in0, in1, op, reads, writes):
        return self.P.add(eng, lambda e: e.tensor_tensor(out=out, in0=in0, in1=in1, op=op), reads=reads, writes=writes)

    def ts(self, eng, out, in0, s1, s2, op0, op1, reads, writes):
        if s2 is None:
            return self.P.add(eng, lambda e: e.tensor_scalar(out=out, in0=in0, scalar1=s1, scalar2=None, op0=op0),
                              reads=reads, writes=writes)
        return self.P.add(eng, lambda e: e.tensor_scalar(out=out, in0=in0, scalar1=s1, scalar2=s2, op0=op0, op1=op1),
                          reads=reads, writes=writes)

    def stt(self, out, in0, scalar, in1, op0, op1, reads, writes):
        return self.P.add("dve", lambda e: e.scalar_tensor_tensor(out=out, in0=in0, scalar=scalar, in1=in1, op0=op0, op1=op1),
                          reads=reads, writes=writes)

    def cp(self, eng, out, in_, reads, writes):
        if eng == "act":
            return self.act(out, in_, AF.Copy, reads, writes)
        return self.P.add(eng, lambda e: e.tensor_copy(out=out, in_=in_), reads=reads, writes=writes)

    def mm(self, out, lhsT, rhs, start, stop, reads, writes, **kw):
        return self.P.add("pe", lambda e: e.matmul(out, lhsT=lhsT, rhs=rhs, start=start, stop=stop, **kw),
                          reads=reads, writes=writes)

    def load(self, out, in_, res, reads=()):
        return self.P.add("sp", lambda e: e.dma_start(out=out, in_=in_), reads=list(reads), writes=[res], dma=True, sem_res=res)

    def store(self, out, in_, res):
        return self.P.add("sp", lambda e: e.dma_start(out=out, in_=in_), reads=[res], dma=True, sem_res=res)

    def rsqrt_inplace(self, ap, res, scale, in_ap=None, in_reads=()):
        src = ap if in_ap is None else in_ap
        self.ts("dve", ap, src, scale, EPS, ALU.mult, ALU.add, [res] + list(in_reads), [res])
        self.act(ap, ap, AF.Sqrt, [res], [res])
        self.P.add("dve", lambda e: e.reciprocal(out=ap, in_=ap), reads=[res], writes=[res])

    def h_res(self, c, c0, n):
        out = []
        if c0 < NMETA:
            out.append(self.hres[c][0])
        lo = max(c0, NMETA) - NMETA
        hi = c0 + n - NMETA
        if hi > lo:
            for i in range(lo // 128, (hi - 1) // 128 + 1):
                out.append(self.hres[c][1 + i])
        return out

    def build(self):
        nc, P = self.nc, self.P
        self.x = self.din("x", [SEQ, D])
        self.meta = self.din("meta_tokens", [NMETA, D])
        self.final_norm = self.din("final_norm", [D])
        self.ident_f = self.din("ident_f", [128, 128])
        L = DEPTH
        self.w = {}
        for nm, shp in [("ffn1_norm", [L, D]), ("ffn1_in", [L, D, 2 * DFF]), ("ffn1_out", [L, DFF, D]),
                        ("ffn2_norm", [L, D]), ("ffn2_in", [L, D, 2 * DFF]), ("ffn2_out", [L, DFF, D])]:
            self.w[nm] = self.din(nm, shp)
        for nm, shp in [("mix_norm", [L, D]), ("w_in", [L, D, DIN]), ("conv_w", [L, 4, 256]), ("conv_b", [L, 256]),
                        ("lru_wa", [L, 4, 64, 64]), ("lru_ba", [L, 4, 64]), ("lru_wx", [L, 4, 64, 64]), ("lru_bx", [L, 4, 64]),
                        ("lru_lambda", [L, 256]), ("lru_out_norm", [L, 256]), ("lam_q1", [L, 64]), ("lam_k1", [L, 64]),
                        ("lam_q2", [L, 64]), ("lam_k2", [L, 64]), ("diff_subln", [L, 128]), ("q_norm", [L, 256]),
                        ("w_uq", [L, 256, 384]), ("kv_norm", [L, 128]), ("w_ukv", [L, 128, 512]),
                        ("mla_out_norm", [L, 256]), ("w_out", [L, D, D])]:
            self.w[nm] = self.din(nm, shp)
        self.rope_cs = self.din("rope_cs", [96, 2, T])
        self.tri_in = self.din("tri_in", [128, 128])
        self.out = self.dout("out", [SEQ, D])
        self.LRU_IN = self.dscratch("lru_in", [4, 128, T], F32)
        self.QKD = self.dscratch("qkd", [8, 128, T], BF16)
        self.VD = self.dscratch("vd", [T, 512], BF16)
        self.QM = self.dscratch("qm", [4, 96, T], BF16)
        self.KM = self.dscratch("km", [4, 96, T], BF16)
        self.VM = self.dscratch("vm", [T, 260], BF16)
        self.YT = self.dscratch("yt", [8, 128, T], BF16)
        self.wf = {}
        for l in range(L):
            for which in (1, 2):
                self.wf[(l, which)] = self.dscratch("wf_%d_%d" % (l, which), [NF, 128, 3072], BF16)
        self.wf_r = {k: Res("wf%d%d" % k) for k in self.wf}

        self.hT = self.sb("hT", [128, NCH, T], F32)
        self.hres = [[Res("h%d_%d" % (c, i)) for i in range(1 + SEQ // 128)] for c in range(NCH)]
        self.arena = self.sb("arena", [128, ARENA_BYTES // 4], F32)
        self.identf = self.sb("identf", [128, 128], F32)
        self.identf_r = Res("identf")
        self.ones_bf = self.sb("ones_bf", [128, 128], BF16)
        self.ones_r = Res("ones")
        self.gains = self.sb("gains", [128, 8, NCH], F32)
        self.gains_r = Res("gains")

        P.add("sp", lambda e: e.dma_start(out=self.identf[:], in_=self.ident_f[:, :]),
              writes=[self.identf_r], dma=True, sem_res=self.identf_r)
        P.add("pool", lambda e: e.memset(self.ones_bf[:], 1.0), writes=[self.ones_r])
        self.gain_idx = {}
        gi = 0
        for l in range(L):
            for nm in ("ffn1_norm", "ffn2_norm", "mix_norm"):
                src = self.w[nm][l, :].rearrange("(c p) -> p c", p=128)
                P.add("sp", lambda e, k=gi, src=src: e.dma_start(out=self.gains[:, k, :], in_=src),
                      writes=[self.gains_r], dma=True, sem_res=self.gains_r)
                self.gain_idx[(nm, l)] = gi
                gi += 1
        src = self.final_norm.rearrange("(c p) -> p c", p=128)
        P.add("sp", lambda e, k=gi, src=src: e.dma_start(out=self.gains[:, k, :], in_=src),
              writes=[self.gains_r], dma=True, sem_res=self.gains_r)
        self.gain_idx["final"] = gi

        self.psum = [self.ps("ps%d" % i, [128, 512], F32) for i in range(8)]
        self.psum_r = [Res("ps%d" % i) for i in range(8)]

        self.stage_load_x()
        ffns = [(l, w) for l in range(self.n_layers) for w in (1, 2) if self.debug.get("ffn%d" % w, True)]
        self.conv_queue = []
        for i, (l, w) in enumerate(ffns):
            if i == 0 or not self.debug.get("mixer", True) or not self.debug.get("diff", True):
                self.stage_convert_ffn(l, w)
            else:
                self.conv_queue += [(l, w, f) for f in range(NF)]
        for l in range(self.n_layers):
            if self.debug.get("ffn1", True):
                self.stage_ffn(l, 1)
            if self.debug.get("mixer", True):
                if self.debug.get("proj1", True):
                    self.stage_proj1(l)
                if self.debug.get("proj2", True):
                    self.stage_proj2(l)
                if self.debug.get("lru", True):
                    self.stage_lru(l)
                if self.debug.get("diff", True):
                    self.stage_diff(l)
                if self.debug.get("mla", True):
                    self.stage_mla(l)
                if self.debug.get("oproj", True):
                    self.stage_oproj(l)
            if self.debug.get("ffn2", True):
                self.stage_ffn(l, 2)
        self.stage_final()
        with nc.allow_non_contiguous_dma(reason="small strided parameter loads"):
            P.emit(nc, self.es)
        return nc

    def psring(self, idxs, name):
        r = Ring([self.psum[i] for i in idxs], name)
        r.res = [self.psum_r[i] for i in idxs]
        return r

    def stage_load_x(self):
        P = self.P
        self.stage_begin()
        ring = Ring([self.alloc([D], F32) for i in range(3)], "xin")
        psr = self.psring([0, 1, 2, 3], "x")
        n = 0
        for i in range(-1, SEQ // 128):
            buf, br = ring.next()
            if i < 0:
                rows, c0 = NMETA, 0
                src = self.meta[:, :]
            else:
                rows, c0 = 128, NMETA + 128 * i
                src = self.x[128 * i:128 * (i + 1), :]
            P.add("sp", lambda e, buf=buf, rows=rows, src=src: e.dma_start(out=buf[0:rows, :], in_=src),
                  writes=[br], dma=True, sem_res=br)
            for half in range(2):
                pt, pr = psr.next()
                for cc in range(4):
                    c = half * 4 + cc
                    P.add("pe", lambda e, pt=pt, cc=cc, buf=buf, rows=rows, c=c: e.transpose(
                        out=pt[:, cc * 128:cc * 128 + rows], in_=buf[0:rows, c * 128:(c + 1) * 128],
                        identity=self.identf[0:rows, 0:rows]),
                        reads=[br, self.identf_r], writes=[pr])
                eng = "act" if (n % 2 == 0) else "dve"
                n += 1
                hr = []
                for cc in range(4):
                    hr += self.h_res(half * 4 + cc, c0, rows)
                dst = self.hT[:, half * 4:half * 4 + 4, c0:c0 + rows]
                srcp = pt[:, :].rearrange("p (c t) -> p c t", c=4)[:, :, 0:rows]
                if eng == "act":
                    P.add("act", lambda e, dst=dst, srcp=srcp: e.activation(out=dst, in_=srcp, func=AF.Copy),
                          reads=[pr], writes=hr)
                else:
                    P.add("dve", lambda e, dst=dst, srcp=srcp: e.tensor_copy(out=dst, in_=srcp),
                          reads=[pr], writes=hr)

    def conv_bufs(self):
        return dict(a=self.alloc([2, NCH, 128], F32), ar=Res("cva"), ao=self.alloc([D], F32), aor=Res("cvo"),
                    b=self.alloc([3072], BF16), br=Res("cvb"))

    def conv_job(self, job, cb, eng="pool"):
        l, which, f = job
        w_in = self.w["ffn%d_in" % which][l]
        w_out = self.w["ffn%d_out" % which][l]
        WF = self.wf[(l, which)]
        a, ar, ao, aor, b, br = cb["a"], cb["ar"], cb["ao"], cb["aor"], cb["b"], cb["br"]
        for gu in range(2):
            col = gu * DFF + f * 128
            self.load(a[:, gu, :, :], w_in[:, col:col + 128].rearrange("(k p) j -> p k j", p=128), ar)
        self.load(ao[:, :], w_out[f * 128:(f + 1) * 128, :], aor)
        for gu in range(2):
            out = b[:, 0:2048].rearrange("p (k j) -> p k j", k=NCH)[:, :, gu * 128:(gu + 1) * 128]
            self.cp(eng, out, a[:, gu, :, :], [ar], [br])
        self.cp(eng, b[:, 2048:3072], ao[:, :], [aor], [br])
        self.store(WF[f, :, :], b[:, :], br)

    def stage_convert_ffn(self, l, which):
        P = self.P
        self.stage_begin()
        w_in = self.w["ffn%d_in" % which][l]
        w_out = self.w["ffn%d_out" % which][l]
        WF = self.wf[(l, which)]
        wfr = self.wf_r[(l, which)]
        s32 = Ring([self.alloc([2, NCH, 256], F32) for _ in range(2)], "cv32")
        s32o = Ring([self.alloc([2, D], F32) for _ in range(2)], "cv32o")
        s16 = Ring([self.alloc([2, 3072], BF16) for _ in range(2)], "cv16")
        engs = ["act", "dve", "pool"]
        n = 0

        def cast(eng, out, in_):
            if eng == "act":
                return lambda e: e.activation(out=out, in_=in_, func=AF.Copy)
            return lambda e: e.tensor_copy(out=out, in_=in_)

        for fp in range(NF // 2):
            f0 = 2 * fp
            a, ar = s32.next()
            ao, aor = s32o.next()
            b, br = s16.next()
            for gu in range(2):
                col = gu * DFF + f0 * 128
                src = w_in[:, col:col + 256].rearrange("(k p) j -> p k j", p=128)
                P.add("sp", lambda e, a=a, gu=gu, src=src: e.dma_start(out=a[:, gu, :, :], in_=src),
                      writes=[ar], dma=True, sem_res=ar)
            src = w_out[f0 * 128:(f0 + 2) * 128, :].rearrange("(f p) c -> p f c", p=128)
            P.add("sp", lambda e, ao=ao, src=src: e.dma_start(out=ao[:, :, :], in_=src),
                  writes=[aor], dma=True, sem_res=aor)
            for ff in range(2):
                for gu in range(2):
                    eng = engs[n % 3]
                    n += 1
                    out = b[:, ff, 0:2048].rearrange("p (k j) -> p k j", k=NCH)[:, :, gu * 128:(gu + 1) * 128]
                    in_ = a[:, gu, :, ff * 128:(ff + 1) * 128]
                    P.add(eng, cast(eng, out, in_), reads=[ar], writes=[br])
            eng = engs[n % 3]
            n += 1
            P.add(eng, cast(eng, b[:, :, 2048:3072], ao[:, :, :]), reads=[aor], writes=[br])
            dst = WF[f0:f0 + 2, :, :].rearrange("f p c -> p f c")
            P.add("sp", lambda e, b=b, dst=dst: e.dma_start(out=dst, in_=b[:, :, :]),
                  reads=[br], writes=[wfr], dma=True, sem_res=br)

    def stage_ffn(self, l, which):
        P = self.P
        self.stage_begin()
        WF = self.wf[(l, which)]
        wfr = self.wf_r[(l, which)]
        gk = self.gain_idx[("ffn%d_norm" % which, l)]
        xn = self.alloc([NCH, 1040], BF16)
        hid = Ring([self.alloc([1040], BF16) for _ in range(4)], "hid")
        wsl = Ring([self.alloc([3072], BF16) for _ in range(6)], "wsl")
        sqr = Ring([self.alloc([512], BF16) for _ in range(2)], "sq")
        rstd = self.alloc([1040], F32)
        rstd_r = Res("rstd")
        sgr = Ring([self.alloc([512], F32) for _ in range(2)], "sg")
        ps_gu = self.psring([0, 1, 2, 3], "gu")
        ps_o = self.psring([4, 5, 6], "o")
        ps_n, ps_nr = self.psum[7], self.psum_r[7]

        sts = [[(0, 16), (16, 512), (528, 512)]]
        for s in range(1, 4):
            b0 = 1040 + 1024 * (s - 1)
            sts.append([(b0, 512), (b0 + 512, 512)])
        nsq = 0
        for subs in sts:
            base = subs[0][0]
            xn_r = [Res("xn%d" % i) for i in range(len(subs))]
            for si, (c0, n) in enumerate(subs):
                lc = c0 - base
                for c in range(NCH):
                    sq, sq_r = sqr.next()
                    src = self.hT[:, c, c0:c0 + n]
                    if nsq % 2 == 0:
                        P.add("act", lambda e, sq=sq, src=src, n=n: e.activation(out=sq[:, 0:n], in_=src, func=AF.Square),
                              reads=self.h_res(c, c0, n), writes=[sq_r])
                    else:
                        P.add("pool", lambda e, sq=sq, src=src, n=n: e.tensor_tensor(out=sq[:, 0:n], in0=src, in1=src, op=ALU.mult),
                              reads=self.h_res(c, c0, n), writes=[sq_r])
                    nsq += 1
                    P.add("pe", lambda e, sq=sq, n=n, c=c: e.matmul(ps_n[:, 0:n], lhsT=self.ones_bf[:, :], rhs=sq[:, 0:n],
                                                                     start=(c == 0), stop=(c == NCH - 1)),
                          reads=[sq_r, self.ones_r], writes=[ps_nr])
                rs = rstd[:, lc:lc + n]
                P.add("dve", lambda e, rs=rs, n=n: e.tensor_scalar(out=rs, in0=ps_n[:, 0:n], scalar1=1.0 / D, scalar2=EPS,
                                                                   op0=ALU.mult, op1=ALU.add),
                      reads=[ps_nr], writes=[rstd_r])
                P.add("act", lambda e, rs=rs: e.activation(out=rs, in_=rs, func=AF.Sqrt), reads=[rstd_r], writes=[rstd_r])
                P.add("dve", lambda e, rs=rs: e.reciprocal(out=rs, in_=rs), reads=[rstd_r], writes=[rstd_r])
                for c in range(NCH):
                    P.add("dve", lambda e, c=c, c0=c0, n=n, lc=lc, rs=rs: e.scalar_tensor_tensor(
                        out=xn[:, c, lc:lc + n], in0=self.hT[:, c, c0:c0 + n], scalar=self.gains[:, gk, c:c + 1],
                        in1=rs, op0=ALU.mult, op1=ALU.mult),
                        reads=self.h_res(c, c0, n) + [rstd_r, self.gains_r], writes=[xn_r[si]])

            def GU(f):
                slot, slot_r = wsl.next()
                P.add("sp", lambda e, slot=slot, f=f: e.dma_start(out=slot[:, :], in_=WF[f, :, :]),
                      reads=[wfr], writes=[slot_r], dma=True, sem_res=slot_r)
                hb, hb_r = hid.next()
                blocks = []
                for si, (c0, n) in enumerate(subs):
                    def block(si=si, c0=c0, n=n):
                        lc = c0 - base
                        pg, pgr = ps_gu.next()
                        pu, pur = ps_gu.next()
                        for gu, (pp, ppr) in enumerate(((pg, pgr), (pu, pur))):
                            for k in range(NCH):
                                self.mm(pp[:, 0:n], slot[:, k * 256 + gu * 128:k * 256 + gu * 128 + 128],
                                        xn[:, k, lc:lc + n], k == 0, k == NCH - 1, [slot_r, xn_r[si]], [ppr])
                        sg, sg_r = sgr.next()
                        self.act(sg[:, 0:n], pg[:, 0:n], AF.Silu, [pgr], [sg_r])
                        self.tt("dve", hb[:, lc:lc + n], sg[:, 0:n], pu[:, 0:n], ALU.mult, [sg_r, pur], [hb_r])
                    blocks.append(block)
                return (slot, slot_r, hb, hb_r), blocks

            def W2(group):
                units = []
                for o in range(NCH):
                    for si, (c0, n) in enumerate(subs):
                        def unit(o=o, c0=c0, n=n):
                            lc = c0 - base
                            po, por = ps_o.next()
                            for gi_, (slot, slot_r, hb, hb_r) in enumerate(group):
                                self.mm(po[:, 0:n], slot[:, 2048 + o * 128:2048 + (o + 1) * 128], hb[:, lc:lc + n],
                                        gi_ == 0, gi_ == len(group) - 1, [slot_r, hb_r], [por])
                            hr = self.h_res(o, c0, n)
                            self.stt(self.hT[:, o, c0:c0 + n], po[:, 0:n], 0.5, self.hT[:, o, c0:c0 + n], ALU.mult, ALU.add,
                                     [por] + hr, hr)
                        units.append(unit)
                return units

            pend = []
            for g in range(NF // 2):
                grp, blocks = [], []
                for f in (2 * g, 2 * g + 1):
                    info, bl = GU(f)
                    grp.append(info)
                    blocks += bl
                per = -(-len(pend) // len(blocks))
                for bl in blocks:
                    bl()
                    for _ in range(per):
                        if pend:
                            pend.pop(0)()
                while pend:
                    pend.pop(0)()
                pend = W2(grp)
            while pend:
                pend.pop(0)()

    def tiles9(self):
        return [(0, NMETA)] + [(NMETA + 512 * j, 512) for j in range(SEQ // 512)]

    def norm_tile(self, c0, n, gk, xn, xn_r, sqr, rstd, rstd_r, ps_n, ps_nr, cnt=[0]):
        P = self.P
        for c in range(NCH):
            sq, sq_r = sqr.next()
            src = self.hT[:, c, c0:c0 + n]
            if cnt[0] % 2 == 0:
                self.act(sq[:, 0:n], src, AF.Square, self.h_res(c, c0, n), [sq_r])
            else:
                self.tt("pool", sq[:, 0:n], src, src, ALU.mult, self.h_res(c, c0, n), [sq_r])
            cnt[0] += 1
            self.mm(ps_n[:, 0:n], self.ones_bf[:, :], sq[:, 0:n], c == 0, c == NCH - 1, [sq_r, self.ones_r], [ps_nr])
        rs = rstd[:, 0:n]
        self.rsqrt_inplace(rs, rstd_r, 1.0 / D, in_ap=ps_n[:, 0:n], in_reads=[ps_nr])
        for c in range(NCH):
            self.stt(xn[:, c, 0:n], self.hT[:, c, c0:c0 + n], self.gains[:, gk, c:c + 1], rs, ALU.mult, ALU.mult,
                     self.h_res(c, c0, n) + [rstd_r, self.gains_r], [xn_r])

    def load_w_bf16(self, dst, dst_r, src, ncols, slab=160):
        st = Ring([self.alloc([src.shape[0] // 128, slab], F32) for _ in range(2)], "wslab")
        engs = ["act", "dve", "pool"]
        i = 0
        for c in range(0, ncols, slab):
            w = min(slab, ncols - c)
            a, ar = st.next()
            self.load(a[:, :, 0:w], src[:, c:c + w].rearrange("(k p) j -> p k j", p=128), ar)
            self.cp(engs[i % 3], dst[:, :, c:c + w], a[:, :, 0:w], [ar], [dst_r])
            i += 1

    def stage_proj1(self, l):
        self.stage_begin()
        gk = self.gain_idx[("mix_norm", l)]
        w_in = self.w["w_in"][l]
        Wp = self.alloc([NCH, 2048], BF16)
        Wp_r = Res("Wp")
        self.load_w_bf16(Wp, Wp_r, w_in[:, 0:2048], 2048)
        xnr = Ring([self.alloc([NCH, 512], BF16) for _ in range(2)], "xn")
        sqr = Ring([self.alloc([512], BF16) for _ in range(2)], "sq")
        rstr = Ring([self.alloc([512], F32) for _ in range(2)], "rstd")
        st32 = Ring([self.alloc([512], F32) for _ in range(3)], "st32")
        st16 = Ring([self.alloc([512], BF16) for _ in range(4)], "st16")
        psr = self.psring([0, 1, 2, 3, 4, 5], "p1")
        ps_n, ps_nr = self.psum[7], self.psum_r[7]
        ne = 0
        tiles = self.tiles9()
        normed = {}

        def do_norm(i):
            if i < len(tiles) and i not in normed:
                xn_, xn_r_ = xnr.next()
                rstd_, rstd_r_ = rstr.next()
                self.norm_tile(tiles[i][0], tiles[i][1], gk, xn_, xn_r_, sqr, rstd_, rstd_r_, ps_n, ps_nr)
                normed[i] = (xn_, xn_r_)
        do_norm(0)
        for ti, (c0, n) in enumerate(tiles):
            xn, xn_r = normed[ti]
            do_norm(ti + 1)
            for g in range(12):
                pt, pr = psr.next()
                for k in range(NCH):
                    self.mm(pt[:, 0:n], Wp[:, k, g * 128:(g + 1) * 128], xn[:, k, 0:n], k == 0, k == NCH - 1,
                            [Wp_r, xn_r], [pr])
                eng = "act" if ne % 2 == 0 else "dve"
                ne += 1
                if g < 4:
                    sb, sr = st32.next()
                    self.cp(eng, sb[:, 0:n], pt[:, 0:n], [pr], [sr])
                    self.store(self.LRU_IN[g, :, c0:c0 + n], sb[:, 0:n], sr)
                else:
                    sb, sr = st16.next()
                    self.cp(eng, sb[:, 0:n], pt[:, 0:n], [pr], [sr])
                    self.store(self.QKD[g - 4, :, c0:c0 + n], sb[:, 0:n], sr)
            for j in range(0, n, 128):
                m = min(128, n - j)
                pt, pr = psr.next()
                for k in range(NCH):
                    self.mm(pt[0:m, 0:512], xn[:, k, j:j + m], Wp[:, k, 1536:2048], k == 0, k == NCH - 1,
                            [Wp_r, xn_r], [pr])
                sb, sr = st16.next()
                eng = "act" if ne % 2 == 0 else "dve"
                ne += 1
                self.cp(eng, sb[0:m, 0:512], pt[0:m, 0:512], [pr], [sr])
                self.store(self.VD[c0 + j:c0 + j + m, :], sb[0:m, 0:512], sr)

    def stage_proj2(self, l):
        self.stage_begin()
        gk = self.gain_idx[("mix_norm", l)]
        w_in = self.w["w_in"][l]
        Wp = self.alloc([NCH, 512], BF16)
        Wp_r = Res("Wp2")
        self.P.add("pool", lambda e: e.memset(Wp[:, :, 416:480], 0.0), writes=[Wp_r])
        self.load_w_bf16(Wp, Wp_r, w_in[:, 2048:2464], 416, slab=104)
        self.cp("dve", Wp[:, :, 480:496], Wp[:, :, 400:416], [Wp_r], [Wp_r])
        self.cp("dve", Wp[:, :, 496:512], Wp[:, :, 384:400], [Wp_r], [Wp_r])
        Wq = self.alloc([2, 768], BF16)
        Wq_r = Res("Wq")
        self.load_w_bf16(Wq, Wq_r, self.w["w_uq"][l], 384, slab=192)
        for h in range(4):
            b = 384 + h * 96
            a = h * 96
            self.cp("dve", Wq[:, :, b:b + 64], Wq[:, :, a:a + 64], [Wq_r], [Wq_r])
            self.cp("dve", Wq[:, :, b + 64:b + 80], Wq[:, :, a + 80:a + 96], [Wq_r], [Wq_r])
            self.cp("dve", Wq[:, :, b + 80:b + 96], Wq[:, :, a + 64:a + 80], [Wq_r], [Wq_r])
        Wkv = self.alloc([512], BF16)
        Wkv_r = Res("Wkv")
        kv32 = self.alloc([512], F32)
        kv32_r = Res("kv32")
        self.load(kv32[:, :], self.w["w_ukv"][l][:, :], kv32_r)
        self.cp("dve", Wkv[:, :].rearrange("p (t h d) -> p h t d", t=2, h=4),
                kv32[:, :].rearrange("p (h t d) -> p h t d", h=4, t=2), [kv32_r], [Wkv_r])
        gq = self.alloc([2], F32)
        gkv = self.alloc([1], F32)
        g_r = Res("gqkv")
        self.load(gq[:, :], self.w["q_norm"][l].rearrange("(c p) -> p c", p=128), g_r)
        self.load(gkv[:, :], self.w["kv_norm"][l].rearrange("(c p) -> p c", p=128), g_r)

        xnr = Ring([self.alloc([NCH, 512], BF16) for _ in range(2)], "xn")
        sqr = Ring([self.alloc([512], BF16) for _ in range(2)], "sq")
        rstr = Ring([self.alloc([512], F32) for _ in range(1)], "rstd")
        cq = self.alloc([2, 512], F32)
        cq_r = Res("cq")
        ckv = self.alloc([512], F32)
        ckv_r = Res("ckv")
        cqn = self.alloc([2, 512], BF16)
        cqn_r = Res("cqn")
        ckvn = self.alloc([512], BF16)
        ckvn_r = Res("ckvn")
        rsq = self.alloc([512], F32)
        rsq_r = Res("rsq")
        rskv = self.alloc([512], F32)
        rskv_r = Res("rskv")
        cs = self.alloc([2, 512], F32)
        cs_r = Res("cs")
        t1r = Ring([self.alloc([512], F32) for _ in range(2)], "t1")
        t2r = Ring([self.alloc([512], F32) for _ in range(2)], "t2")
        qst = Ring([self.alloc([4, 512], BF16) for _ in range(1)], "qst")
        kst = Ring([self.alloc([4, 512], BF16) for _ in range(1)], "kst")
        vst = Ring([self.alloc([4, 65], BF16) for _ in range(1)], "vst")
        psr = self.psring([0, 1, 2, 3, 4, 5], "p2")
        ps_n, ps_nr = self.psum[7], self.psum_r[7]
        ps_m, ps_mr = self.psum[6], self.psum_r[6]
        tiles = self.tiles9()
        normed = {}

        def do_norm(i):
            if i < len(tiles) and i not in normed:
                xn_, xn_r_ = xnr.next()
                rstd_, rstd_r_ = rstr.next()
                self.norm_tile(tiles[i][0], tiles[i][1], gk, xn_, xn_r_, sqr, rstd_, rstd_r_, ps_n, ps_nr)
                normed[i] = (xn_, xn_r_)
        do_norm(0)
        for ti, (c0, n) in enumerate(tiles):
            xn, xn_r = normed[ti]
            do_norm(ti + 1)
            self.load(cs[0:96, :, 0:n], self.rope_cs[:, :, c0:c0 + n], cs_r)
            for j in range(3):
                pt, pr = psr.next()
                for k in range(NCH):
                    self.mm(pt[:, 0:n], Wp[:, k, j * 128:(j + 1) * 128], xn[:, k, 0:n], k == 0, k == NCH - 1,
                            [Wp_r, xn_r], [pr])
                if j < 2:
                    self.cp("act", cq[:, j, 0:n], pt[:, 0:n], [pr], [cq_r])
                else:
                    self.cp("act", ckv[:, 0:n], pt[:, 0:n], [pr], [ckv_r])
            for j in range(2):
                sq, sq_r = sqr.next()
                self.tt("pool", sq[:, 0:n], cq[:, j, 0:n], cq[:, j, 0:n], ALU.mult, [cq_r], [sq_r])
                self.mm(ps_m[:, 0:n], self.ones_bf[:, :], sq[:, 0:n], j == 0, j == 1, [sq_r, self.ones_r], [ps_mr])
            self.rsqrt_inplace(rsq[:, 0:n], rsq_r, 1.0 / 256, in_ap=ps_m[:, 0:n], in_reads=[ps_mr])
            for j in range(2):
                self.stt(cqn[:, j, 0:n], cq[:, j, 0:n], gq[:, j:j + 1], rsq[:, 0:n], ALU.mult, ALU.mult,
                         [cq_r, g_r, rsq_r], [cqn_r])
            sq, sq_r = sqr.next()
            self.tt("pool", sq[:, 0:n], ckv[:, 0:n], ckv[:, 0:n], ALU.mult, [ckv_r], [sq_r])
            self.mm(ps_m[:, 0:n], self.ones_bf[:, :], sq[:, 0:n], True, True, [sq_r, self.ones_r], [ps_mr])
            self.rsqrt_inplace(rskv[:, 0:n], rskv_r, 1.0 / 128, in_ap=ps_m[:, 0:n], in_reads=[ps_mr])
            self.stt(ckvn[:, 0:n], ckv[:, 0:n], gkv[:, 0:1], rskv[:, 0:n], ALU.mult, ALU.mult,
                     [ckv_r, g_r, rskv_r], [ckvn_r])
            qs, qs_r = qst.next()
            for h in range(4):
                pa, par = psr.next()
                pb, pbr = psr.next()
                for k in range(2):
                    self.mm(pa[0:96, 0:n], Wq[:, k, h * 96:(h + 1) * 96], cqn[:, k, 0:n], k == 0, k == 1, [Wq_r, cqn_r], [par])
                for k in range(2):
                    self.mm(pb[0:96, 0:n], Wq[:, k, 384 + h * 96:384 + (h + 1) * 96], cqn[:, k, 0:n], k == 0, k == 1,
                            [Wq_r, cqn_r], [pbr])
                t1, t1_r = t1r.next()
                t2, t2_r = t2r.next()
                self.tt("dve", t1[0:96, 0:n], pa[0:96, 0:n], cs[0:96, 0, 0:n], ALU.mult, [par, cs_r], [t1_r])
                self.tt("dve", t2[0:96, 0:n], pb[0:96, 0:n], cs[0:96, 1, 0:n], ALU.mult, [pbr, cs_r], [t2_r])
                self.tt("pool", qs[0:96, h, 0:n], t1[0:96, 0:n], t2[0:96, 0:n], ALU.add, [t1_r, t2_r], [qs_r])
            self.store(self.QM[:, :, c0:c0 + n].rearrange("h r t -> r h t"), qs[0:96, :, 0:n], qs_r)
            ks, ks_r = kst.next()
            for h in range(4):
                pt, pr = psr.next()
                self.mm(pt[0:64, 0:n], Wkv[:, h * 64:(h + 1) * 64], ckvn[:, 0:n], True, True, [Wkv_r, ckvn_r], [pr])
                self.cp("act", ks[0:64, h, 0:n], pt[0:64, 0:n], [pr], [ks_r])
            pa, par = psr.next()
            pb, pbr = psr.next()
            for k in range(NCH):
                self.mm(pa[0:96, 0:n], Wp[:, k, 320:416], xn[:, k, 0:n], k == 0, k == NCH - 1, [Wp_r, xn_r], [par])
            for k in range(NCH):
                self.mm(pb[0:96, 0:n], Wp[:, k, 416:512], xn[:, k, 0:n], k == 0, k == NCH - 1, [Wp_r, xn_r], [pbr])
            t1, t1_r = t1r.next()
            t2, t2_r = t2r.next()
            self.tt("dve", t1[64:96, 0:n], pa[64:96, 0:n], cs[64:96, 0, 0:n], ALU.mult, [par, cs_r], [t1_r])
            self.tt("dve", t2[64:96, 0:n], pb[64:96, 0:n], cs[64:96, 1, 0:n], ALU.mult, [pbr, cs_r], [t2_r])
            self.tt("pool", t1[64:96, 0:n], t1[64:96, 0:n], t2[64:96, 0:n], ALU.add, [t1_r, t2_r], [t1_r])
            for h in range(4):
                self.cp("pool" if h % 2 else "act", ks[64:96, h, 0:n], t1[64:96, 0:n], [t1_r], [ks_r])
            self.store(self.KM[:, :, c0:c0 + n].rearrange("h r t -> r h t"), ks[0:96, :, 0:n], ks_r)
            for j in range(0, n, 128):
                m = min(128, n - j)
                pt, pr = psr.next()
                self.mm(pt[0:m, 0:256], ckvn[:, j:j + m], Wkv[:, 256:512], True, True, [Wkv_r, ckvn_r], [pr])
                vs, vs_r = vst.next()
                self.P.add("pool", lambda e, vs=vs: e.memset(vs[:, :, 64:65], 1.0), writes=[vs_r])
                self.cp("act", vs[0:m, :, 0:64], pt[0:m, 0:256].rearrange("p (h d) -> p h d", h=4), [pr], [vs_r])
                self.store(self.VM[c0 + j:c0 + j + m, :], vs[0:m, :, :].rearrange("p h d -> p (h d)"), vs_r)

    def stage_lru(self, l):
        self.stage_begin()
        P = self.P
        pr_ = Res("lrup")
        convw = self.alloc([2, 4], F32)
        convb = self.alloc([2], F32)
        ba = self.alloc([2], F32)
        bx = self.alloc([2], F32)
        lam = self.alloc([2], F32)
        gl = self.alloc([2], F32)
        for c in range(2):
            for j in range(4):
                self.load(convw[:, c, j:j + 1], self.w["conv_w"][l][j, c * 128:(c + 1) * 128].rearrange("(p o) -> p o", o=1), pr_)
        self.load(convb[:, :], self.w["conv_b"][l].rearrange("(c p) -> p c", p=128), pr_)
        self.load(ba[:, :], self.w["lru_ba"][l].rearrange("n d -> (n d)").rearrange("(c p) -> p c", p=128), pr_)
        self.load(bx[:, :], self.w["lru_bx"][l].rearrange("n d -> (n d)").rearrange("(c p) -> p c", p=128), pr_)
        self.load(lam[:, :], self.w["lru_lambda"][l].rearrange("(c p) -> p c", p=128), pr_)
        self.load(gl[:, :], self.w["lru_out_norm"][l].rearrange("(c p) -> p c", p=128), pr_)
        nsp = self.alloc([2], F32)
        nsp2 = self.alloc([2], F32)
        self.act(nsp[:, :], lam[:, :], AF.Exp, [pr_], [pr_], scale=-1.0)
        self.ts("dve", nsp[:, :], nsp[:, :], 1.0, None, ALU.add, None, [pr_], [pr_])
        self.act(nsp[:, :], nsp[:, :], AF.Ln, [pr_], [pr_])
        self.ts("dve", nsp2[:, :], nsp[:, :], -16.0, None, ALU.mult, None, [pr_], [pr_])
        self.ts("dve", nsp[:, :], nsp[:, :], -8.0, None, ALU.mult, None, [pr_], [pr_])
        bd32 = self.alloc([2, 2, 128], F32)
        bd = self.alloc([2, 2, 128], BF16)
        bd_r = Res("bd")
        P.add("pool", lambda e: e.memset(bd32[:, :, :, :], 0.0), writes=[bd_r])
        for wi, nm in enumerate(("lru_wa", "lru_wx")):
            for nb in range(4):
                c, hb = nb // 2, nb % 2
                self.load(bd32[hb * 64:(hb + 1) * 64, wi, c, hb * 64:(hb + 1) * 64], self.w[nm][l][nb, :, :], bd_r)
        self.cp("dve", bd[:, :, :, :], bd32[:, :, :, :], [bd_r], [bd_r])
        carry = self.alloc([2], F32)
        carry_r = Res("carry")
        NB = 515
        uar = Ring([self.alloc([2, NB], F32) for _ in range(2)], "ua")
        gar = Ring([self.alloc([2, 512], F32) for _ in range(2)], "ga")
        fr = Ring([self.alloc([512], F32) for _ in range(14)], "lf")
        ucb_r = Ring([self.alloc([512], BF16) for _ in range(2)], "ucb")
        sqr = Ring([self.alloc([512], BF16) for _ in range(2)], "sq")
        yp = self.alloc([2, 512], F32)
        yp_r = Res("yp")
        rstd = self.alloc([512], F32)
        rstd_r = Res("rstd")
        yst = Ring([self.alloc([2, 512], BF16) for _ in range(2)], "yst")
        psr = self.psring([0, 1, 2, 3], "lru")
        ps_n, ps_nr = self.psum[7], self.psum_r[7]
        first = True
        for (c0, n) in self.tiles9():
            ua, ua_r = uar.next()
            ga, ga_r = gar.next()
            if first:
                P.add("pool", lambda e, ua=ua: e.memset(ua[:, :, 0:3], 0.0), writes=[ua_r])
                self.load(ua[:, :, 3:3 + n], self.LRU_IN[2:4, :, c0:c0 + n].rearrange("c p t -> p c t"), ua_r)
            else:
                self.load(ua[:, :, 0:3 + n], self.LRU_IN[2:4, :, c0 - 3:c0 + n].rearrange("c p t -> p c t"), ua_r)
            self.load(ga[:, :, 0:n], self.LRU_IN[0:2, :, c0:c0 + n].rearrange("c p t -> p c t"), ga_r)
            def chunk(j, first=first, c0=c0, n=n, ua=ua, ua_r=ua_r, ga=ga, ga_r=ga_r):
                uc, uc_r = fr.next()
                yield
                self.ts("dve", uc[:, 0:n], ua[:, j, 0:n], convw[:, j, 0:1], convb[:, j:j + 1], ALU.mult, ALU.add,
                        [ua_r, pr_], [uc_r])
                for t in range(1, 4):
                    self.stt(uc[:, 0:n], ua[:, j, t:t + n], convw[:, j, t:t + 1], uc[:, 0:n], ALU.mult, ALU.add,
                             [ua_r, pr_, uc_r], [uc_r])
                ucb, ucb_rr = ucb_r.next()
                yield
                self.cp("act", ucb[:, 0:n], uc[:, 0:n], [uc_r], [ucb_rr])
                yield
                pa, par = psr.next()
                yield
                px, pxr = psr.next()
                yield
                self.mm(pa[:, 0:n], bd[:, 0, j, :], ucb[:, 0:n], True, True, [bd_r, ucb_rr], [par])
                yield
                self.mm(px[:, 0:n], bd[:, 1, j, :], ucb[:, 0:n], True, True, [bd_r, ucb_rr], [pxr])
                yield
                r, r_r = fr.next()
                yield
                ig, ig_r = fr.next()
                yield
                self.act(r[:, 0:n], pa[:, 0:n], AF.Sigmoid, [par, pr_], [r_r], bias=ba[:, j:j + 1])
                yield
                self.act(ig[:, 0:n], px[:, 0:n], AF.Sigmoid, [pxr, pr_], [ig_r], bias=bx[:, j:j + 1])
                yield
                a, a_r = fr.next()
                yield
                a2, a2_r = fr.next()
                yield
                self.act(a[:, 0:n], r[:, 0:n], AF.Exp, [r_r, pr_], [a_r], scale=nsp[:, j:j + 1])
                yield
                self.act(a2[:, 0:n], r[:, 0:n], AF.Exp, [r_r, pr_], [a2_r], scale=nsp2[:, j:j + 1])
                yield
                self.ts("pool", a2[:, 0:n], a2[:, 0:n], -1.0, 1.0, ALU.mult, ALU.add, [a2_r], [a2_r])
                yield
                self.act(a2[:, 0:n], a2[:, 0:n], AF.Sqrt, [a2_r], [a2_r])
                yield
                self.tt("pool", ig[:, 0:n], ig[:, 0:n], uc[:, 0:n], ALU.mult, [ig_r, uc_r], [ig_r])
                yield
                self.tt("pool", ig[:, 0:n], ig[:, 0:n], a2[:, 0:n], ALU.mult, [ig_r, a2_r], [ig_r])
                yield
                hs, hs_r = fr.next()
                yield
                if first:
                    P.add("dve", lambda e, hs=hs, a=a, ig=ig, n=n: e.tensor_tensor_scan(
                        out=hs[:, 0:n], data0=a[:, 0:n], data1=ig[:, 0:n], initial=0.0, op0=ALU.mult, op1=ALU.add),
                        reads=[a_r, ig_r], writes=[hs_r])
                else:
                    P.add("dve", lambda e, hs=hs, a=a, ig=ig, n=n, j=j: e.tensor_tensor_scan(
                        out=hs[:, 0:n], data0=a[:, 0:n], data1=ig[:, 0:n], initial=carry[:, j:j + 1],
                        op0=ALU.mult, op1=ALU.add),
                        reads=[a_r, ig_r, carry_r], writes=[hs_r])
                self.cp("dve", carry[:, j:j + 1], hs[:, n - 1:n], [hs_r], [carry_r])
                yield
                yield
                g = ga[:, j, 0:n]
                t, t_r = fr.next()
                yield
                self.tt("pool", t[:, 0:n], g, g, ALU.mult, [ga_r], [t_r])
                yield
                self.ts("pool", t[:, 0:n], t[:, 0:n], 0.044715, 1.0, ALU.mult, ALU.add, [t_r], [t_r])
                yield
                self.tt("pool", t[:, 0:n], t[:, 0:n], g, ALU.mult, [t_r, ga_r], [t_r])
                yield
                self.act(t[:, 0:n], t[:, 0:n], AF.Sigmoid, [t_r], [t_r], scale=1.5957691216057308)
                yield
                self.tt("pool", t[:, 0:n], t[:, 0:n], g, ALU.mult, [t_r, ga_r], [t_r])
                yield
                self.tt("dve", yp[:, j, 0:n], hs[:, 0:n], t[:, 0:n], ALU.mult, [hs_r, t_r], [yp_r])
                yield
                sq, sq_r = sqr.next()
                yield
                self.tt("pool", sq[:, 0:n], yp[:, j, 0:n], yp[:, j, 0:n], ALU.mult, [yp_r], [sq_r])
                yield
                self.mm(ps_n[:, 0:n], self.ones_bf[:, :], sq[:, 0:n], j == 0, j == 1, [sq_r, self.ones_r], [ps_nr])
                yield

            import itertools
            for _ in itertools.zip_longest(chunk(0), chunk(1)):
                pass
            self.rsqrt_inplace(rstd[:, 0:n], rstd_r, 1.0 / 256, in_ap=ps_n[:, 0:n], in_reads=[ps_nr])
            ys, ys_r = yst.next()
            for j in range(2):
                self.stt(ys[:, j, 0:n], yp[:, j, 0:n], gl[:, j:j + 1], rstd[:, 0:n], ALU.mult, ALU.mult,
                         [yp_r, pr_, rstd_r], [ys_r])
            self.store(self.YT[0:2, :, c0:c0 + n].rearrange("c p t -> p c t"), ys[:, :, 0:n], ys_r)
            first = False

    def attn_sweep(self, qi, Kt, Qt, r0, r1, kq_r, V, v_r, dvp, obanks, scale, s_ring, pt_ring, tri, tri_r):
        if qi < 0:
            q0, nq, nsub = 0, NMETA, 1
            chunks = [(0, 0, NMETA, 0)]
        else:
            q0, nq, nsub = NMETA + 512 * qi, 512, 4
            chunks = [(0, 0, NMETA, -1)] + [(1 + i, NMETA + 128 * i, 128, -1) for i in range(4 * qi)]
            chunks += [(1 + 4 * qi + c, NMETA + 128 * (4 * qi + c), 128, c) for c in range(4)]
        started = set()
        for (kc, kcol, nk, diag) in chunks:
            s0 = max(diag, 0)
            qa = q0 + 128 * s0 if qi >= 0 else 0
            nqa = nq - 128 * s0 if qi >= 0 else nq
            ps, ps_r = s_ring.next()
            self.mm(ps[0:nk, 0:nqa], Kt[r0:r1, kcol:kcol + nk], Qt[r0:r1, qa:qa + nqa], True, True, [kq_r], [ps_r])
            pt, pt_r = pt_ring.next()
            self.act(pt[0:nk, 0:nqa], ps[0:nk, 0:nqa], AF.Exp, [ps_r], [pt_r], scale=scale)
            if diag >= 0:
                w = min(128, nqa)
                self.tt("dve", pt[0:nk, 0:w], pt[0:nk, 0:w], tri[0:nk, 0:w], ALU.mult, [pt_r, tri_r], [pt_r])
            self.flush_pv()

            def pv(s0=s0, pt=pt, pt_r=pt_r, nk=nk, kc=kc):
                for s in range(s0, nsub):
                    ob, ob_r, oc = obanks[s]
                    ws = min(128, nq)
                    lo = (s - s0) * 128
                    st = id(ob) not in started
                    started.add(id(ob))
                    self.mm(ob[0:ws, oc:oc + dvp], pt[0:nk, lo:lo + ws], V[0:nk, kc, 0:dvp], st, False,
                            [pt_r, v_r], [ob_r], skip_group_check=True)
            self._pend = pv
            self.run_evac()
            self.run_deferred(4)

    _pend = None
    _deferred = None

    def defer(self, fn):
        if self._deferred is None:
            self._deferred = []
        self._deferred.append(fn)

    def evac(self, fn):
        if getattr(self, "_evac", None) is None:
            self._evac = []
        self._evac.append(fn)

    def run_evac(self):
        q = getattr(self, "_evac", None) or []
        while q:
            q.pop(0)()

    def run_deferred(self, k=None):
        d = self._deferred or []
        n = len(d) if k is None else min(k, len(d))
        for _ in range(n):
            d.pop(0)()

    def flush_pv(self):
        if self._pend is not None:
            p = self._pend
            self._pend = None
            p()

    def load_consts_attn(self):
        tri32 = self.alloc([128], F32)
        tri = self.alloc([128], BF16)
        tri_r = Res("tri")
        self.load(tri32[:, :], self.tri_in[:, :], tri_r)
        self.cp("dve", tri[:, :], tri32[:, :], [tri_r], [tri_r])
        idb32 = self.alloc([128], F32)
        idb = self.alloc([128], BF16)
        idb_r = Res("idb")
        self.cp("dve", idb[:, :], self.identf[:, :], [self.identf_r], [idb_r])
        return tri, tri_r, idb, idb_r

    def stage_diff(self, l):
        self.stage_begin()
        P = self.P
        lam_init = 0.8 - 0.6 * math.exp(-0.3 * l)
        tri, tri_r, idb, idb_r = self.load_consts_attn()
        lv = self.alloc([4, 64], F32)
        lv_r = Res("lv")
        for i, nm in enumerate(("lam_q1", "lam_k1", "lam_q2", "lam_k2")):
            self.load(lv[:, i, :], self.w[nm][l].partition_broadcast(128), lv_r)
        lsc = self.alloc([8], F32)
        self.tt("dve", lv[:, 0, :], lv[:, 0, :], lv[:, 1, :], ALU.mult, [lv_r], [lv_r])
        self.tt("dve", lv[:, 2, :], lv[:, 2, :], lv[:, 3, :], ALU.mult, [lv_r], [lv_r])
        P.add("dve", lambda e: e.reduce_sum(out=lsc[:, 0:1], in_=lv[:, 0, :], axis=AX.X), reads=[lv_r], writes=[lv_r])
        P.add("dve", lambda e: e.reduce_sum(out=lsc[:, 1:2], in_=lv[:, 2, :], axis=AX.X), reads=[lv_r], writes=[lv_r])
        self.act(lsc[:, 0:2], lsc[:, 0:2], AF.Exp, [lv_r], [lv_r])
        self.tt("dve", lsc[:, 2:3], lsc[:, 1:2], lsc[:, 0:1], ALU.subtract, [lv_r], [lv_r])
        self.ts("dve", lsc[:, 2:3], lsc[:, 2:3], -lam_init, None, ALU.add, None, [lv_r], [lv_r])
        gsub = self.alloc([128], F32)
        self.load(gsub[:, :], self.w["diff_subln"][l].partition_broadcast(128), lv_r)
        self.ts("dve", gsub[:, :], gsub[:, :], 1.0 - lam_init, None, ALU.mult, None, [lv_r], [lv_r])

        Kt = self.alloc([T], BF16)
        Qm = [self.alloc([T], BF16) for _ in range(2)]
        kq_r = Res("kq")
        P.add("pool", lambda e: e.memset(Qm[0][64:128, :], 0.0), writes=[kq_r])
        P.add("pool", lambda e: e.memset(Qm[1][0:64, :], 0.0), writes=[kq_r])
        V = self.alloc([33, 129], BF16)
        v_r = Res("V")
        pt_ring = Ring([self.alloc([512], BF16) for _ in range(3)], "pt")
        o1r = Ring([self.alloc([128], F32) for _ in range(8)], "o1")
        o2r = Ring([self.alloc([128], F32) for _ in range(8)], "o2")
        scr = Ring([self.alloc([128], F32) for _ in range(2)], "scr")
        smr = Ring([self.alloc([8], F32) for _ in range(8)], "sm")
        ybr = Ring([self.alloc([128], BF16) for _ in range(4)], "yb")
        ytr = Ring([self.alloc([512], BF16) for _ in range(3)], "yts")
        s_ring = self.psring([0, 1, 2], "s")
        tps = self.psum[7][:, :].bitcast(BF16)
        tps_r = self.psum_r[7]
        cb = self.conv_bufs()
        n_slots = 4 * (1 + SEQ // 512)
        need_now = [j for j in self.conv_queue if j[0] == l or (j[0] == l + 1 and j[1] == 1)]
        slot_i = 0
        for h in range(4):
            self.flush_pv()
            self.run_evac()
            self.load(Qm[0][0:64, :], self.QKD[h, 0:64, :], kq_r)
            self.load(Qm[1][64:128, :], self.QKD[h, 64:128, :], kq_r)
            self.load(Kt[:, :], self.QKD[4 + h, :, :], kq_r)
            P.add("pool", lambda e: e.memset(V[:, :, 128:129], 1.0), writes=[v_r])
            self.load(V[0:NMETA, 0, 0:128], self.VD[0:NMETA, h * 128:(h + 1) * 128], v_r)
            for i0 in range(0, 32, 8):
                self.load(V[:, 1 + i0:9 + i0, 0:128],
                          self.VD[NMETA + 128 * i0:NMETA + 128 * (i0 + 8), h * 128:(h + 1) * 128].rearrange("(i p) d -> p i d", p=128), v_r)
            for qi in range(-1, SEQ // 512):
                nsub = 1 if qi < 0 else 4
                ws = NMETA if qi < 0 else 128
                q0 = 0 if qi < 0 else NMETA + 512 * qi
                ob = {}
                for m in range(2):
                    banks = [(self.psum[3 + 2 * m], self.psum_r[3 + 2 * m], 0), (self.psum[3 + 2 * m], self.psum_r[3 + 2 * m], 129),
                             (self.psum[4 + 2 * m], self.psum_r[4 + 2 * m], 0), (self.psum[4 + 2 * m], self.psum_r[4 + 2 * m], 129)]
                    ob[m] = banks
                    self.attn_sweep(qi, Kt, Qm[m], 0, 128, kq_r, V, v_r, 129, banks, 0.125, s_ring, pt_ring, tri, tri_r)
                    if m == 0:
                        subs_bufs = [(smr.next(), o1r.next(), o2r.next()) for _ in range(nsub)]
                    for s_ in range(nsub):
                        (sm, sm_r), o1p, o2p = subs_bufs[s_]
                        o, o_r = o1p if m == 0 else o2p
                        b, b_r, oc = banks[s_]

                        def ev(sm=sm, sm_r=sm_r, o=o, o_r=o_r, b=b, b_r=b_r, oc=oc, m=m, ws=ws):
                            P.add("dve", lambda e: e.reciprocal(out=sm[0:ws, m:m + 1], in_=b[0:ws, oc + 128:oc + 129]),
                                  reads=[b_r], writes=[sm_r])
                            self.ts("dve", o[0:ws, :], b[0:ws, oc:oc + 128], sm[0:ws, m:m + 1], None, ALU.mult, None,
                                    [b_r, sm_r], [o_r])
                        self.evac(ev)
                left = n_slots - slot_i
                k = -(-len(need_now) // left) if left > 0 else len(need_now)
                for _ in range(k):
                    if need_now:
                        job = need_now.pop(0)
                        self.conv_queue.remove(job)
                        self.conv_job(job, cb)
                slot_i += 1
                yts, yts_r = ytr.next()
                for s in range(nsub):
                    (sm, sm_r), (o1, o1_r), (o2, o2_r) = subs_bufs[s]

                    def tail(s=s, sm=sm, sm_r=sm_r, o1=o1, o1_r=o1_r, o2=o2, o2_r=o2_r, ws=ws, yts=yts, yts_r=yts_r):
                        sc, sc_r = scr.next()
                        yb, yb_r = ybr.next()
                        return [
                            lambda: self.stt(o1[0:ws, :], o2[0:ws, :], lsc[0:ws, 2:3], o1[0:ws, :], ALU.mult, ALU.add,
                                             [o1_r, o2_r, lv_r], [o1_r]),
                            lambda: self.tt("dve", sc[0:ws, :], o1[0:ws, :], o1[0:ws, :], ALU.mult, [o1_r], [sc_r]),
                            lambda: P.add("dve", lambda e: e.reduce_sum(out=sm[0:ws, 2:3], in_=sc[0:ws, :], axis=AX.X),
                                          reads=[sc_r], writes=[sm_r]),
                            lambda: self.ts("dve", sm[0:ws, 2:3], sm[0:ws, 2:3], 1.0 / 128, EPS, ALU.mult, ALU.add, [sm_r], [sm_r]),
                            lambda: self.act(sm[0:ws, 2:3], sm[0:ws, 2:3], AF.Ln, [sm_r], [sm_r]),
                            lambda: self.act(sm[0:ws, 2:3], sm[0:ws, 2:3], AF.Exp, [sm_r], [sm_r], scale=-0.5),
                            lambda: self.stt(yb[0:ws, :], o1[0:ws, :], sm[0:ws, 2:3], gsub[0:ws, :], ALU.mult, ALU.mult,
                                             [o1_r, sm_r, lv_r], [yb_r]),
                            lambda: P.add("pe", lambda e: e.transpose(out=tps[:, s * 128:s * 128 + ws], in_=yb[0:ws, :],
                                                                       identity=idb[0:ws, 0:ws]),
                                          reads=[yb_r, idb_r], writes=[tps_r]),
                            lambda: self.cp("dve", yts[:, s * 128:s * 128 + ws], tps[:, s * 128:s * 128 + ws], [tps_r], [yts_r]),
                        ]
                    for fn in tail():
                        self.defer(fn)
                nq = NMETA if qi < 0 else 512
                self.defer(lambda h=h, q0=q0, nq=nq, yts=yts, yts_r=yts_r: self.store(self.YT[2 + h, :, q0:q0 + nq], yts[:, 0:nq], yts_r))
        self.flush_pv()
        self.run_evac()
        self.run_deferred()

    def stage_mla(self, l):
        self.stage_begin()
        P = self.P
        tri, tri_r, idb, idb_r = self.load_consts_attn()
        gm = self.alloc([256], F32)
        gm_r = Res("gm")
        self.load(gm[:, :], self.w["mla_out_norm"][l].partition_broadcast(128), gm_r)
        Kt = self.alloc([4, T], BF16)
        kq_r = Res("kq")
        V = self.alloc([33, 4, 65], BF16)
        v_r = Res("V")
        self.load(Kt[0:96, :, :], self.KM[:, :, :].rearrange("h r t -> r h t"), kq_r)
        Vf = V.rearrange("p i h d -> p i (h d)")
        self.load(Vf[0:NMETA, 0, :], self.VM[0:NMETA, :], v_r)
        for i0 in range(0, 32, 8):
            self.load(Vf[:, 1 + i0:9 + i0, :],
                      self.VM[NMETA + 128 * i0:NMETA + 128 * (i0 + 8), :].rearrange("(i p) d -> p i d", p=128), v_r)
        qtr = Ring([self.alloc([4, 512], BF16) for _ in range(2)], "qt")
        pt_ring = Ring([self.alloc([512], BF16) for _ in range(3)], "pt")
        ocr = Ring([self.alloc([256], F32) for _ in range(5)], "oc")
        scr = Ring([self.alloc([256], F32) for _ in range(1)], "scr")
        smr = Ring([self.alloc([8], F32) for _ in range(8)], "sm")
        ybr = Ring([self.alloc([256], BF16) for _ in range(3)], "yb")
        ytr = Ring([self.alloc([2, 512], BF16) for _ in range(2)], "yts")
        s_ring = self.psring([0, 1, 2], "s")
        tps = self.psum[7][:, :].bitcast(BF16)
        tps_r = self.psum_r[7]
        scale = 96.0 ** -0.5
        for qi in range(-1, SEQ // 512):
            nsub = 1 if qi < 0 else 4
            ws = NMETA if qi < 0 else 128
            q0 = 0 if qi < 0 else NMETA + 512 * qi
            nq = NMETA if qi < 0 else 512
            qt, qt_r = qtr.next()
            self.load(qt[0:96, :, 0:nq], self.QM[:, :, q0:q0 + nq].rearrange("h r t -> r h t"), qt_r)
            ob = {}
            for h in range(4):
                banks = [(self.psum[3 + h], self.psum_r[3 + h], 65 * s) for s in range(4)]
                ob[h] = banks
                self.attn_sweep_local(qi, Kt[:, h, :], qt[:, h, :], kq_r, qt_r, V[:, :, h, :], v_r, 65, banks, scale,
                                      s_ring, pt_ring, tri, tri_r)
                if h == 0:
                    subs_bufs = [(smr.next(), ocr.next()) for _ in range(nsub)]
                for s_ in range(nsub):
                    (sm, sm_r), (oc, oc_r) = subs_bufs[s_]
                    b, b_r, c = banks[s_]

                    def ev(sm=sm, sm_r=sm_r, oc=oc, oc_r=oc_r, b=b, b_r=b_r, c=c, h=h, ws=ws):
                        P.add("dve", lambda e: e.reciprocal(out=sm[0:ws, h:h + 1], in_=b[0:ws, c + 64:c + 65]),
                              reads=[b_r], writes=[sm_r])
                        self.ts("dve", oc[0:ws, h * 64:(h + 1) * 64], b[0:ws, c:c + 64], sm[0:ws, h:h + 1], None, ALU.mult, None,
                                [b_r, sm_r], [oc_r])
                    self.evac(ev)
            yts, yts_r = ytr.next()
            for s in range(nsub):
                (sm, sm_r), (oc, oc_r) = subs_bufs[s]

                def tail(s=s, sm=sm, sm_r=sm_r, oc=oc, oc_r=oc_r, ws=ws, yts=yts, yts_r=yts_r):
                    sc, sc_r = scr.next()
                    yb, yb_r = ybr.next()
                    fns = [
                        lambda: self.tt("dve", sc[0:ws, :], oc[0:ws, :], oc[0:ws, :], ALU.mult, [oc_r], [sc_r]),
                        lambda: P.add("dve", lambda e: e.reduce_sum(out=sm[0:ws, 4:5], in_=sc[0:ws, :], axis=AX.X),
                                      reads=[sc_r], writes=[sm_r]),
                        lambda: self.ts("dve", sm[0:ws, 4:5], sm[0:ws, 4:5], 1.0 / 256, EPS, ALU.mult, ALU.add, [sm_r], [sm_r]),
                        lambda: self.act(sm[0:ws, 4:5], sm[0:ws, 4:5], AF.Ln, [sm_r], [sm_r]),
                        lambda: self.act(sm[0:ws, 4:5], sm[0:ws, 4:5], AF.Exp, [sm_r], [sm_r], scale=-0.5),
                        lambda: self.stt(yb[0:ws, :], oc[0:ws, :], sm[0:ws, 4:5], gm[0:ws, :], ALU.mult, ALU.mult,
                                         [oc_r, sm_r, gm_r], [yb_r]),
                    ]
                    for c in range(2):
                        fns.append(lambda c=c: P.add("pe", lambda e: e.transpose(
                            out=tps[:, c * 512 + s * 128:c * 512 + s * 128 + ws], in_=yb[0:ws, c * 128:(c + 1) * 128],
                            identity=idb[0:ws, 0:ws]), reads=[yb_r, idb_r], writes=[tps_r]))
                        fns.append(lambda c=c: self.cp("dve", yts[:, c, s * 128:s * 128 + ws],
                                                       tps[:, c * 512 + s * 128:c * 512 + s * 128 + ws], [tps_r], [yts_r]))
                    return fns
                for fn in tail():
                    self.defer(fn)
            self.defer(lambda q0=q0, nq=nq, yts=yts, yts_r=yts_r: self.store(
                self.YT[6:8, :, q0:q0 + nq].rearrange("c p t -> p c t"), yts[:, :, 0:nq], yts_r))
        self.flush_pv()
        self.run_evac()
        self.run_deferred()

    def attn_sweep_local(self, qi, Kt, Qtile, kq_r, qt_r, V, v_r, dvp, obanks, scale, s_ring, pt_ring, tri, tri_r):
        if qi < 0:
            nq, nsub = NMETA, 1
            chunks = [(0, 0, NMETA, 0)]
        else:
            nq, nsub = 512, 4
            chunks = [(0, 0, NMETA, -1)] + [(1 + i, NMETA + 128 * i, 128, -1) for i in range(4 * qi)]
            chunks += [(1 + 4 * qi + c, NMETA + 128 * (4 * qi + c), 128, c) for c in range(4)]
        started = set()
        for (kc, kcol, nk, diag) in chunks:
            s0 = max(diag, 0)
            qa = 128 * s0 if qi >= 0 else 0
            nqa = nq - qa
            ps, ps_r = s_ring.next()
            self.mm(ps[0:nk, 0:nqa], Kt[0:96, kcol:kcol + nk], Qtile[0:96, qa:qa + nqa], True, True, [kq_r, qt_r], [ps_r])
            pt, pt_r = pt_ring.next()
            self.act(pt[0:nk, 0:nqa], ps[0:nk, 0:nqa], AF.Exp, [ps_r], [pt_r], scale=scale)
            if diag >= 0:
                w = min(128, nqa)
                self.tt("dve", pt[0:nk, 0:w], pt[0:nk, 0:w], tri[0:nk, 0:w], ALU.mult, [pt_r, tri_r], [pt_r])
            self.flush_pv()

            def pv(s0=s0, pt=pt, pt_r=pt_r, nk=nk, kc=kc):
                for s in range(s0, nsub):
                    ob, ob_r, oc = obanks[s]
                    ws = min(128, nq)
                    lo = (s - s0) * 128
                    st = id(ob) not in started
                    started.add(id(ob))
                    self.mm(ob[0:ws, oc:oc + dvp], pt[0:nk, lo:lo + ws], V[0:nk, kc, 0:dvp], st, False,
                            [pt_r, v_r], [ob_r], skip_group_check=True)
            self._pend = pv
            self.run_evac()
            self.run_deferred(4)

    def stage_oproj(self, l):
        self.stage_begin()
        Wo = self.alloc([NCH, D], BF16)
        Wo_r = Res("Wo")
        self.load_w_bf16(Wo, Wo_r, self.w["w_out"][l], D, slab=256)
        ytr = Ring([self.alloc([NCH, 512], BF16) for _ in range(2)], "yt")
        psr = self.psring([0, 1, 2, 3], "op")
        for (c0, n) in self.tiles9():
            yt, yt_r = ytr.next()
            self.load(yt[:, :, 0:n], self.YT[:, :, c0:c0 + n].rearrange("c p t -> p c t"), yt_r)
            for o in range(NCH):
                po, por = psr.next()
                for k in range(NCH):
                    self.mm(po[:, 0:n], Wo[:, k, o * 128:(o + 1) * 128], yt[:, k, 0:n], k == 0, k == NCH - 1, [Wo_r, yt_r], [por])
                hr = self.h_res(o, c0, n)
                self.stt(self.hT[:, o, c0:c0 + n], po[:, 0:n], 1.0, self.hT[:, o, c0:c0 + n], ALU.mult, ALU.add,
                         [por] + hr, hr)


    def stage_final(self):
        P = self.P
        self.stage_begin()
        gbc = self.alloc([D], F32)
        gbc_r = Res("gbc")
        P.add("sp", lambda e: e.dma_start(out=gbc[:, :], in_=self.final_norm.partition_broadcast(128)),
              writes=[gbc_r], dma=True, sem_res=gbc_r)
        otr = Ring([self.alloc([D], F32) for i in range(2)], "ot")
        sq = self.alloc([512], F32)
        sq_r = Res("fsq")
        ssr = Ring([self.alloc([4], F32) for i in range(2)], "fss")
        psr = self.psring([0, 1, 2, 3], "f")
        for i in range(SEQ // 128):
            c0 = NMETA + 128 * i
            o, o_r = otr.next()
            s, s_r = ssr.next()
            halves = []
            for half in range(2):
                pt, pr = psr.next()
                for cc in range(4):
                    c = half * 4 + cc
                    P.add("pe", lambda e, pt=pt, cc=cc, c=c, c0=c0: e.transpose(
                        out=pt[:, cc * 128:(cc + 1) * 128], in_=self.hT[:, c, c0:c0 + 128],
                        identity=self.identf[:, :]),
                        reads=self.h_res(c, c0, 128) + [self.identf_r], writes=[pr])
                halves.append((pt, pr))
            for half, (pt, pr) in enumerate(halves):
                P.add("act", lambda e, pt=pt, s=s, half=half: e.activation(
                    out=sq[:, 0:512], in_=pt[:, :], func=AF.Square, accum_out=s[:, half:half + 1]),
                    reads=[pr], writes=[sq_r, s_r])
            P.add("dve", lambda e, s=s: e.tensor_tensor(out=s[:, 2:3], in0=s[:, 0:1], in1=s[:, 1:2], op=ALU.add),
                  reads=[s_r], writes=[s_r])
            P.add("dve", lambda e, s=s: e.tensor_scalar(out=s[:, 2:3], in0=s[:, 2:3], scalar1=1.0 / D, scalar2=EPS,
                                                         op0=ALU.mult, op1=ALU.add),
                  reads=[s_r], writes=[s_r])
            P.add("act", lambda e, s=s: e.activation(out=s[:, 3:4], in_=s[:, 2:3], func=AF.Sqrt),
                  reads=[s_r], writes=[s_r])
            P.add("dve", lambda e, s=s: e.reciprocal(out=s[:, 3:4], in_=s[:, 3:4]),
                  reads=[s_r], writes=[s_r])
            for half, (pt, pr) in enumerate(halves):
                P.add("dve", lambda e, pt=pt, s=s, o=o, half=half: e.scalar_tensor_tensor(
                    out=o[:, half * 512:(half + 1) * 512], in0=pt[:, :], scalar=s[:, 3:4],
                    in1=gbc[:, half * 512:(half + 1) * 512], op0=ALU.mult, op1=ALU.mult),
                    reads=[pr, s_r, gbc_r], writes=[o_r])
            P.add("sp", lambda e, o=o, i=i: e.dma_start(out=self.out[128 * i:128 * (i + 1), :], in_=o[:, :]),
                  reads=[o_r], dma=True, sem_res=o_r)
        self.P.barrier()


def build_program(n_layers=DEPTH, debug=None):
    b = Builder(n_layers, debug)
    nc = b.build()
    return nc, b


_CONSTS = None


def consts():
    global _CONSTS
    if _CONSTS is None:
        half = 16
        inv = (np.float32(10000.0) ** (-(np.arange(half, dtype=np.float32)) / np.float32(half))).astype(np.float32)
        ang = (np.arange(T, dtype=np.float32)[None, :] * inv[:, None]).astype(np.float32).astype(np.float64)
        cs = np.zeros((96, 2, T), np.float32)
        cs[0:64, 0, :] = 1.0
        cs[64:80, 0, :] = np.cos(ang)
        cs[80:96, 0, :] = np.cos(ang)
        cs[64:80, 1, :] = -np.sin(ang)
        cs[80:96, 1, :] = np.sin(ang)
        tri = (np.arange(128)[:, None] <= np.arange(128)[None, :]).astype(np.float32)
        _CONSTS = {"ident_f": np.eye(128, dtype=np.float32), "rope_cs": cs, "tri_in": tri}
    return _CONSTS


def make_in_maps(inputs, b):
    c = consts()
    maps = []
    for core in range(8):
        m = {}
        for nm in b.inputs:
            if nm == "x":
                m[nm] = np.ascontiguousarray(inputs["x"][core])
            elif nm in c:
                m[nm] = c[nm]
            else:
                m[nm] = np.ascontiguousarray(inputs[nm])
        maps.append(m)
    return maps


def kernel(**inputs):
    inputs = {k: np.asarray(v) for k, v in inputs.items()}
    nc, b = build_program()
    maps = make_in_maps(inputs, b)
    res = run_bass_kernel_spmd(nc, maps, core_ids=list(range(8)))
    out = np.stack([np.asarray(res.results[i]["out"]) for i in range(8)], axis=0)
    return out.astype(np.float32)
```

```python
import contextlib
import math
import numpy as np
import ml_dtypes
import concourse.bass as bass
import concourse.mybir as mybir
from concourse.bass_utils import run_bass_kernel_spmd

F32 = mybir.dt.float32
BF16 = mybir.dt.bfloat16
AF = mybir.ActivationFunctionType
ALU = mybir.AluOpType
AX = mybir.AxisListType

D = 1024
SEQ = 4096
NMETA = 16
T = SEQ + NMETA
DFF = 2816
NF = DFF // 128
NCH = D // 128
DEPTH = 2
EPS = 1e-6
DIN = 2464


class Res:
    __slots__ = ("name", "w", "r", "slot")

    def __init__(self, name):
        self.name = name
        self.w = None
        self.r = []
        self.slot = None


class SemSlot:
    __slots__ = ("sem", "dcount")

    def __init__(self):
        self.sem = None
        self.dcount = 0


class Op:
    __slots__ = ("eng", "fn", "deps", "dma", "sem_res", "val", "needs_inc", "idx", "gen")

    def __init__(self, eng, fn, dma=False):
        self.eng = eng
        self.fn = fn
        self.deps = []
        self.dma = dma
        self.sem_res = None
        self.val = 0
        self.needs_inc = False


ENGS = ("pe", "act", "dve", "pool", "sp")
MAX_DMA_INFLIGHT = 4


class Prog:
    def __init__(self):
        self.q = {e: [] for e in ENGS}
        self.pool = []
        self.stage_k = 0
        self.gen = 0
        self.last_dma = {}
        self.dma_fifo = []

    def barrier(self):
        deps = []
        for e in ENGS:
            for op in reversed(self.q[e]):
                if not op.dma and op.fn is not None:
                    deps.append((op, 0))
                    break
        for op in self.last_dma.values():
            deps.append((op, op.sem_res.dcount))
        for e in ENGS:
            op = Op(e, None)
            op.gen = self.gen
            op.deps = [(o, s) for (o, s) in deps if not (o.eng == e and not o.dma)]
            for o, _ in op.deps:
                o.needs_inc = True
            self.q[e].append(op)
        self.gen += 1
        self.stage_k = 0
        self.last_dma = {}

    def add(self, eng, fn, reads=(), writes=(), dma=False, sem_res=None):
        op = Op(eng, fn, dma)
        op.gen = self.gen
        deps = {}

        def dep(o):
            if o is None or o.gen < self.gen:
                return
            if (not o.dma) and o.eng == "pe" and eng == "pe" and not dma:
                return
            snap = o.sem_res.dcount if o.dma else 0
            deps[id(o)] = (o, snap)

        for r in reads:
            dep(r.w)
        for w in writes:
            dep(w.w)
            for o in w.r:
                dep(o)
        if dma:
            if len(self.dma_fifo) >= MAX_DMA_INFLIGHT:
                old = self.dma_fifo[-MAX_DMA_INFLIGHT]
                if old.gen == self.gen and id(old) not in deps:
                    deps[id(old)] = (old, 0)
            self.dma_fifo.append(op)
        op.deps = list(deps.values())
        for o, _ in op.deps:
            o.needs_inc = True
        if dma:
            assert sem_res is not None
            if sem_res.slot is None or sem_res.slot[1] != self.gen:
                if self.stage_k >= len(self.pool):
                    self.pool.append(SemSlot())
                sem_res.slot = (self.pool[self.stage_k], self.gen)
                self.stage_k += 1
            sl = sem_res.slot[0]
            sl.dcount += 16
            op.sem_res = sl
            op.val = sl.dcount
            self.last_dma[id(sl)] = op
        for r in reads:
            if not dma:
                r.r = [o for o in r.r if o.dma or o.eng != eng]
            r.r.append(op)
        for w in writes:
            w.w = op
            w.r = []
        self.q[eng].append(op)
        return op

    def emit(self, nc, es):
        esem = {e: es.enter_context(nc.semaphore("s_" + e)) for e in ENGS}
        for i, sl in enumerate(self.pool):
            sl.sem = es.enter_context(nc.semaphore("d%d" % i))
        for e in ENGS:
            cnt = 0
            for op in self.q[e]:
                if op.dma or op.fn is None:
                    continue
                if op.needs_inc:
                    cnt += 1
                    op.val = cnt
        block = es.enter_context(nc.Block())
        engobj = {"pe": "tensor", "act": "scalar", "dve": "vector", "pool": "gpsimd", "sp": "sync"}

        def body_for(e):
            ops = self.q[e]

            def body(eng):
                seen = {}
                for op in ops:
                    for d, snap in op.deps:
                        if d.dma:
                            sem, val = d.sem_res.sem, max(d.val, snap)
                        else:
                            sem, val = esem[d.eng], d.val
                        key = id(sem)
                        if seen.get(key, 0) >= val:
                            continue
                        seen[key] = val
                        eng.wait_ge(sem, val)
                    if op.fn is None:
                        continue
                    ins = op.fn(eng)
                    if op.dma:
                        ins.then_inc(op.sem_res.sem, 16)
                    elif op.needs_inc:
                        ins.then_inc(esem[e], 1)
            return body

        for e in ENGS:
            getattr(block, engobj[e])(body_for(e))


class Ring:
    def __init__(self, aps, name):
        self.aps = aps
        self.res = [Res("%s%d" % (name, i)) for i in range(len(aps))]
        self.i = 0

    def next(self):
        k = self.i % len(self.aps)
        self.i += 1
        return self.aps[k], self.res[k]


ARENA_BYTES = 79360


def dsize(dt):
    return 4 if dt == F32 else 2


class Builder:
    def __init__(self, n_layers=DEPTH, debug=None):
        self.n_layers = n_layers
        self.debug = debug or {}
        self.nc = bass.Bass("TRN2", target_bir_lowering=False)
        self.P = Prog()
        self.es = contextlib.ExitStack()
        self.inputs = {}
        self.aoff = 0

    def din(self, name, shape, dt=F32):
        ap = self.nc.dram_tensor(name, list(shape), dt, kind="ExternalInput").ap()
        self.inputs[name] = ap
        return ap

    def dout(self, name, shape, dt=F32):
        return self.nc.dram_tensor(name, list(shape), dt, kind="ExternalOutput").ap()

    def dscratch(self, name, shape, dt):
        return self.nc.dram_tensor(name, list(shape), dt, kind="Internal").ap()

    def sb(self, name, shape, dt):
        return self.es.enter_context(self.nc.sbuf_tensor(name, list(shape), dt))

    def ps(self, name, shape, dt=F32):
        return self.es.enter_context(self.nc.psum_tensor(name, list(shape), dt))

    def stage_begin(self):
        self.P.barrier()
        self.aoff = 0

    def alloc(self, shape, dt):
        n = int(np.prod(shape))
        nbytes = (n * dsize(dt) + 31) // 32 * 32
        assert self.aoff + nbytes <= ARENA_BYTES, (self.aoff, nbytes)
        ap = self.arena[:, self.aoff // 4:(self.aoff + nbytes) // 4]
        self.aoff += nbytes
        if dt != F32:
            ap = ap.bitcast(dt)
        ap = ap[:, 0:n]
        if len(shape) == 2:
            ap = ap.rearrange("p (a b) -> p a b", a=shape[0])
        elif len(shape) == 3:
            ap = ap.rearrange("p (a b c) -> p a b c", a=shape[0], b=shape[1])
        return ap


    def act(self, out, in_, func, reads, writes, **kw):
        return self.P.add("act", lambda e: e.activation(out=out, in_=in_, func=func, **kw), reads=reads, writes=writes)

    def tt(self, eng, out, in0, in1, op, reads, writes):
        return self.P.add(eng, lambda e: e.tensor_tensor(out=out, in0=in0, in1=in1, op=op), reads=reads, writes=writes)

    def ts(self, eng, out, in0, s1, s2, op0, op1, reads, writes):
        if s2 is None:
            return self.P.add(eng, lambda e: e.tensor_scalar(out=out, in0=in0, scalar1=s1, scalar2=None, op0=op0),
                              reads=reads, writes=writes)
        return self.P.add(eng, lambda e: e.tensor_scalar(out=out, in0=in0, scalar1=s1, scalar2=s2, op0=op0, op1=op1),
                          reads=reads, writes=writes)

    def stt(self, out, in0, scalar, in1, op0, op1, reads, writes):
        return self.P.add("dve", lambda e: e.scalar_tensor_tensor(out=out, in0=in0, scalar=scalar, in1=in1, op0=op0, op1=op1),
                          reads=reads, writes=writes)

    def cp(self, eng, out, in_, reads, writes):
        if eng == "act":
            return self.act(out, in_, AF.Copy, reads, writes)
        return self.P.add(eng, lambda e: e.tensor_copy(out=out, in_=in_), reads=reads, writes=writes)

    def mm(self, out, lhsT, rhs, start, stop, reads, writes, **kw):
        return self.P.add("pe", lambda e: e.matmul(out, lhsT=lhsT, rhs=rhs, start=start, stop=stop, **kw),
                          reads=reads, writes=writes)

    def load(self, out, in_, res, reads=()):
        return self.P.add("sp", lambda e: e.dma_start(out=out, in_=in_), reads=list(reads), writes=[res], dma=True, sem_res=res)

    def store(self, out, in_, res):
        return self.P.add("sp", lambda e: e.dma_start(out=out, in_=in_), reads=[res], dma=True, sem_res=res)

    def rsqrt_inplace(self, ap, res, scale, in_ap=None, in_reads=()):
        src = ap if in_ap is None else in_ap
        self.ts("dve", ap, src, scale, EPS, ALU.mult, ALU.add, [res] + list(in_reads), [res])
        self.act(ap, ap, AF.Sqrt, [res], [res])
        self.P.add("dve", lambda e: e.reciprocal(out=ap, in_=ap), reads=[res], writes=[res])

    def h_res(self, c, c0, n):
        out = []
        if c0 < NMETA:
            out.append(self.hres[c][0])
        lo = max(c0, NMETA) - NMETA
        hi = c0 + n - NMETA
        if hi > lo:
            for i in range(lo // 128, (hi - 1) // 128 + 1):
                out.append(self.hres[c][1 + i])
        return out

    def build(self):
        nc, P = self.nc, self.P
        self.x = self.din("x", [SEQ, D])
        self.meta = self.din("meta_tokens", [NMETA, D])
        self.final_norm = self.din("final_norm", [D])
        self.ident_f = self.din("ident_f", [128, 128])
        L = DEPTH
        self.w = {}
        for nm, shp in [("ffn1_norm", [L, D]), ("ffn1_in", [L, D, 2 * DFF]), ("ffn1_out", [L, DFF, D]),
                        ("ffn2_norm", [L, D]), ("ffn2_in", [L, D, 2 * DFF]), ("ffn2_out", [L, DFF, D])]:
            self.w[nm] = self.din(nm, shp)
        for nm, shp in [("mix_norm", [L, D]), ("w_in", [L, D, DIN]), ("conv_w", [L, 4, 256]), ("conv_b", [L, 256]),
                        ("lru_wa", [L, 4, 64, 64]), ("lru_ba", [L, 4, 64]), ("lru_wx", [L, 4, 64, 64]), ("lru_bx", [L, 4, 64]),
                        ("lru_lambda", [L, 256]), ("lru_out_norm", [L, 256]), ("lam_q1", [L, 64]), ("lam_k1", [L, 64]),
                        ("lam_q2", [L, 64]), ("lam_k2", [L, 64]), ("diff_subln", [L, 128]), ("q_norm", [L, 256]),
                        ("w_uq", [L, 256, 384]), ("kv_norm", [L, 128]), ("w_ukv", [L, 128, 512]),
                        ("mla_out_norm", [L, 256]), ("w_out", [L, D, D])]:
            self.w[nm] = self.din(nm, shp)
        self.rope_cs = self.din("rope_cs", [96, 2, T])
        self.tri_in = self.din("tri_in", [128, 128])
        self.out = self.dout("out", [SEQ, D])
        self.LRU_IN = self.dscratch("lru_in", [4, 128, T], F32)
        self.QKD = self.dscratch("qkd", [8, 128, T], BF16)
        self.VD = self.dscratch("vd", [T, 512], BF16)
        self.QM = self.dscratch("qm", [4, 96, T], BF16)
        self.KM = self.dscratch("km", [4, 96, T], BF16)
        self.VM = self.dscratch("vm", [T, 260], BF16)
        self.YT = self.dscratch("yt", [8, 128, T], BF16)
        self.wf = {}
        for l in range(L):
            for which in (1, 2):
                self.wf[(l, which)] = self.dscratch("wf_%d_%d" % (l, which), [NF, 128, 3072], BF16)
        self.wf_r = {k: Res("wf%d%d" % k) for k in self.wf}

        self.hT = self.sb("hT", [128, NCH, T], F32)
        self.hres = [[Res("h%d_%d" % (c, i)) for i in range(1 + SEQ // 128)] for c in range(NCH)]
        self.arena = self.sb("arena", [128, ARENA_BYTES // 4], F32)
        self.identf = self.sb("identf", [128, 128], F32)
        self.identf_r = Res("identf")
        self.ones_bf = self.sb("ones_bf", [128, 128], BF16)
        self.ones_r = Res("ones")
        self.gains = self.sb("gains", [128, 8, NCH], F32)
        self.gains_r = Res("gains")

        P.add("sp", lambda e: e.dma_start(out=self.identf[:], in_=self.ident_f[:, :]),
              writes=[self.identf_r], dma=True, sem_res=self.identf_r)
        P.add("pool", lambda e: e.memset(self.ones_bf[:], 1.0), writes=[self.ones_r])
        self.gain_idx = {}
        gi = 0
        for l in range(L):
            for nm in ("ffn1_norm", "ffn2_norm", "mix_norm"):
                src = self.w[nm][l, :].rearrange("(c p) -> p c", p=128)
                P.add("sp", lambda e, k=gi, src=src: e.dma_start(out=self.gains[:, k, :], in_=src),
                      writes=[self.gains_r], dma=True, sem_res=self.gains_r)
                self.gain_idx[(nm, l)] = gi
                gi += 1
        src = self.final_norm.rearrange("(c p) -> p c", p=128)
        P.add("sp", lambda e, k=gi, src=src: e.dma_start(out=self.gains[:, k, :], in_=src),
              writes=[self.gains_r], dma=True, sem_res=self.gains_r)
        self.gain_idx["final"] = gi

        self.psum = [self.ps("ps%d" % i, [128, 512], F32) for i in range(8)]
        self.psum_r = [Res("ps%d" % i) for i in range(8)]

        self.stage_load_x()
        ffns = [(l, w) for l in range(self.n_layers) for w in (1, 2) if self.debug.get("ffn%d" % w, True)]
        self.conv_queue = []
        for i, (l, w) in enumerate(ffns):
            if i == 0 or not self.debug.get("mixer", True) or not self.debug.get("diff", True):
                self.stage_convert_ffn(l, w)
            else:
                self.conv_queue += [(l, w, f) for f in range(NF)]
        for l in range(self.n_layers):
            if self.debug.get("ffn1", True):
                self.stage_ffn(l, 1)
            if self.debug.get("mixer", True):
                if self.debug.get("proj1", True):
                    self.stage_proj1(l)
                if self.debug.get("proj2", True):
                    self.stage_proj2(l)
                if self.debug.get("lru", True):
                    self.stage_lru(l)
                if self.debug.get("diff", True):
                    self.stage_diff(l)
                if self.debug.get("mla", True):
                    self.stage_mla(l)
                if self.debug.get("oproj", True):
                    self.stage_oproj(l)
            if self.debug.get("ffn2", True):
                self.stage_ffn(l, 2)
        self.stage_final()
        with nc.allow_non_contiguous_dma(reason="small strided parameter loads"):
            P.emit(nc, self.es)
        return nc

    def psring(self, idxs, name):
        r = Ring([self.psum[i] for i in idxs], name)
        r.res = [self.psum_r[i] for i in idxs]
        return r

    def stage_load_x(self):
        P = self.P
        self.stage_begin()
        ring = Ring([self.alloc([D], F32) for i in range(3)], "xin")
        psr = self.psring([0, 1, 2, 3], "x")
        n = 0
        for i in range(-1, SEQ // 128):
            buf, br = ring.next()
            if i < 0:
                rows, c0 = NMETA, 0
                src = self.meta[:, :]
            else:
                rows, c0 = 128, NMETA + 128 * i
                src = self.x[128 * i:128 * (i + 1), :]
            P.add("sp", lambda e, buf=buf, rows=rows, src=src: e.dma_start(out=buf[0:rows, :], in_=src),
                  writes=[br], dma=True, sem_res=br)
            for half in range(2):
                pt, pr = psr.next()
                for cc in range(4):
                    c = half * 4 + cc
                    P.add("pe", lambda e, pt=pt, cc=cc, buf=buf, rows=rows, c=c: e.transpose(
                        out=pt[:, cc * 128:cc * 128 + rows], in_=buf[0:rows, c * 128:(c + 1) * 128],
                        identity=self.identf[0:rows, 0:rows]),
                        reads=[br, self.identf_r], writes=[pr])
                eng = "act" if (n % 2 == 0) else "dve"
                n += 1
                hr = []
                for cc in range(4):
                    hr += self.h_res(half * 4 + cc, c0, rows)
                dst = self.hT[:, half * 4:half * 4 + 4, c0:c0 + rows]
                srcp = pt[:, :].rearrange("p (c t) -> p c t", c=4)[:, :, 0:rows]
                if eng == "act":
                    P.add("act", lambda e, dst=dst, srcp=srcp: e.activation(out=dst, in_=srcp, func=AF.Copy),
                          reads=[pr], writes=hr)
                else:
                    P.add("dve", lambda e, dst=dst, srcp=srcp: e.tensor_copy(out=dst, in_=srcp),
                          reads=[pr], writes=hr)

    def conv_bufs(self):
        return dict(a=self.alloc([2, NCH, 128], F32), ar=Res("cva"), ao=self.alloc([D], F32), aor=Res("cvo"),
                    b=self.alloc([3072], BF16), br=Res("cvb"))

    def conv_job(self, job, cb, eng="pool"):
        l, which, f = job
        w_in = self.w["ffn%d_in" % which][l]
        w_out = self.w["ffn%d_out" % which][l]
        WF = self.wf[(l, which)]
        a, ar, ao, aor, b, br = cb["a"], cb["ar"], cb["ao"], cb["aor"], cb["b"], cb["br"]
        for gu in range(2):
            col = gu * DFF + f * 128
            self.load(a[:, gu, :, :], w_in[:, col:col + 128].rearrange("(k p) j -> p k j", p=128), ar)
        self.load(ao[:, :], w_out[f * 128:(f + 1) * 128, :], aor)
        for gu in range(2):
            out = b[:, 0:2048].rearrange("p (k j) -> p k j", k=NCH)[:, :, gu * 128:(gu + 1) * 128]
            self.cp(eng, out, a[:, gu, :, :], [ar], [br])
        self.cp(eng, b[:, 2048:3072], ao[:, :], [aor], [br])
        self.store(WF[f, :, :], b[:, :], br)

    def stage_convert_ffn(self, l, which):
        P = self.P
        self.stage_begin()
        w_in = self.w["ffn%d_in" % which][l]
        w_out = self.w["ffn%d_out" % which][l]
        WF = self.wf[(l, which)]
        wfr = self.wf_r[(l, which)]
        s32 = Ring([self.alloc([2, NCH, 256], F32) for _ in range(2)], "cv32")
        s32o = Ring([self.alloc([2, D], F32) for _ in range(2)], "cv32o")
        s16 = Ring([self.alloc([2, 3072], BF16) for _ in range(2)], "cv16")
        engs = ["act", "dve", "pool"]
        n = 0

        def cast(eng, out, in_):
            if eng == "act":
                return lambda e: e.activation(out=out, in_=in_, func=AF.Copy)
            return lambda e: e.tensor_copy(out=out, in_=in_)

        for fp in range(NF // 2):
            f0 = 2 * fp
            a, ar = s32.next()
            ao, aor = s32o.next()
            b, br = s16.next()
            for gu in range(2):
                col = gu * DFF + f0 * 128
                src = w_in[:, col:col + 256].rearrange("(k p) j -> p k j", p=128)
                P.add("sp", lambda e, a=a, gu=gu, src=src: e.dma_start(out=a[:, gu, :, :], in_=src),
                      writes=[ar], dma=True, sem_res=ar)
            src = w_out[f0 * 128:(f0 + 2) * 128, :].rearrange("(f p) c -> p f c", p=128)
            P.add("sp", lambda e, ao=ao, src=src: e.dma_start(out=ao[:, :, :], in_=src),
                  writes=[aor], dma=True, sem_res=aor)
            for ff in range(2):
                for gu in range(2):
                    eng = engs[n % 3]
                    n += 1
                    out = b[:, ff, 0:2048].rearrange("p (k j) -> p k j", k=NCH)[:, :, gu * 128:(gu + 1) * 128]
                    in_ = a[:, gu, :, ff * 128:(ff + 1) * 128]
                    P.add(eng, cast(eng, out, in_), reads=[ar], writes=[br])
            eng = engs[n % 3]
            n += 1
            P.add(eng, cast(eng, b[:, :, 2048:3072], ao[:, :, :]), reads=[aor], writes=[br])
            dst = WF[f0:f0 + 2, :, :].rearrange("f p c -> p f c")
            P.add("sp", lambda e, b=b, dst=dst: e.dma_start(out=dst, in_=b[:, :, :]),
                  reads=[br], writes=[wfr], dma=True, sem_res=br)

    def stage_ffn(self, l, which):
        P = self.P
        self.stage_begin()
        WF = self.wf[(l, which)]
        wfr = self.wf_r[(l, which)]
        gk = self.gain_idx[("ffn%d_norm" % which, l)]
        xn = self.alloc([NCH, 1040], BF16)
        hid = Ring([self.alloc([1040], BF16) for _ in range(4)], "hid")
        wsl = Ring([self.alloc([3072], BF16) for _ in range(6)], "wsl")
        sqr = Ring([self.alloc([512], BF16) for _ in range(2)], "sq")
        rstd = self.alloc([1040], F32)
        rstd_r = Res("rstd")
        sgr = Ring([self.alloc([512], F32) for _ in range(2)], "sg")
        ps_gu = self.psring([0, 1, 2, 3], "gu")
        ps_o = self.psring([4, 5, 6], "o")
        ps_n, ps_nr = self.psum[7], self.psum_r[7]

        sts = [[(0, 16), (16, 512), (528, 512)]]
        for s in range(1, 4):
            b0 = 1040 + 1024 * (s - 1)
            sts.append([(b0, 512), (b0 + 512, 512)])
        nsq = 0
        for subs in sts:
            base = subs[0][0]
            xn_r = [Res("xn%d" % i) for i in range(len(subs))]
            for si, (c0, n) in enumerate(subs):
                lc = c0 - base
                for c in range(NCH):
                    sq, sq_r = sqr.next()
                    src = self.hT[:, c, c0:c0 + n]
                    if nsq % 2 == 0:
                        P.add("act", lambda e, sq=sq, src=src, n=n: e.activation(out=sq[:, 0:n], in_=src, func=AF.Square),
                              reads=self.h_res(c, c0, n), writes=[sq_r])
                    else:
                        P.add("pool", lambda e, sq=sq, src=src, n=n: e.tensor_tensor(out=sq[:, 0:n], in0=src, in1=src, op=ALU.mult),
                              reads=self.h_res(c, c0, n), writes=[sq_r])
                    nsq += 1
                    P.add("pe", lambda e, sq=sq, n=n, c=c: e.matmul(ps_n[:, 0:n], lhsT=self.ones_bf[:, :], rhs=sq[:, 0:n],
                                                                     start=(c == 0), stop=(c == NCH - 1)),
                          reads=[sq_r, self.ones_r], writes=[ps_nr])
                rs = rstd[:, lc:lc + n]
                P.add("dve", lambda e, rs=rs, n=n: e.tensor_scalar(out=rs, in0=ps_n[:, 0:n], scalar1=1.0 / D, scalar2=EPS,
                                                                   op0=ALU.mult, op1=ALU.add),
                      reads=[ps_nr], writes=[rstd_r])
                P.add("act", lambda e, rs=rs: e.activation(out=rs, in_=rs, func=AF.Sqrt), reads=[rstd_r], writes=[rstd_r])
                P.add("dve", lambda e, rs=rs: e.reciprocal(out=rs, in_=rs), reads=[rstd_r], writes=[rstd_r])
                for c in range(NCH):
                    P.add("dve", lambda e, c=c, c0=c0, n=n, lc=lc, rs=rs: e.scalar_tensor_tensor(
                        out=xn[:, c, lc:lc + n], in0=self.hT[:, c, c0:c0 + n], scalar=self.gains[:, gk, c:c + 1],
                        in1=rs, op0=ALU.mult, op1=ALU.mult),
                        reads=self.h_res(c, c0, n) + [rstd_r, self.gains_r], writes=[xn_r[si]])

            def GU(f):
                slot, slot_r = wsl.next()
                P.add("sp", lambda e, slot=slot, f=f: e.dma_start(out=slot[:, :], in_=WF[f, :, :]),
                      reads=[wfr], writes=[slot_r], dma=True, sem_res=slot_r)
                hb, hb_r = hid.next()
                blocks = []
                for si, (c0, n) in enumerate(subs):
                    def block(si=si, c0=c0, n=n):
                        lc = c0 - base
                        pg, pgr = ps_gu.next()
                        pu, pur = ps_gu.next()
                        for gu, (pp, ppr) in enumerate(((pg, pgr), (pu, pur))):
                            for k in range(NCH):
                                self.mm(pp[:, 0:n], slot[:, k * 256 + gu * 128:k * 256 + gu * 128 + 128],
                                        xn[:, k, lc:lc + n], k == 0, k == NCH - 1, [slot_r, xn_r[si]], [ppr])
                        sg, sg_r = sgr.next()
                        self.act(sg[:, 0:n], pg[:, 0:n], AF.Silu, [pgr], [sg_r])
                        self.tt("dve", hb[:, lc:lc + n], sg[:, 0:n], pu[:, 0:n], ALU.mult, [sg_r, pur], [hb_r])
                    blocks.append(block)
                return (slot, slot_r, hb, hb_r), blocks

            def W2(group):
                units = []
                for o in range(NCH):
                    for si, (c0, n) in enumerate(subs):
                        def unit(o=o, c0=c0, n=n):
                            lc = c0 - base
                            po, por = ps_o.next()
                            for gi_, (slot, slot_r, hb, hb_r) in enumerate(group):
                                self.mm(po[:, 0:n], slot[:, 2048 + o * 128:2048 + (o + 1) * 128], hb[:, lc:lc + n],
                                        gi_ == 0, gi_ == len(group) - 1, [slot_r, hb_r], [por])
                            hr = self.h_res(o, c0, n)
                            self.stt(self.hT[:, o, c0:c0 + n], po[:, 0:n], 0.5, self.hT[:, o, c0:c0 + n], ALU.mult, ALU.add,
                                     [por] + hr, hr)
                        units.append(unit)
                return units

            pend = []
            for g in range(NF // 2):
                grp, blocks = [], []
                for f in (2 * g, 2 * g + 1):
                    info, bl = GU(f)
                    grp.append(info)
                    blocks += bl
                per = -(-len(pend) // len(blocks))
                for bl in blocks:
                    bl()
                    for _ in range(per):
                        if pend:
                            pend.pop(0)()
                while pend:
                    pend.pop(0)()
                pend = W2(grp)
            while pend:
                pend.pop(0)()

    def tiles9(self):
        return [(0, NMETA)] + [(NMETA + 512 * j, 512) for j in range(SEQ // 512)]

    def norm_tile(self, c0, n, gk, xn, xn_r, sqr, rstd, rstd_r, ps_n, ps_nr, cnt=[0]):
        P = self.P
        for c in range(NCH):
            sq, sq_r = sqr.next()
            src = self.hT[:, c, c0:c0 + n]
            if cnt[0] % 2 == 0:
                self.act(sq[:, 0:n], src, AF.Square, self.h_res(c, c0, n), [sq_r])
            else:
                self.tt("pool", sq[:, 0:n], src, src, ALU.mult, self.h_res(c, c0, n), [sq_r])
            cnt[0] += 1
            self.mm(ps_n[:, 0:n], self.ones_bf[:, :], sq[:, 0:n], c == 0, c == NCH - 1, [sq_r, self.ones_r], [ps_nr])
        rs = rstd[:, 0:n]
        self.rsqrt_inplace(rs, rstd_r, 1.0 / D, in_ap=ps_n[:, 0:n], in_reads=[ps_nr])
        for c in range(NCH):
            self.stt(xn[:, c, 0:n], self.hT[:, c, c0:c0 + n], self.gains[:, gk, c:c + 1], rs, ALU.mult, ALU.mult,
                     self.h_res(c, c0, n) + [rstd_r, self.gains_r], [xn_r])

    def load_w_bf16(self, dst, dst_r, src, ncols, slab=160):
        st = Ring([self.alloc([src.shape[0] // 128, slab], F32) for _ in range(2)], "wslab")
        engs = ["act", "dve", "pool"]
        i = 0
        for c in range(0, ncols, slab):
            w = min(slab, ncols - c)
            a, ar = st.next()
            self.load(a[:, :, 0:w], src[:, c:c + w].rearrange("(k p) j -> p k j", p=128), ar)
            self.cp(engs[i % 3], dst[:, :, c:c + w], a[:, :, 0:w], [ar], [dst_r])
            i += 1

    def stage_proj1(self, l):
        self.stage_begin()
        gk = self.gain_idx[("mix_norm", l)]
        w_in = self.w["w_in"][l]
        Wp = self.alloc([NCH, 2048], BF16)
        Wp_r = Res("Wp")
        self.load_w_bf16(Wp, Wp_r, w_in[:, 0:2048], 2048)
        xnr = Ring([self.alloc([NCH, 512], BF16) for _ in range(2)], "xn")
        sqr = Ring([self.alloc([512], BF16) for _ in range(2)], "sq")
        rstr = Ring([self.alloc([512], F32) for _ in range(2)], "rstd")
        st32 = Ring([self.alloc([512], F32) for _ in range(3)], "st32")
        st16 = Ring([self.alloc([512], BF16) for _ in range(4)], "st16")
        psr = self.psring([0, 1, 2, 3, 4, 5], "p1")
        ps_n, ps_nr = self.psum[7], self.psum_r[7]
        ne = 0
        tiles = self.tiles9()
        normed = {}

        def do_norm(i):
            if i < len(tiles) and i not in normed:
                xn_, xn_r_ = xnr.next()
                rstd_, rstd_r_ = rstr.next()
                self.norm_tile(tiles[i][0], tiles[i][1], gk, xn_, xn_r_, sqr, rstd_, rstd_r_, ps_n, ps_nr)
                normed[i] = (xn_, xn_r_)
        do_norm(0)
        for ti, (c0, n) in enumerate(tiles):
            xn, xn_r = normed[ti]
            do_norm(ti + 1)
            for g in range(12):
                pt, pr = psr.next()
                for k in range(NCH):
                    self.mm(pt[:, 0:n], Wp[:, k, g * 128:(g + 1) * 128], xn[:, k, 0:n], k == 0, k == NCH - 1,
                            [Wp_r, xn_r], [pr])
                eng = "act" if ne % 2 == 0 else "dve"
                ne += 1
                if g < 4:
                    sb, sr = st32.next()
                    self.cp(eng, sb[:, 0:n], pt[:, 0:n], [pr], [sr])
                    self.store(self.LRU_IN[g, :, c0:c0 + n], sb[:, 0:n], sr)
                else:
                    sb, sr = st16.next()
                    self.cp(eng, sb[:, 0:n], pt[:, 0:n], [pr], [sr])
                    self.store(self.QKD[g - 4, :, c0:c0 + n], sb[:, 0:n], sr)
            for j in range(0, n, 128):
                m = min(128, n - j)
                pt, pr = psr.next()
                for k in range(NCH):
                    self.mm(pt[0:m, 0:512], xn[:, k, j:j + m], Wp[:, k, 1536:2048], k == 0, k == NCH - 1,
                            [Wp_r, xn_r], [pr])
                sb, sr = st16.next()
                eng = "act" if ne % 2 == 0 else "dve"
                ne += 1
                self.cp(eng, sb[0:m, 0:512], pt[0:m, 0:512], [pr], [sr])
                self.store(self.VD[c0 + j:c0 + j + m, :], sb[0:m, 0:512], sr)

    def stage_proj2(self, l):
        self.stage_begin()
        gk = self.gain_idx[("mix_norm", l)]
        w_in = self.w["w_in"][l]
        Wp = self.alloc([NCH, 512], BF16)
        Wp_r = Res("Wp2")
        self.P.add("pool", lambda e: e.memset(Wp[:, :, 416:480], 0.0), writes=[Wp_r])
        self.load_w_bf16(Wp, Wp_r, w_in[:, 2048:2464], 416, slab=104)
        self.cp("dve", Wp[:, :, 480:496], Wp[:, :, 400:416], [Wp_r], [Wp_r])
        self.cp("dve", Wp[:, :, 496:512], Wp[:, :, 384:400], [Wp_r], [Wp_r])
        Wq = self.alloc([2, 768], BF16)
        Wq_r = Res("Wq")
        self.load_w_bf16(Wq, Wq_r, self.w["w_uq"][l], 384, slab=192)
        for h in range(4):
            b = 384 + h * 96
            a = h * 96
            self.cp("dve", Wq[:, :, b:b + 64], Wq[:, :, a:a + 64], [Wq_r], [Wq_r])
            self.cp("dve", Wq[:, :, b + 64:b + 80], Wq[:, :, a + 80:a + 96], [Wq_r], [Wq_r])
            self.cp("dve", Wq[:, :, b + 80:b + 96], Wq[:, :, a + 64:a + 80], [Wq_r], [Wq_r])
        Wkv = self.alloc([512], BF16)
        Wkv_r = Res("Wkv")
        kv32 = self.alloc([512], F32)
        kv32_r = Res("kv32")
        self.load(kv32[:, :], self.w["w_ukv"][l][:, :], kv32_r)
        self.cp("dve", Wkv[:, :].rearrange("p (t h d) -> p h t d", t=2, h=4),
                kv32[:, :].rearrange("p (h t d) -> p h t d", h=4, t=2), [kv32_r], [Wkv_r])
        gq = self.alloc([2], F32)
        gkv = self.alloc([1], F32)
        g_r = Res("gqkv")
        self.load(gq[:, :], self.w["q_norm"][l].rearrange("(c p) -> p c", p=128), g_r)
        self.load(gkv[:, :], self.w["kv_norm"][l].rearrange("(c p) -> p c", p=128), g_r)

        xnr = Ring([self.alloc([NCH, 512], BF16) for _ in range(2)], "xn")
        sqr = Ring([self.alloc([512], BF16) for _ in range(2)], "sq")
        rstr = Ring([self.alloc([512], F32) for _ in range(1)], "rstd")
        cq = self.alloc([2, 512], F32)
        cq_r = Res("cq")
        ckv = self.alloc([512], F32)
        ckv_r = Res("ckv")
        cqn = self.alloc([2, 512], BF16)
        cqn_r = Res("cqn")
        ckvn = self.alloc([512], BF16)
        ckvn_r = Res("ckvn")
        rsq = self.alloc([512], F32)
        rsq_r = Res("rsq")
        rskv = self.alloc([512], F32)
        rskv_r = Res("rskv")
        cs = self.alloc([2, 512], F32)
        cs_r = Res("cs")
        t1r = Ring([self.alloc([512], F32) for _ in range(2)], "t1")
        t2r = Ring([self.alloc([512], F32) for _ in range(2)], "t2")
        qst = Ring([self.alloc([4, 512], BF16) for _ in range(1)], "qst")
        kst = Ring([self.alloc([4, 512], BF16) for _ in range(1)], "kst")
        vst = Ring([self.alloc([4, 65], BF16) for _ in range(1)], "vst")
        psr = self.psring([0, 1, 2, 3, 4, 5], "p2")
        ps_n, ps_nr = self.psum[7], self.psum_r[7]
        ps_m, ps_mr = self.psum[6], self.psum_r[6]
        tiles = self.tiles9()
        normed = {}

        def do_norm(i):
            if i < len(tiles) and i not in normed:
                xn_, xn_r_ = xnr.next()
                rstd_, rstd_r_ = rstr.next()
                self.norm_tile(tiles[i][0], tiles[i][1], gk, xn_, xn_r_, sqr, rstd_, rstd_r_, ps_n, ps_nr)
                normed[i] = (xn_, xn_r_)
        do_norm(0)
        for ti, (c0, n) in enumerate(tiles):
            xn, xn_r = normed[ti]
            do_norm(ti + 1)
            self.load(cs[0:96, :, 0:n], self.rope_cs[:, :, c0:c0 + n], cs_r)
            for j in range(3):
                pt, pr = psr.next()
                for k in range(NCH):
                    self.mm(pt[:, 0:n], Wp[:, k, j * 128:(j + 1) * 128], xn[:, k, 0:n], k == 0, k == NCH - 1,
                            [Wp_r, xn_r], [pr])
                if j < 2:
                    self.cp("act", cq[:, j, 0:n], pt[:, 0:n], [pr], [cq_r])
                else:
                    self.cp("act", ckv[:, 0:n], pt[:, 0:n], [pr], [ckv_r])
            for j in range(2):
                sq, sq_r = sqr.next()
                self.tt("pool", sq[:, 0:n], cq[:, j, 0:n], cq[:, j, 0:n], ALU.mult, [cq_r], [sq_r])
                self.mm(ps_m[:, 0:n], self.ones_bf[:, :], sq[:, 0:n], j == 0, j == 1, [sq_r, self.ones_r], [ps_mr])
            self.rsqrt_inplace(rsq[:, 0:n], rsq_r, 1.0 / 256, in_ap=ps_m[:, 0:n], in_reads=[ps_mr])
            for j in range(2):
                self.stt(cqn[:, j, 0:n], cq[:, j, 0:n], gq[:, j:j + 1], rsq[:, 0:n], ALU.mult, ALU.mult,
                         [cq_r, g_r, rsq_r], [cqn_r])
            sq, sq_r = sqr.next()
            self.tt("pool", sq[:, 0:n], ckv[:, 0:n], ckv[:, 0:n], ALU.mult, [ckv_r], [sq_r])
            self.mm(ps_m[:, 0:n], self.ones_bf[:, :], sq[:, 0:n], True, True, [sq_r, self.ones_r], [ps_mr])
            self.rsqrt_inplace(rskv[:, 0:n], rskv_r, 1.0 / 128, in_ap=ps_m[:, 0:n], in_reads=[ps_mr])
            self.stt(ckvn[:, 0:n], ckv[:, 0:n], gkv[:, 0:1], rskv[:, 0:n], ALU.mult, ALU.mult,
                     [ckv_r, g_r, rskv_r], [ckvn_r])
            qs, qs_r = qst.next()
            for h in range(4):
                pa, par = psr.next()
                pb, pbr = psr.next()
                for k in range(2):
                    self.mm(pa[0:96, 0:n], Wq[:, k, h * 96:(h + 1) * 96], cqn[:, k, 0:n], k == 0, k == 1, [Wq_r, cqn_r], [par])
                for k in range(2):
                    self.mm(pb[0:96, 0:n], Wq[:, k, 384 + h * 96:384 + (h + 1) * 96], cqn[:, k, 0:n], k == 0, k == 1,
                            [Wq_r, cqn_r], [pbr])
                t1, t1_r = t1r.next()
                t2, t2_r = t2r.next()
                self.tt("dve", t1[0:96, 0:n], pa[0:96, 0:n], cs[0:96, 0, 0:n], ALU.mult, [par, cs_r], [t1_r])
                self.tt("dve", t2[0:96, 0:n], pb[0:96, 0:n], cs[0:96, 1, 0:n], ALU.mult, [pbr, cs_r], [t2_r])
                self.tt("pool", qs[0:96, h, 0:n], t1[0:96, 0:n], t2[0:96, 0:n], ALU.add, [t1_r, t2_r], [qs_r])
            self.store(self.QM[:, :, c0:c0 + n].rearrange("h r t -> r h t"), qs[0:96, :, 0:n], qs_r)
            ks, ks_r = kst.next()
            for h in range(4):
                pt, pr = psr.next()
                self.mm(pt[0:64, 0:n], Wkv[:, h * 64:(h + 1) * 64], ckvn[:, 0:n], True, True, [Wkv_r, ckvn_r], [pr])
                self.cp("act", ks[0:64, h, 0:n], pt[0:64, 0:n], [pr], [ks_r])
            pa, par = psr.next()
            pb, pbr = psr.next()
            for k in range(NCH):
                self.mm(pa[0:96, 0:n], Wp[:, k, 320:416], xn[:, k, 0:n], k == 0, k == NCH - 1, [Wp_r, xn_r], [par])
            for k in range(NCH):
                self.mm(pb[0:96, 0:n], Wp[:, k, 416:512], xn[:, k, 0:n], k == 0, k == NCH - 1, [Wp_r, xn_r], [pbr])
            t1, t1_r = t1r.next()
            t2, t2_r = t2r.next()
            self.tt("dve", t1[64:96, 0:n], pa[64:96, 0:n], cs[64:96, 0, 0:n], ALU.mult, [par, cs_r], [t1_r])
            self.tt("dve", t2[64:96, 0:n], pb[64:96, 0:n], cs[64:96, 1, 0:n], ALU.mult, [pbr, cs_r], [t2_r])
            self.tt("pool", t1[64:96, 0:n], t1[64:96, 0:n], t2[64:96, 0:n], ALU.add, [t1_r, t2_r], [t1_r])
            for h in range(4):
                self.cp("pool" if h % 2 else "act", ks[64:96, h, 0:n], t1[64:96, 0:n], [t1_r], [ks_r])
            self.store(self.KM[:, :, c0:c0 + n].rearrange("h r t -> r h t"), ks[0:96, :, 0:n], ks_r)
            for j in range(0, n, 128):
                m = min(128, n - j)
                pt, pr = psr.next()
                self.mm(pt[0:m, 0:256], ckvn[:, j:j + m], Wkv[:, 256:512], True, True, [Wkv_r, ckvn_r], [pr])
                vs, vs_r = vst.next()
                self.P.add("pool", lambda e, vs=vs: e.memset(vs[:, :, 64:65], 1.0), writes=[vs_r])
                self.cp("act", vs[0:m, :, 0:64], pt[0:m, 0:256].rearrange("p (h d) -> p h d", h=4), [pr], [vs_r])
                self.store(self.VM[c0 + j:c0 + j + m, :], vs[0:m, :, :].rearrange("p h d -> p (h d)"), vs_r)

    def stage_lru(self, l):
        self.stage_begin()
        P = self.P
        pr_ = Res("lrup")
        convw = self.alloc([2, 4], F32)
        convb = self.alloc([2], F32)
        ba = self.alloc([2], F32)
        bx = self.alloc([2], F32)
        lam = self.alloc([2], F32)
        gl = self.alloc([2], F32)
        for c in range(2):
            for j in range(4):
                self.load(convw[:, c, j:j + 1], self.w["conv_w"][l][j, c * 128:(c + 1) * 128].rearrange("(p o) -> p o", o=1), pr_)
        self.load(convb[:, :], self.w["conv_b"][l].rearrange("(c p) -> p c", p=128), pr_)
        self.load(ba[:, :], self.w["lru_ba"][l].rearrange("n d -> (n d)").rearrange("(c p) -> p c", p=128), pr_)
        self.load(bx[:, :], self.w["lru_bx"][l].rearrange("n d -> (n d)").rearrange("(c p) -> p c", p=128), pr_)
        self.load(lam[:, :], self.w["lru_lambda"][l].rearrange("(c p) -> p c", p=128), pr_)
        self.load(gl[:, :], self.w["lru_out_norm"][l].rearrange("(c p) -> p c", p=128), pr_)
        nsp = self.alloc([2], F32)
        nsp2 = self.alloc([2], F32)
        self.act(nsp[:, :], lam[:, :], AF.Exp, [pr_], [pr_], scale=-1.0)
        self.ts("dve", nsp[:, :], nsp[:, :], 1.0, None, ALU.add, None, [pr_], [pr_])
        self.act(nsp[:, :], nsp[:, :], AF.Ln, [pr_], [pr_])
        self.ts("dve", nsp2[:, :], nsp[:, :], -16.0, None, ALU.mult, None, [pr_], [pr_])
        self.ts("dve", nsp[:, :], nsp[:, :], -8.0, None, ALU.mult, None, [pr_], [pr_])
        bd32 = self.alloc([2, 2, 128], F32)
        bd = self.alloc([2, 2, 128], BF16)
        bd_r = Res("bd")
        P.add("pool", lambda e: e.memset(bd32[:, :, :, :], 0.0), writes=[bd_r])
        for wi, nm in enumerate(("lru_wa", "lru_wx")):
            for nb in range(4):
                c, hb = nb // 2, nb % 2
                self.load(bd32[hb * 64:(hb + 1) * 64, wi, c, hb * 64:(hb + 1) * 64], self.w[nm][l][nb, :, :], bd_r)
        self.cp("dve", bd[:, :, :, :], bd32[:, :, :, :], [bd_r], [bd_r])
        carry = self.alloc([2], F32)
        carry_r = Res("carry")
        NB = 515
        uar = Ring([self.alloc([2, NB], F32) for _ in range(2)], "ua")
        gar = Ring([self.alloc([2, 512], F32) for _ in range(2)], "ga")
        fr = Ring([self.alloc([512], F32) for _ in range(14)], "lf")
        ucb_r = Ring([self.alloc([512], BF16) for _ in range(2)], "ucb")
        sqr = Ring([self.alloc([512], BF16) for _ in range(2)], "sq")
        yp = self.alloc([2, 512], F32)
        yp_r = Res("yp")
        rstd = self.alloc([512], F32)
        rstd_r = Res("rstd")
        yst = Ring([self.alloc([2, 512], BF16) for _ in range(2)], "yst")
        psr = self.psring([0, 1, 2, 3], "lru")
        ps_n, ps_nr = self.psum[7], self.psum_r[7]
        first = True
        for (c0, n) in self.tiles9():
            ua, ua_r = uar.next()
            ga, ga_r = gar.next()
            if first:
                P.add("pool", lambda e, ua=ua: e.memset(ua[:, :, 0:3], 0.0), writes=[ua_r])
                self.load(ua[:, :, 3:3 + n], self.LRU_IN[2:4, :, c0:c0 + n].rearrange("c p t -> p c t"), ua_r)
            else:
                self.load(ua[:, :, 0:3 + n], self.LRU_IN[2:4, :, c0 - 3:c0 + n].rearrange("c p t -> p c t"), ua_r)
            self.load(ga[:, :, 0:n], self.LRU_IN[0:2, :, c0:c0 + n].rearrange("c p t -> p c t"), ga_r)
            def chunk(j, first=first, c0=c0, n=n, ua=ua, ua_r=ua_r, ga=ga, ga_r=ga_r):
                uc, uc_r = fr.next()
                yield
                self.ts("dve", uc[:, 0:n], ua[:, j, 0:n], convw[:, j, 0:1], convb[:, j:j + 1], ALU.mult, ALU.add,
                        [ua_r, pr_], [uc_r])
                for t in range(1, 4):
                    self.stt(uc[:, 0:n], ua[:, j, t:t + n], convw[:, j, t:t + 1], uc[:, 0:n], ALU.mult, ALU.add,
                             [ua_r, pr_, uc_r], [uc_r])
                ucb, ucb_rr = ucb_r.next()
                yield
                self.cp("act", ucb[:, 0:n], uc[:, 0:n], [uc_r], [ucb_rr])
                yield
                pa, par = psr.next()
                yield
                px, pxr = psr.next()
                yield
                self.mm(pa[:, 0:n], bd[:, 0, j, :], ucb[:, 0:n], True, True, [bd_r, ucb_rr], [par])
                yield
                self.mm(px[:, 0:n], bd[:, 1, j, :], ucb[:, 0:n], True, True, [bd_r, ucb_rr], [pxr])
                yield
                r, r_r = fr.next()
                yield
                ig, ig_r = fr.next()
                yield
                self.act(r[:, 0:n], pa[:, 0:n], AF.Sigmoid, [par, pr_], [r_r], bias=ba[:, j:j + 1])
                yield
                self.act(ig[:, 0:n], px[:, 0:n], AF.Sigmoid, [pxr, pr_], [ig_r], bias=bx[:, j:j + 1])
                yield
                a, a_r = fr.next()
                yield
                a2, a2_r = fr.next()
                yield
                self.act(a[:, 0:n], r[:, 0:n], AF.Exp, [r_r, pr_], [a_r], scale=nsp[:, j:j + 1])
                yield
                self.act(a2[:, 0:n], r[:, 0:n], AF.Exp, [r_r, pr_], [a2_r], scale=nsp2[:, j:j + 1])
                yield
                self.ts("pool", a2[:, 0:n], a2[:, 0:n], -1.0, 1.0, ALU.mult, ALU.add, [a2_r], [a2_r])
                yield
                self.act(a2[:, 0:n], a2[:, 0:n], AF.Sqrt, [a2_r], [a2_r])
                yield
                self.tt("pool", ig[:, 0:n], ig[:, 0:n], uc[:, 0:n], ALU.mult, [ig_r, uc_r], [ig_r])
                yield
                self.tt("pool", ig[:, 0:n], ig[:, 0:n], a2[:, 0:n], ALU.mult, [ig_r, a2_r], [ig_r])
                yield
                hs, hs_r = fr.next()
                yield
                if first:
                    P.add("dve", lambda e, hs=hs, a=a, ig=ig, n=n: e.tensor_tensor_scan(
                        out=hs[:, 0:n], data0=a[:, 0:n], data1=ig[:, 0:n], initial=0.0, op0=ALU.mult, op1=ALU.add),
                        reads=[a_r, ig_r], writes=[hs_r])
                else:
                    P.add("dve", lambda e, hs=hs, a=a, ig=ig, n=n, j=j: e.tensor_tensor_scan(
                        out=hs[:, 0:n], data0=a[:, 0:n], data1=ig[:, 0:n], initial=carry[:, j:j + 1],
                        op0=ALU.mult, op1=ALU.add),
                        reads=[a_r, ig_r, carry_r], writes=[hs_r])
                self.cp("dve", carry[:, j:j + 1], hs[:, n - 1:n], [hs_r], [carry_r])
                yield
                yield
                g = ga[:, j, 0:n]
                t, t_r = fr.next()
                yield
                self.tt("pool", t[:, 0:n], g, g, ALU.mult, [ga_r], [t_r])
                yield
                self.ts("pool", t[:, 0:n], t[:, 0:n], 0.044715, 1.0, ALU.mult, ALU.add, [t_r], [t_r])
                yield
                self.tt("pool", t[:, 0:n], t[:, 0:n], g, ALU.mult, [t_r, ga_r], [t_r])
                yield
                self.act(t[:, 0:n], t[:, 0:n], AF.Sigmoid, [t_r], [t_r], scale=1.5957691216057308)
                yield
                self.tt("pool", t[:, 0:n], t[:, 0:n], g, ALU.mult, [t_r, ga_r], [t_r])
                yield
                self.tt("dve", yp[:, j, 0:n], hs[:, 0:n], t[:, 0:n], ALU.mult, [hs_r, t_r], [yp_r])
                yield
                sq, sq_r = sqr.next()
                yield
                self.tt("pool", sq[:, 0:n], yp[:, j, 0:n], yp[:, j, 0:n], ALU.mult, [yp_r], [sq_r])
                yield
                self.mm(ps_n[:, 0:n], self.ones_bf[:, :], sq[:, 0:n], j == 0, j == 1, [sq_r, self.ones_r], [ps_nr])
                yield

            import itertools
            for _ in itertools.zip_longest(chunk(0), chunk(1)):
                pass
            self.rsqrt_inplace(rstd[:, 0:n], rstd_r, 1.0 / 256, in_ap=ps_n[:, 0:n], in_reads=[ps_nr])
            ys, ys_r = yst.next()
            for j in range(2):
                self.stt(ys[:, j, 0:n], yp[:, j, 0:n], gl[:, j:j + 1], rstd[:, 0:n], ALU.mult, ALU.mult,
                         [yp_r, pr_, rstd_r], [ys_r])
            self.store(self.YT[0:2, :, c0:c0 + n].rearrange("c p t -> p c t"), ys[:, :, 0:n], ys_r)
            first = False

    def attn_sweep(self, qi, Kt, Qt, r0, r1, kq_r, V, v_r, dvp, obanks, scale, s_ring, pt_ring, tri, tri_r):
        if qi < 0:
            q0, nq, nsub = 0, NMETA, 1
            chunks = [(0, 0, NMETA, 0)]
        else:
            q0, nq, nsub = NMETA + 512 * qi, 512, 4
            chunks = [(0, 0, NMETA, -1)] + [(1 + i, NMETA + 128 * i, 128, -1) for i in range(4 * qi)]
            chunks += [(1 + 4 * qi + c, NMETA + 128 * (4 * qi + c), 128, c) for c in range(4)]
        started = set()
        for (kc, kcol, nk, diag) in chunks:
            s0 = max(diag, 0)
            qa = q0 + 128 * s0 if qi >= 0 else 0
            nqa = nq - 128 * s0 if qi >= 0 else nq
            ps, ps_r = s_ring.next()
            self.mm(ps[0:nk, 0:nqa], Kt[r0:r1, kcol:kcol + nk], Qt[r0:r1, qa:qa + nqa], True, True, [kq_r], [ps_r])
            if diag >= 0:
                w = min(128, nqa)
                self.mm(ps[0:nk, 0:w], self.idb[0:nk, 0:nk], self.ntri[0:nk, 0:w], False, True, [self.idb_r, tri_r], [ps_r],
                        skip_group_check=True)
            pt, pt_r = pt_ring.next()
            self.act(pt[0:nk, 0:nqa], ps[0:nk, 0:nqa], AF.Exp, [ps_r], [pt_r], scale=scale)
            self.flush_pv()

            def pv(s0=s0, pt=pt, pt_r=pt_r, nk=nk, kc=kc):
                for s in range(s0, nsub):
                    ob, ob_r, oc = obanks[s]
                    ws = min(128, nq)
                    lo = (s - s0) * 128
                    st = id(ob) not in started
                    started.add(id(ob))
                    self.mm(ob[0:ws, oc:oc + dvp], pt[0:nk, lo:lo + ws], V[0:nk, kc, 0:dvp], st, False,
                            [pt_r, v_r], [ob_r], skip_group_check=True)
            self._pend = pv
            self.run_evac()
            self.run_deferred(3)

    _pend = None
    _deferred = None

    def defer(self, fn):
        if self._deferred is None:
            self._deferred = []
        self._deferred.append(fn)

    def evac(self, fn):
        if getattr(self, "_evac", None) is None:
            self._evac = []
        self._evac.append(fn)

    def run_evac(self):
        q = getattr(self, "_evac", None) or []
        while q:
            q.pop(0)()

    def run_deferred(self, k=None):
        d = self._deferred or []
        n = len(d) if k is None else min(k, len(d))
        for _ in range(n):
            d.pop(0)()

    def flush_pv(self):
        if self._pend is not None:
            p = self._pend
            self._pend = None
            p()

    def load_consts_attn(self):
        tri32 = self.alloc([128], F32)
        tri = self.alloc([128], BF16)
        tri_r = Res("tri")
        self.load(tri32[:, :], self.tri_in[:, :], tri_r)
        self.cp("dve", tri[:, :], tri32[:, :], [tri_r], [tri_r])
        idb = self.alloc([128], BF16)
        idb_r = Res("idb")
        self.cp("dve", idb[:, :], self.identf[:, :], [self.identf_r], [idb_r])
        ntri = self.alloc([128], BF16)
        self.ts("dve", ntri[:, :], tri32[:, :], -1.0, 30000.0, ALU.add, ALU.mult, [tri_r], [tri_r])
        self.ntri, self.idb, self.idb_r = ntri, idb, idb_r
        return tri, tri_r, idb, idb_r

    def stage_diff(self, l):
        self.stage_begin()
        P = self.P
        lam_init = 0.8 - 0.6 * math.exp(-0.3 * l)
        tri, tri_r, idb, idb_r = self.load_consts_attn()
        lv = self.alloc([4, 64], F32)
        lv_r = Res("lv")
        for i, nm in enumerate(("lam_q1", "lam_k1", "lam_q2", "lam_k2")):
            self.load(lv[:, i, :], self.w[nm][l].partition_broadcast(128), lv_r)
        lsc = self.alloc([8], F32)
        self.tt("dve", lv[:, 0, :], lv[:, 0, :], lv[:, 1, :], ALU.mult, [lv_r], [lv_r])
        self.tt("dve", lv[:, 2, :], lv[:, 2, :], lv[:, 3, :], ALU.mult, [lv_r], [lv_r])
        P.add("dve", lambda e: e.reduce_sum(out=lsc[:, 0:1], in_=lv[:, 0, :], axis=AX.X), reads=[lv_r], writes=[lv_r])
        P.add("dve", lambda e: e.reduce_sum(out=lsc[:, 1:2], in_=lv[:, 2, :], axis=AX.X), reads=[lv_r], writes=[lv_r])
        self.act(lsc[:, 0:2], lsc[:, 0:2], AF.Exp, [lv_r], [lv_r])
        self.tt("dve", lsc[:, 2:3], lsc[:, 1:2], lsc[:, 0:1], ALU.subtract, [lv_r], [lv_r])
        self.ts("dve", lsc[:, 2:3], lsc[:, 2:3], -lam_init, None, ALU.add, None, [lv_r], [lv_r])
        gsub = self.alloc([128], F32)
        self.load(gsub[:, :], self.w["diff_subln"][l].partition_broadcast(128), lv_r)
        self.ts("dve", gsub[:, :], gsub[:, :], 1.0 - lam_init, None, ALU.mult, None, [lv_r], [lv_r])

        Kt = self.alloc([T], BF16)
        Qm = [self.alloc([T], BF16) for _ in range(2)]
        kq_r = Res("kq")
        P.add("pool", lambda e: e.memset(Qm[0][64:128, :], 0.0), writes=[kq_r])
        P.add("pool", lambda e: e.memset(Qm[1][0:64, :], 0.0), writes=[kq_r])
        V = self.alloc([33, 129], BF16)
        v_r = Res("V")
        pt_ring = Ring([self.alloc([512], BF16) for _ in range(3)], "pt")
        o1r = Ring([self.alloc([128], F32) for _ in range(8)], "o1")
        o2r = Ring([self.alloc([128], F32) for _ in range(8)], "o2")
        scr = Ring([self.alloc([128], F32) for _ in range(2)], "scr")
        smr = Ring([self.alloc([8], F32) for _ in range(8)], "sm")
        ybr = Ring([self.alloc([128], BF16) for _ in range(4)], "yb")
        ytr = Ring([self.alloc([512], BF16) for _ in range(3)], "yts")
        s_ring = self.psring([0, 1, 2], "s")
        tps = self.psum[7][:, :].bitcast(BF16)
        tps_r = self.psum_r[7]
        cb = self.conv_bufs()
        n_slots = 4 * (1 + SEQ // 512)
        need_now = [j for j in self.conv_queue if j[0] == l or (j[0] == l + 1 and j[1] == 1)]
        slot_i = 0
        for h in range(4):
            self.flush_pv()
            self.run_evac()
            self.load(Qm[0][0:64, :], self.QKD[h, 0:64, :], kq_r)
            self.load(Qm[1][64:128, :], self.QKD[h, 64:128, :], kq_r)
            self.load(Kt[:, :], self.QKD[4 + h, :, :], kq_r)
            P.add("pool", lambda e: e.memset(V[:, :, 128:129], 1.0), writes=[v_r])
            self.load(V[0:NMETA, 0, 0:128], self.VD[0:NMETA, h * 128:(h + 1) * 128], v_r)
            for i0 in range(0, 32, 8):
                self.load(V[:, 1 + i0:9 + i0, 0:128],
                          self.VD[NMETA + 128 * i0:NMETA + 128 * (i0 + 8), h * 128:(h + 1) * 128].rearrange("(i p) d -> p i d", p=128), v_r)
            for qi in range(-1, SEQ // 512):
                nsub = 1 if qi < 0 else 4
                ws = NMETA if qi < 0 else 128
                q0 = 0 if qi < 0 else NMETA + 512 * qi
                ob = {}
                for m in range(2):
                    banks = [(self.psum[3 + 2 * m], self.psum_r[3 + 2 * m], 0), (self.psum[3 + 2 * m], self.psum_r[3 + 2 * m], 129),
                             (self.psum[4 + 2 * m], self.psum_r[4 + 2 * m], 0), (self.psum[4 + 2 * m], self.psum_r[4 + 2 * m], 129)]
                    ob[m] = banks
                    self.attn_sweep(qi, Kt, Qm[m], 0, 128, kq_r, V, v_r, 129, banks, 0.125, s_ring, pt_ring, tri, tri_r)
                    if m == 0:
                        subs_bufs = [(smr.next(), o1r.next(), o2r.next()) for _ in range(nsub)]
                    for s_ in range(nsub):
                        (sm, sm_r), o1p, o2p = subs_bufs[s_]
                        o, o_r = o1p if m == 0 else o2p
                        b, b_r, oc = banks[s_]

                        def ev(sm=sm, sm_r=sm_r, o=o, o_r=o_r, b=b, b_r=b_r, oc=oc, m=m, ws=ws):
                            P.add("dve", lambda e: e.reciprocal(out=sm[0:ws, m:m + 1], in_=b[0:ws, oc + 128:oc + 129]),
                                  reads=[b_r], writes=[sm_r])
                            self.ts("dve", o[0:ws, :], b[0:ws, oc:oc + 128], sm[0:ws, m:m + 1], None, ALU.mult, None,
                                    [b_r, sm_r], [o_r])
                        self.evac(ev)
                left = n_slots - slot_i
                k = -(-len(need_now) // left) if left > 0 else len(need_now)
                for _ in range(k):
                    if need_now:
                        job = need_now.pop(0)
                        self.conv_queue.remove(job)
                        self.conv_job(job, cb)
                slot_i += 1
                yts, yts_r = ytr.next()
                for s in range(nsub):
                    (sm, sm_r), (o1, o1_r), (o2, o2_r) = subs_bufs[s]

                    def tail(s=s, sm=sm, sm_r=sm_r, o1=o1, o1_r=o1_r, o2=o2, o2_r=o2_r, ws=ws, yts=yts, yts_r=yts_r):
                        sc, sc_r = scr.next()
                        yb, yb_r = ybr.next()
                        return [
                            lambda: self.stt(o1[0:ws, :], o2[0:ws, :], lsc[0:ws, 2:3], o1[0:ws, :], ALU.mult, ALU.add,
                                             [o1_r, o2_r, lv_r], [o1_r]),
                            lambda: self.tt("dve", sc[0:ws, :], o1[0:ws, :], o1[0:ws, :], ALU.mult, [o1_r], [sc_r]),
                            lambda: P.add("dve", lambda e: e.reduce_sum(out=sm[0:ws, 2:3], in_=sc[0:ws, :], axis=AX.X),
                                          reads=[sc_r], writes=[sm_r]),
                            lambda: self.ts("dve", sm[0:ws, 2:3], sm[0:ws, 2:3], 1.0 / 128, EPS, ALU.mult, ALU.add, [sm_r], [sm_r]),
                            lambda: self.act(sm[0:ws, 2:3], sm[0:ws, 2:3], AF.Ln, [sm_r], [sm_r]),
                            lambda: self.act(sm[0:ws, 2:3], sm[0:ws, 2:3], AF.Exp, [sm_r], [sm_r], scale=-0.5),
                            lambda: self.stt(yb[0:ws, :], o1[0:ws, :], sm[0:ws, 2:3], gsub[0:ws, :], ALU.mult, ALU.mult,
                                             [o1_r, sm_r, lv_r], [yb_r]),
                            lambda: P.add("pe", lambda e: e.transpose(out=tps[:, s * 128:s * 128 + ws], in_=yb[0:ws, :],
                                                                       identity=idb[0:ws, 0:ws]),
                                          reads=[yb_r, idb_r], writes=[tps_r]),
                            lambda: self.cp("dve", yts[:, s * 128:s * 128 + ws], tps[:, s * 128:s * 128 + ws], [tps_r], [yts_r]),
                        ]
                    for fn in tail():
                        self.defer(fn)
                nq = NMETA if qi < 0 else 512
                self.defer(lambda h=h, q0=q0, nq=nq, yts=yts, yts_r=yts_r: self.store(self.YT[2 + h, :, q0:q0 + nq], yts[:, 0:nq], yts_r))
        self.flush_pv()
        self.run_evac()
        self.run_deferred()

    def stage_mla(self, l):
        self.stage_begin()
        P = self.P
        tri, tri_r, idb, idb_r = self.load_consts_attn()
        gm = self.alloc([256], F32)
        gm_r = Res("gm")
        self.load(gm[:, :], self.w["mla_out_norm"][l].partition_broadcast(128), gm_r)
        Kt = self.alloc([4, T], BF16)
        kq_r = Res("kq")
        V = self.alloc([33, 4, 65], BF16)
        v_r = Res("V")
        self.load(Kt[0:96, :, :], self.KM[:, :, :].rearrange("h r t -> r h t"), kq_r)
        Vf = V.rearrange("p i h d -> p i (h d)")
        self.load(Vf[0:NMETA, 0, :], self.VM[0:NMETA, :], v_r)
        for i0 in range(0, 32, 8):
            self.load(Vf[:, 1 + i0:9 + i0, :],
                      self.VM[NMETA + 128 * i0:NMETA + 128 * (i0 + 8), :].rearrange("(i p) d -> p i d", p=128), v_r)
        qtr = Ring([self.alloc([4, 512], BF16) for _ in range(2)], "qt")
        pt_ring = Ring([self.alloc([512], BF16) for _ in range(3)], "pt")
        ocr = Ring([self.alloc([256], F32) for _ in range(5)], "oc")
        scr = Ring([self.alloc([256], F32) for _ in range(1)], "scr")
        smr = Ring([self.alloc([8], F32) for _ in range(8)], "sm")
        ybr = Ring([self.alloc([256], BF16) for _ in range(3)], "yb")
        ytr = Ring([self.alloc([2, 512], BF16) for _ in range(2)], "yts")
        s_ring = self.psring([0, 1, 2], "s")
        tps = self.psum[7][:, :].bitcast(BF16)
        tps_r = self.psum_r[7]
        scale = 96.0 ** -0.5
        for qi in range(-1, SEQ // 512):
            nsub = 1 if qi < 0 else 4
            ws = NMETA if qi < 0 else 128
            q0 = 0 if qi < 0 else NMETA + 512 * qi
            nq = NMETA if qi < 0 else 512
            qt, qt_r = qtr.next()
            self.load(qt[0:96, :, 0:nq], self.QM[:, :, q0:q0 + nq].rearrange("h r t -> r h t"), qt_r)
            ob = {}
            for h in range(4):
                banks = [(self.psum[3 + h], self.psum_r[3 + h], 65 * s) for s in range(4)]
                ob[h] = banks
                self.attn_sweep_local(qi, Kt[:, h, :], qt[:, h, :], kq_r, qt_r, V[:, :, h, :], v_r, 65, banks, scale,
                                      s_ring, pt_ring, tri, tri_r)
                if h == 0:
                    subs_bufs = [(smr.next(), ocr.next()) for _ in range(nsub)]
                for s_ in range(nsub):
                    (sm, sm_r), (oc, oc_r) = subs_bufs[s_]
                    b, b_r, c = banks[s_]

                    def ev(sm=sm, sm_r=sm_r, oc=oc, oc_r=oc_r, b=b, b_r=b_r, c=c, h=h, ws=ws):
                        P.add("dve", lambda e: e.reciprocal(out=sm[0:ws, h:h + 1], in_=b[0:ws, c + 64:c + 65]),
                              reads=[b_r], writes=[sm_r])
                        self.ts("dve", oc[0:ws, h * 64:(h + 1) * 64], b[0:ws, c:c + 64], sm[0:ws, h:h + 1], None, ALU.mult, None,
                                [b_r, sm_r], [oc_r])
                    self.evac(ev)
            yts, yts_r = ytr.next()
            for s in range(nsub):
                (sm, sm_r), (oc, oc_r) = subs_bufs[s]

                def tail(s=s, sm=sm, sm_r=sm_r, oc=oc, oc_r=oc_r, ws=ws, yts=yts, yts_r=yts_r):
                    sc, sc_r = scr.next()
                    yb, yb_r = ybr.next()
                    fns = [
                        lambda: self.tt("dve", sc[0:ws, :], oc[0:ws, :], oc[0:ws, :], ALU.mult, [oc_r], [sc_r]),
                        lambda: P.add("dve", lambda e: e.reduce_sum(out=sm[0:ws, 4:5], in_=sc[0:ws, :], axis=AX.X),
                                      reads=[sc_r], writes=[sm_r]),
                        lambda: self.ts("dve", sm[0:ws, 4:5], sm[0:ws, 4:5], 1.0 / 256, EPS, ALU.mult, ALU.add, [sm_r], [sm_r]),
                        lambda: self.act(sm[0:ws, 4:5], sm[0:ws, 4:5], AF.Ln, [sm_r], [sm_r]),
                        lambda: self.act(sm[0:ws, 4:5], sm[0:ws, 4:5], AF.Exp, [sm_r], [sm_r], scale=-0.5),
                        lambda: self.stt(yb[0:ws, :], oc[0:ws, :], sm[0:ws, 4:5], gm[0:ws, :], ALU.mult, ALU.mult,
                                         [oc_r, sm_r, gm_r], [yb_r]),
                    ]
                    for c in range(2):
                        fns.append(lambda c=c: P.add("pe", lambda e: e.transpose(
                            out=tps[:, c * 512 + s * 128:c * 512 + s * 128 + ws], in_=yb[0:ws, c * 128:(c + 1) * 128],
                            identity=idb[0:ws, 0:ws]), reads=[yb_r, idb_r], writes=[tps_r]))
                        fns.append(lambda c=c: self.cp("dve", yts[:, c, s * 128:s * 128 + ws],
                                                       tps[:, c * 512 + s * 128:c * 512 + s * 128 + ws], [tps_r], [yts_r]))
                    return fns
                for fn in tail():
                    self.defer(fn)
            self.defer(lambda q0=q0, nq=nq, yts=yts, yts_r=yts_r: self.store(
                self.YT[6:8, :, q0:q0 + nq].rearrange("c p t -> p c t"), yts[:, :, 0:nq], yts_r))
        self.flush_pv()
        self.run_evac()
        self.run_deferred()

    def attn_sweep_local(self, qi, Kt, Qtile, kq_r, qt_r, V, v_r, dvp, obanks, scale, s_ring, pt_ring, tri, tri_r):
        if qi < 0:
            nq, nsub = NMETA, 1
            chunks = [(0, 0, NMETA, 0)]
        else:
            nq, nsub = 512, 4
            chunks = [(0, 0, NMETA, -1)] + [(1 + i, NMETA + 128 * i, 128, -1) for i in range(4 * qi)]
            chunks += [(1 + 4 * qi + c, NMETA + 128 * (4 * qi + c), 128, c) for c in range(4)]
        started = set()
        for (kc, kcol, nk, diag) in chunks:
            s0 = max(diag, 0)
            qa = 128 * s0 if qi >= 0 else 0
            nqa = nq - qa
            ps, ps_r = s_ring.next()
            self.mm(ps[0:nk, 0:nqa], Kt[0:96, kcol:kcol + nk], Qtile[0:96, qa:qa + nqa], True, True, [kq_r, qt_r], [ps_r])
            if diag >= 0:
                w = min(128, nqa)
                self.mm(ps[0:nk, 0:w], self.idb[0:nk, 0:nk], self.ntri[0:nk, 0:w], False, True, [self.idb_r, tri_r], [ps_r],
                        skip_group_check=True)
            pt, pt_r = pt_ring.next()
            self.act(pt[0:nk, 0:nqa], ps[0:nk, 0:nqa], AF.Exp, [ps_r], [pt_r], scale=scale)
            self.flush_pv()

            def pv(s0=s0, pt=pt, pt_r=pt_r, nk=nk, kc=kc):
                for s in range(s0, nsub):
                    ob, ob_r, oc = obanks[s]
                    ws = min(128, nq)
                    lo = (s - s0) * 128
                    st = id(ob) not in started
                    started.add(id(ob))
                    self.mm(ob[0:ws, oc:oc + dvp], pt[0:nk, lo:lo + ws], V[0:nk, kc, 0:dvp], st, False,
                            [pt_r, v_r], [ob_r], skip_group_check=True)
            self._pend = pv
            self.run_evac()
            self.run_deferred(3)

    def stage_oproj(self, l):
        self.stage_begin()
        Wo = self.alloc([NCH, D], BF16)
        Wo_r = Res("Wo")
        self.load_w_bf16(Wo, Wo_r, self.w["w_out"][l], D, slab=256)
        ytr = Ring([self.alloc([NCH, 512], BF16) for _ in range(2)], "yt")
        psr = self.psring([0, 1, 2, 3], "op")
        for (c0, n) in self.tiles9():
            yt, yt_r = ytr.next()
            self.load(yt[:, :, 0:n], self.YT[:, :, c0:c0 + n].rearrange("c p t -> p c t"), yt_r)
            for o in range(NCH):
                po, por = psr.next()
                for k in range(NCH):
                    self.mm(po[:, 0:n], Wo[:, k, o * 128:(o + 1) * 128], yt[:, k, 0:n], k == 0, k == NCH - 1, [Wo_r, yt_r], [por])
                hr = self.h_res(o, c0, n)
                self.stt(self.hT[:, o, c0:c0 + n], po[:, 0:n], 1.0, self.hT[:, o, c0:c0 + n], ALU.mult, ALU.add,
                         [por] + hr, hr)


    def stage_final(self):
        P = self.P
        self.stage_begin()
        gbc = self.alloc([D], F32)
        gbc_r = Res("gbc")
        P.add("sp", lambda e: e.dma_start(out=gbc[:, :], in_=self.final_norm.partition_broadcast(128)),
              writes=[gbc_r], dma=True, sem_res=gbc_r)
        otr = Ring([self.alloc([D], F32) for i in range(2)], "ot")
        sq = self.alloc([512], F32)
        sq_r = Res("fsq")
        ssr = Ring([self.alloc([4], F32) for i in range(2)], "fss")
        psr = self.psring([0, 1, 2, 3], "f")
        for i in range(SEQ // 128):
            c0 = NMETA + 128 * i
            o, o_r = otr.next()
            s, s_r = ssr.next()
            halves = []
            for half in range(2):
                pt, pr = psr.next()
                for cc in range(4):
                    c = half * 4 + cc
                    P.add("pe", lambda e, pt=pt, cc=cc, c=c, c0=c0: e.transpose(
                        out=pt[:, cc * 128:(cc + 1) * 128], in_=self.hT[:, c, c0:c0 + 128],
                        identity=self.identf[:, :]),
                        reads=self.h_res(c, c0, 128) + [self.identf_r], writes=[pr])
                halves.append((pt, pr))
            for half, (pt, pr) in enumerate(halves):
                P.add("act", lambda e, pt=pt, s=s, half=half: e.activation(
                    out=sq[:, 0:512], in_=pt[:, :], func=AF.Square, accum_out=s[:, half:half + 1]),
                    reads=[pr], writes=[sq_r, s_r])
            P.add("dve", lambda e, s=s: e.tensor_tensor(out=s[:, 2:3], in0=s[:, 0:1], in1=s[:, 1:2], op=ALU.add),
                  reads=[s_r], writes=[s_r])
            P.add("dve", lambda e, s=s: e.tensor_scalar(out=s[:, 2:3], in0=s[:, 2:3], scalar1=1.0 / D, scalar2=EPS,
                                                         op0=ALU.mult, op1=ALU.add),
                  reads=[s_r], writes=[s_r])
            P.add("act", lambda e, s=s: e.activation(out=s[:, 3:4], in_=s[:, 2:3], func=AF.Sqrt),
                  reads=[s_r], writes=[s_r])
            P.add("dve", lambda e, s=s: e.reciprocal(out=s[:, 3:4], in_=s[:, 3:4]),
                  reads=[s_r], writes=[s_r])
            for half, (pt, pr) in enumerate(halves):
                P.add("dve", lambda e, pt=pt, s=s, o=o, half=half: e.scalar_tensor_tensor(
                    out=o[:, half * 512:(half + 1) * 512], in0=pt[:, :], scalar=s[:, 3:4],
                    in1=gbc[:, half * 512:(half + 1) * 512], op0=ALU.mult, op1=ALU.mult),
                    reads=[pr, s_r, gbc_r], writes=[o_r])
            P.add("sp", lambda e, o=o, i=i: e.dma_start(out=self.out[128 * i:128 * (i + 1), :], in_=o[:, :]),
                  reads=[o_r], dma=True, sem_res=o_r)
        self.P.barrier()


def build_program(n_layers=DEPTH, debug=None):
    b = Builder(n_layers, debug)
    nc = b.build()
    return nc, b


_CONSTS = None


def consts():
    global _CONSTS
    if _CONSTS is None:
        half = 16
        inv = (np.float32(10000.0) ** (-(np.arange(half, dtype=np.float32)) / np.float32(half))).astype(np.float32)
        ang = (np.arange(T, dtype=np.float32)[None, :] * inv[:, None]).astype(np.float32).astype(np.float64)
        cs = np.zeros((96, 2, T), np.float32)
        cs[0:64, 0, :] = 1.0
        cs[64:80, 0, :] = np.cos(ang)
        cs[80:96, 0, :] = np.cos(ang)
        cs[64:80, 1, :] = -np.sin(ang)
        cs[80:96, 1, :] = np.sin(ang)
        tri = (np.arange(128)[:, None] <= np.arange(128)[None, :]).astype(np.float32)
        _CONSTS = {"ident_f": np.eye(128, dtype=np.float32), "rope_cs": cs, "tri_in": tri}
    return _CONSTS


def make_in_maps(inputs, b):
    c = consts()
    maps = []
    for core in range(8):
        m = {}
        for nm in b.inputs:
            if nm == "x":
                m[nm] = np.ascontiguousarray(inputs["x"][core])
            elif nm in c:
                m[nm] = c[nm]
            else:
                m[nm] = np.ascontiguousarray(inputs[nm])
        maps.append(m)
    return maps


def kernel(**inputs):
    inputs = {k: np.asarray(v) for k, v in inputs.items()}
    nc, b = build_program()
    maps = make_in_maps(inputs, b)
    res = run_bass_kernel_spmd(nc, maps, core_ids=list(range(8)))
    out = np.stack([np.asarray(res.results[i]["out"]) for i in range(8)], axis=0)
    return out.astype(np.float32)
```
